# Optimizing a Trainium2 kernel written in Bass

```python
import math
import jax, jax.numpy as jnp
from jax import lax
import numpy as np

D_MODEL = 1024
BATCH = 16
SEQ = 4096
DEPTH = 2

GRID_W = 64
CTX_LEN = 256
EPS = 1e-6

MLA_WIDTH = D_MODEL // 2
POOL_WIDTH = D_MODEL // 4
HY_WIDTH = D_MODEL // 4

MLA_HEADS = 8
MLA_V = MLA_WIDTH // MLA_HEADS
MLA_NOPE = 64
MLA_ROPE = 32
MLA_Q_RANK = D_MODEL // 4
MLA_KV_RANK = D_MODEL // 8
MLA_SCALE = (MLA_NOPE + MLA_ROPE) ** -0.5
ROPE_BASE = 10000.0
Q_BLOCK = 128

POOL_WINDOWS = (2, 4, 8, 16)
POOL_GROUP = POOL_WIDTH // len(POOL_WINDOWS)

HY_ORDER = 2
HY_EMB = 33
HY_FFN = 64
HY_DECAY_TARGET = 1e-2
HY_DECAY_SHORT_PCT = 0.3
HY_DECAY_LONG_PCT = 1.5

D_FF = 4 * D_MODEL

COL_KV = 0
COL_KR = COL_KV + MLA_KV_RANK
COL_Q = COL_KR + MLA_ROPE
COL_POOL = COL_Q + MLA_Q_RANK
COL_HY = COL_POOL + POOL_WIDTH
N_IN = COL_HY + (HY_ORDER + 1) * HY_WIDTH

kernel_name = "hybrid_mla_pool_hyena_dit_prefix"


def rmsnorm(x, g):
    x32 = x.astype(jnp.float32)
    y = x32 * lax.rsqrt(jnp.mean(x32 * x32, axis=-1, keepdims=True) + EPS)
    return (y * g.astype(jnp.float32)).astype(x.dtype)


def modulate(x, g, shift, scale):
    return rmsnorm(x, g) * (1 + scale) + shift


def adaln(cond, w, b):
    return jnp.split(jax.nn.silu(cond) @ w + b, 6, axis=-1)


def axial_rope_tables(n_tokens, dtype):
    n_rows = n_tokens // GRID_W
    r, cidx = jnp.meshgrid(jnp.arange(n_rows), jnp.arange(GRID_W), indexing="ij")
    r = r.reshape(-1).astype(jnp.float32)
    cidx = cidx.reshape(-1).astype(jnp.float32)
    n_freq = MLA_ROPE // 4
    inv = ROPE_BASE ** (-jnp.arange(n_freq, dtype=jnp.float32) / n_freq)
    ang = jnp.stack([r[:, None] * inv, cidx[:, None] * inv], axis=1)
    return jnp.cos(ang).astype(dtype), jnp.sin(ang).astype(dtype)


def apply_rope(x, cos, sin):
    xr = x.reshape(x.shape[:-1] + (2, 2, MLA_ROPE // 4))
    x1 = xr[..., 0, :]
    x2 = xr[..., 1, :]
    out = jnp.stack([x1 * cos - x2 * sin, x2 * cos + x1 * sin], axis=-2)
    return out.reshape(x.shape)


def mla_kv(p_kv, g_kv, w_kv_up):
    kv = rmsnorm(p_kv, g_kv) @ w_kv_up
    kv = kv.reshape(kv.shape[:-1] + (MLA_HEADS, MLA_NOPE + MLA_V))
    return kv[..., :MLA_NOPE], kv[..., MLA_NOPE:]


def mla_q(p_q, g_q, w_q_up):
    q = rmsnorm(p_q, g_q) @ w_q_up
    q = q.reshape(q.shape[:-1] + (MLA_HEADS, MLA_NOPE + MLA_ROPE))
    return q[..., :MLA_NOPE], q[..., MLA_NOPE:]


def attend(qn, qr, kn, kr, v):
    s = jnp.einsum("bqhd,bkhd->bhqk", qn, kn) + jnp.einsum("bqhr,bkr->bhqk", qr, kr)
    p = jax.nn.softmax(s.astype(jnp.float32) * MLA_SCALE, axis=-1).astype(v.dtype)
    return jnp.einsum("bhqk,bkhd->bqhd", p, v)


def blocked_attend(qn, qr, kn, kr, v):
    b, n = qn.shape[:2]
    nb = n // Q_BLOCK

    def split(t):
        return t.reshape((b, nb, Q_BLOCK) + t.shape[2:]).swapaxes(0, 1)

    out = lax.map(lambda qs: attend(qs[0], qs[1], kn, kr, v), (split(qn), split(qr)))
    return out.swapaxes(0, 1).reshape(b, n, MLA_HEADS * MLA_V)


def pool_mixer(u, pool_w, pool_scale):
    n = u.shape[1]
    t = jnp.arange(n)
    cs = jnp.pad(jnp.cumsum(u.astype(jnp.float32), axis=1), ((0, 0), (1, 0), (0, 0)))
    outs = []
    for g, w in enumerate(POOL_WINDOWS):
        lo = jnp.clip(t - w // 2, 0, n)
        hi = jnp.clip(t + w // 2, 0, n)
        sl = slice(g * POOL_GROUP, (g + 1) * POOL_GROUP)
        csg = cs[..., sl]
        mean = (csg[:, hi] - csg[:, lo]) / (hi - lo).astype(jnp.float32)[:, None]
        outs.append((mean.astype(u.dtype) - u[..., sl]) @ pool_w[g])
    return jnp.concatenate(outs, axis=-1) * pool_scale


def short_conv(u, w, b):
    n = u.shape[1]
    up = jnp.pad(u, ((0, 0), (1, 1), (0, 0)))
    return up[:, :n] * w[0] + up[:, 1:n + 1] * w[1] + up[:, 2:] * w[2] + b


def hyena_filter_spectra(n, p):
    f32 = jnp.float32
    t = jnp.linspace(0.0, 1.0, n, dtype=f32)[:, None]
    bands = (HY_EMB - 1) // 2
    freqs = jnp.linspace(1e-4, bands - 1, bands, dtype=f32)[None, :]
    wpos = 2.0 * math.pi * jnp.arange(n, dtype=f32)[:, None] / n
    z = jnp.concatenate([t, jnp.cos(freqs * wpos), -jnp.sin(freqs * wpos)], axis=-1)
    h = jnp.sin(p["hy_f_freq1"].astype(f32) * (z @ p["hy_f_w1"].astype(f32) + p["hy_f_b1"].astype(f32)))
    h = jnp.sin(p["hy_f_freq2"].astype(f32) * (h @ p["hy_f_w2"].astype(f32) + p["hy_f_b2"].astype(f32)))
    h = (h @ p["hy_f_w3"].astype(f32)).reshape(n, HY_ORDER, 2, HY_WIDTH)
    deltas = jnp.abs(jnp.linspace(math.log(HY_DECAY_TARGET) / HY_DECAY_LONG_PCT,
                                  math.log(HY_DECAY_TARGET) / HY_DECAY_SHORT_PCT, HY_WIDTH, dtype=f32))
    h = h * jnp.exp(-t[:, :, None] * deltas)[:, :, None, :]
    fwd = h[:, :, 0]
    bwd = h[1:, :, 1][::-1]
    circ = jnp.concatenate([fwd, jnp.zeros((1, HY_ORDER, HY_WIDTH), f32), bwd], axis=0)
    circ = circ / jnp.sum(jnp.abs(circ), axis=0, keepdims=True)
    return jnp.fft.rfft(circ, axis=0)


def fftconv(u, spec, bias):
    n = u.shape[1]
    u32 = u.astype(jnp.float32)
    y = jnp.fft.irfft(jnp.fft.rfft(u32, n=2 * n, axis=1) * spec, n=2 * n, axis=1)[:, :n]
    return (y + u32 * bias.astype(jnp.float32)).astype(u.dtype)


def hyena_mixer(u, p):
    uc = short_conv(u, p["hy_conv_w"], p["hy_conv_b"])
    v, x1, x2 = jnp.split(uc, 3, axis=-1)
    spec = hyena_filter_spectra(u.shape[1], p)
    z = x1 * fftconv(v, spec[:, 0], p["hy_bias"][0])
    return x2 * fftconv(z, spec[:, 1], p["hy_bias"][1])


def merge_groups(proj, attn, p):
    pool = pool_mixer(proj[..., COL_POOL:COL_HY], p["pool_w"], p["pool_scale"])
    hy = hyena_mixer(proj[..., COL_HY:], p)
    g = p["g_out"]
    parts = [rmsnorm(attn, g[:MLA_WIDTH]),
             rmsnorm(pool, g[MLA_WIDTH:MLA_WIDTH + POOL_WIDTH]),
             rmsnorm(hy, g[MLA_WIDTH + POOL_WIDTH:])]
    return jnp.concatenate(parts, axis=-1) @ p["w_out"]


def sq_relu_mlp(h, w1, w2):
    return jnp.square(jax.nn.relu(h @ w1)) @ w2


def hybrid_layer(x, xc, c, c_ctx, p, last):
    sh1, sc1, g1, sh2, sc2, g2 = [m[:, None, :] for m in adaln(c, p["w_mod"], p["b_mod"])]
    csh1, csc1, cg1, csh2, csc2, cg2 = adaln(c_ctx, p["w_mod"], p["b_mod"])
    h = modulate(x, p["g_mix"], sh1, sc1)
    hc = modulate(xc, p["g_mix"], csh1, csc1)
    proj = h @ p["w_in"]
    proj_c = hc @ (p["w_in"][:, :COL_Q] if last else p["w_in"])
    cos, sin = axial_rope_tables(x.shape[1], x.dtype)
    kn_c, v_c = mla_kv(proj_c[..., COL_KV:COL_KR], p["g_kv"], p["w_kv_up"])
    kr_c = proj_c[..., COL_KR:COL_Q]
    kn_l, v_l = mla_kv(proj[..., COL_KV:COL_KR], p["g_kv"], p["w_kv_up"])
    kr_l = apply_rope(proj[..., COL_KR:COL_Q], cos, sin)
    qn, qr = mla_q(proj[..., COL_Q:COL_POOL], p["g_q"], p["w_q_up"])
    qr = apply_rope(qr, cos[:, None], sin[:, None])
    attn = blocked_attend(qn, qr,
                          jnp.concatenate([kn_c, kn_l], axis=1),
                          jnp.concatenate([kr_c, kr_l], axis=1),
                          jnp.concatenate([v_c, v_l], axis=1))
    x = x + g1 * merge_groups(proj, attn, p)
    x = x + g2 * sq_relu_mlp(modulate(x, p["g_mlp"], sh2, sc2), p["w_mlp1"], p["w_mlp2"])
    if last:
        return x, xc
    qn_c, qr_c = mla_q(proj_c[..., COL_Q:COL_POOL], p["g_q"], p["w_q_up"])
    attn_c = attend(qn_c, qr_c, kn_c, kr_c, v_c).reshape(xc.shape[0], xc.shape[1], MLA_WIDTH)
    xc = xc + cg1 * merge_groups(proj_c, attn_c, p)
    xc = xc + cg2 * sq_relu_mlp(modulate(xc, p["g_mlp"], csh2, csc2), p["w_mlp1"], p["w_mlp2"])
    return x, xc


def setup_inputs(seed: int = 0) -> dict:
    key = jax.random.key(seed)
    ks = iter(jax.random.split(key, 40))

    def nrm(shape, scale=1.0):
        return jax.random.normal(next(ks), shape, jnp.float32) * scale

    def gain(shape):
        return 1.0 + nrm(shape, 0.05)

    L = DEPTH
    return {
        "x": nrm((BATCH, SEQ, D_MODEL)),
        "c": nrm((BATCH, D_MODEL)),
        "ctx": nrm((BATCH, CTX_LEN, D_MODEL)),
        "c_ctx": nrm((D_MODEL,)),
        "w_mod": nrm((L, D_MODEL, 6 * D_MODEL), 0.5 * D_MODEL ** -0.5),
        "b_mod": nrm((L, 6 * D_MODEL), 0.02),
        "g_mix": gain((L, D_MODEL)),
        "g_mlp": gain((L, D_MODEL)),
        "w_in": nrm((L, D_MODEL, N_IN), D_MODEL ** -0.5),
        "g_q": gain((L, MLA_Q_RANK)),
        "w_q_up": nrm((L, MLA_Q_RANK, MLA_HEADS * (MLA_NOPE + MLA_ROPE)), MLA_Q_RANK ** -0.5),
        "g_kv": gain((L, MLA_KV_RANK)),
        "w_kv_up": nrm((L, MLA_KV_RANK, MLA_HEADS * (MLA_NOPE + MLA_V)), MLA_KV_RANK ** -0.5),
        "pool_w": nrm((L, len(POOL_WINDOWS), POOL_GROUP, POOL_GROUP), POOL_GROUP ** -0.5),
        "pool_scale": 1.0 + nrm((L, POOL_WIDTH), 0.1),
        "hy_conv_w": nrm((L, 3, (HY_ORDER + 1) * HY_WIDTH), 3 ** -0.5),
        "hy_conv_b": nrm((L, (HY_ORDER + 1) * HY_WIDTH), 0.02),
        "hy_f_w1": nrm((L, HY_EMB, HY_FFN), HY_EMB ** -0.5),
        "hy_f_b1": nrm((L, HY_FFN), 0.02),
        "hy_f_freq1": 1.0 + nrm((L, HY_FFN), 0.05),
        "hy_f_w2": nrm((L, HY_FFN, HY_FFN), HY_FFN ** -0.5),
        "hy_f_b2": nrm((L, HY_FFN), 0.02),
        "hy_f_freq2": 1.0 + nrm((L, HY_FFN), 0.05),
        "hy_f_w3": nrm((L, HY_FFN, HY_ORDER * 2 * HY_WIDTH), HY_FFN ** -0.5),
        "hy_bias": nrm((L, HY_ORDER, HY_WIDTH), 0.5),
        "g_out": gain((L, D_MODEL)),
        "w_out": nrm((L, D_MODEL, D_MODEL), D_MODEL ** -0.5),
        "w_mlp1": nrm((L, D_MODEL, D_FF), D_MODEL ** -0.5),
        "w_mlp2": nrm((L, D_FF, D_MODEL), D_FF ** -0.5),
        "g_final": gain((D_MODEL,)),
    }


def reference(x, c, ctx, c_ctx, w_mod, b_mod, g_mix, g_mlp, w_in, g_q, w_q_up, g_kv, w_kv_up,
              pool_w, pool_scale, hy_conv_w, hy_conv_b, hy_f_w1, hy_f_b1, hy_f_freq1, hy_f_w2,
              hy_f_b2, hy_f_freq2, hy_f_w3, hy_bias, g_out, w_out, w_mlp1, w_mlp2, g_final):
    xc = ctx
    for l in range(DEPTH):
        p = {
            "w_mod": w_mod[l], "b_mod": b_mod[l], "g_mix": g_mix[l], "g_mlp": g_mlp[l],
            "w_in": w_in[l], "g_q": g_q[l], "w_q_up": w_q_up[l], "g_kv": g_kv[l],
            "w_kv_up": w_kv_up[l], "pool_w": pool_w[l], "pool_scale": pool_scale[l],
            "hy_conv_w": hy_conv_w[l], "hy_conv_b": hy_conv_b[l], "hy_f_w1": hy_f_w1[l],
            "hy_f_b1": hy_f_b1[l], "hy_f_freq1": hy_f_freq1[l], "hy_f_w2": hy_f_w2[l],
            "hy_f_b2": hy_f_b2[l], "hy_f_freq2": hy_f_freq2[l], "hy_f_w3": hy_f_w3[l],
            "hy_bias": hy_bias[l], "g_out": g_out[l], "w_out": w_out[l],
            "w_mlp1": w_mlp1[l], "w_mlp2": w_mlp2[l],
        }
        x, xc = hybrid_layer(x, xc, c, c_ctx, p, last=(l == DEPTH - 1))
    return rmsnorm(x, g_final)
```

```python
import math
import numpy as np
import ml_dtypes
import concourse.bass as bass
import concourse.mybir as mybir
from concourse.bass_utils import run_bass_kernel_spmd

F32 = mybir.dt.float32
BF16 = mybir.dt.bfloat16
AF = mybir.ActivationFunctionType
ALU = mybir.AluOpType
AX = mybir.AxisListType

SAME_ENGINE_SYNC = True
N_DMA_SEMS = 24


class _Op:
    __slots__ = ("eng", "fn", "deps", "need_inc", "cnt", "dsem", "dval", "is_dma", "idx")

    def __init__(self, eng, fn, is_dma):
        self.eng = eng
        self.fn = fn
        self.deps = []
        self.need_inc = False
        self.cnt = 0
        self.dsem = -1
        self.dval = 0
        self.is_dma = is_dma
        self.idx = 0


class KB:
    ENGS = ("pe", "act", "dve", "pool", "sp")

    def __init__(self, nc):
        self.nc = nc
        self.ops = []
        self.last_w = {}
        self.readers = {}
        self.n_dma = 0
        self._bar_start = 0

    def _add(self, eng, fn, reads, writes, is_dma):
        op = _Op(eng, fn, is_dma)
        op.idx = len(self.ops)
        deps = set()
        for k in reads:
            w = self.last_w.get(k)
            if w is not None:
                deps.add(w)
        for k in writes:
            w = self.last_w.get(k)
            if w is not None:
                deps.add(w)
            for r in self.readers.get(k, ()):
                deps.add(r)
        deps.discard(op.idx)
        op.deps = sorted(deps)
        for k in reads:
            lst = self.readers.setdefault(k, [])
            if not is_dma:
                lst[:] = [r for r in lst if self.ops[r].is_dma or self.ops[r].eng != eng]
            lst.append(op.idx)
        for k in writes:
            self.last_w[k] = op.idx
            self.readers[k] = []
        self.ops.append(op)
        return op

    def pe(self, fn, reads=(), writes=()):
        return self._add("pe", fn, reads, writes, False)

    def act(self, fn, reads=(), writes=()):
        return self._add("act", fn, reads, writes, False)

    def dve(self, fn, reads=(), writes=()):
        return self._add("dve", fn, reads, writes, False)

    def pool(self, fn, reads=(), writes=()):
        return self._add("pool", fn, reads, writes, False)

    def dma(self, out, in_, reads=(), writes=(), q="sp", **kw):
        def fn(e, out=out, in_=in_, kw=kw):
            return e.dma_start(out=out, in_=in_, **kw)
        return self._add(q, fn, reads, writes, True)

    def barrier(self):
        last = {}
        dmas = []
        for op in self.ops[self._bar_start:]:
            if op.is_dma:
                dmas.append(op.idx)
            elif op.fn is not None:
                last[op.eng] = op.idx
        deps = sorted(set(list(last.values()) + dmas))
        for e in self.ENGS:
            op = _Op(e, None, False)
            op.idx = len(self.ops)
            op.deps = list(deps)
            self.ops.append(op)
        self._bar_start = len(self.ops)
        self.last_w = {}
        self.readers = {}

    def emit(self):
        nc = self.nc
        ops = self.ops
        for op in ops:
            best = {}
            for d in op.deps:
                dop = ops[d]
                if dop.is_dma or dop.fn is None:
                    continue
                if dop.eng == op.eng and not op.is_dma:
                    if dop.eng == "pe" or not SAME_ENGINE_SYNC:
                        continue
                if d > best.get(dop.eng, -1):
                    best[dop.eng] = d
            for d in best.values():
                ops[d].need_inc = True
        cnt = {e: 0 for e in self.ENGS}
        dma_cnt = [0] * N_DMA_SEMS
        nd = 0
        for op in ops:
            if op.is_dma:
                s = nd % N_DMA_SEMS
                nd += 1
                dma_cnt[s] += 1
                op.dsem = s
                op.dval = 16 * dma_cnt[s]
            elif op.need_inc:
                cnt[op.eng] += 1
                op.cnt = cnt[op.eng]
        per_eng = {e: [] for e in self.ENGS}
        for op in ops:
            per_eng[op.eng].append(op)
        self.stats = {e: len(per_eng[e]) for e in self.ENGS}
        self.stats["incs"] = dict(cnt)

        import contextlib
        with contextlib.ExitStack() as st:
            esem = {e: st.enter_context(nc.semaphore("s_" + e)) for e in self.ENGS}
            dsem = [st.enter_context(nc.semaphore("d%d" % i)) for i in range(N_DMA_SEMS)]
            block = st.enter_context(nc.Block())

            def run(eng_name, e):
                waited_e = {x: 0 for x in self.ENGS}
                waited_d = [0] * N_DMA_SEMS
                for op in per_eng[eng_name]:
                    need_e = {}
                    need_d = {}
                    for d in op.deps:
                        dop = ops[d]
                        if dop.is_dma:
                            if dop.dval > waited_d[dop.dsem]:
                                need_d[dop.dsem] = max(need_d.get(dop.dsem, 0), dop.dval)
                        else:
                            if dop.eng == eng_name and not op.is_dma:
                                if eng_name == "pe" or not SAME_ENGINE_SYNC:
                                    continue
                            if dop.cnt > waited_e[dop.eng]:
                                need_e[dop.eng] = max(need_e.get(dop.eng, 0), dop.cnt)
                    if op.is_dma:
                        prev = op.dval - 16
                        if prev > waited_d[op.dsem]:
                            need_d[op.dsem] = max(need_d.get(op.dsem, 0), prev)
                    for x, v in need_e.items():
                        e.wait_ge(esem[x], v)
                        waited_e[x] = v
                    for s, v in need_d.items():
                        e.wait_ge(dsem[s], v)
                        waited_d[s] = v
                    if op.fn is None:
                        continue
                    ins = op.fn(e)
                    if op.is_dma:
                        ins.then_inc(dsem[op.dsem], 16)
                    elif op.need_inc:
                        ins.then_inc(esem[eng_name], 1)
                last = {}
                for op in per_eng[eng_name]:
                    if op.is_dma:
                        last[op.dsem] = max(last.get(op.dsem, 0), op.dval)
                for s, v in last.items():
                    if v > waited_d[s]:
                        e.wait_ge(dsem[s], v)

            @block.sync
            def _(e):
                run("sp", e)

            @block.scalar
            def _(e):
                run("act", e)

            @block.vector
            def _(e):
                run("dve", e)

            @block.gpsimd
            def _(e):
                run("pool", e)

            @block.tensor
            def _(e):
                run("pe", e)


D = 1024
S = 4096
CL = 256
L = 2
NCORES = 8
NBC = 2
NH = 8
NOPE = 64
ROPE = 32
DV = 64
QR = 256
KVR = 128
NIN = 1440
C_KV, C_KR, C_Q, C_POOL, C_HY = 0, 128, 160, 416, 672
DFF = 4096
EPS = 1e-6
SCALE = float((NOPE + ROPE) ** -0.5)
POOL_WINDOWS = (2, 4, 8, 16)
HY_EMB = 33
HY_FFN = 64
MAGIC = 12582912.0
BFNP = ml_dtypes.bfloat16


def _rope_perm():
    perm = np.zeros(32, np.int64)
    for a in range(2):
        for hf in range(2):
            for f in range(8):
                perm[a * 16 + hf * 8 + f] = a * 16 + (1 - hf) * 8 + f
    return perm


def _rope_tables():
    n = S
    rows = n // 64
    r = np.repeat(np.arange(rows), 64).astype(np.float32)
    cidx = np.tile(np.arange(64), rows).astype(np.float32)
    inv = np.power(np.float32(10000.0), -(np.arange(8, dtype=np.float32) / np.float32(8))).astype(np.float32)
    ang = np.stack([r[:, None] * inv, cidx[:, None] * inv], axis=1).astype(np.float32)
    cos = np.cos(ang).astype(np.float32)
    sin = np.sin(ang).astype(np.float32)
    cosT = np.zeros((32, n), np.float32)
    sinT = np.zeros((32, n), np.float32)
    for a in range(2):
        for hf in range(2):
            for f in range(8):
                rr = a * 16 + hf * 8 + f
                cosT[rr] = cos[:, a, f]
                sinT[rr] = (-sin[:, a, f]) if hf == 0 else sin[:, a, f]
    return cosT, sinT


def _pool_bands(n, W):
    blocks = []
    index = {}
    table = {}
    nt = n // W
    for g, w in enumerate(POOL_WINDOWS):
        t = np.arange(n)
        lo = np.clip(t - w // 2, 0, n)
        hi = np.clip(t + w // 2, 0, n)
        for i in range(nt):
            T0 = i * W
            m_lo = max((T0 - w // 2) // 128, 0)
            m_hi = min((T0 + W + w // 2 - 1) // 128, n // 128 - 1)
            for m in range(m_lo, m_hi + 1):
                blk = np.zeros((128, 512), np.float32)
                for tt in range(T0, T0 + W):
                    s0 = max(lo[tt], m * 128)
                    s1 = min(hi[tt], (m + 1) * 128)
                    if s1 > s0:
                        blk[s0 - m * 128:s1 - m * 128, tt - T0] += 1.0 / float(hi[tt] - lo[tt])
                    if m * 128 <= tt < (m + 1) * 128:
                        blk[tt - m * 128, tt - T0] -= 1.0
                if not blk.any():
                    continue
                key = blk.tobytes()
                if key not in index:
                    index[key] = len(blocks)
                    blocks.append(blk)
                table[(g, i, m)] = index[key]
    return np.stack(blocks).astype(BFNP), table


def _dft_blocks(n):
    nt = n // 128
    t = np.arange(n, dtype=np.int64)
    m = (t[:, None] * t[None, :]) % (2 * n)
    ang = m.astype(np.float64) * (2.0 * np.pi / (2 * n))
    Cm = np.cos(ang)
    Sm = -np.sin(ang)

    def tile(M):
        M4 = M.reshape(nt, 128, nt, 128)
        return np.ascontiguousarray(M4.transpose(2, 1, 0, 3)).astype(BFNP)
    return tile(Cm), tile(Sm)


def _hy_consts(n):
    f32 = np.float32
    t = np.linspace(0.0, 1.0, n, dtype=f32)[:, None]
    bands = (HY_EMB - 1) // 2
    freqs = np.linspace(1e-4, bands - 1, bands, dtype=f32)[None, :]
    wpos = (f32(2.0 * math.pi) * np.arange(n, dtype=f32)[:, None] / f32(n)).astype(f32)
    z = np.concatenate([t, np.cos(freqs * wpos), -np.sin(freqs * wpos)], axis=-1).astype(f32)
    deltas = np.abs(np.linspace(math.log(1e-2) / 1.5, math.log(1e-2) / 0.3, 256, dtype=f32))
    dec = np.exp(-t * deltas[None, :]).astype(f32)
    pm1 = np.where(np.arange(n) % 2 == 0, 1.0, -1.0).astype(f32)
    return np.ascontiguousarray(z.T), dec, pm1


_CONST_CACHE = {}


def _host_consts():
    if _CONST_CACHE:
        return _CONST_CACHE
    c = _CONST_CACHE
    c["ident_f"] = np.eye(128, dtype=np.float32)
    cosT, sinT = _rope_tables()
    c["rope_cos"] = cosT
    c["rope_sin"] = sinT
    c["rope_cos_q"] = (cosT * np.float32(SCALE)).astype(np.float32)
    c["rope_sin_q"] = (sinT * np.float32(SCALE)).astype(np.float32)
    bl, tl = _pool_bands(S, 512)
    bc, tc = _pool_bands(CL, 256)
    c["band_lat"] = bl
    c["band_ctx"] = bc
    c["_band_tab_lat"] = tl
    c["_band_tab_ctx"] = tc
    for n, nm in ((S, "lat"), (CL, "ctx")):
        Cb, Sb = _dft_blocks(n)
        c["dftc_" + nm] = Cb
        c["dfts_" + nm] = Sb
        zT, dec, pm1 = _hy_consts(n)
        c["hyz_" + nm] = zT
        c["hydec_" + nm] = dec
        c["pm1_" + nm] = pm1
    return c


class Arena:
    def __init__(self, nc, nbytes):
        self.t = nc.alloc_sbuf_tensor("arena", [128, nbytes // 2], BF16).ap()
        self.cap = nbytes
        self.off = 0
        self.cnt = 0

    def reset(self):
        self.off = 0

    def alloc(self, free_shape, dtype, name="t"):
        esz = 4 if dtype == F32 else 2
        n = 1
        for d_ in free_shape:
            n *= d_
        nb = n * esz
        off = (self.off + 63) // 64 * 64
        assert off + nb <= self.cap, "arena overflow %s %d+%d>%d" % (name, off, nb, self.cap)
        self.off = off + nb
        ap = self.t[:, off // 2:(off + nb) // 2]
        if dtype == F32:
            ap = ap.bitcast(F32)
        if len(free_shape) == 2:
            ap = ap.rearrange("p (a b) -> p a b", b=free_shape[1])
        elif len(free_shape) == 3:
            ap = ap.rearrange("p (a b c) -> p a b c", b=free_shape[1], c=free_shape[2])
        self.cnt += 1
        return ap, "%s#%d" % (name, self.cnt)


class Seq:
    def __init__(self, name, n, v, kind, b):
        self.name = name
        self.n = n
        self.v = v
        self.kind = kind
        self.b = b
        self.W = 512 if n >= 512 else n
        self.nt = n // self.W


class Prog:
    def __init__(self, nc, dbg=()):
        self.nc = nc
        self.kb = KB(nc)
        self.dbg = set(dbg)
        self.dram = {}
        self.consts = _host_consts()

    def din(self, name, shape, dtype=F32):
        t = self.nc.dram_tensor(name, list(shape), dtype, kind="ExternalInput").ap()
        self.dram[name] = t
        return t

    def dscr(self, name, shape, dtype=F32):
        kind = "ExternalOutput" if name in self.dbg else "Internal"
        t = self.nc.dram_tensor(name, list(shape), dtype, kind=kind).ap()
        self.dram[name] = t
        return t

    def declare(self):
        nc = self.nc
        c = self.consts
        self.x_in = self.din("x", [NBC, S, D])
        self.ctx_in = self.din("ctx", [NBC, CL, D])
        self.cv_in = self.din("cv", [3, D])
        self.y_out = nc.dram_tensor("y", [NBC, S, D], F32, kind="ExternalOutput").ap()
        w = {}
        w["w_mod"] = self.din("w_mod", [L, D, 6 * D])
        w["b_mod"] = self.din("b_mod", [L, 6 * D])
        w["g_mix"] = self.din("g_mix", [L, D])
        w["g_mlp"] = self.din("g_mlp", [L, D])
        w["w_in"] = self.din("w_in", [L, D, NIN])
        w["w_in_rot"] = self.din("w_in_rot", [L, D, ROPE])
        w["g_q"] = self.din("g_q", [L, QR])
        w["w_q_up"] = self.din("w_q_up", [L, QR, NH * 96])
        w["w_q_rot"] = self.din("w_q_rot", [L, QR, NH * 96])
        w["g_kv"] = self.din("g_kv", [L, KVR])
        w["w_kv_up"] = self.din("w_kv_up", [L, KVR, NH * 128])
        w["pool_w"] = self.din("pool_w", [L, 4, 64, 64])
        w["pool_scale"] = self.din("pool_scale", [L, 256])
        w["hy_conv_w"] = self.din("hy_conv_w", [L, 3, 768])
        w["hy_conv_b"] = self.din("hy_conv_b", [L, 768])
        w["hy_f_w1"] = self.din("hy_f_w1", [L, HY_EMB, HY_FFN])
        w["hy_f_b1"] = self.din("hy_f_b1", [L, HY_FFN])
        w["hy_f_freq1"] = self.din("hy_f_freq1", [L, HY_FFN])
        w["hy_f_w2"] = self.din("hy_f_w2", [L, HY_FFN, HY_FFN])
        w["hy_f_b2"] = self.din("hy_f_b2", [L, HY_FFN])
        w["hy_f_freq2"] = self.din("hy_f_freq2", [L, HY_FFN])
        w["hy_f_w3"] = self.din("hy_f_w3", [L, HY_FFN, 1024])
        w["hy_bias"] = self.din("hy_bias", [L, 2, 256])
        w["g_out"] = self.din("g_out", [L, D])
        w["w_out"] = self.din("w_out", [L, D, D])
        w["w_mlp1"] = self.din("w_mlp1", [L, D, DFF])
        w["w_mlp2"] = self.din("w_mlp2", [L, DFF, D])
        w["g_final"] = self.din("g_final", [D])
        self.w = w
        k = {}
        for name, arr in c.items():
            if name.startswith("_"):
                continue
            dt_ = BF16 if arr.dtype == BFNP else F32
            k[name] = self.din(name, arr.shape, dt_)
        self.k = k
        self.lat = [Seq("b%d" % b, S, b, "lat", b) for b in range(NBC)]
        self.ctx = [Seq("c%d" % b, CL, 2, "ctx", b) for b in range(NBC)]
        for s in self.lat + self.ctx:
            n = s.n
            s.XT = self.dscr("XT_" + s.name, [8, 128, n])
            s.PKV = self.dscr("PKV_" + s.name, [128, n], BF16)
            s.KR = self.dscr("KR_" + s.name, [32, n], BF16)
            s.PQ = self.dscr("PQ_" + s.name, [2, 128, n], BF16)
            s.POOLU = self.dscr("POOLU_" + s.name, [n, 256], BF16)
            s.HYP = self.dscr("HYP_" + s.name, [6, 128, n + 2])
            s.ATT = self.dscr("ATT_" + s.name, [n, 512])
            s.POOLO = self.dscr("POOLO_" + s.name, [n, 256])
        for nm, n in (("lat", S), ("ctx", CL)):
            setattr(self, "HV_" + nm, self.dscr("HV_" + nm, [n, 512], BF16))
            setattr(self, "HX1_" + nm, self.dscr("HX1_" + nm, [n, 512], BF16))
            setattr(self, "HX2_" + nm, self.dscr("HX2_" + nm, [n, 512], BF16))
            setattr(self, "HYO_" + nm, self.dscr("HYO_" + nm, [n, 512]))
            setattr(self, "HS_" + nm, self.dscr("HS_" + nm, [2, n // 128, 128, 512], BF16))
            setattr(self, "HNQ_" + nm, self.dscr("HNQ_" + nm, [1, 512]))
        A = lambda name, shape, dt_: nc.alloc_sbuf_tensor(name, shape, dt_).ap()
        self.ident_f = A("ident_f_sb", [128, 128], F32)
        self.ident_b = A("ident_b", [128, 128], BF16)
        self.ones_b = A("ones_b", [128, 128], BF16)
        self.ones_f = A("ones_f", [128, 128], F32)
        self.eps = A("eps_t", [128, 1], F32)
        self.modT = A("modT", [128, L * 6 * 8 * 3], F32).rearrange("p (l k j v) -> p l k j v", l=L, k=6, j=8)
        self.ps2 = [nc.alloc_psum_tensor("ps2_%d" % i, [128, 1024], F32).ap() for i in range(4)]
        self.arena = Arena(nc, 204 * 1024)
        kb = self.kb
        kb.dma(self.ident_f, self.k["ident_f"], writes=["ident_f"])
        kb.act(lambda e: e.activation(out=self.ident_b, in_=self.ident_f, func=AF.Copy), reads=["ident_f"], writes=["ident_b"])
        kb.dve(lambda e: e.memset(self.ones_b, 1.0), writes=["ones_b"])
        kb.dve(lambda e: e.memset(self.ones_f, 1.0), writes=["ones_f"])
        kb.dve(lambda e: e.memset(self.eps, EPS), writes=["eps"])

    def bank(self, kidx):
        return self.ps2[kidx // 2][:, (kidx % 2) * 512:(kidx % 2 + 1) * 512], "pb%d" % kidx

    def mod(self, l, kind, v):
        return self.modT[:, l, kind, :, v]

    def phase_end(self):
        self.kb.barrier()
        self.arena.reset()

    def phase_mod(self):
        kb, ar, w = self.kb, self.arena, self.w
        cvs, kcvs = ar.alloc([D], F32, "cvs")
        sil, ksil = ar.alloc([D], F32, "sil")
        silT, ksilT = ar.alloc([8, 3], F32, "silT")
        kb.dma(cvs[0:3, :], self.cv_in, writes=[kcvs])
        kb.act(lambda e: e.activation(out=sil[0:3, :], in_=cvs[0:3, :], func=AF.Silu), reads=[kcvs], writes=[ksil])
        pb, kpb = self.bank(0)
        for j in range(8):
            kb.pe(lambda e, j=j: e.transpose(pb[:, j * 3:(j + 1) * 3], sil[0:3, j * 128:(j + 1) * 128], self.ident_f[0:3, 0:3]),
                  reads=[ksil, "ident_f"], writes=[kpb])
        kb.dve(lambda e: e.tensor_copy(out=silT.rearrange("p j v -> p (j v)"), in_=pb[:, 0:24]), reads=[kpb], writes=[ksilT])
        wbuf = [ar.alloc([8, 512], F32, "wmod") for _ in range(2)]
        modrow, kmodrow = ar.alloc([6 * D], F32, "modrow")
        brow, kbrow = ar.alloc([6 * D], F32, "brow")
        gbc, kgbc = ar.alloc([2, D], F32, "gbc")
        for l in range(L):
            kb.dma(brow[0:3, :], w["b_mod"][l].partition_broadcast(3), writes=[kbrow])
            kb.dma(gbc[0:3, 0, :], w["g_mix"][l].partition_broadcast(3), writes=[kgbc])
            kb.dma(gbc[0:3, 1, :], w["g_mlp"][l].partition_broadcast(3), writes=[kgbc])
            for ncn in range(12):
                wb, kwb = wbuf[ncn % 2]
                kb.dma(wb, w["w_mod"][l][:, ncn * 512:(ncn + 1) * 512].rearrange("(j p) n -> p j n", p=128), writes=[kwb])
                pm, kpm = self.bank(1 + ncn % 2)
                for j in range(8):
                    kb.pe(lambda e, j=j, wb=wb, pm=pm: e.matmul(pm[0:3, :], silT[:, j, :], wb[:, j, :], start=(j == 0), stop=(j == 7)),
                          reads=[ksilT, kwb], writes=[kpm])
                kb.dve(lambda e, pm=pm, ncn=ncn: e.tensor_tensor(out=modrow[0:3, ncn * 512:(ncn + 1) * 512], in0=pm[0:3, :],
                                                                in1=brow[0:3, ncn * 512:(ncn + 1) * 512], op=ALU.add),
                       reads=[kpm, kbrow], writes=[kmodrow])
            for (kind, gi) in ((1, 0), (4, 1)):
                sl = modrow[0:3, kind * D:(kind + 1) * D]
                kb.dve(lambda e, sl=sl, gi=gi: e.scalar_tensor_tensor(out=sl, in0=sl, scalar=1.0, in1=gbc[0:3, gi, :], op0=ALU.add, op1=ALU.mult),
                       reads=[kmodrow, kgbc], writes=[kmodrow])
            pt, kpt = self.bank(3)
            for kind in range(6):
                for j in range(8):
                    col = (kind * 8 + j) * 3
                    kb.pe(lambda e, kind=kind, j=j, col=col: e.transpose(pt[:, col:col + 3], modrow[0:3, kind * D + j * 128: kind * D + (j + 1) * 128],
                                                                        self.ident_f[0:3, 0:3]),
                          reads=[kmodrow, "ident_f"], writes=[kpt])
            kb.dve(lambda e, l=l: e.tensor_copy(out=self.modT[:, l].rearrange("p k j v -> p (k j v)"), in_=pt[:, 0:144]),
                   reads=[kpt], writes=["modT"])
        self.phase_end()

    def nb(self):
        self._nb = (getattr(self, "_nb", -1) + 1) % 8
        return self.bank(self._nb)

    def evac(self, i, fn_act, fn_dve, reads, writes):
        if i % 2 == 0:
            self.kb.act(fn_act, reads=reads, writes=writes)
        else:
            self.kb.dve(fn_dve, reads=reads, writes=writes)

    def phase_t0(self):
        kb, ar = self.kb, self.arena
        for s in self.lat + self.ctx:
            src = self.x_in[s.b] if s.kind == "lat" else self.ctx_in[s.b]
            W, nsub = s.W, s.W // 128
            xin = [ar.alloc([nsub, D], F32, "xin") for _ in range(2)]
            xt = [ar.alloc([8, W], F32, "xt") for _ in range(2)]
            for i in range(s.nt):
                xi, kxi = xin[i % 2]
                xo, kxo = xt[i % 2]
                kb.dma(xi, src[i * W:(i + 1) * W, :].rearrange("(a p) d -> p a d", p=128), writes=[kxi])
                for j in range(8):
                    pb, kpb = self.nb()
                    for sub in range(nsub):
                        kb.pe(lambda e, pb=pb, xi=xi, j=j, sub=sub: e.transpose(pb[:, sub * 128:(sub + 1) * 128], xi[:, sub, j * 128:(j + 1) * 128], self.ident_f),
                              reads=[kxi, "ident_f"], writes=[kpb])
                    self.evac(j, lambda e, pb=pb, xo=xo, j=j, W=W: e.activation(out=xo[:, j, :], in_=pb[:, 0:W], func=AF.Copy),
                              lambda e, pb=pb, xo=xo, j=j, W=W: e.tensor_copy(out=xo[:, j, :], in_=pb[:, 0:W]),
                              reads=[kpb], writes=[kxo + ".%d" % j])
                kb.dma(s.XT[:, :, i * W:(i + 1) * W].rearrange("j p t -> p j t"), xo,
                       reads=[kxo + ".%d" % j for j in range(8)], writes=["XT_" + s.name])
            self.phase_end()

    def fm_rstd(self, chunks, nfeat, W, sq, ksq, rs, krs):
        kb = self.kb
        pss, kpss = self.nb()
        n = len(chunks)
        for ci, (ap, key, P) in enumerate(chunks):
            kb.act(lambda e, ap=ap, ci=ci, P=P: e.activation(out=sq[0:P, ci, 0:W], in_=ap, func=AF.Square),
                   reads=[key], writes=[ksq + ".%d" % ci])
            kb.pe(lambda e, ci=ci, P=P: e.matmul(pss[:, 0:W], self.ones_b[0:P, :], sq[0:P, ci, 0:W], start=(ci == 0), stop=(ci == n - 1)),
                  reads=[ksq + ".%d" % ci, "ones_b"], writes=[kpss])
        kb.act(lambda e: e.activation(out=rs[:, 0:W], in_=pss[:, 0:W], func=AF.Sqrt, bias=self.eps[:, 0:1], scale=1.0 / nfeat),
               reads=[kpss, "eps"], writes=[krs])
        kb.dve(lambda e: e.reciprocal(out=rs[:, 0:W], in_=rs[:, 0:W]), reads=[krs], writes=[krs])

    def mod_norm(self, xt, kxt, W, gm, sh, sq, ksq, rs, krs, tmp, ktmp, hT, khT, extra=None):
        kb = self.kb
        self.fm_rstd([(xt[:, j, 0:W], kxt + ".%d" % j, 128) for j in range(8)], D, W, sq, ksq, rs, krs)
        for j in range(8):
            kb.dve(lambda e, j=j: e.scalar_tensor_tensor(out=tmp[:, j, 0:W], in0=xt[:, j, 0:W], scalar=gm[:, j:j + 1], in1=rs[:, 0:W],
                                                         op0=ALU.mult, op1=ALU.mult),
                   reads=[kxt + ".%d" % j, krs, "modT"], writes=[ktmp + ".%d" % j] + (extra(j) if extra else []))
            kb.act(lambda e, j=j: e.activation(out=hT[:, j, 0:W], in_=tmp[:, j, 0:W], func=AF.Identity, bias=sh[:, j:j + 1], scale=1.0),
                   reads=[ktmp + ".%d" % j, "modT"], writes=[khT + ".%d" % j])

    def load_cast_rows(self, dst, src2d, nj, split=1):
        ap, key = dst
        n = src2d.shape[-1]
        step = n // split
        for j in range(nj):
            for sp_ in range(split):
                self.kb.dma(ap[:, j, sp_ * step:(sp_ + 1) * step], src2d[j * 128:(j + 1) * 128, sp_ * step:(sp_ + 1) * step],
                            writes=[key + ".%d" % j], q="pool")

    def phase_a(self, l):
        kb, ar, w = self.kb, self.arena, self.w
        last = (l == L - 1)
        win = ar.alloc([8, NIN], BF16, "win")
        wrot = ar.alloc([8, ROPE], BF16, "wrot")
        self.load_cast_rows(win, w["w_in"][l], 8)
        self.load_cast_rows(wrot, w["w_in_rot"][l], 8)
        win_ap, kwin = win
        wrot_ap, kwrot = wrot
        kwin_all = [kwin + ".%d" % j for j in range(8)]
        kwrot_all = [kwrot + ".%d" % j for j in range(8)]
        gkv, kgkv = ar.alloc([1], F32, "gkv")
        gq, kgq = ar.alloc([2], F32, "gq")
        kb.dma(gkv, w["g_kv"][l].rearrange("(p o) -> p o", o=1), writes=[kgkv])
        for c_ in range(2):
            kb.dma(gq[:, c_:c_ + 1], w["g_q"][l][c_ * 128:(c_ + 1) * 128].rearrange("(p o) -> p o", o=1), writes=[kgq])
        zt, kzt = ar.alloc([6, 1], F32, "zt")
        kb.dve(lambda e: e.memset(zt, 0.0), writes=[kzt])
        WM = 512
        xts = [ar.alloc([8, WM], F32, "xt") for _ in range(2)]
        sq, ksq = ar.alloc([8, WM], BF16, "sq")
        rs, krs = ar.alloc([WM], F32, "rs")
        tmp, ktmp = ar.alloc([8, WM], F32, "tmp")
        hTs = [ar.alloc([8, WM], BF16, "hT") for _ in range(2)]
        sq2, ksq2 = ar.alloc([2, WM], BF16, "sq2")
        rs2, krs2 = ar.alloc([WM], F32, "rs2")
        pkvn = [ar.alloc([WM], BF16, "pkvn") for _ in range(2)]
        krs_ = [ar.alloc([WM], BF16, "kr") for _ in range(2)]
        pqn = [ar.alloc([2, WM], BF16, "pqn") for _ in range(2)]
        poolu = [ar.alloc([4, 256], BF16, "poolu") for _ in range(2)]
        hyp = [ar.alloc([6, WM], F32, "hyp") for _ in range(2)]
        ropec = [ar.alloc([WM], F32, "ropec") for _ in range(2)]
        ropes = [ar.alloc([WM], F32, "ropes") for _ in range(2)]
        rt1, krt1 = ar.alloc([WM], F32, "rt1")
        rt2, krt2 = ar.alloc([WM], F32, "rt2")
        it = 0
        for s in self.lat + self.ctx:
            full = not (last and s.kind == "ctx")
            W, nsub = s.W, s.W // 128
            gm, sh = self.mod(l, 1, s.v), self.mod(l, 0, s.v)
            if full:
                for (a0, a1) in ((0, 1), (s.n + 1, s.n + 2)):
                    kb.dma(s.HYP[:, :, a0:a1].rearrange("c p o -> p c o"), zt, reads=[kzt], writes=["HYP_" + s.name], allow_slow_non_contiguous=True)
            for i in range(s.nt):
                xt, kxt = xts[it % 2]
                hT, khT = hTs[it % 2]
                t0 = i * W
                kb.dma(xt[:, :, 0:W], s.XT[:, :, t0:t0 + W].rearrange("j p t -> p j t"), reads=["XT_" + s.name],
                       writes=[kxt + ".%d" % j for j in range(8)])
                self.mod_norm(xt, kxt, W, gm, sh, sq, ksq, rs, krs, tmp, ktmp, hT, khT)
                khT_all = [khT + ".%d" % j for j in range(8)]

                def proj_fm(col0, ncol, dstps, wt=win_ap, kw=kwin_all, hT=hT, khT_all=khT_all, W=W):
                    for j in range(8):
                        kb.pe(lambda e, j=j: e.matmul(dstps[0:ncol, 0:W], wt[:, j, col0:col0 + ncol], hT[:, j, 0:W], start=(j == 0), stop=(j == 7)),
                              reads=[kw[j], khT_all[j]], writes=[dstps_key[0]])
                pkv, kpkv = self.nb()
                dstps_key = [kpkv]
                proj_fm(C_KV, 128, pkv)
                self.fm_rstd([(pkv[:, 0:W], kpkv, 128)], KVR, W, sq2, ksq2, rs2, krs2)
                o_, ko_ = pkvn[it % 2]
                kb.dve(lambda e, o_=o_, pkv=pkv, W=W: e.scalar_tensor_tensor(out=o_[:, 0:W], in0=pkv[:, 0:W], scalar=gkv[:, 0:1], in1=rs2[:, 0:W],
                                                                            op0=ALU.mult, op1=ALU.mult),
                       reads=[kpkv, krs2, kgkv], writes=[ko_])
                kb.dma(s.PKV[:, t0:t0 + W], o_[:, 0:W], reads=[ko_], writes=["PKV_" + s.name])
                pka, kpka = self.nb()
                dstps_key = [kpka]
                proj_fm(C_KR, ROPE, pka)
                o_, ko_ = krs_[it % 2]
                if s.kind == "lat":
                    pkb, kpkb = self.nb()
                    dstps_key = [kpkb]
                    proj_fm(0, ROPE, pkb, wt=wrot_ap, kw=kwrot_all)
                    rc, krc = ropec[it % 2]
                    rsn, krsn = ropes[it % 2]
                    kb.dma(rc[0:32, 0:W], self.k["rope_cos"][:, t0:t0 + W], writes=[krc])
                    kb.dma(rsn[0:32, 0:W], self.k["rope_sin"][:, t0:t0 + W], writes=[krsn])
                    kb.dve(lambda e, pka=pka, rc=rc, W=W: e.tensor_tensor(out=rt1[0:32, 0:W], in0=pka[0:32, 0:W], in1=rc[0:32, 0:W], op=ALU.mult),
                           reads=[kpka, krc], writes=[krt1])
                    kb.dve(lambda e, pkb=pkb, rsn=rsn, W=W: e.tensor_tensor(out=rt2[0:32, 0:W], in0=pkb[0:32, 0:W], in1=rsn[0:32, 0:W], op=ALU.mult),
                           reads=[kpkb, krsn], writes=[krt2])
                    kb.dve(lambda e, o_=o_, W=W: e.tensor_tensor(out=o_[0:32, 0:W], in0=rt1[0:32, 0:W], in1=rt2[0:32, 0:W], op=ALU.add),
                           reads=[krt1, krt2], writes=[ko_])
                else:
                    kb.act(lambda e, o_=o_, pka=pka, W=W: e.activation(out=o_[0:32, 0:W], in_=pka[0:32, 0:W], func=AF.Copy),
                           reads=[kpka], writes=[ko_])
                kb.dma(s.KR[:, t0:t0 + W], o_[0:32, 0:W], reads=[ko_], writes=["KR_" + s.name])
                if not full:
                    it += 1
                    continue
                pq = [self.nb() for _ in range(2)]
                for c_ in range(2):
                    dstps_key = [pq[c_][1]]
                    proj_fm(C_Q + c_ * 128, 128, pq[c_][0])
                self.fm_rstd([(pq[c_][0][:, 0:W], pq[c_][1], 128) for c_ in range(2)], QR, W, sq2, ksq2, rs2, krs2)
                o_, ko_ = pqn[it % 2]
                for c_ in range(2):
                    kb.dve(lambda e, o_=o_, c_=c_, pq=pq, W=W: e.scalar_tensor_tensor(out=o_[:, c_, 0:W], in0=pq[c_][0][:, 0:W], scalar=gq[:, c_:c_ + 1],
                                                                                  in1=rs2[:, 0:W], op0=ALU.mult, op1=ALU.mult),
                           reads=[pq[c_][1], krs2, kgq], writes=[ko_ + ".%d" % c_])
                kb.dma(s.PQ[:, :, t0:t0 + W].rearrange("c p t -> p c t"), o_[:, :, 0:W], reads=[ko_ + ".0", ko_ + ".1"], writes=["PQ_" + s.name])
                o_, ko_ = poolu[it % 2]
                for sub in range(nsub):
                    pp, kpp = self.nb()
                    for j in range(8):
                        kb.pe(lambda e, j=j, sub=sub, pp=pp, hT=hT: e.matmul(pp[:, 0:256], hT[:, j, sub * 128:(sub + 1) * 128], win_ap[:, j, C_POOL:C_POOL + 256],
                                                                            start=(j == 0), stop=(j == 7)),
                              reads=[kwin_all[j], khT_all[j]], writes=[kpp])
                    self.evac(sub, lambda e, o_=o_, pp=pp, sub=sub: e.activation(out=o_[:, sub, :], in_=pp[:, 0:256], func=AF.Copy),
                              lambda e, o_=o_, pp=pp, sub=sub: e.tensor_copy(out=o_[:, sub, :], in_=pp[:, 0:256]),
                              reads=[kpp], writes=[ko_ + ".%d" % sub])
                kb.dma(s.POOLU[t0:t0 + W, :].rearrange("(a p) c -> p a c", p=128), o_[:, 0:nsub, :],
                       reads=[ko_ + ".%d" % sub for sub in range(nsub)], writes=["POOLU_" + s.name])
                o_, ko_ = hyp[it % 2]
                for c6 in range(6):
                    ph, kph = self.nb()
                    dstps_key = [kph]
                    proj_fm(C_HY + c6 * 128, 128, ph)
                    self.evac(c6, lambda e, o_=o_, ph=ph, c6=c6, W=W: e.activation(out=o_[:, c6, 0:W], in_=ph[:, 0:W], func=AF.Copy),
                              lambda e, o_=o_, ph=ph, c6=c6, W=W: e.tensor_copy(out=o_[:, c6, 0:W], in_=ph[:, 0:W]),
                              reads=[kph], writes=[ko_ + ".%d" % c6])
                kb.dma(s.HYP[:, :, 1 + t0:1 + t0 + W].rearrange("c p t -> p c t"), o_[:, :, 0:W],
                       reads=[ko_ + ".%d" % c6 for c6 in range(6)], writes=["HYP_" + s.name])
                it += 1
        self.phase_end()


def _shared_inputs(inp):
    f = lambda a: np.ascontiguousarray(np.asarray(a, dtype=np.float32))
    sh = {}
    for k_ in ("w_mod", "b_mod", "g_mix", "g_mlp", "w_in", "g_q", "g_kv", "w_kv_up", "pool_w", "pool_scale", "hy_conv_w",
               "hy_conv_b", "hy_f_w1", "hy_f_b1", "hy_f_freq1", "hy_f_w2", "hy_f_b2", "hy_f_freq2", "hy_f_w3", "hy_bias",
               "g_out", "w_out", "w_mlp1", "w_mlp2", "g_final"):
        sh[k_] = f(inp[k_])
    perm = _rope_perm()
    w_in = sh["w_in"]
    sh["w_in_rot"] = np.ascontiguousarray(w_in[:, :, C_KR:C_KR + ROPE][:, :, perm])
    wq = f(inp["w_q_up"]).reshape(L, QR, NH, 96)
    sh["w_q_up"] = np.ascontiguousarray(wq.reshape(L, QR, NH * 96))
    wrot = np.zeros_like(wq)
    wrot[..., NOPE:] = wq[..., NOPE:][..., perm]
    sh["w_q_rot"] = np.ascontiguousarray(wrot.reshape(L, QR, NH * 96))
    for name, arr in _host_consts().items():
        if not name.startswith("_"):
            sh[name] = arr
    return sh


def _core_inputs(inp, core, shared):
    b0 = core * NBC
    m = dict(shared)
    m["x"] = np.ascontiguousarray(np.asarray(inp["x"][b0:b0 + NBC], dtype=np.float32))
    m["ctx"] = np.ascontiguousarray(np.asarray(inp["ctx"][b0:b0 + NBC], dtype=np.float32))
    cv = np.concatenate([np.asarray(inp["c"][b0:b0 + NBC], dtype=np.float32), np.asarray(inp["c_ctx"], dtype=np.float32)[None, :]], axis=0)
    m["cv"] = np.ascontiguousarray(cv)
    return m


def _attn_phase(self, l):
    kb, ar, w = self.kb, self.arena, self.w
    last = (l == L - 1)
    NK = CL + S
    NKT = NK // 128
    wkv = ar.alloc([1, NH * 128], BF16, "wkv")
    self.load_cast_rows(wkv, w["w_kv_up"][l], 1)
    wq = ar.alloc([2, NH * 96], BF16, "wq")
    wqr = ar.alloc([2, NH * 96], BF16, "wqr")
    self.load_cast_rows(wq, w["w_q_up"][l], 2)
    self.load_cast_rows(wqr, w["w_q_rot"][l], 2)
    wkv_ap, kwkv = wkv[0], wkv[1] + ".0"
    wq_ap, wqr_ap = wq[0], wqr[0]
    kwq = [wq[1] + ".0", wq[1] + ".1"]
    kwqr = [wqr[1] + ".0", wqr[1] + ".1"]
    cosq, kcosq = ar.alloc([S], F32, "cosq")
    sinq, ksinq = ar.alloc([S], F32, "sinq")
    kb.dma(cosq[64:96, :], self.k["rope_cos_q"], writes=[kcosq])
    kb.dma(sinq[64:96, :], self.k["rope_sin_q"], writes=[ksinq])
    pkv, kpkv = ar.alloc([NK], BF16, "pkv")
    pq, kpq = ar.alloc([2, S], BF16, "pq")
    pqc, kpqc = ar.alloc([2, CL], BF16, "pqc")
    KT = [ar.alloc([NK], BF16, "KT") for _ in range(2)]
    VA = [ar.alloc([NKT, 128], BF16, "VA") for _ in range(2)]
    QT = [ar.alloc([S], BF16, "QT") for _ in range(2)]
    QTc = [ar.alloc([CL], BF16, "QTc") for _ in range(2)]
    PT = [ar.alloc([1024], BF16, "PT") for _ in range(2)]
    oT = [ar.alloc([512], F32, "oT") for _ in range(2)]
    rc = [ar.alloc([4, 1], F32, "rc") for _ in range(2)]
    stg = [ar.alloc([4, 64], F32, "stg") for _ in range(2)]
    rt1, krt1 = ar.alloc([512], F32, "rt1")
    rt2, krt2 = ar.alloc([512], F32, "rt2")
    for hb in range(2):
        kb.pool(lambda e, hb=hb: e.memset(VA[hb][0][:, :, 64:128], 1.0), writes=[VA[hb][1] + ".ones"])
    misc = [self.bank(6), self.bank(7)]
    mi = [0]

    def mbank():
        mi[0] += 1
        return misc[mi[0] % 2]
    cnt_o = [0]
    for b in range(NBC):
        lat, cx = self.lat[b], self.ctx[b]
        kb.dma(pkv[:, 0:CL], cx.PKV, reads=["PKV_" + cx.name], writes=[kpkv])
        kb.dma(pkv[:, CL:NK], lat.PKV, reads=["PKV_" + lat.name], writes=[kpkv])
        for hb in range(2):
            kb.dma(KT[hb][0][64:96, 0:CL], cx.KR, reads=["KR_" + cx.name], writes=[KT[hb][1] + ".r"])
            kb.dma(KT[hb][0][64:96, CL:NK], lat.KR, reads=["KR_" + lat.name], writes=[KT[hb][1] + ".r"])
        kb.dma(pq, lat.PQ.rearrange("c p t -> p c t"), reads=["PQ_" + lat.name], writes=[kpq])
        if not last:
            kb.dma(pqc, cx.PQ.rearrange("c p t -> p c t"), reads=["PQ_" + cx.name], writes=[kpqc])
        for h in range(NH):
            hb = h % 2
            kt_ap, kkt = KT[hb]
            va_ap, kva = VA[hb]
            for kc in range((NK + 511) // 512):
                k0 = kc * 512
                kw_ = min(512, NK - k0)
                pb, kpb = mbank()
                kb.pe(lambda e, pb=pb, h=h, k0=k0, kw_=kw_: e.matmul(pb[0:64, 0:kw_], wkv_ap[:, 0, h * 128:h * 128 + 64], pkv[:, k0:k0 + kw_], start=True, stop=True),
                      reads=[kwkv, kpkv], writes=[kpb])
                self.evac(kc, lambda e, pb=pb, kt_ap=kt_ap, k0=k0, kw_=kw_: e.activation(out=kt_ap[0:64, k0:k0 + kw_], in_=pb[0:64, 0:kw_], func=AF.Copy),
                          lambda e, pb=pb, kt_ap=kt_ap, k0=k0, kw_=kw_: e.tensor_copy(out=kt_ap[0:64, k0:k0 + kw_], in_=pb[0:64, 0:kw_]),
                          reads=[kpb], writes=[kkt + ".n"])
            for g8 in range((NKT + 7) // 8):
                k0 = g8 * 8
                ng = min(8, NKT - k0)
                pb, kpb = mbank()
                for i in range(ng):
                    kb.pe(lambda e, pb=pb, h=h, i=i, k0=k0: e.matmul(pb[:, i * 64:(i + 1) * 64], pkv[:, (k0 + i) * 128:(k0 + i + 1) * 128],
                                                                    wkv_ap[:, 0, h * 128 + 64:h * 128 + 128], start=True, stop=True),
                          reads=[kwkv, kpkv], writes=[kpb])
                self.evac(g8, lambda e, pb=pb, va_ap=va_ap, k0=k0, ng=ng: e.activation(out=va_ap[:, k0:k0 + ng, 0:64], in_=pb[:, 0:ng * 64].rearrange("p (a c) -> p a c", c=64), func=AF.Copy),
                          lambda e, pb=pb, va_ap=va_ap, k0=k0, ng=ng: e.tensor_copy(out=va_ap[:, k0:k0 + ng, 0:64], in_=pb[:, 0:ng * 64].rearrange("p (a c) -> p a c", c=64)),
                          reads=[kpb], writes=[kva + ".v"])
            qsets = [(lat, S, True, NKT, pq, kpq, QT[hb])]
            if not last:
                qsets.append((cx, CL, False, CL // 128, pqc, kpqc, QTc[hb]))
            for (qs, nq, rope, nkt, pq_ap, kpq_, (qt_ap, kqt)) in qsets:
                QW = min(512, nq)
                for qc in range(nq // QW):
                    q0 = qc * QW
                    pa, kpa = mbank()
                    for c_ in range(2):
                        kb.pe(lambda e, pa=pa, c_=c_, h=h, q0=q0, QW=QW, pq_ap=pq_ap: e.matmul(pa[0:96, 0:QW], wq_ap[:, c_, h * 96:(h + 1) * 96], pq_ap[:, c_, q0:q0 + QW],
                                                                                        start=(c_ == 0), stop=(c_ == 1)),
                              reads=[kwq[c_], kpq_], writes=[kpa])
                    kb.act(lambda e, pa=pa, qt_ap=qt_ap, q0=q0, QW=QW: e.activation(out=qt_ap[0:64, q0:q0 + QW], in_=pa[0:64, 0:QW], func=AF.Copy, scale=SCALE),
                           reads=[kpa], writes=[kqt + ".n"])
                    if rope:
                        pb, kpb = mbank()
                        for c_ in range(2):
                            kb.pe(lambda e, pb=pb, c_=c_, h=h, q0=q0, QW=QW, pq_ap=pq_ap: e.matmul(pb[0:96, 0:QW], wqr_ap[:, c_, h * 96:(h + 1) * 96], pq_ap[:, c_, q0:q0 + QW],
                                                                                            start=(c_ == 0), stop=(c_ == 1)),
                                  reads=[kwqr[c_], kpq_], writes=[kpb])
                        kb.dve(lambda e, pa=pa, q0=q0, QW=QW: e.tensor_tensor(out=rt1[64:96, 0:QW], in0=pa[64:96, 0:QW], in1=cosq[64:96, q0:q0 + QW], op=ALU.mult),
                               reads=[kpa, kcosq], writes=[krt1])
                        kb.dve(lambda e, pb=pb, q0=q0, QW=QW: e.tensor_tensor(out=rt2[64:96, 0:QW], in0=pb[64:96, 0:QW], in1=sinq[64:96, q0:q0 + QW], op=ALU.mult),
                               reads=[kpb, ksinq], writes=[krt2])
                        kb.dve(lambda e, qt_ap=qt_ap, q0=q0, QW=QW: e.tensor_tensor(out=qt_ap[64:96, q0:q0 + QW], in0=rt1[64:96, 0:QW], in1=rt2[64:96, 0:QW], op=ALU.add),
                               reads=[krt1, krt2], writes=[kqt + ".r"])
                    else:
                        kb.act(lambda e, pa=pa, qt_ap=qt_ap, q0=q0, QW=QW: e.activation(out=qt_ap[64:96, q0:q0 + QW], in_=pa[64:96, 0:QW], func=AF.Copy, scale=SCALE),
                               reads=[kpa], writes=[kqt + ".r"])
                for qb in range(nq // QW):
                    q0 = qb * QW
                    cnt_o[0] += 1
                    po, kpo = self.bank(4 + cnt_o[0] % 2)
                    npair = (nkt + 1) // 2
                    for kp in range(npair):
                        ps_s = self.ps2[kp % 2]
                        kps = ["pb%d" % (2 * (kp % 2)), "pb%d" % (2 * (kp % 2) + 1)]
                        pt_ap, kpt = PT[kp % 2]
                        nh = min(2, nkt - 2 * kp)
                        for half in range(nh):
                            kt = 2 * kp + half
                            kb.pe(lambda e, ps_s=ps_s, half=half, kt=kt, kt_ap=kt_ap, qt_ap=qt_ap, q0=q0, QW=QW:
                                  e.matmul(ps_s[:, half * 512:half * 512 + QW], kt_ap[0:96, kt * 128:(kt + 1) * 128], qt_ap[0:96, q0:q0 + QW], start=True, stop=True),
                                  reads=[kkt + ".n", kkt + ".r", kqt + ".n", kqt + ".r"], writes=[kps[half]])
                        if QW == 512:
                            kb.act(lambda e, ps_s=ps_s, pt_ap=pt_ap, nh=nh: e.activation(out=pt_ap[:, 0:nh * 512], in_=ps_s[:, 0:nh * 512], func=AF.Exp),
                                   reads=kps[0:nh], writes=[kpt])
                        else:
                            for half in range(nh):
                                kb.act(lambda e, ps_s=ps_s, pt_ap=pt_ap, half=half, QW=QW: e.activation(out=pt_ap[:, half * 512:half * 512 + QW], in_=ps_s[:, half * 512:half * 512 + QW], func=AF.Exp),
                                       reads=[kps[half]], writes=[kpt])
                        for half in range(nh):
                            kt = 2 * kp + half
                            kb.pe(lambda e, po=po, va_ap=va_ap, pt_ap=pt_ap, half=half, kt=kt, QW=QW, nkt=nkt:
                                  e.matmul(po[:, 0:QW], va_ap[:, kt, :], pt_ap[:, half * 512:half * 512 + QW], start=(kt == 0), stop=(kt == nkt - 1)),
                                  reads=[kva + ".v", kva + ".ones", kpt], writes=[kpo])
                    o_ap, ko = oT[cnt_o[0] % 2]
                    r_ap, kr_ = rc[cnt_o[0] % 2]
                    s_ap, ks_ = stg[cnt_o[0] % 2]
                    nsub = QW // 128
                    kb.dve(lambda e, o_ap=o_ap, po=po, QW=QW: e.tensor_copy(out=o_ap[:, 0:QW], in_=po[:, 0:QW]), reads=[kpo], writes=[ko])
                    ptr, kptr = mbank()
                    for sub in range(nsub):
                        kb.pe(lambda e, ptr=ptr, o_ap=o_ap, sub=sub: e.transpose(ptr[:, sub * 128:(sub + 1) * 128], o_ap[:, sub * 128:(sub + 1) * 128], self.ident_f),
                              reads=[ko, "ident_f"], writes=[kptr])
                    kb.dve(lambda e, ptr=ptr, r_ap=r_ap, nsub=nsub: e.reciprocal(out=r_ap[:, 0:nsub, :], in_=ptr[:, 0:nsub * 128].rearrange("p (a c) -> p a c", c=128)[:, :, 64:65]),
                           reads=[kptr], writes=[kr_])
                    for sub in range(nsub):
                        kb.act(lambda e, ptr=ptr, s_ap=s_ap, r_ap=r_ap, sub=sub: e.activation(out=s_ap[:, sub, :], in_=ptr[:, sub * 128:sub * 128 + 64], func=AF.Copy, scale=r_ap[:, sub, :]),
                               reads=[kptr, kr_], writes=[ks_ + ".%d" % sub])
                    kb.dma(qs.ATT[q0:q0 + QW, h * 64:(h + 1) * 64].rearrange("(a p) c -> p a c", p=128), s_ap[:, 0:nsub, :],
                           reads=[ks_ + ".%d" % sub for sub in range(nsub)], writes=["ATT_" + qs.name])
    self.phase_end()


Prog.phase_attn = _attn_phase


def _pool_phase(self, l):
    kb, ar, w = self.kb, self.arena, self.w
    last = (l == L - 1)
    pw = ar.alloc([4, 64], BF16, "pw")
    kb.dma(pw[0][0:64, :, :], w["pool_w"][l].rearrange("g i o -> i g o"), writes=[pw[1]], q="pool")
    psc, kpsc = ar.alloc([256], F32, "psc")
    kb.dma(psc, w["pool_scale"][l].partition_broadcast(128), writes=[kpsc])
    groups = [("lat", self.lat)] + ([] if last else [("ctx", self.ctx)])
    for nm, seqs in groups:
        bandc = self.k["band_" + nm]
        tab = self.consts["_band_tab_" + nm]
        nblk = bandc.shape[0]
        band, kband = ar.alloc([nblk, 512], BF16, "band")
        kb.dma(band, bandc.rearrange("b p t -> p b t"), writes=[kband])
        n = seqs[0].n
        NT = n // 128
        u, ku = ar.alloc([NT, 256], BF16, "u")
        dg = [ar.alloc([4, 512], BF16, "dg") for _ in range(2)]
        stg = [ar.alloc([4, 256], F32, "pstg") for _ in range(2)]
        it = 0
        for s in seqs:
            W, nsub = s.W, s.W // 128
            kb.dma(u, s.POOLU.rearrange("(a p) c -> p a c", p=128), reads=["POOLU_" + s.name], writes=[ku])
            for i in range(s.nt):
                d_ap, kd = dg[it % 2]
                s_ap, ks = stg[it % 2]
                for g in range(4):
                    ms = sorted(m for (g_, i_, m) in tab if g_ == g and i_ == i)
                    pb, kpb = self.nb()
                    for mi_, m in enumerate(ms):
                        bi = tab[(g, i, m)]
                        kb.pe(lambda e, pb=pb, g=g, m=m, bi=bi, W=W, mi_=mi_, nm_=len(ms), u=u, band=band: e.matmul(pb[0:64, 0:W], u[:, m, g * 64:(g + 1) * 64], band[:, bi, 0:W],
                                                                                            start=(mi_ == 0), stop=(mi_ == nm_ - 1)),
                              reads=[ku, kband], writes=[kpb])
                    self.evac(g, lambda e, pb=pb, d_ap=d_ap, g=g, W=W: e.activation(out=d_ap[0:64, g, 0:W], in_=pb[0:64, 0:W], func=AF.Copy),
                              lambda e, pb=pb, d_ap=d_ap, g=g, W=W: e.tensor_copy(out=d_ap[0:64, g, 0:W], in_=pb[0:64, 0:W]),
                              reads=[kpb], writes=[kd + ".%d" % g])
                for sub in range(nsub):
                    pb, kpb = self.nb()
                    for g in range(4):
                        kb.pe(lambda e, pb=pb, g=g, sub=sub, d_ap=d_ap: e.matmul(pb[:, g * 64:(g + 1) * 64], d_ap[0:64, g, sub * 128:(sub + 1) * 128], pw[0][0:64, g, :],
                                                                                start=True, stop=True),
                              reads=[kd + ".%d" % g, pw[1]], writes=[kpb])
                    kb.dve(lambda e, pb=pb, s_ap=s_ap, sub=sub: e.tensor_tensor(out=s_ap[:, sub, :], in0=pb[:, 0:256], in1=psc, op=ALU.mult),
                           reads=[kpb, kpsc], writes=[ks + ".%d" % sub])
                kb.dma(s.POOLO[i * W:(i + 1) * W, :].rearrange("(a p) c -> p a c", p=128), s_ap[:, 0:nsub, :],
                       reads=[ks + ".%d" % sub for sub in range(nsub)], writes=["POOLO_" + s.name])
                it += 1
    self.phase_end()


def _c1_phase(self, l):
    kb, ar, w = self.kb, self.arena, self.w
    last = (l == L - 1)
    wout = ar.alloc([8, D], BF16, "wout")
    self.load_cast_rows(wout, w["w_out"][l], 8)
    wo_ap, kwo = wout
    gout, kgout = ar.alloc([D], F32, "gout")
    kb.dma(gout, w["g_out"][l].partition_broadcast(128), writes=[kgout])
    xts = [ar.alloc([8, 512], F32, "xt") for _ in range(2)]
    att = [ar.alloc([4, 512], F32, "att") for _ in range(2)]
    pl = [ar.alloc([4, 256], F32, "pl") for _ in range(2)]
    hy = [ar.alloc([4, 256], F32, "hy") for _ in range(2)]
    junk, kjunk = ar.alloc([512], BF16, "junk")
    ss = [ar.alloc([3, 4], F32, "ss") for _ in range(2)]
    mrg = [ar.alloc([4, D], BF16, "mrg") for _ in range(2)]
    mT = [ar.alloc([8, 512], BF16, "mT") for _ in range(2)]
    seqs = self.lat + ([] if last else self.ctx)
    GR = ((0, 512, 0), (512, 256, 1), (768, 256, 2))
    it = 0
    for s in seqs:
        W, nsub = s.W, s.W // 128
        g1 = self.mod(l, 2, s.v)
        HYO = self.HYO_lat if s.kind == "lat" else self.HYO_ctx
        for i in range(s.nt):
            t0 = i * W
            xt, kxt = xts[it % 2]
            a_ap, ka = att[it % 2]
            p_ap, kp = pl[it % 2]
            h_ap, kh = hy[it % 2]
            ss_ap, kss = ss[it % 2]
            m_ap, km = mrg[it % 2]
            t_ap, kt = mT[it % 2]
            kb.dma(xt[:, :, 0:W], s.XT[:, :, t0:t0 + W].rearrange("j p t -> p j t"), reads=["XT_" + s.name], writes=[kxt + ".%d" % j for j in range(8)])
            kb.dma(a_ap[:, 0:nsub, :], s.ATT[t0:t0 + W, :].rearrange("(a p) c -> p a c", p=128), reads=["ATT_" + s.name], writes=[ka])
            kb.dma(p_ap[:, 0:nsub, :], s.POOLO[t0:t0 + W, :].rearrange("(a p) c -> p a c", p=128), reads=["POOLO_" + s.name], writes=[kp])
            kb.dma(h_ap[:, 0:nsub, :], HYO[t0:t0 + W, s.b * 256:(s.b + 1) * 256].rearrange("(a p) c -> p a c", p=128), reads=["HYO"], writes=[kh])
            kb.dve(lambda e, ss_ap=ss_ap: e.memset(ss_ap, 0.0), writes=[kss])
            srcs = ((a_ap, ka), (p_ap, kp), (h_ap, kh))
            for sub in range(nsub):
                for (c0, ng, gi) in GR:
                    src, ksrc = srcs[gi]
                    kb.act(lambda e, src=src, sub=sub, ng=ng, gi=gi, ss_ap=ss_ap: e.activation(out=junk[:, 0:ng], in_=src[:, sub, :], func=AF.Square,
                                                                                            accum_out=ss_ap[:, gi, sub:sub + 1]),
                           reads=[ksrc, kss], writes=[kss, kjunk])
            for (c0, ng, gi) in GR:
                kb.act(lambda e, ss_ap=ss_ap, gi=gi, ng=ng, nsub=nsub: e.activation(out=ss_ap[:, gi, 0:nsub], in_=ss_ap[:, gi, 0:nsub], func=AF.Sqrt,
                                                                                bias=self.eps[:, 0:1], scale=1.0 / ng),
                       reads=[kss, "eps"], writes=[kss])
            kb.dve(lambda e, ss_ap=ss_ap: e.reciprocal(out=ss_ap, in_=ss_ap), reads=[kss], writes=[kss])
            for sub in range(nsub):
                for (c0, ng, gi) in GR:
                    src, ksrc = srcs[gi]
                    kb.dve(lambda e, src=src, sub=sub, c0=c0, ng=ng, gi=gi, ss_ap=ss_ap, m_ap=m_ap:
                           e.scalar_tensor_tensor(out=m_ap[:, sub, c0:c0 + ng], in0=src[:, sub, :], scalar=ss_ap[:, gi, sub:sub + 1], in1=gout[:, c0:c0 + ng],
                                                  op0=ALU.mult, op1=ALU.mult),
                           reads=[ksrc, kss, kgout], writes=[km + ".%d" % sub])
                pb, kpb = self.nb()
                pbb = pb.bitcast(BF16)
                for j in range(8):
                    kb.pe(lambda e, pbb=pbb, m_ap=m_ap, sub=sub, j=j: e.transpose(pbb[:, j * 128:(j + 1) * 128], m_ap[:, sub, j * 128:(j + 1) * 128], self.ident_b),
                          reads=[km + ".%d" % sub, "ident_b"], writes=[kpb])
                self.evac(sub, lambda e, pbb=pbb, t_ap=t_ap, sub=sub: e.activation(out=t_ap[:, :, sub * 128:(sub + 1) * 128], in_=pbb.rearrange("p (j t) -> p j t", t=128), func=AF.Copy),
                          lambda e, pbb=pbb, t_ap=t_ap, sub=sub: e.tensor_copy(out=t_ap[:, :, sub * 128:(sub + 1) * 128], in_=pbb.rearrange("p (j t) -> p j t", t=128)),
                          reads=[kpb], writes=[kt + ".%d" % sub])
            for oc in range(8):
                pb, kpb = self.nb()
                for k_ in range(8):
                    kb.pe(lambda e, pb=pb, k_=k_, oc=oc, t_ap=t_ap, W=W: e.matmul(pb[:, 0:W], wo_ap[:, k_, oc * 128:(oc + 1) * 128], t_ap[:, k_, 0:W], start=(k_ == 0), stop=(k_ == 7)),
                          reads=[kwo + ".%d" % k_] + [kt + ".%d" % sub for sub in range(nsub)], writes=[kpb])
                kb.dve(lambda e, pb=pb, xt=xt, oc=oc, W=W, g1=g1: e.scalar_tensor_tensor(out=xt[:, oc, 0:W], in0=pb[:, 0:W], scalar=g1[:, oc:oc + 1], in1=xt[:, oc, 0:W],
                                                                                    op0=ALU.mult, op1=ALU.add),
                       reads=[kpb, kxt + ".%d" % oc, "modT"], writes=[kxt + ".%d" % oc])
            kb.dma(s.XT[:, :, t0:t0 + W].rearrange("j p t -> p j t"), xt[:, :, 0:W], reads=[kxt + ".%d" % j for j in range(8)], writes=["XT_" + s.name])
            it += 1
    self.phase_end()


def _c2_phase(self, l):
    kb, ar, w = self.kb, self.arena, self.w
    last = (l == L - 1)
    w1 = ar.alloc([8, DFF], BF16, "w1")
    w2 = ar.alloc([32, D], BF16, "w2")
    self.load_cast_rows(w1, w["w_mlp1"][l], 8, split=2)
    self.load_cast_rows(w2, w["w_mlp2"][l], 32)
    w1_ap, kw1 = w1
    w2_ap, kw2 = w2
    xt, kxt = ar.alloc([8, 512], F32, "xt")
    sq, ksq = ar.alloc([8, 512], BF16, "sq")
    rs, krs = ar.alloc([512], F32, "rs")
    hT, khT = ar.alloc([8, 512], BF16, "hT")
    hid, khid = ar.alloc([32, 512], BF16, "hid")
    tmp = hid[:, 0:16, :].rearrange("p a b -> p (a b)").bitcast(F32).rearrange("p (a b) -> p a b", b=512)
    ktmp = khid + ".tmp"
    rl = [ar.alloc([512], F32, "rl") for _ in range(2)]
    seqs = self.lat + ([] if last else self.ctx)
    for s in seqs:
        W = s.W
        gm, sh, g2 = self.mod(l, 4, s.v), self.mod(l, 3, s.v), self.mod(l, 5, s.v)
        for i in range(s.nt):
            t0 = i * W
            kb.dma(xt[:, :, 0:W], s.XT[:, :, t0:t0 + W].rearrange("j p t -> p j t"), reads=["XT_" + s.name], writes=[kxt + ".%d" % j for j in range(8)])
            self.mod_norm(xt, kxt, W, gm, sh, sq, ksq, rs, krs, tmp, ktmp, hT, khT,
                          extra=lambda j: [khid + ".%d" % (2 * j), khid + ".%d" % (2 * j + 1)])
            khT_all = [khT + ".%d" % j for j in range(8)]
            for hc in range(32):
                pb, kpb = self.nb()
                for j in range(8):
                    kb.pe(lambda e, pb=pb, j=j, hc=hc, W=W: e.matmul(pb[:, 0:W], w1_ap[:, j, hc * 128:(hc + 1) * 128], hT[:, j, 0:W], start=(j == 0), stop=(j == 7)),
                          reads=[kw1 + ".%d" % j, khT_all[j]], writes=[kpb])
                r_ap, kr_ = rl[hc % 2]
                kb.act(lambda e, pb=pb, r_ap=r_ap, W=W: e.activation(out=r_ap[:, 0:W], in_=pb[:, 0:W], func=AF.Relu), reads=[kpb], writes=[kr_])
                kb.dve(lambda e, r_ap=r_ap, hc=hc, W=W: e.tensor_tensor(out=hid[:, hc, 0:W], in0=r_ap[:, 0:W], in1=r_ap[:, 0:W], op=ALU.mult),
                       reads=[kr_], writes=[khid + ".%d" % hc])
            for oc in range(8):
                pb, kpb = self.nb()
                for hc in range(32):
                    kb.pe(lambda e, pb=pb, hc=hc, oc=oc, W=W: e.matmul(pb[:, 0:W], w2_ap[:, hc, oc * 128:(oc + 1) * 128], hid[:, hc, 0:W], start=(hc == 0), stop=(hc == 31)),
                          reads=[kw2 + ".%d" % hc, khid + ".%d" % hc], writes=[kpb])
                kb.dve(lambda e, pb=pb, oc=oc, W=W, g2=g2: e.scalar_tensor_tensor(out=xt[:, oc, 0:W], in0=pb[:, 0:W], scalar=g2[:, oc:oc + 1], in1=xt[:, oc, 0:W],
                                                                             op0=ALU.mult, op1=ALU.add),
                       reads=[kpb, kxt + ".%d" % oc, "modT"], writes=[kxt + ".%d" % oc])
            kb.dma(s.XT[:, :, t0:t0 + W].rearrange("j p t -> p j t"), xt[:, :, 0:W], reads=[kxt + ".%d" % j for j in range(8)], writes=["XT_" + s.name])
    self.phase_end()


def _final_phase(self):
    kb, ar, w = self.kb, self.arena, self.w
    gf, kgf = ar.alloc([8], F32, "gf")
    for j in range(8):
        kb.dma(gf[:, j:j + 1], w["g_final"][j * 128:(j + 1) * 128].rearrange("(p o) -> p o", o=1), writes=[kgf])
    xts = [ar.alloc([8, 512], F32, "xt") for _ in range(2)]
    sq, ksq = ar.alloc([8, 512], BF16, "sq")
    rs, krs = ar.alloc([512], F32, "rs")
    xn, kxn = ar.alloc([8, 512], F32, "xn")
    yts = [ar.alloc([4, D], F32, "yt") for _ in range(2)]
    it = 0
    for s in self.lat:
        W = 512
        for i in range(s.nt):
            t0 = i * W
            xt, kxt = xts[it % 2]
            yt, kyt = yts[it % 2]
            kb.dma(xt, s.XT[:, :, t0:t0 + W].rearrange("j p t -> p j t"), reads=["XT_" + s.name], writes=[kxt + ".%d" % j for j in range(8)])
            self.fm_rstd([(xt[:, j, :], kxt + ".%d" % j, 128) for j in range(8)], D, W, sq, ksq, rs, krs)
            for j in range(8):
                kb.dve(lambda e, xt=xt, j=j: e.scalar_tensor_tensor(out=xn[:, j, :], in0=xt[:, j, :], scalar=gf[:, j:j + 1], in1=rs, op0=ALU.mult, op1=ALU.mult),
                       reads=[kxt + ".%d" % j, krs, kgf], writes=[kxn + ".%d" % j])
            for sub in range(4):
                pp = self.ps2[sub % 2]
                kpp = ["pb%d" % (2 * (sub % 2)), "pb%d" % (2 * (sub % 2) + 1)]
                for j in range(8):
                    kb.pe(lambda e, pp=pp, j=j, sub=sub: e.transpose(pp[:, j * 128:(j + 1) * 128], xn[:, j, sub * 128:(sub + 1) * 128], self.ident_f),
                          reads=[kxn + ".%d" % j, "ident_f"], writes=[kpp[j // 4]])
                self.evac(sub, lambda e, pp=pp, yt=yt, sub=sub: e.activation(out=yt[:, sub, :], in_=pp, func=AF.Copy),
                          lambda e, pp=pp, yt=yt, sub=sub: e.tensor_copy(out=yt[:, sub, :], in_=pp),
                          reads=kpp, writes=[kyt + ".%d" % sub])
            kb.dma(self.y_out[s.b][t0:t0 + W, :].rearrange("(a p) d -> p a d", p=128), yt, reads=[kyt + ".%d" % sub for sub in range(4)], writes=["y"])
            it += 1
    self.phase_end()


Prog.phase_pool = _pool_phase
Prog.phase_c1 = _c1_phase
Prog.phase_c2 = _c2_phase
Prog.phase_final = _final_phase


def _hy_h0(self, l, nm, seqs, n):
    kb, ar, w = self.kb, self.arena, self.w
    NT = n // 128
    cw, kcw = ar.alloc([6, 3], F32, "cw")
    cb, kcb = ar.alloc([6], F32, "cb")
    for c6 in range(6):
        for k_ in range(3):
            kb.dma(cw[:, c6, k_:k_ + 1], w["hy_conv_w"][l][k_, c6 * 128:(c6 + 1) * 128].rearrange("(p o) -> p o", o=1), writes=[kcw])
        kb.dma(cb[:, c6:c6 + 1], w["hy_conv_b"][l][c6 * 128:(c6 + 1) * 128].rearrange("(p o) -> p o", o=1), writes=[kcb])
    hp = [ar.alloc([n + 2], F32, "hp") for _ in range(2)]
    acc = [ar.alloc([n], F32, "acc") for _ in range(2)]
    ucb = [ar.alloc([n], BF16, "ucb") for _ in range(2)]
    tm = [ar.alloc([NT, 128], BF16, "tm") for _ in range(2)]
    dests = (getattr(self, "HV_" + nm), getattr(self, "HX1_" + nm), getattr(self, "HX2_" + nm))
    it = 0
    for si, s in enumerate(seqs):
        for c6 in range(6):
            h_ap, kh = hp[it % 2]
            a_ap, ka = acc[it % 2]
            u_ap, ku = ucb[it % 2]
            t_ap, kt = tm[it % 2]
            kb.dma(h_ap, s.HYP[c6], reads=["HYP_" + s.name], writes=[kh])
            kb.act(lambda e, h_ap=h_ap, a_ap=a_ap, c6=c6: e.activation(out=a_ap, in_=h_ap[:, 1:n + 1], func=AF.Identity, bias=cb[:, c6:c6 + 1], scale=cw[:, c6, 1:2]),
                   reads=[kh, kcw, kcb], writes=[ka])
            kb.dve(lambda e, h_ap=h_ap, a_ap=a_ap, c6=c6: e.scalar_tensor_tensor(out=a_ap, in0=h_ap[:, 0:n], scalar=cw[:, c6, 0:1], in1=a_ap, op0=ALU.mult, op1=ALU.add),
                   reads=[kh, kcw, ka], writes=[ka])
            kb.dve(lambda e, h_ap=h_ap, a_ap=a_ap, u_ap=u_ap, c6=c6: e.scalar_tensor_tensor(out=u_ap, in0=h_ap[:, 2:n + 2], scalar=cw[:, c6, 2:3], in1=a_ap, op0=ALU.mult, op1=ALU.add),
                   reads=[kh, kcw, ka], writes=[ku])
            for g8 in range((NT + 7) // 8):
                ng = min(8, NT - g8 * 8)
                pb, kpb = self.nb()
                pbb = pb.bitcast(BF16)
                for i in range(ng):
                    tt = g8 * 8 + i
                    kb.pe(lambda e, pbb=pbb, u_ap=u_ap, i=i, tt=tt: e.transpose(pbb[:, i * 128:(i + 1) * 128], u_ap[:, tt * 128:(tt + 1) * 128], self.ident_b),
                          reads=[ku, "ident_b"], writes=[kpb])
                self.evac(g8, lambda e, pbb=pbb, t_ap=t_ap, g8=g8, ng=ng: e.activation(out=t_ap[:, g8 * 8:g8 * 8 + ng, :], in_=pbb[:, 0:ng * 128].rearrange("p (a c) -> p a c", c=128), func=AF.Copy),
                          lambda e, pbb=pbb, t_ap=t_ap, g8=g8, ng=ng: e.tensor_copy(out=t_ap[:, g8 * 8:g8 * 8 + ng, :], in_=pbb[:, 0:ng * 128].rearrange("p (a c) -> p a c", c=128)),
                          reads=[kpb], writes=[kt + ".%d" % g8])
            dst = dests[c6 // 2]
            col0 = si * 256 + (c6 % 2) * 128
            kb.dma(dst[:, col0:col0 + 128].rearrange("(a p) c -> p a c", p=128), t_ap,
                   reads=[kt + ".%d" % g8 for g8 in range((NT + 7) // 8)], writes=["HU_" + nm])
            it += 1
    self.phase_end()


def _hy_h1(self, l, nm, n):
    kb, ar, w = self.kb, self.arena, self.w
    NT = n // 128
    N2 = 2 * n
    CW = min(512, n)
    zT, kzT = ar.alloc([n], F32, "zT")
    kb.dma(zT[0:HY_EMB, :], self.k["hyz_" + nm], writes=[kzT])
    w1s, kw1s = ar.alloc([HY_FFN], F32, "w1s")
    w2s, kw2s = ar.alloc([HY_FFN], F32, "w2s")
    w3s, kw3s = ar.alloc([1024], F32, "w3s")
    kb.dma(w1s[0:HY_EMB, :], w["hy_f_w1"][l], writes=[kw1s])
    kb.dma(w2s[0:HY_FFN, :], w["hy_f_w2"][l], writes=[kw2s])
    kb.dma(w3s[0:HY_FFN, :], w["hy_f_w3"][l], writes=[kw3s])
    par, kpar = ar.alloc([6], F32, "par")
    for ci, nm_ in enumerate(("hy_f_b1", "hy_f_freq1", "hy_f_b2", "hy_f_freq2")):
        kb.dma(par[0:64, ci:ci + 1], w[nm_][l].rearrange("(p o) -> p o", o=1), writes=[kpar])
    kb.dve(lambda e: e.tensor_tensor(out=par[0:64, 4:5], in0=par[0:64, 0:1], in1=par[0:64, 1:2], op=ALU.mult), reads=[kpar], writes=[kpar])
    kb.dve(lambda e: e.tensor_tensor(out=par[0:64, 5:6], in0=par[0:64, 2:3], in1=par[0:64, 3:4], op=ALU.mult), reads=[kpar], writes=[kpar])
    h1T, kh1 = ar.alloc([n], F32, "h1T")
    h2T, kh2 = ar.alloc([n], F32, "h2T")
    arg, karg = ar.alloc([512], F32, "arg")
    kk, kkk = ar.alloc([512], F32, "kk")

    def layer(srcT, ksrc, wS, kwS, K, fcol, fbcol, dstT, kdst):
        for ch in range(n // CW):
            c0 = ch * CW
            pb, kpb = self.nb()
            kb.pe(lambda e, pb=pb, c0=c0: e.matmul(pb[0:64, 0:CW], wS[0:K, 0:64], srcT[0:K, c0:c0 + CW], start=True, stop=True),
                  reads=[ksrc, kwS], writes=[kpb])
            kb.act(lambda e, pb=pb: e.activation(out=arg[0:64, 0:CW], in_=pb[0:64, 0:CW], func=AF.Identity, bias=par[0:64, fbcol:fbcol + 1], scale=par[0:64, fcol:fcol + 1]),
                   reads=[kpb, kpar], writes=[karg])
            kb.dve(lambda e: e.tensor_scalar(out=kk[0:64, 0:CW], in0=arg[0:64, 0:CW], scalar1=1.0 / (2 * math.pi), scalar2=MAGIC, op0=ALU.mult, op1=ALU.add),
                   reads=[karg], writes=[kkk])
            kb.dve(lambda e: e.tensor_scalar(out=kk[0:64, 0:CW], in0=kk[0:64, 0:CW], scalar1=-MAGIC, scalar2=-2 * math.pi, op0=ALU.add, op1=ALU.mult),
                   reads=[kkk], writes=[kkk])
            kb.dve(lambda e: e.tensor_tensor(out=arg[0:64, 0:CW], in0=arg[0:64, 0:CW], in1=kk[0:64, 0:CW], op=ALU.add), reads=[karg, kkk], writes=[karg])
            kb.act(lambda e, c0=c0: e.activation(out=dstT[0:64, c0:c0 + CW], in_=arg[0:64, 0:CW], func=AF.Sin), reads=[karg], writes=[kdst])
    layer(zT, kzT, w1s, kw1s, HY_EMB, 1, 4, h1T, kh1)
    layer(h1T, kh1, w2s, kw2s, HY_FFN, 3, 5, h2T, kh2)
    HP, kHP = ar.alloc([NT, 512], BF16, "HP")
    HM, kHM = ar.alloc([NT, 512], BF16, "HM")
    dec = [ar.alloc([256], F32, "dec") for _ in range(2)]
    tp = [ar.alloc([4, 256], F32, "tp") for _ in range(2)]
    ab = [ar.alloc([4, 256], F32, "ab") for _ in range(2)]
    psZ = self.ps2[3]
    kZ = ["pb6", "pb7"]
    for tt in range(NT):
        pt = self.ps2[tt % 2]
        kpt = ["pb%d" % (2 * (tt % 2)), "pb%d" % (2 * (tt % 2) + 1)]
        d_ap, kd = dec[tt % 2]
        t_ap, ktp = tp[tt % 2]
        a_ap, kab = ab[tt % 2]
        for hf in range(2):
            kb.pe(lambda e, pt=pt, hf=hf, tt=tt: e.matmul(pt[:, hf * 512:(hf + 1) * 512], h2T[0:64, tt * 128:(tt + 1) * 128], w3s[0:64, hf * 512:(hf + 1) * 512], start=True, stop=True),
                  reads=[kh2, kw3s], writes=[kpt[hf]])
        kb.dma(d_ap, self.k["hydec_" + nm][tt * 128:(tt + 1) * 128, :], writes=[kd])
        for q in range(4):
            kb.dve(lambda e, pt=pt, t_ap=t_ap, d_ap=d_ap, q=q: e.tensor_tensor(out=t_ap[:, q, :], in0=pt[:, q * 256:(q + 1) * 256], in1=d_ap, op=ALU.mult),
                   reads=[kpt[q // 2], kd], writes=[ktp])
        if tt == 0:
            for q in (1, 3):
                kb.dve(lambda e, t_ap=t_ap, q=q: e.memset(t_ap[0:1, q, :], 0.0), reads=[ktp], writes=[ktp])
        kb.act(lambda e, t_ap=t_ap, a_ap=a_ap: e.activation(out=a_ap, in_=t_ap, func=AF.Abs), reads=[ktp], writes=[kab])
        for hf in range(2):
            kb.pe(lambda e, a_ap=a_ap, hf=hf, tt=tt: e.matmul(psZ[:, hf * 512:(hf + 1) * 512], self.ones_f, a_ap[:, 2 * hf:2 * hf + 2, :].rearrange("p a c -> p (a c)"),
                                                               start=(tt == 0), stop=(tt == NT - 1)),
                  reads=[kab, "ones_f"], writes=[kZ[hf]])
        for o in range(2):
            kb.pool(lambda e, t_ap=t_ap, o=o, tt=tt: e.tensor_tensor(out=HP[:, tt, o * 256:(o + 1) * 256], in0=t_ap[:, 2 * o, :], in1=t_ap[:, 2 * o + 1, :], op=ALU.add),
                    reads=[ktp], writes=[kHP + ".%d" % tt])
            kb.pool(lambda e, t_ap=t_ap, o=o, tt=tt: e.tensor_tensor(out=HM[:, tt, o * 256:(o + 1) * 256], in0=t_ap[:, 2 * o, :], in1=t_ap[:, 2 * o + 1, :], op=ALU.subtract),
                    reads=[ktp], writes=[kHM + ".%d" % tt])
    zc, kzc = ar.alloc([1024], F32, "zc")
    rz, krz = ar.alloc([512], F32, "rz")
    kb.act(lambda e: e.activation(out=zc, in_=psZ, func=AF.Copy), reads=kZ, writes=[kzc])
    for o in range(2):
        kb.dve(lambda e, o=o: e.tensor_tensor(out=rz[:, o * 256:(o + 1) * 256], in0=zc[:, o * 512:o * 512 + 256], in1=zc[:, o * 512 + 256:(o + 1) * 512], op=ALU.add),
               reads=[kzc], writes=[krz])
    kb.dve(lambda e: e.reciprocal(out=rz, in_=rz), reads=[krz], writes=[krz])
    cf, kcf = ar.alloc([2], F32, "cf")
    kb.dve(lambda e: e.memset(cf, 2.0 / N2), writes=[kcf])
    kb.dve(lambda e: e.memset(cf[0:1, 0:1], 1.0 / N2), reads=[kcf], writes=[kcf])
    Cb = [ar.alloc([NT, 128], BF16, "Cb") for _ in range(2)]
    Sb = [ar.alloc([NT, 128], BF16, "Sb") for _ in range(2)]
    hs = [ar.alloc([2, 512], BF16, "hs") for _ in range(2)]
    HS = getattr(self, "HS_" + nm)
    kHPall = [kHP + ".%d" % tt for tt in range(NT)]
    kHMall = [kHM + ".%d" % tt for tt in range(NT)]
    for ft in range(NT):
        c_ap, kc = Cb[ft % 2]
        s_ap, ks = Sb[ft % 2]
        h_ap, kh = hs[ft % 2]
        kb.dma(c_ap, self.k["dftc_" + nm][ft], writes=[kc])
        kb.dma(s_ap, self.k["dfts_" + nm][ft], writes=[ks])
        pre, kpre = self.nb()
        pim, kpim = self.nb()
        for tt in range(NT):
            kb.pe(lambda e, pre=pre, c_ap=c_ap, tt=tt: e.matmul(pre, c_ap[:, tt, :], HP[:, tt, :], start=(tt == 0), stop=(tt == NT - 1)), reads=[kc, kHPall[tt]], writes=[kpre])
        for tt in range(NT):
            kb.pe(lambda e, pim=pim, s_ap=s_ap, tt=tt: e.matmul(pim, s_ap[:, tt, :], HM[:, tt, :], start=(tt == 0), stop=(tt == NT - 1)), reads=[ks, kHMall[tt]], writes=[kpim])
        ccol = 0 if ft == 0 else 1
        kb.dve(lambda e, pre=pre, h_ap=h_ap, ccol=ccol: e.scalar_tensor_tensor(out=h_ap[:, 0, :], in0=pre, scalar=cf[:, ccol:ccol + 1], in1=rz, op0=ALU.mult, op1=ALU.mult),
               reads=[kpre, kcf, krz], writes=[kh + ".0"])
        kb.dve(lambda e, pim=pim, h_ap=h_ap, ccol=ccol: e.scalar_tensor_tensor(out=h_ap[:, 1, :], in0=pim, scalar=cf[:, ccol:ccol + 1], in1=rz, op0=ALU.mult, op1=ALU.mult),
               reads=[kpim, kcf, krz], writes=[kh + ".1"])
        kb.dma(HS[:, ft].rearrange("r p c -> p r c"), h_ap, reads=[kh + ".0", kh + ".1"], writes=["HS_" + nm])
    pmf, kpmf = ar.alloc([1], F32, "pmf")
    pmc, kpmc = ar.alloc([1], BF16, "pmc")
    kb.dma(pmf, self.k["pm1_" + nm][0:128].rearrange("(p o) -> p o", o=1), writes=[kpmf])
    kb.act(lambda e: e.activation(out=pmc, in_=pmf, func=AF.Copy), reads=[kpmf], writes=[kpmc])
    psn, kpsn = self.nb()
    for tt in range(NT):
        kb.pe(lambda e, tt=tt: e.matmul(psn[0:1, :], pmc[:, 0:1], HP[:, tt, :], start=(tt == 0), stop=(tt == NT - 1)), reads=[kpmc, kHPall[tt]], writes=[kpsn])
    hn, khn = ar.alloc([512], F32, "hn")
    kb.dve(lambda e: e.scalar_tensor_tensor(out=hn[0:1, :], in0=psn[0:1, :], scalar=1.0 / N2, in1=rz[0:1, :], op0=ALU.mult, op1=ALU.mult),
           reads=[kpsn, krz], writes=[khn])
    kb.dma(getattr(self, "HNQ_" + nm), hn[0:1, :], reads=[khn], writes=["HNQ_" + nm])
    self.phase_end()


def _hy_h2(self, l, nm, seqs, n):
    kb, ar, w = self.kb, self.arena, self.w
    NT = n // 128
    HV, HX1, HX2 = getattr(self, "HV_" + nm), getattr(self, "HX1_" + nm), getattr(self, "HX2_" + nm)
    HYO, HS, HNQ = getattr(self, "HYO_" + nm), getattr(self, "HS_" + nm), getattr(self, "HNQ_" + nm)
    U, kU = ar.alloc([NT, 512], BF16, "U")
    Zb, kZb = ar.alloc([NT, 512], BF16, "Zb")
    Y, kY = ar.alloc([2, NT, 512], BF16, "Y")
    kb.dma(U, HV.rearrange("(a p) c -> p a c", p=128), writes=[kU + ".%d" % tt for tt in range(NT)])
    Cb = [ar.alloc([NT, 128], BF16, "Cb") for _ in range(2)]
    Sb = [ar.alloc([NT, 128], BF16, "Sb") for _ in range(2)]
    hsb = [ar.alloc([2, 256], BF16, "hsb") for _ in range(2)]
    bias, kbias = ar.alloc([2, 256], F32, "bias")
    kb.dma(bias, w["hy_bias"][l].rearrange("o c -> (o c)").partition_broadcast(128), writes=[kbias])
    hnq, khnq = ar.alloc([512], F32, "hnq")
    kb.dma(hnq[0:1, :], HNQ, writes=[khnq])
    pmf, kpmf = ar.alloc([1], F32, "pmf")
    pmc, kpmc = ar.alloc([1], BF16, "pmc")
    pmrf, kpmrf = ar.alloc([128], F32, "pmrf")
    pmr, kpmr = ar.alloc([128], BF16, "pmr")
    kb.dma(pmf, self.k["pm1_" + nm][0:128].rearrange("(p o) -> p o", o=1), writes=[kpmf])
    kb.act(lambda e: e.activation(out=pmc, in_=pmf, func=AF.Copy), reads=[kpmf], writes=[kpmc])
    kb.dma(pmrf[0:1, :], self.k["pm1_" + nm][0:128].rearrange("(o t) -> o t", o=1), writes=[kpmrf])
    kb.act(lambda e: e.activation(out=pmr[0:1, :], in_=pmrf[0:1, :], func=AF.Copy), reads=[kpmrf], writes=[kpmr])
    tq = [[ar.alloc([256], F32, "tq") for _ in range(4)] for _ in range(2)]
    ynq, kynq = ar.alloc([512], BF16, "ynq")
    gt = [ar.alloc([512], BF16, "gt") for _ in range(2)]
    tb = [ar.alloc([512], F32, "tb") for _ in range(2)]
    t2b = [ar.alloc([512], F32, "t2b") for _ in range(2)]
    ostg = [ar.alloc([512], F32, "ostg") for _ in range(2)]
    for o in range(2):
        src, ksrc = (U, kU) if o == 0 else (Zb, kZb)
        ksrc_all = [ksrc + ".%d" % tt for tt in range(NT)]
        for ft in range(NT):
            c_ap, kc = Cb[ft % 2]
            s_ap, ks = Sb[ft % 2]
            h_ap, kh = hsb[ft % 2]
            kb.dma(c_ap, self.k["dftc_" + nm][ft], writes=[kc])
            kb.dma(s_ap, self.k["dfts_" + nm][ft], writes=[ks])
            kb.dma(h_ap, HS[:, ft, :, o * 256:(o + 1) * 256].rearrange("r p c -> p r c"), writes=[kh])
            pre, kpre = self.nb()
            pim, kpim = self.nb()
            for tt in range(NT):
                kb.pe(lambda e, pre=pre, c_ap=c_ap, tt=tt, src=src: e.matmul(pre, c_ap[:, tt, :], src[:, tt, :], start=(tt == 0), stop=(tt == NT - 1)),
                      reads=[kc, ksrc_all[tt]], writes=[kpre])
            for tt in range(NT):
                kb.pe(lambda e, pim=pim, s_ap=s_ap, tt=tt, src=src: e.matmul(pim, s_ap[:, tt, :], src[:, tt, :], start=(tt == 0), stop=(tt == NT - 1)),
                      reads=[ks, ksrc_all[tt]], writes=[kpim])
            for b in range(2):
                (t1, k1), (t2, k2), (t3, k3), (t4, k4) = tq[b]
                bs = slice(b * 256, (b + 1) * 256)
                kb.dve(lambda e, pre=pre, h_ap=h_ap, t1=t1, bs=bs: e.tensor_tensor(out=t1, in0=pre[:, bs], in1=h_ap[:, 0, :], op=ALU.mult), reads=[kpre, kh], writes=[k1])
                kb.dve(lambda e, pim=pim, h_ap=h_ap, t2=t2, bs=bs: e.tensor_tensor(out=t2, in0=pim[:, bs], in1=h_ap[:, 1, :], op=ALU.mult), reads=[kpim, kh], writes=[k2])
                kb.dve(lambda e, pre=pre, h_ap=h_ap, t3=t3, bs=bs: e.tensor_tensor(out=t3, in0=pre[:, bs], in1=h_ap[:, 1, :], op=ALU.mult), reads=[kpre, kh], writes=[k3])
                kb.dve(lambda e, pim=pim, h_ap=h_ap, t4=t4, bs=bs: e.tensor_tensor(out=t4, in0=pim[:, bs], in1=h_ap[:, 0, :], op=ALU.mult), reads=[kpim, kh], writes=[k4])
                kb.pool(lambda e, t1=t1, t2=t2, ft=ft, bs=bs: e.tensor_tensor(out=Y[:, 0, ft, bs], in0=t1, in1=t2, op=ALU.subtract), reads=[k1, k2], writes=[kY + ".0.%d" % ft])
                kb.pool(lambda e, t3=t3, t4=t4, ft=ft, bs=bs: e.tensor_tensor(out=Y[:, 1, ft, bs], in0=t3, in1=t4, op=ALU.add), reads=[k3, k4], writes=[kY + ".1.%d" % ft])
        psn, kpsn = self.nb()
        for tt in range(NT):
            kb.pe(lambda e, psn=psn, tt=tt, src=src: e.matmul(psn[0:1, :], pmc[:, 0:1], src[:, tt, :], start=(tt == 0), stop=(tt == NT - 1)), reads=[kpmc, ksrc_all[tt]], writes=[kpsn])
        for b in range(2):
            kb.dve(lambda e, psn=psn, b=b, o=o: e.tensor_tensor(out=ynq[0:1, b * 256:(b + 1) * 256], in0=psn[0:1, b * 256:(b + 1) * 256], in1=hnq[0:1, o * 256:(o + 1) * 256], op=ALU.mult),
                   reads=[kpsn, khnq], writes=[kynq])
        gateD = HX1 if o == 0 else HX2
        kY0 = [kY + ".0.%d" % ft for ft in range(NT)]
        kY1 = [kY + ".1.%d" % ft for ft in range(NT)]
        for j in range(NT):
            c_ap, kc = Cb[j % 2]
            s_ap, ks = Sb[j % 2]
            g_ap, kg = gt[j % 2]
            tb_ap, ktb = tb[j % 2]
            t2_ap, kt2 = t2b[j % 2]
            o_ap, ko = ostg[j % 2]
            kb.dma(c_ap, self.k["dftc_" + nm][j], writes=[kc])
            kb.dma(s_ap, self.k["dfts_" + nm][j], writes=[ks])
            kb.dma(g_ap, gateD[j * 128:(j + 1) * 128, :], writes=[kg])
            py, kpy = self.nb()
            for ft in range(NT):
                kb.pe(lambda e, py=py, c_ap=c_ap, ft=ft: e.matmul(py, c_ap[:, ft, :], Y[:, 0, ft, :], start=(ft == 0), stop=False), reads=[kc, kY0[ft]], writes=[kpy])
                kb.pe(lambda e, py=py, s_ap=s_ap, ft=ft: e.matmul(py, s_ap[:, ft, :], Y[:, 1, ft, :], start=False, stop=False), reads=[ks, kY1[ft]], writes=[kpy])
            kb.pe(lambda e, py=py: e.matmul(py, pmr[0:1, :], ynq[0:1, :], start=False, stop=True), reads=[kpmr, kynq], writes=[kpy])
            for b in range(2):
                kb.pool(lambda e, tb_ap=tb_ap, src=src, j=j, b=b, o=o: e.tensor_tensor(out=tb_ap[:, b * 256:(b + 1) * 256], in0=src[:, j, b * 256:(b + 1) * 256], in1=bias[:, o, :], op=ALU.mult),
                        reads=[ksrc_all[j], kbias], writes=[ktb])
            kb.dve(lambda e, py=py, tb_ap=tb_ap, t2_ap=t2_ap: e.tensor_tensor(out=t2_ap, in0=py, in1=tb_ap, op=ALU.add), reads=[kpy, ktb], writes=[kt2])
            if o == 0:
                kb.pool(lambda e, t2_ap=t2_ap, g_ap=g_ap, j=j: e.tensor_tensor(out=Zb[:, j, :], in0=t2_ap, in1=g_ap, op=ALU.mult), reads=[kt2, kg], writes=[kZb + ".%d" % j])
            else:
                kb.pool(lambda e, t2_ap=t2_ap, g_ap=g_ap, o_ap=o_ap: e.tensor_tensor(out=o_ap, in0=t2_ap, in1=g_ap, op=ALU.mult), reads=[kt2, kg], writes=[ko])
                kb.dma(HYO[j * 128:(j + 1) * 128, :], o_ap, reads=[ko], writes=["HYO"])
    self.phase_end()


def _hyena_phase(self, l):
    last = (l == L - 1)
    groups = [("lat", self.lat, S)] + ([] if last else [("ctx", self.ctx, CL)])
    for nm, seqs, n in groups:
        self.hy_h0(l, nm, seqs, n)
        self.hy_h1(l, nm, n)
        self.hy_h2(l, nm, seqs, n)


Prog.hy_h0 = _hy_h0
Prog.hy_h1 = _hy_h1
Prog.hy_h2 = _hy_h2
Prog.phase_hyena = _hyena_phase


def build_program(nc, dbg=(), upto=None):
    P = Prog(nc, dbg=dbg)
    P.declare()
    P.phase_mod()
    P.phase_t0()
    for l in range(L):
        P.phase_a(l)
        P.phase_pool(l)
        P.phase_hyena(l)
        P.phase_attn(l)
        P.phase_c1(l)
        P.phase_c2(l)
    P.phase_final()
    P.kb.emit()
    return P


_PROG_CACHE = {}


def kernel(**inputs):
    shared = _shared_inputs(inputs)
    nc = bass.Bass("TRN2", target_bir_lowering=False)
    P = build_program(nc)
    in_maps = []
    for core in range(NCORES):
        m = _core_inputs(inputs, core, shared)
        in_maps.append({k_: v for k_, v in m.items() if k_ in P.dram})
    res = run_bass_kernel_spmd(nc, in_maps, core_ids=list(range(NCORES)))
    out = np.concatenate([np.asarray(r["y"], dtype=np.float32) for r in res.results], axis=0)
    return out
```

```python
import math
import numpy as np
import ml_dtypes
import concourse.bass as bass
import concourse.mybir as mybir
from concourse.bass_utils import run_bass_kernel_spmd

F32 = mybir.dt.float32
BF16 = mybir.dt.bfloat16
AF = mybir.ActivationFunctionType
ALU = mybir.AluOpType
AX = mybir.AxisListType

SAME_ENGINE_SYNC = True
N_DMA_SEMS = 24


class _Op:
    __slots__ = ("eng", "fn", "deps", "need_inc", "cnt", "dsem", "dval", "is_dma", "idx", "phase")

    def __init__(self, eng, fn, is_dma):
        self.eng = eng
        self.fn = fn
        self.deps = []
        self.need_inc = False
        self.cnt = 0
        self.dsem = -1
        self.dval = 0
        self.is_dma = is_dma
        self.idx = 0


class KB:
    ENGS = ("pe", "act", "dve", "pool", "sp")

    def __init__(self, nc):
        self.nc = nc
        self.ops = []
        self.last_w = {}
        self.readers = {}
        self.n_dma = 0
        self._bar_start = 0
        self.phase = None
        self.scopes = False

    def _add(self, eng, fn, reads, writes, is_dma):
        op = _Op(eng, fn, is_dma)
        op.idx = len(self.ops)
        op.phase = self.phase
        deps = set()
        for k in reads:
            w = self.last_w.get(k)
            if w is not None:
                deps.add(w)
        for k in writes:
            w = self.last_w.get(k)
            if w is not None:
                deps.add(w)
            for r in self.readers.get(k, ()):
                deps.add(r)
        deps.discard(op.idx)
        op.deps = sorted(deps)
        for k in reads:
            lst = self.readers.setdefault(k, [])
            if not is_dma:
                lst[:] = [r for r in lst if self.ops[r].is_dma or self.ops[r].eng != eng]
            lst.append(op.idx)
        for k in writes:
            self.last_w[k] = op.idx
            self.readers[k] = []
        self.ops.append(op)
        return op

    def pe(self, fn, reads=(), writes=()):
        return self._add("pe", fn, reads, writes, False)

    def act(self, fn, reads=(), writes=()):
        return self._add("act", fn, reads, writes, False)

    def dve(self, fn, reads=(), writes=()):
        return self._add("dve", fn, reads, writes, False)

    def pool(self, fn, reads=(), writes=()):
        return self._add("pool", fn, reads, writes, False)

    def dma(self, out, in_, reads=(), writes=(), q="sp", **kw):
        def fn(e, out=out, in_=in_, kw=kw):
            return e.dma_start(out=out, in_=in_, **kw)
        return self._add(q, fn, reads, writes, True)

    def barrier(self):
        last = {}
        dmas = []
        for op in self.ops[self._bar_start:]:
            if op.is_dma:
                dmas.append(op.idx)
            elif op.fn is not None:
                last[op.eng] = op.idx
        deps = sorted(set(list(last.values()) + dmas))
        for e in self.ENGS:
            op = _Op(e, None, False)
            op.idx = len(self.ops)
            op.phase = self.phase
            op.deps = list(deps)
            self.ops.append(op)
        self._bar_start = len(self.ops)
        self.last_w = {}
        self.readers = {}

    def emit(self):
        nc = self.nc
        ops = self.ops
        for op in ops:
            best = {}
            for d in op.deps:
                dop = ops[d]
                if dop.is_dma or dop.fn is None:
                    continue
                if dop.eng == op.eng and not op.is_dma:
                    if dop.eng == "pe" or not SAME_ENGINE_SYNC:
                        continue
                if d > best.get(dop.eng, -1):
                    best[dop.eng] = d
            for d in best.values():
                ops[d].need_inc = True
        cnt = {e: 0 for e in self.ENGS}
        dma_cnt = [0] * N_DMA_SEMS
        nd = 0
        for op in ops:
            if op.is_dma:
                s = nd % N_DMA_SEMS
                nd += 1
                dma_cnt[s] += 1
                op.dsem = s
                op.dval = 16 * dma_cnt[s]
            elif op.need_inc:
                cnt[op.eng] += 1
                op.cnt = cnt[op.eng]
        per_eng = {e: [] for e in self.ENGS}
        for op in ops:
            per_eng[op.eng].append(op)
        self.stats = {e: len(per_eng[e]) for e in self.ENGS}
        self.stats["incs"] = dict(cnt)

        import contextlib
        with contextlib.ExitStack() as st:
            esem = {e: st.enter_context(nc.semaphore("s_" + e)) for e in self.ENGS}
            dsem = [st.enter_context(nc.semaphore("d%d" % i)) for i in range(N_DMA_SEMS)]
            block = st.enter_context(nc.Block())

            def run(eng_name, e):
                waited_e = {x: 0 for x in self.ENGS}
                waited_d = [0] * N_DMA_SEMS
                cur = None
                for op in per_eng[eng_name]:
                    if self.scopes and op.phase != cur:
                        if cur is not None:
                            nc.pop_named_scope(cur)
                        cur = op.phase
                        if cur is not None:
                            nc.push_named_scope(cur)
                    need_e = {}
                    need_d = {}
                    for d in op.deps:
                        dop = ops[d]
                        if dop.is_dma:
                            if dop.dval > waited_d[dop.dsem]:
                                need_d[dop.dsem] = max(need_d.get(dop.dsem, 0), dop.dval)
                        else:
                            if dop.eng == eng_name and not op.is_dma:
                                if eng_name == "pe" or not SAME_ENGINE_SYNC:
                                    continue
                            if dop.cnt > waited_e[dop.eng]:
                                need_e[dop.eng] = max(need_e.get(dop.eng, 0), dop.cnt)
                    if op.is_dma:
                        prev = op.dval - 16
                        if prev > waited_d[op.dsem]:
                            need_d[op.dsem] = max(need_d.get(op.dsem, 0), prev)
                    for x, v in need_e.items():
                        e.wait_ge(esem[x], v)
                        waited_e[x] = v
                    for s, v in need_d.items():
                        e.wait_ge(dsem[s], v)
                        waited_d[s] = v
                    if op.fn is None:
                        continue
                    ins = op.fn(e)
                    if op.is_dma:
                        ins.then_inc(dsem[op.dsem], 16)
                    elif op.need_inc:
                        ins.then_inc(esem[eng_name], 1)
                if self.scopes and cur is not None:
                    nc.pop_named_scope(cur)
                last = {}
                for op in per_eng[eng_name]:
                    if op.is_dma:
                        last[op.dsem] = max(last.get(op.dsem, 0), op.dval)
                for s, v in last.items():
                    if v > waited_d[s]:
                        e.wait_ge(dsem[s], v)

            @block.sync
            def _(e):
                run("sp", e)

            @block.scalar
            def _(e):
                run("act", e)

            @block.vector
            def _(e):
                run("dve", e)

            @block.gpsimd
            def _(e):
                run("pool", e)

            @block.tensor
            def _(e):
                run("pe", e)


D = 1024
S = 4096
CL = 256
L = 2
NCORES = 8
NBC = 2
NH = 8
NOPE = 64
ROPE = 32
DV = 64
QR = 256
KVR = 128
NIN = 1440
C_KV, C_KR, C_Q, C_POOL, C_HY = 0, 128, 160, 416, 672
DFF = 4096
EPS = 1e-6
SCALE = float((NOPE + ROPE) ** -0.5)
POOL_WINDOWS = (2, 4, 8, 16)
HY_EMB = 33
HY_FFN = 64
MAGIC = 12582912.0
BFNP = ml_dtypes.bfloat16


def _rope_perm():
    perm = np.zeros(32, np.int64)
    for a in range(2):
        for hf in range(2):
            for f in range(8):
                perm[a * 16 + hf * 8 + f] = a * 16 + (1 - hf) * 8 + f
    return perm


def _rope_tables():
    n = S
    rows = n // 64
    r = np.repeat(np.arange(rows), 64).astype(np.float32)
    cidx = np.tile(np.arange(64), rows).astype(np.float32)
    inv = np.power(np.float32(10000.0), -(np.arange(8, dtype=np.float32) / np.float32(8))).astype(np.float32)
    ang = np.stack([r[:, None] * inv, cidx[:, None] * inv], axis=1).astype(np.float32)
    cos = np.cos(ang).astype(np.float32)
    sin = np.sin(ang).astype(np.float32)
    cosT = np.zeros((32, n), np.float32)
    sinT = np.zeros((32, n), np.float32)
    for a in range(2):
        for hf in range(2):
            for f in range(8):
                rr = a * 16 + hf * 8 + f
                cosT[rr] = cos[:, a, f]
                sinT[rr] = (-sin[:, a, f]) if hf == 0 else sin[:, a, f]
    return cosT, sinT


def _pool_bands(n, W):
    blocks = []
    index = {}
    table = {}
    nt = n // W
    for g, w in enumerate(POOL_WINDOWS):
        t = np.arange(n)
        lo = np.clip(t - w // 2, 0, n)
        hi = np.clip(t + w // 2, 0, n)
        for i in range(nt):
            T0 = i * W
            m_lo = max((T0 - w // 2) // 128, 0)
            m_hi = min((T0 + W + w // 2 - 1) // 128, n // 128 - 1)
            for m in range(m_lo, m_hi + 1):
                blk = np.zeros((128, 512), np.float32)
                for tt in range(T0, T0 + W):
                    s0 = max(lo[tt], m * 128)
                    s1 = min(hi[tt], (m + 1) * 128)
                    if s1 > s0:
                        blk[s0 - m * 128:s1 - m * 128, tt - T0] += 1.0 / float(hi[tt] - lo[tt])
                    if m * 128 <= tt < (m + 1) * 128:
                        blk[tt - m * 128, tt - T0] -= 1.0
                if not blk.any():
                    continue
                key = blk.tobytes()
                if key not in index:
                    index[key] = len(blocks)
                    blocks.append(blk)
                table[(g, i, m)] = index[key]
    return np.stack(blocks).astype(BFNP), table


def _dft_blocks(n):
    nt = n // 128
    t = np.arange(n, dtype=np.int64)
    m = (t[:, None] * t[None, :]) % (2 * n)
    ang = m.astype(np.float64) * (2.0 * np.pi / (2 * n))
    Cm = np.cos(ang)
    Sm = -np.sin(ang)

    def tile(M):
        M4 = M.reshape(nt, 128, nt, 128)
        return np.ascontiguousarray(M4.transpose(2, 1, 0, 3)).astype(BFNP)
    return tile(Cm), tile(Sm)


def _hy_consts(n):
    f32 = np.float32
    t = np.linspace(0.0, 1.0, n, dtype=f32)[:, None]
    bands = (HY_EMB - 1) // 2
    freqs = np.linspace(1e-4, bands - 1, bands, dtype=f32)[None, :]
    wpos = (f32(2.0 * math.pi) * np.arange(n, dtype=f32)[:, None] / f32(n)).astype(f32)
    z = np.concatenate([t, np.cos(freqs * wpos), -np.sin(freqs * wpos)], axis=-1).astype(f32)
    deltas = np.abs(np.linspace(math.log(1e-2) / 1.5, math.log(1e-2) / 0.3, 256, dtype=f32))
    dec = np.exp(-t * deltas[None, :]).astype(f32)
    pm1 = np.where(np.arange(n) % 2 == 0, 1.0, -1.0).astype(f32)
    return np.ascontiguousarray(z.T), dec, pm1


_CONST_CACHE = {}


def _host_consts():
    if _CONST_CACHE:
        return _CONST_CACHE
    c = _CONST_CACHE
    c["ident_f"] = np.eye(128, dtype=np.float32)
    cosT, sinT = _rope_tables()
    c["rope_cos"] = cosT
    c["rope_sin"] = sinT
    c["rope_cos_q"] = (cosT * np.float32(SCALE)).astype(np.float32)
    c["rope_sin_q"] = (sinT * np.float32(SCALE)).astype(np.float32)
    bl, tl = _pool_bands(S, 512)
    bc, tc = _pool_bands(CL, 256)
    c["band_lat"] = bl
    c["band_ctx"] = bc
    c["_band_tab_lat"] = tl
    c["_band_tab_ctx"] = tc
    for n, nm in ((S, "lat"), (CL, "ctx")):
        Cb, Sb = _dft_blocks(n)
        c["dftc_" + nm] = Cb
        c["dfts_" + nm] = Sb
        zT, dec, pm1 = _hy_consts(n)
        c["hyz_" + nm] = zT
        c["hydec_" + nm] = dec
        c["pm1_" + nm] = pm1
    return c


class Arena:
    def __init__(self, nc, nbytes):
        self.t = nc.alloc_sbuf_tensor("arena", [128, nbytes // 2], BF16).ap()
        self.cap = nbytes
        self.off = 0
        self.cnt = 0

    def reset(self):
        self.off = 0

    def alloc(self, free_shape, dtype, name="t"):
        esz = 4 if dtype == F32 else 2
        n = 1
        for d_ in free_shape:
            n *= d_
        nb = n * esz
        off = (self.off + 63) // 64 * 64
        assert off + nb <= self.cap, "arena overflow %s %d+%d>%d" % (name, off, nb, self.cap)
        self.off = off + nb
        ap = self.t[:, off // 2:(off + nb) // 2]
        if dtype == F32:
            ap = ap.bitcast(F32)
        if len(free_shape) == 2:
            ap = ap.rearrange("p (a b) -> p a b", b=free_shape[1])
        elif len(free_shape) == 3:
            ap = ap.rearrange("p (a b c) -> p a b c", b=free_shape[1], c=free_shape[2])
        self.cnt += 1
        return ap, "%s#%d" % (name, self.cnt)


class Seq:
    def __init__(self, name, n, v, kind, b):
        self.name = name
        self.n = n
        self.v = v
        self.kind = kind
        self.b = b
        self.W = 512 if n >= 512 else n
        self.nt = n // self.W


class Prog:
    def __init__(self, nc, dbg=()):
        self.nc = nc
        self.kb = KB(nc)
        self.dbg = set(dbg)
        self.dram = {}
        self.consts = _host_consts()

    def din(self, name, shape, dtype=F32):
        t = self.nc.dram_tensor(name, list(shape), dtype, kind="ExternalInput").ap()
        self.dram[name] = t
        return t

    def dscr(self, name, shape, dtype=F32):
        kind = "ExternalOutput" if name in self.dbg else "Internal"
        t = self.nc.dram_tensor(name, list(shape), dtype, kind=kind).ap()
        self.dram[name] = t
        return t

    def declare(self):
        nc = self.nc
        c = self.consts
        self.x_in = self.din("x", [NBC, S, D])
        self.ctx_in = self.din("ctx", [NBC, CL, D])
        self.cv_in = self.din("cv", [3, D])
        self.y_out = nc.dram_tensor("y", [NBC, S, D], F32, kind="ExternalOutput").ap()
        w = {}
        w["w_mod"] = self.din("w_mod", [L, D, 6 * D])
        w["b_mod"] = self.din("b_mod", [L, 6 * D])
        w["g_mix"] = self.din("g_mix", [L, D])
        w["g_mlp"] = self.din("g_mlp", [L, D])
        w["w_in"] = self.din("w_in", [L, D, NIN])
        w["w_in_rot"] = self.din("w_in_rot", [L, D, ROPE])
        w["g_q"] = self.din("g_q", [L, QR])
        w["w_q_up"] = self.din("w_q_up", [L, QR, NH * 96])
        w["w_q_rot"] = self.din("w_q_rot", [L, QR, NH * 96])
        w["g_kv"] = self.din("g_kv", [L, KVR])
        w["w_kv_up"] = self.din("w_kv_up", [L, KVR, NH * 128])
        w["pool_w"] = self.din("pool_w", [L, 4, 64, 64])
        w["pool_scale"] = self.din("pool_scale", [L, 256])
        w["hy_conv_w"] = self.din("hy_conv_w", [L, 3, 768])
        w["hy_conv_b"] = self.din("hy_conv_b", [L, 768])
        w["hy_f_w1"] = self.din("hy_f_w1", [L, HY_EMB, HY_FFN])
        w["hy_f_b1"] = self.din("hy_f_b1", [L, HY_FFN])
        w["hy_f_freq1"] = self.din("hy_f_freq1", [L, HY_FFN])
        w["hy_f_w2"] = self.din("hy_f_w2", [L, HY_FFN, HY_FFN])
        w["hy_f_b2"] = self.din("hy_f_b2", [L, HY_FFN])
        w["hy_f_freq2"] = self.din("hy_f_freq2", [L, HY_FFN])
        w["hy_f_w3"] = self.din("hy_f_w3", [L, HY_FFN, 1024])
        w["hy_bias"] = self.din("hy_bias", [L, 2, 256])
        w["g_out"] = self.din("g_out", [L, D])
        w["w_out"] = self.din("w_out", [L, D, D])
        w["w_mlp1"] = self.din("w_mlp1", [L, D, DFF])
        w["w_mlp2"] = self.din("w_mlp2", [L, DFF, D])
        w["g_final"] = self.din("g_final", [D])
        self.w = w
        k = {}
        for name, arr in c.items():
            if name.startswith("_"):
                continue
            dt_ = BF16 if arr.dtype == BFNP else F32
            k[name] = self.din(name, arr.shape, dt_)
        self.k = k
        self.lat = [Seq("b%d" % b, S, b, "lat", b) for b in range(NBC)]
        self.ctx = [Seq("c%d" % b, CL, 2, "ctx", b) for b in range(NBC)]
        for s in self.lat + self.ctx:
            n = s.n
            s.XT = self.dscr("XT_" + s.name, [8, 128, n])
            s.PKV = self.dscr("PKV_" + s.name, [128, n], BF16)
            s.KR = self.dscr("KR_" + s.name, [32, n], BF16)
            s.PQ = self.dscr("PQ_" + s.name, [2, 128, n], BF16)
            s.POOLU = self.dscr("POOLU_" + s.name, [n, 256], BF16)
            s.HYP = self.dscr("HYP_" + s.name, [6, 128, n + 2])
            s.ATT = self.dscr("ATT_" + s.name, [n, 512])
            s.POOLO = self.dscr("POOLO_" + s.name, [n, 256])
        for nm, n in (("lat", S), ("ctx", CL)):
            setattr(self, "HV_" + nm, self.dscr("HV_" + nm, [n, 512], BF16))
            setattr(self, "HX1_" + nm, self.dscr("HX1_" + nm, [n, 512], BF16))
            setattr(self, "HX2_" + nm, self.dscr("HX2_" + nm, [n, 512], BF16))
            setattr(self, "HYO_" + nm, self.dscr("HYO_" + nm, [n, 512]))
            setattr(self, "HS_" + nm, self.dscr("HS_" + nm, [2, n // 128, 128, 512], BF16))
            setattr(self, "HNQ_" + nm, self.dscr("HNQ_" + nm, [1, 512]))
        A = lambda name, shape, dt_: nc.alloc_sbuf_tensor(name, shape, dt_).ap()
        self.ident_f = A("ident_f_sb", [128, 128], F32)
        self.ident_b = A("ident_b", [128, 128], BF16)
        self.ones_b = A("ones_b", [128, 128], BF16)
        self.ones_f = A("ones_f", [128, 128], F32)
        self.eps = A("eps_t", [128, 1], F32)
        self.modT = A("modT", [128, L * 6 * 8 * 3], F32).rearrange("p (l k j v) -> p l k j v", l=L, k=6, j=8)
        self.ps2 = [nc.alloc_psum_tensor("ps2_%d" % i, [128, 1024], F32).ap() for i in range(4)]
        self.arena = Arena(nc, 204 * 1024)
        kb = self.kb
        kb.dma(self.ident_f, self.k["ident_f"], writes=["ident_f"])
        kb.act(lambda e: e.activation(out=self.ident_b, in_=self.ident_f, func=AF.Copy), reads=["ident_f"], writes=["ident_b"])
        kb.dve(lambda e: e.memset(self.ones_b, 1.0), writes=["ones_b"])
        kb.dve(lambda e: e.memset(self.ones_f, 1.0), writes=["ones_f"])
        kb.dve(lambda e: e.memset(self.eps, EPS), writes=["eps"])

    def bank(self, kidx):
        return self.ps2[kidx // 2][:, (kidx % 2) * 512:(kidx % 2 + 1) * 512], "pb%d" % kidx

    def mod(self, l, kind, v):
        return self.modT[:, l, kind, :, v]

    def phase_end(self):
        self.kb.barrier()
        self.arena.reset()

    def phase_mod(self):
        kb, ar, w = self.kb, self.arena, self.w
        cvs, kcvs = ar.alloc([D], F32, "cvs")
        sil, ksil = ar.alloc([D], F32, "sil")
        silT, ksilT = ar.alloc([8, 3], F32, "silT")
        kb.dma(cvs[0:3, :], self.cv_in, writes=[kcvs])
        kb.act(lambda e: e.activation(out=sil[0:3, :], in_=cvs[0:3, :], func=AF.Silu), reads=[kcvs], writes=[ksil])
        pb, kpb = self.bank(0)
        for j in range(8):
            kb.pe(lambda e, j=j: e.transpose(pb[:, j * 3:(j + 1) * 3], sil[0:3, j * 128:(j + 1) * 128], self.ident_f[0:3, 0:3]),
                  reads=[ksil, "ident_f"], writes=[kpb])
        kb.dve(lambda e: e.tensor_copy(out=silT.rearrange("p j v -> p (j v)"), in_=pb[:, 0:24]), reads=[kpb], writes=[ksilT])
        wbuf = [ar.alloc([8, 512], F32, "wmod") for _ in range(2)]
        modrow, kmodrow = ar.alloc([6 * D], F32, "modrow")
        brow, kbrow = ar.alloc([6 * D], F32, "brow")
        gbc, kgbc = ar.alloc([2, D], F32, "gbc")
        for l in range(L):
            kb.dma(brow[0:3, :], w["b_mod"][l].partition_broadcast(3), writes=[kbrow])
            kb.dma(gbc[0:3, 0, :], w["g_mix"][l].partition_broadcast(3), writes=[kgbc])
            kb.dma(gbc[0:3, 1, :], w["g_mlp"][l].partition_broadcast(3), writes=[kgbc])
            for ncn in range(12):
                wb, kwb = wbuf[ncn % 2]
                kb.dma(wb, w["w_mod"][l][:, ncn * 512:(ncn + 1) * 512].rearrange("(j p) n -> p j n", p=128), writes=[kwb])
                pm, kpm = self.bank(1 + ncn % 2)
                for j in range(8):
                    kb.pe(lambda e, j=j, wb=wb, pm=pm: e.matmul(pm[0:3, :], silT[:, j, :], wb[:, j, :], start=(j == 0), stop=(j == 7)),
                          reads=[ksilT, kwb], writes=[kpm])
                kb.dve(lambda e, pm=pm, ncn=ncn: e.tensor_tensor(out=modrow[0:3, ncn * 512:(ncn + 1) * 512], in0=pm[0:3, :],
                                                                in1=brow[0:3, ncn * 512:(ncn + 1) * 512], op=ALU.add),
                       reads=[kpm, kbrow], writes=[kmodrow])
            for (kind, gi) in ((1, 0), (4, 1)):
                sl = modrow[0:3, kind * D:(kind + 1) * D]
                kb.dve(lambda e, sl=sl, gi=gi: e.scalar_tensor_tensor(out=sl, in0=sl, scalar=1.0, in1=gbc[0:3, gi, :], op0=ALU.add, op1=ALU.mult),
                       reads=[kmodrow, kgbc], writes=[kmodrow])
            pt, kpt = self.bank(3)
            for kind in range(6):
                for j in range(8):
                    col = (kind * 8 + j) * 3
                    kb.pe(lambda e, kind=kind, j=j, col=col: e.transpose(pt[:, col:col + 3], modrow[0:3, kind * D + j * 128: kind * D + (j + 1) * 128],
                                                                        self.ident_f[0:3, 0:3]),
                          reads=[kmodrow, "ident_f"], writes=[kpt])
            kb.dve(lambda e, l=l: e.tensor_copy(out=self.modT[:, l].rearrange("p k j v -> p (k j v)"), in_=pt[:, 0:144]),
                   reads=[kpt], writes=["modT"])
        self.phase_end()

    def nb(self):
        self._nb = (getattr(self, "_nb", -1) + 1) % 8
        return self.bank(self._nb)

    def evac(self, i, fn_act, fn_dve, reads, writes):
        if i % 2 == 0:
            self.kb.act(fn_act, reads=reads, writes=writes)
        else:
            self.kb.dve(fn_dve, reads=reads, writes=writes)

    def phase_t0(self):
        kb, ar = self.kb, self.arena
        for s in self.lat + self.ctx:
            src = self.x_in[s.b] if s.kind == "lat" else self.ctx_in[s.b]
            W, nsub = s.W, s.W // 128
            xin = [ar.alloc([nsub, D], F32, "xin") for _ in range(2)]
            xt = [ar.alloc([8, W], F32, "xt") for _ in range(2)]
            for i in range(s.nt):
                xi, kxi = xin[i % 2]
                xo, kxo = xt[i % 2]
                kb.dma(xi, src[i * W:(i + 1) * W, :].rearrange("(a p) d -> p a d", p=128), writes=[kxi])
                for j in range(8):
                    pb, kpb = self.nb()
                    for sub in range(nsub):
                        kb.pe(lambda e, pb=pb, xi=xi, j=j, sub=sub: e.transpose(pb[:, sub * 128:(sub + 1) * 128], xi[:, sub, j * 128:(j + 1) * 128], self.ident_f),
                              reads=[kxi, "ident_f"], writes=[kpb])
                    self.evac(j, lambda e, pb=pb, xo=xo, j=j, W=W: e.activation(out=xo[:, j, :], in_=pb[:, 0:W], func=AF.Copy),
                              lambda e, pb=pb, xo=xo, j=j, W=W: e.tensor_copy(out=xo[:, j, :], in_=pb[:, 0:W]),
                              reads=[kpb], writes=[kxo + ".%d" % j])
                kb.dma(s.XT[:, :, i * W:(i + 1) * W].rearrange("j p t -> p j t"), xo,
                       reads=[kxo + ".%d" % j for j in range(8)], writes=["XT_" + s.name])
            self.phase_end()

    def fm_rstd(self, chunks, nfeat, W, sq, ksq, rs, krs, extra_sq=None):
        kb = self.kb
        pss, kpss = self.nb()
        n = len(chunks)
        for ci, (ap, key, P) in enumerate(chunks):
            kb.act(lambda e, ap=ap, ci=ci, P=P: e.activation(out=sq[0:P, ci, 0:W], in_=ap, func=AF.Square),
                   reads=[key], writes=[ksq + ".%d" % ci] + (extra_sq(ci) if extra_sq else []))
            kb.pe(lambda e, ci=ci, P=P: e.matmul(pss[:, 0:W], self.ones_b[0:P, :], sq[0:P, ci, 0:W], start=(ci == 0), stop=(ci == n - 1)),
                  reads=[ksq + ".%d" % ci, "ones_b"], writes=[kpss])
        kb.act(lambda e: e.activation(out=rs[:, 0:W], in_=pss[:, 0:W], func=AF.Sqrt, bias=self.eps[:, 0:1], scale=1.0 / nfeat),
               reads=[kpss, "eps"], writes=[krs])
        kb.dve(lambda e: e.reciprocal(out=rs[:, 0:W], in_=rs[:, 0:W]), reads=[krs], writes=[krs])

    def mod_norm(self, xt, kxt, W, gm, sh, sq, ksq, rs, krs, tmp, ktmp, hT, khT, extra=None, extra_sq=None):
        kb = self.kb
        self.fm_rstd([(xt[:, j, 0:W], kxt + ".%d" % j, 128) for j in range(8)], D, W, sq, ksq, rs, krs, extra_sq=extra_sq)
        for j in range(8):
            kb.dve(lambda e, j=j: e.scalar_tensor_tensor(out=tmp[:, j, 0:W], in0=xt[:, j, 0:W], scalar=gm[:, j:j + 1], in1=rs[:, 0:W],
                                                         op0=ALU.mult, op1=ALU.mult),
                   reads=[kxt + ".%d" % j, krs, "modT"], writes=[ktmp + ".%d" % j] + (extra(j) if extra else []))
            kb.act(lambda e, j=j: e.activation(out=hT[:, j, 0:W], in_=tmp[:, j, 0:W], func=AF.Identity, bias=sh[:, j:j + 1], scale=1.0),
                   reads=[ktmp + ".%d" % j, "modT"], writes=[khT + ".%d" % j])

    def load_cast_rows(self, dst, src2d, nj, split=1):
        ap, key = dst
        n = src2d.shape[-1]
        step = n // split
        for j in range(nj):
            for sp_ in range(split):
                self.kb.dma(ap[:, j, sp_ * step:(sp_ + 1) * step], src2d[j * 128:(j + 1) * 128, sp_ * step:(sp_ + 1) * step],
                            writes=[key + ".%d" % j], q="pool")

    def phase_a(self, l):
        kb, ar, w = self.kb, self.arena, self.w
        last = (l == L - 1)
        win = ar.alloc([8, NIN], BF16, "win")
        wrot = ar.alloc([8, ROPE], BF16, "wrot")
        self.load_cast_rows(win, w["w_in"][l], 8)
        self.load_cast_rows(wrot, w["w_in_rot"][l], 8)
        win_ap, kwin = win
        wrot_ap, kwrot = wrot
        kwin_all = [kwin + ".%d" % j for j in range(8)]
        kwrot_all = [kwrot + ".%d" % j for j in range(8)]
        gkv, kgkv = ar.alloc([1], F32, "gkv")
        gq, kgq = ar.alloc([2], F32, "gq")
        kb.dma(gkv, w["g_kv"][l].rearrange("(p o) -> p o", o=1), writes=[kgkv])
        for c_ in range(2):
            kb.dma(gq[:, c_:c_ + 1], w["g_q"][l][c_ * 128:(c_ + 1) * 128].rearrange("(p o) -> p o", o=1), writes=[kgq])
        zt, kzt = ar.alloc([6, 1], F32, "zt")
        kb.dve(lambda e: e.memset(zt, 0.0), writes=[kzt])
        WM = 512
        xts = [ar.alloc([8, WM], F32, "xt") for _ in range(2)]
        sq, ksq = ar.alloc([8, WM], BF16, "sq")
        rs, krs = ar.alloc([WM], F32, "rs")
        tmp, ktmp = ar.alloc([8, WM], F32, "tmp")
        hTs = [ar.alloc([8, WM], BF16, "hT") for _ in range(2)]
        sq2, ksq2 = ar.alloc([2, WM], BF16, "sq2")
        rs2, krs2 = ar.alloc([WM], F32, "rs2")
        pkvn = [ar.alloc([WM], BF16, "pkvn") for _ in range(2)]
        krs_ = [ar.alloc([WM], BF16, "kr") for _ in range(2)]
        pqn = [ar.alloc([2, WM], BF16, "pqn") for _ in range(2)]
        poolu = [ar.alloc([4, 256], BF16, "poolu") for _ in range(2)]
        hyp = [ar.alloc([6, WM], F32, "hyp") for _ in range(2)]
        ropec = [ar.alloc([WM], F32, "ropec") for _ in range(2)]
        ropes = [ar.alloc([WM], F32, "ropes") for _ in range(2)]
        rt1, krt1 = ar.alloc([WM], F32, "rt1")
        rt2, krt2 = ar.alloc([WM], F32, "rt2")
        it = 0
        for s in self.lat + self.ctx:
            full = not (last and s.kind == "ctx")
            W, nsub = s.W, s.W // 128
            gm, sh = self.mod(l, 1, s.v), self.mod(l, 0, s.v)
            if full:
                for (a0, a1) in ((0, 1), (s.n + 1, s.n + 2)):
                    kb.dma(s.HYP[:, :, a0:a1].rearrange("c p o -> p c o"), zt, reads=[kzt], writes=["HYP_" + s.name], allow_slow_non_contiguous=True)
            for i in range(s.nt):
                xt, kxt = xts[it % 2]
                hT, khT = hTs[it % 2]
                t0 = i * W
                kb.dma(xt[:, :, 0:W], s.XT[:, :, t0:t0 + W].rearrange("j p t -> p j t"), reads=["XT_" + s.name],
                       writes=[kxt + ".%d" % j for j in range(8)])
                self.mod_norm(xt, kxt, W, gm, sh, sq, ksq, rs, krs, tmp, ktmp, hT, khT)
                khT_all = [khT + ".%d" % j for j in range(8)]

                def proj_fm(col0, ncol, dstps, wt=win_ap, kw=kwin_all, hT=hT, khT_all=khT_all, W=W):
                    for j in range(8):
                        kb.pe(lambda e, j=j: e.matmul(dstps[0:ncol, 0:W], wt[:, j, col0:col0 + ncol], hT[:, j, 0:W], start=(j == 0), stop=(j == 7)),
                              reads=[kw[j], khT_all[j]], writes=[dstps_key[0]])
                pkv, kpkv = self.nb()
                dstps_key = [kpkv]
                proj_fm(C_KV, 128, pkv)
                self.fm_rstd([(pkv[:, 0:W], kpkv, 128)], KVR, W, sq2, ksq2, rs2, krs2)
                o_, ko_ = pkvn[it % 2]
                kb.dve(lambda e, o_=o_, pkv=pkv, W=W: e.scalar_tensor_tensor(out=o_[:, 0:W], in0=pkv[:, 0:W], scalar=gkv[:, 0:1], in1=rs2[:, 0:W],
                                                                            op0=ALU.mult, op1=ALU.mult),
                       reads=[kpkv, krs2, kgkv], writes=[ko_])
                kb.dma(s.PKV[:, t0:t0 + W], o_[:, 0:W], reads=[ko_], writes=["PKV_" + s.name])
                pka, kpka = self.nb()
                dstps_key = [kpka]
                proj_fm(C_KR, ROPE, pka)
                o_, ko_ = krs_[it % 2]
                if s.kind == "lat":
                    pkb, kpkb = self.nb()
                    dstps_key = [kpkb]
                    proj_fm(0, ROPE, pkb, wt=wrot_ap, kw=kwrot_all)
                    rc, krc = ropec[it % 2]
                    rsn, krsn = ropes[it % 2]
                    kb.dma(rc[0:32, 0:W], self.k["rope_cos"][:, t0:t0 + W], writes=[krc])
                    kb.dma(rsn[0:32, 0:W], self.k["rope_sin"][:, t0:t0 + W], writes=[krsn])
                    kb.dve(lambda e, pka=pka, rc=rc, W=W: e.tensor_tensor(out=rt1[0:32, 0:W], in0=pka[0:32, 0:W], in1=rc[0:32, 0:W], op=ALU.mult),
                           reads=[kpka, krc], writes=[krt1])
                    kb.dve(lambda e, pkb=pkb, rsn=rsn, W=W: e.tensor_tensor(out=rt2[0:32, 0:W], in0=pkb[0:32, 0:W], in1=rsn[0:32, 0:W], op=ALU.mult),
                           reads=[kpkb, krsn], writes=[krt2])
                    kb.dve(lambda e, o_=o_, W=W: e.tensor_tensor(out=o_[0:32, 0:W], in0=rt1[0:32, 0:W], in1=rt2[0:32, 0:W], op=ALU.add),
                           reads=[krt1, krt2], writes=[ko_])
                else:
                    kb.act(lambda e, o_=o_, pka=pka, W=W: e.activation(out=o_[0:32, 0:W], in_=pka[0:32, 0:W], func=AF.Copy),
                           reads=[kpka], writes=[ko_])
                kb.dma(s.KR[:, t0:t0 + W], o_[0:32, 0:W], reads=[ko_], writes=["KR_" + s.name])
                if not full:
                    it += 1
                    continue
                pq = [self.nb() for _ in range(2)]
                for c_ in range(2):
                    dstps_key = [pq[c_][1]]
                    proj_fm(C_Q + c_ * 128, 128, pq[c_][0])
                self.fm_rstd([(pq[c_][0][:, 0:W], pq[c_][1], 128) for c_ in range(2)], QR, W, sq2, ksq2, rs2, krs2)
                o_, ko_ = pqn[it % 2]
                for c_ in range(2):
                    kb.dve(lambda e, o_=o_, c_=c_, pq=pq, W=W: e.scalar_tensor_tensor(out=o_[:, c_, 0:W], in0=pq[c_][0][:, 0:W], scalar=gq[:, c_:c_ + 1],
                                                                                  in1=rs2[:, 0:W], op0=ALU.mult, op1=ALU.mult),
                           reads=[pq[c_][1], krs2, kgq], writes=[ko_ + ".%d" % c_])
                kb.dma(s.PQ[:, :, t0:t0 + W].rearrange("c p t -> p c t"), o_[:, :, 0:W], reads=[ko_ + ".0", ko_ + ".1"], writes=["PQ_" + s.name])
                o_, ko_ = poolu[it % 2]
                for sub in range(nsub):
                    pp, kpp = self.nb()
                    for j in range(8):
                        kb.pe(lambda e, j=j, sub=sub, pp=pp, hT=hT: e.matmul(pp[:, 0:256], hT[:, j, sub * 128:(sub + 1) * 128], win_ap[:, j, C_POOL:C_POOL + 256],
                                                                            start=(j == 0), stop=(j == 7)),
                              reads=[kwin_all[j], khT_all[j]], writes=[kpp])
                    self.evac(sub, lambda e, o_=o_, pp=pp, sub=sub: e.activation(out=o_[:, sub, :], in_=pp[:, 0:256], func=AF.Copy),
                              lambda e, o_=o_, pp=pp, sub=sub: e.tensor_copy(out=o_[:, sub, :], in_=pp[:, 0:256]),
                              reads=[kpp], writes=[ko_ + ".%d" % sub])
                kb.dma(s.POOLU[t0:t0 + W, :].rearrange("(a p) c -> p a c", p=128), o_[:, 0:nsub, :],
                       reads=[ko_ + ".%d" % sub for sub in range(nsub)], writes=["POOLU_" + s.name])
                o_, ko_ = hyp[it % 2]
                for c6 in range(6):
                    ph, kph = self.nb()
                    dstps_key = [kph]
                    proj_fm(C_HY + c6 * 128, 128, ph)
                    self.evac(c6, lambda e, o_=o_, ph=ph, c6=c6, W=W: e.activation(out=o_[:, c6, 0:W], in_=ph[:, 0:W], func=AF.Copy),
                              lambda e, o_=o_, ph=ph, c6=c6, W=W: e.tensor_copy(out=o_[:, c6, 0:W], in_=ph[:, 0:W]),
                              reads=[kph], writes=[ko_ + ".%d" % c6])
                kb.dma(s.HYP[:, :, 1 + t0:1 + t0 + W].rearrange("c p t -> p c t"), o_[:, :, 0:W],
                       reads=[ko_ + ".%d" % c6 for c6 in range(6)], writes=["HYP_" + s.name])
                it += 1
        self.phase_end()


def _shared_inputs(inp):
    f = lambda a: np.ascontiguousarray(np.asarray(a, dtype=np.float32))
    sh = {}
    for k_ in ("w_mod", "b_mod", "g_mix", "g_mlp", "w_in", "g_q", "g_kv", "w_kv_up", "pool_w", "pool_scale", "hy_conv_w",
               "hy_conv_b", "hy_f_w1", "hy_f_b1", "hy_f_freq1", "hy_f_w2", "hy_f_b2", "hy_f_freq2", "hy_f_w3", "hy_bias",
               "g_out", "w_out", "w_mlp1", "w_mlp2", "g_final"):
        sh[k_] = f(inp[k_])
    perm = _rope_perm()
    w_in = sh["w_in"]
    sh["w_in_rot"] = np.ascontiguousarray(w_in[:, :, C_KR:C_KR + ROPE][:, :, perm])
    wq = f(inp["w_q_up"]).reshape(L, QR, NH, 96)
    sh["w_q_up"] = np.ascontiguousarray(wq.reshape(L, QR, NH * 96))
    wrot = np.zeros_like(wq)
    wrot[..., NOPE:] = wq[..., NOPE:][..., perm]
    sh["w_q_rot"] = np.ascontiguousarray(wrot.reshape(L, QR, NH * 96))
    for name, arr in _host_consts().items():
        if not name.startswith("_"):
            sh[name] = arr
    return sh


def _core_inputs(inp, core, shared):
    b0 = core * NBC
    m = dict(shared)
    m["x"] = np.ascontiguousarray(np.asarray(inp["x"][b0:b0 + NBC], dtype=np.float32))
    m["ctx"] = np.ascontiguousarray(np.asarray(inp["ctx"][b0:b0 + NBC], dtype=np.float32))
    cv = np.concatenate([np.asarray(inp["c"][b0:b0 + NBC], dtype=np.float32), np.asarray(inp["c_ctx"], dtype=np.float32)[None, :]], axis=0)
    m["cv"] = np.ascontiguousarray(cv)
    return m


def _attn_phase(self, l):
    kb, ar, w = self.kb, self.arena, self.w
    last = (l == L - 1)
    NK = CL + S
    NKT = NK // 128
    wkv = ar.alloc([1, NH * 128], BF16, "wkv")
    self.load_cast_rows(wkv, w["w_kv_up"][l], 1)
    wq = ar.alloc([2, NH * 96], BF16, "wq")
    wqr = ar.alloc([2, NH * 96], BF16, "wqr")
    self.load_cast_rows(wq, w["w_q_up"][l], 2)
    self.load_cast_rows(wqr, w["w_q_rot"][l], 2)
    wkv_ap, kwkv = wkv[0], wkv[1] + ".0"
    wq_ap, wqr_ap = wq[0], wqr[0]
    kwq = [wq[1] + ".0", wq[1] + ".1"]
    kwqr = [wqr[1] + ".0", wqr[1] + ".1"]
    cosq, kcosq = ar.alloc([S], F32, "cosq")
    sinq, ksinq = ar.alloc([S], F32, "sinq")
    kb.dma(cosq[64:96, :], self.k["rope_cos_q"], writes=[kcosq])
    kb.dma(sinq[64:96, :], self.k["rope_sin_q"], writes=[ksinq])
    pkv, kpkv = ar.alloc([NK], BF16, "pkv")
    pq, kpq = ar.alloc([2, S], BF16, "pq")
    pqc, kpqc = ar.alloc([2, CL], BF16, "pqc")
    KT = [ar.alloc([NK], BF16, "KT") for _ in range(2)]
    VA = [ar.alloc([NKT, 128], BF16, "VA") for _ in range(2)]
    QT = [ar.alloc([S], BF16, "QT") for _ in range(2)]
    QTc = [ar.alloc([CL], BF16, "QTc") for _ in range(2)]
    PT = [ar.alloc([1024], BF16, "PT") for _ in range(2)]
    oT = [ar.alloc([512], F32, "oT") for _ in range(2)]
    rc = [ar.alloc([4, 1], F32, "rc") for _ in range(2)]
    stg = [ar.alloc([4, 64], F32, "stg") for _ in range(2)]
    rt1, krt1 = ar.alloc([512], F32, "rt1")
    rt2, krt2 = ar.alloc([512], F32, "rt2")
    for hb in range(2):
        kb.pool(lambda e, hb=hb: e.memset(VA[hb][0][:, :, 64:128], 1.0), writes=[VA[hb][1] + ".ones"])
    misc = [self.bank(6), self.bank(7)]
    mi = [0]

    def mbank():
        mi[0] += 1
        return misc[mi[0] % 2]
    cnt_o = [0]
    for b in range(NBC):
        lat, cx = self.lat[b], self.ctx[b]
        kb.dma(pkv[:, 0:CL], cx.PKV, reads=["PKV_" + cx.name], writes=[kpkv])
        kb.dma(pkv[:, CL:NK], lat.PKV, reads=["PKV_" + lat.name], writes=[kpkv])
        for hb in range(2):
            kb.dma(KT[hb][0][64:96, 0:CL], cx.KR, reads=["KR_" + cx.name], writes=[KT[hb][1] + ".r"])
            kb.dma(KT[hb][0][64:96, CL:NK], lat.KR, reads=["KR_" + lat.name], writes=[KT[hb][1] + ".r"])
        kb.dma(pq, lat.PQ.rearrange("c p t -> p c t"), reads=["PQ_" + lat.name], writes=[kpq])
        if not last:
            kb.dma(pqc, cx.PQ.rearrange("c p t -> p c t"), reads=["PQ_" + cx.name], writes=[kpqc])
        for h in range(NH):
            hb = h % 2
            kt_ap, kkt = KT[hb]
            va_ap, kva = VA[hb]
            for kc in range((NK + 511) // 512):
                k0 = kc * 512
                kw_ = min(512, NK - k0)
                pb, kpb = mbank()
                kb.pe(lambda e, pb=pb, h=h, k0=k0, kw_=kw_: e.matmul(pb[0:64, 0:kw_], wkv_ap[:, 0, h * 128:h * 128 + 64], pkv[:, k0:k0 + kw_], start=True, stop=True),
                      reads=[kwkv, kpkv], writes=[kpb])
                self.evac(kc, lambda e, pb=pb, kt_ap=kt_ap, k0=k0, kw_=kw_: e.activation(out=kt_ap[0:64, k0:k0 + kw_], in_=pb[0:64, 0:kw_], func=AF.Copy),
                          lambda e, pb=pb, kt_ap=kt_ap, k0=k0, kw_=kw_: e.tensor_copy(out=kt_ap[0:64, k0:k0 + kw_], in_=pb[0:64, 0:kw_]),
                          reads=[kpb], writes=[kkt + ".n"])
            for g8 in range((NKT + 7) // 8):
                k0 = g8 * 8
                ng = min(8, NKT - k0)
                pb, kpb = mbank()
                for i in range(ng):
                    kb.pe(lambda e, pb=pb, h=h, i=i, k0=k0: e.matmul(pb[:, i * 64:(i + 1) * 64], pkv[:, (k0 + i) * 128:(k0 + i + 1) * 128],
                                                                    wkv_ap[:, 0, h * 128 + 64:h * 128 + 128], start=True, stop=True),
                          reads=[kwkv, kpkv], writes=[kpb])
                self.evac(g8, lambda e, pb=pb, va_ap=va_ap, k0=k0, ng=ng: e.activation(out=va_ap[:, k0:k0 + ng, 0:64], in_=pb[:, 0:ng * 64].rearrange("p (a c) -> p a c", c=64), func=AF.Copy),
                          lambda e, pb=pb, va_ap=va_ap, k0=k0, ng=ng: e.tensor_copy(out=va_ap[:, k0:k0 + ng, 0:64], in_=pb[:, 0:ng * 64].rearrange("p (a c) -> p a c", c=64)),
                          reads=[kpb], writes=[kva + ".v"])
            qsets = [(lat, S, True, NKT, pq, kpq, QT[hb])]
            if not last:
                qsets.append((cx, CL, False, CL // 128, pqc, kpqc, QTc[hb]))
            for (qs, nq, rope, nkt, pq_ap, kpq_, (qt_ap, kqt)) in qsets:
                QW = min(512, nq)
                for qc in range(nq // QW):
                    q0 = qc * QW
                    pa, kpa = mbank()
                    for c_ in range(2):
                        kb.pe(lambda e, pa=pa, c_=c_, h=h, q0=q0, QW=QW, pq_ap=pq_ap: e.matmul(pa[0:96, 0:QW], wq_ap[:, c_, h * 96:(h + 1) * 96], pq_ap[:, c_, q0:q0 + QW],
                                                                                        start=(c_ == 0), stop=(c_ == 1)),
                              reads=[kwq[c_], kpq_], writes=[kpa])
                    kb.act(lambda e, pa=pa, qt_ap=qt_ap, q0=q0, QW=QW: e.activation(out=qt_ap[0:64, q0:q0 + QW], in_=pa[0:64, 0:QW], func=AF.Copy, scale=SCALE),
                           reads=[kpa], writes=[kqt + ".n"])
                    if rope:
                        pb, kpb = mbank()
                        for c_ in range(2):
                            kb.pe(lambda e, pb=pb, c_=c_, h=h, q0=q0, QW=QW, pq_ap=pq_ap: e.matmul(pb[0:96, 0:QW], wqr_ap[:, c_, h * 96:(h + 1) * 96], pq_ap[:, c_, q0:q0 + QW],
                                                                                            start=(c_ == 0), stop=(c_ == 1)),
                                  reads=[kwqr[c_], kpq_], writes=[kpb])
                        kb.dve(lambda e, pa=pa, q0=q0, QW=QW: e.tensor_tensor(out=rt1[64:96, 0:QW], in0=pa[64:96, 0:QW], in1=cosq[64:96, q0:q0 + QW], op=ALU.mult),
                               reads=[kpa, kcosq], writes=[krt1])
                        kb.dve(lambda e, pb=pb, q0=q0, QW=QW: e.tensor_tensor(out=rt2[64:96, 0:QW], in0=pb[64:96, 0:QW], in1=sinq[64:96, q0:q0 + QW], op=ALU.mult),
                               reads=[kpb, ksinq], writes=[krt2])
                        kb.dve(lambda e, qt_ap=qt_ap, q0=q0, QW=QW: e.tensor_tensor(out=qt_ap[64:96, q0:q0 + QW], in0=rt1[64:96, 0:QW], in1=rt2[64:96, 0:QW], op=ALU.add),
                               reads=[krt1, krt2], writes=[kqt + ".r"])
                    else:
                        kb.act(lambda e, pa=pa, qt_ap=qt_ap, q0=q0, QW=QW: e.activation(out=qt_ap[64:96, q0:q0 + QW], in_=pa[64:96, 0:QW], func=AF.Copy, scale=SCALE),
                               reads=[kpa], writes=[kqt + ".r"])
                npair = (nkt + 1) // 2
                items = [(qb, kp) for qb in range(nq // QW) for kp in range(npair)]
                po_of = {}
                for qb in range(nq // QW):
                    cnt_o[0] += 1
                    po_of[qb] = (self.bank(4 + cnt_o[0] % 2), cnt_o[0] % 2)

                def emit_S(ii, nkt=nkt, QW=QW, kt_ap=kt_ap, qt_ap=qt_ap, kkt=kkt, kqt=kqt):
                    qb, kp = items[ii]
                    q0 = qb * QW
                    ps_s = self.ps2[ii % 2]
                    kps = ["pb%d" % (2 * (ii % 2)), "pb%d" % (2 * (ii % 2) + 1)]
                    pt_ap, kpt = PT[ii % 2]
                    nh = min(2, nkt - 2 * kp)
                    for half in range(nh):
                        kt = 2 * kp + half
                        kb.pe(lambda e, ps_s=ps_s, half=half, kt=kt, q0=q0:
                              e.matmul(ps_s[:, half * 512:half * 512 + QW], kt_ap[0:96, kt * 128:(kt + 1) * 128], qt_ap[0:96, q0:q0 + QW], start=True, stop=True),
                              reads=[kkt + ".n", kkt + ".r", kqt + ".n", kqt + ".r"], writes=[kps[half]])
                    if QW == 512:
                        kb.act(lambda e, ps_s=ps_s, pt_ap=pt_ap, nh=nh: e.activation(out=pt_ap[:, 0:nh * 512], in_=ps_s[:, 0:nh * 512], func=AF.Exp),
                               reads=kps[0:nh], writes=[kpt])
                    else:
                        for half in range(nh):
                            kb.act(lambda e, ps_s=ps_s, pt_ap=pt_ap, half=half: e.activation(out=pt_ap[:, half * 512:half * 512 + QW], in_=ps_s[:, half * 512:half * 512 + QW], func=AF.Exp),
                                   reads=[kps[half]], writes=[kpt])

                def emit_PV(ii, nkt=nkt, QW=QW, va_ap=va_ap, kva=kva, qs=qs, h=h, npair=npair):
                    qb, kp = items[ii]
                    q0 = qb * QW
                    (po, kpo), par = po_of[qb]
                    pt_ap, kpt = PT[ii % 2]
                    nh = min(2, nkt - 2 * kp)
                    for half in range(nh):
                        kt = 2 * kp + half
                        kb.pe(lambda e, po=po, pt_ap=pt_ap, half=half, kt=kt:
                              e.matmul(po[:, 0:QW], va_ap[:, kt, :], pt_ap[:, half * 512:half * 512 + QW], start=(kt == 0), stop=(kt == nkt - 1)),
                              reads=[kva + ".v", kva + ".ones", kpt], writes=[kpo])
                    if kp != npair - 1:
                        return
                    o_ap, ko = oT[par]
                    r_ap, kr_ = rc[par]
                    s_ap, ks_ = stg[par]
                    nsub = QW // 128
                    kb.dve(lambda e, o_ap=o_ap, po=po: e.tensor_copy(out=o_ap[:, 0:QW], in_=po[:, 0:QW]), reads=[kpo], writes=[ko])
                    ptr, kptr = mbank()
                    for sub in range(nsub):
                        kb.pe(lambda e, ptr=ptr, o_ap=o_ap, sub=sub: e.transpose(ptr[:, sub * 128:(sub + 1) * 128], o_ap[:, sub * 128:(sub + 1) * 128], self.ident_f),
                              reads=[ko, "ident_f"], writes=[kptr])
                    kb.dve(lambda e, ptr=ptr, r_ap=r_ap, nsub=nsub: e.reciprocal(out=r_ap[:, 0:nsub, :], in_=ptr[:, 0:nsub * 128].rearrange("p (a c) -> p a c", c=128)[:, :, 64:65]),
                           reads=[kptr], writes=[kr_])
                    for sub in range(nsub):
                        kb.act(lambda e, ptr=ptr, s_ap=s_ap, r_ap=r_ap, sub=sub: e.activation(out=s_ap[:, sub, :], in_=ptr[:, sub * 128:sub * 128 + 64], func=AF.Copy, scale=r_ap[:, sub, :]),
                               reads=[kptr, kr_], writes=[ks_ + ".%d" % sub])
                    kb.dma(qs.ATT[q0:q0 + QW, h * 64:(h + 1) * 64].rearrange("(a p) c -> p a c", p=128), s_ap[:, 0:nsub, :],
                           reads=[ks_ + ".%d" % sub for sub in range(nsub)], writes=["ATT_" + qs.name])

                emit_S(0)
                for ii in range(len(items)):
                    if ii + 1 < len(items):
                        emit_S(ii + 1)
                    emit_PV(ii)
    self.phase_end()


Prog.phase_attn = _attn_phase


def _pool_phase(self, l):
    kb, ar, w = self.kb, self.arena, self.w
    last = (l == L - 1)
    pw = ar.alloc([4, 64], BF16, "pw")
    kb.dma(pw[0][0:64, :, :], w["pool_w"][l].rearrange("g i o -> i g o"), writes=[pw[1]], q="pool")
    psc, kpsc = ar.alloc([256], F32, "psc")
    kb.dma(psc, w["pool_scale"][l].partition_broadcast(128), writes=[kpsc])
    groups = [("lat", self.lat)] + ([] if last else [("ctx", self.ctx)])
    for nm, seqs in groups:
        bandc = self.k["band_" + nm]
        tab = self.consts["_band_tab_" + nm]
        nblk = bandc.shape[0]
        band, kband = ar.alloc([nblk, 512], BF16, "band")
        kb.dma(band, bandc.rearrange("b p t -> p b t"), writes=[kband])
        n = seqs[0].n
        NT = n // 128
        u, ku = ar.alloc([NT, 256], BF16, "u")
        dg = [ar.alloc([4, 512], BF16, "dg") for _ in range(2)]
        stg = [ar.alloc([4, 256], F32, "pstg") for _ in range(2)]
        it = 0
        for s in seqs:
            W, nsub = s.W, s.W // 128
            kb.dma(u, s.POOLU.rearrange("(a p) c -> p a c", p=128), reads=["POOLU_" + s.name], writes=[ku])
            for i in range(s.nt):
                d_ap, kd = dg[it % 2]
                s_ap, ks = stg[it % 2]
                for g in range(4):
                    ms = sorted(m for (g_, i_, m) in tab if g_ == g and i_ == i)
                    pb, kpb = self.nb()
                    for mi_, m in enumerate(ms):
                        bi = tab[(g, i, m)]
                        kb.pe(lambda e, pb=pb, g=g, m=m, bi=bi, W=W, mi_=mi_, nm_=len(ms), u=u, band=band: e.matmul(pb[0:64, 0:W], u[:, m, g * 64:(g + 1) * 64], band[:, bi, 0:W],
                                                                                            start=(mi_ == 0), stop=(mi_ == nm_ - 1)),
                              reads=[ku, kband], writes=[kpb])
                    self.evac(g, lambda e, pb=pb, d_ap=d_ap, g=g, W=W: e.activation(out=d_ap[0:64, g, 0:W], in_=pb[0:64, 0:W], func=AF.Copy),
                              lambda e, pb=pb, d_ap=d_ap, g=g, W=W: e.tensor_copy(out=d_ap[0:64, g, 0:W], in_=pb[0:64, 0:W]),
                              reads=[kpb], writes=[kd + ".%d" % g])
                for sub in range(nsub):
                    pb, kpb = self.nb()
                    for g in range(4):
                        kb.pe(lambda e, pb=pb, g=g, sub=sub, d_ap=d_ap: e.matmul(pb[:, g * 64:(g + 1) * 64], d_ap[0:64, g, sub * 128:(sub + 1) * 128], pw[0][0:64, g, :],
                                                                                start=True, stop=True),
                              reads=[kd + ".%d" % g, pw[1]], writes=[kpb])
                    kb.dve(lambda e, pb=pb, s_ap=s_ap, sub=sub: e.tensor_tensor(out=s_ap[:, sub, :], in0=pb[:, 0:256], in1=psc, op=ALU.mult),
                           reads=[kpb, kpsc], writes=[ks + ".%d" % sub])
                kb.dma(s.POOLO[i * W:(i + 1) * W, :].rearrange("(a p) c -> p a c", p=128), s_ap[:, 0:nsub, :],
                       reads=[ks + ".%d" % sub for sub in range(nsub)], writes=["POOLO_" + s.name])
                it += 1
    self.phase_end()


def _c1_phase(self, l):
    kb, ar, w = self.kb, self.arena, self.w
    last = (l == L - 1)
    wout = ar.alloc([8, D], BF16, "wout")
    self.load_cast_rows(wout, w["w_out"][l], 8)
    wo_ap, kwo = wout
    gout, kgout = ar.alloc([D], F32, "gout")
    kb.dma(gout, w["g_out"][l].partition_broadcast(128), writes=[kgout])
    xts = [ar.alloc([8, 512], F32, "xt") for _ in range(2)]
    att = [ar.alloc([4, 512], F32, "att") for _ in range(2)]
    pl = [ar.alloc([4, 256], F32, "pl") for _ in range(2)]
    hy = [ar.alloc([4, 256], F32, "hy") for _ in range(2)]
    junk, kjunk = ar.alloc([512], BF16, "junk")
    ss = [ar.alloc([3, 4], F32, "ss") for _ in range(2)]
    mrg = [ar.alloc([4, D], BF16, "mrg") for _ in range(2)]
    mT = [ar.alloc([8, 512], BF16, "mT") for _ in range(2)]
    seqs = self.lat + ([] if last else self.ctx)
    GR = ((0, 512, 0), (512, 256, 1), (768, 256, 2))
    it = 0
    for s in seqs:
        W, nsub = s.W, s.W // 128
        g1 = self.mod(l, 2, s.v)
        HYO = self.HYO_lat if s.kind == "lat" else self.HYO_ctx
        for i in range(s.nt):
            t0 = i * W
            xt, kxt = xts[it % 2]
            a_ap, ka = att[it % 2]
            p_ap, kp = pl[it % 2]
            h_ap, kh = hy[it % 2]
            ss_ap, kss = ss[it % 2]
            m_ap, km = mrg[it % 2]
            t_ap, kt = mT[it % 2]
            kb.dma(xt[:, :, 0:W], s.XT[:, :, t0:t0 + W].rearrange("j p t -> p j t"), reads=["XT_" + s.name], writes=[kxt + ".%d" % j for j in range(8)])
            kb.dma(a_ap[:, 0:nsub, :], s.ATT[t0:t0 + W, :].rearrange("(a p) c -> p a c", p=128), reads=["ATT_" + s.name], writes=[ka])
            kb.dma(p_ap[:, 0:nsub, :], s.POOLO[t0:t0 + W, :].rearrange("(a p) c -> p a c", p=128), reads=["POOLO_" + s.name], writes=[kp])
            kb.dma(h_ap[:, 0:nsub, :], HYO[t0:t0 + W, s.b * 256:(s.b + 1) * 256].rearrange("(a p) c -> p a c", p=128), reads=["HYO"], writes=[kh])
            kb.dve(lambda e, ss_ap=ss_ap: e.memset(ss_ap, 0.0), writes=[kss])
            srcs = ((a_ap, ka), (p_ap, kp), (h_ap, kh))
            for sub in range(nsub):
                for (c0, ng, gi) in GR:
                    src, ksrc = srcs[gi]
                    kb.act(lambda e, src=src, sub=sub, ng=ng, gi=gi, ss_ap=ss_ap: e.activation(out=junk[:, 0:ng], in_=src[:, sub, :], func=AF.Square,
                                                                                            accum_out=ss_ap[:, gi, sub:sub + 1]),
                           reads=[ksrc, kss], writes=[kss, kjunk])
            for (c0, ng, gi) in GR:
                kb.act(lambda e, ss_ap=ss_ap, gi=gi, ng=ng, nsub=nsub: e.activation(out=ss_ap[:, gi, 0:nsub], in_=ss_ap[:, gi, 0:nsub], func=AF.Sqrt,
                                                                                bias=self.eps[:, 0:1], scale=1.0 / ng),
                       reads=[kss, "eps"], writes=[kss])
            kb.dve(lambda e, ss_ap=ss_ap: e.reciprocal(out=ss_ap, in_=ss_ap), reads=[kss], writes=[kss])
            for sub in range(nsub):
                for (c0, ng, gi) in GR:
                    src, ksrc = srcs[gi]
                    kb.dve(lambda e, src=src, sub=sub, c0=c0, ng=ng, gi=gi, ss_ap=ss_ap, m_ap=m_ap:
                           e.scalar_tensor_tensor(out=m_ap[:, sub, c0:c0 + ng], in0=src[:, sub, :], scalar=ss_ap[:, gi, sub:sub + 1], in1=gout[:, c0:c0 + ng],
                                                  op0=ALU.mult, op1=ALU.mult),
                           reads=[ksrc, kss, kgout], writes=[km + ".%d" % sub])
                pb, kpb = self.nb()
                pbb = pb.bitcast(BF16)
                for j in range(8):
                    kb.pe(lambda e, pbb=pbb, m_ap=m_ap, sub=sub, j=j: e.transpose(pbb[:, j * 128:(j + 1) * 128], m_ap[:, sub, j * 128:(j + 1) * 128], self.ident_b),
                          reads=[km + ".%d" % sub, "ident_b"], writes=[kpb])
                self.evac(sub, lambda e, pbb=pbb, t_ap=t_ap, sub=sub: e.activation(out=t_ap[:, :, sub * 128:(sub + 1) * 128], in_=pbb.rearrange("p (j t) -> p j t", t=128), func=AF.Copy),
                          lambda e, pbb=pbb, t_ap=t_ap, sub=sub: e.tensor_copy(out=t_ap[:, :, sub * 128:(sub + 1) * 128], in_=pbb.rearrange("p (j t) -> p j t", t=128)),
                          reads=[kpb], writes=[kt + ".%d" % sub])
            for oc in range(8):
                pb, kpb = self.nb()
                for k_ in range(8):
                    kb.pe(lambda e, pb=pb, k_=k_, oc=oc, t_ap=t_ap, W=W: e.matmul(pb[:, 0:W], wo_ap[:, k_, oc * 128:(oc + 1) * 128], t_ap[:, k_, 0:W], start=(k_ == 0), stop=(k_ == 7)),
                          reads=[kwo + ".%d" % k_] + [kt + ".%d" % sub for sub in range(nsub)], writes=[kpb])
                kb.dve(lambda e, pb=pb, xt=xt, oc=oc, W=W, g1=g1: e.scalar_tensor_tensor(out=xt[:, oc, 0:W], in0=pb[:, 0:W], scalar=g1[:, oc:oc + 1], in1=xt[:, oc, 0:W],
                                                                                    op0=ALU.mult, op1=ALU.add),
                       reads=[kpb, kxt + ".%d" % oc, "modT"], writes=[kxt + ".%d" % oc])
            kb.dma(s.XT[:, :, t0:t0 + W].rearrange("j p t -> p j t"), xt[:, :, 0:W], reads=[kxt + ".%d" % j for j in range(8)], writes=["XT_" + s.name])
            it += 1
    self.phase_end()


def _c2_phase(self, l):
    kb, ar, w = self.kb, self.arena, self.w
    last = (l == L - 1)
    w1 = ar.alloc([8, DFF], BF16, "w1")
    w2 = ar.alloc([32, D], BF16, "w2")
    self.load_cast_rows(w1, w["w_mlp1"][l], 8, split=2)
    self.load_cast_rows(w2, w["w_mlp2"][l], 32)
    w1_ap, kw1 = w1
    w2_ap, kw2 = w2
    xts = [ar.alloc([8, 512], F32, "xt")] * 2
    sq, ksq = ar.alloc([8, 512], BF16, "sq")
    rs, krs = ar.alloc([512], F32, "rs")
    hT, khT = ar.alloc([8, 512], BF16, "hT")
    hid, khid = ar.alloc([32, 512], BF16, "hid")
    tmp = hid[:, 0:16, :].rearrange("p a b -> p (a b)").bitcast(F32).rearrange("p (a b) -> p a b", b=512)
    ktmp = khid + ".tmp"
    rl = [ar.alloc([512], F32, "rl") for _ in range(2)]
    seqs = self.lat + ([] if last else self.ctx)
    itc = [0]
    for s in seqs:
        W = s.W
        gm, sh, g2 = self.mod(l, 4, s.v), self.mod(l, 3, s.v), self.mod(l, 5, s.v)
        for i in range(s.nt):
            t0 = i * W
            xt, kxt = xts[itc[0] % 2]
            itc[0] += 1
            kb.dma(xt[:, :, 0:W], s.XT[:, :, t0:t0 + W].rearrange("j p t -> p j t"), reads=["XT_" + s.name], writes=[kxt + ".%d" % j for j in range(8)])
            self.mod_norm(xt, kxt, W, gm, sh, sq, ksq, rs, krs, tmp, ktmp, hT, khT,
                          extra=lambda j: [khid + ".%d" % (2 * j), khid + ".%d" % (2 * j + 1)])
            khT_all = [khT + ".%d" % j for j in range(8)]
            for hc in range(32):
                pb, kpb = self.nb()
                for j in range(8):
                    kb.pe(lambda e, pb=pb, j=j, hc=hc, W=W: e.matmul(pb[:, 0:W], w1_ap[:, j, hc * 128:(hc + 1) * 128], hT[:, j, 0:W], start=(j == 0), stop=(j == 7)),
                          reads=[kw1 + ".%d" % j, khT_all[j]], writes=[kpb])
                r_ap, kr_ = rl[hc % 2]
                kb.act(lambda e, pb=pb, r_ap=r_ap, W=W: e.activation(out=r_ap[:, 0:W], in_=pb[:, 0:W], func=AF.Relu), reads=[kpb], writes=[kr_])
                kb.dve(lambda e, r_ap=r_ap, hc=hc, W=W: e.tensor_tensor(out=hid[:, hc, 0:W], in0=r_ap[:, 0:W], in1=r_ap[:, 0:W], op=ALU.mult),
                       reads=[kr_], writes=[khid + ".%d" % hc])
            for oc in range(8):
                pb, kpb = self.nb()
                for hc in range(32):
                    kb.pe(lambda e, pb=pb, hc=hc, oc=oc, W=W: e.matmul(pb[:, 0:W], w2_ap[:, hc, oc * 128:(oc + 1) * 128], hid[:, hc, 0:W], start=(hc == 0), stop=(hc == 31)),
                          reads=[kw2 + ".%d" % hc, khid + ".%d" % hc], writes=[kpb])
                kb.dve(lambda e, pb=pb, oc=oc, W=W, g2=g2, xt=xt: e.scalar_tensor_tensor(out=xt[:, oc, 0:W], in0=pb[:, 0:W], scalar=g2[:, oc:oc + 1], in1=xt[:, oc, 0:W],
                                                                             op0=ALU.mult, op1=ALU.add),
                       reads=[kpb, kxt + ".%d" % oc, "modT"], writes=[kxt + ".%d" % oc])
            kb.dma(s.XT[:, :, t0:t0 + W].rearrange("j p t -> p j t"), xt[:, :, 0:W], reads=[kxt + ".%d" % j for j in range(8)], writes=["XT_" + s.name])
    self.phase_end()


def _final_phase(self):
    kb, ar, w = self.kb, self.arena, self.w
    gf, kgf = ar.alloc([8], F32, "gf")
    for j in range(8):
        kb.dma(gf[:, j:j + 1], w["g_final"][j * 128:(j + 1) * 128].rearrange("(p o) -> p o", o=1), writes=[kgf])
    xts = [ar.alloc([8, 512], F32, "xt") for _ in range(2)]
    sq, ksq = ar.alloc([8, 512], BF16, "sq")
    rs, krs = ar.alloc([512], F32, "rs")
    xn, kxn = ar.alloc([8, 512], F32, "xn")
    yts = [ar.alloc([4, D], F32, "yt") for _ in range(2)]
    it = 0
    for s in self.lat:
        W = 512
        for i in range(s.nt):
            t0 = i * W
            xt, kxt = xts[it % 2]
            yt, kyt = yts[it % 2]
            kb.dma(xt, s.XT[:, :, t0:t0 + W].rearrange("j p t -> p j t"), reads=["XT_" + s.name], writes=[kxt + ".%d" % j for j in range(8)])
            self.fm_rstd([(xt[:, j, :], kxt + ".%d" % j, 128) for j in range(8)], D, W, sq, ksq, rs, krs)
            for j in range(8):
                kb.dve(lambda e, xt=xt, j=j: e.scalar_tensor_tensor(out=xn[:, j, :], in0=xt[:, j, :], scalar=gf[:, j:j + 1], in1=rs, op0=ALU.mult, op1=ALU.mult),
                       reads=[kxt + ".%d" % j, krs, kgf], writes=[kxn + ".%d" % j])
            for sub in range(4):
                pp = self.ps2[sub % 2]
                kpp = ["pb%d" % (2 * (sub % 2)), "pb%d" % (2 * (sub % 2) + 1)]
                for j in range(8):
                    kb.pe(lambda e, pp=pp, j=j, sub=sub: e.transpose(pp[:, j * 128:(j + 1) * 128], xn[:, j, sub * 128:(sub + 1) * 128], self.ident_f),
                          reads=[kxn + ".%d" % j, "ident_f"], writes=[kpp[j // 4]])
                self.evac(sub, lambda e, pp=pp, yt=yt, sub=sub: e.activation(out=yt[:, sub, :], in_=pp, func=AF.Copy),
                          lambda e, pp=pp, yt=yt, sub=sub: e.tensor_copy(out=yt[:, sub, :], in_=pp),
                          reads=kpp, writes=[kyt + ".%d" % sub])
            kb.dma(self.y_out[s.b][t0:t0 + W, :].rearrange("(a p) d -> p a d", p=128), yt, reads=[kyt + ".%d" % sub for sub in range(4)], writes=["y"])
            it += 1
    self.phase_end()


Prog.phase_pool = _pool_phase
Prog.phase_c1 = _c1_phase
Prog.phase_c2 = _c2_phase
Prog.phase_final = _final_phase


def _hy_h0(self, l, nm, seqs, n):
    kb, ar, w = self.kb, self.arena, self.w
    NT = n // 128
    cw, kcw = ar.alloc([6, 3], F32, "cw")
    cb, kcb = ar.alloc([6], F32, "cb")
    for c6 in range(6):
        for k_ in range(3):
            kb.dma(cw[:, c6, k_:k_ + 1], w["hy_conv_w"][l][k_, c6 * 128:(c6 + 1) * 128].rearrange("(p o) -> p o", o=1), writes=[kcw])
        kb.dma(cb[:, c6:c6 + 1], w["hy_conv_b"][l][c6 * 128:(c6 + 1) * 128].rearrange("(p o) -> p o", o=1), writes=[kcb])
    hp = [ar.alloc([n + 2], F32, "hp") for _ in range(2)]
    acc = [ar.alloc([n], F32, "acc") for _ in range(2)]
    ucb = [ar.alloc([n], BF16, "ucb") for _ in range(2)]
    tm = [ar.alloc([NT, 128], BF16, "tm") for _ in range(2)]
    dests = (getattr(self, "HV_" + nm), getattr(self, "HX1_" + nm), getattr(self, "HX2_" + nm))
    it = 0
    for si, s in enumerate(seqs):
        for c6 in range(6):
            h_ap, kh = hp[it % 2]
            a_ap, ka = acc[it % 2]
            u_ap, ku = ucb[it % 2]
            t_ap, kt = tm[it % 2]
            kb.dma(h_ap, s.HYP[c6], reads=["HYP_" + s.name], writes=[kh])
            kb.act(lambda e, h_ap=h_ap, a_ap=a_ap, c6=c6: e.activation(out=a_ap, in_=h_ap[:, 1:n + 1], func=AF.Identity, bias=cb[:, c6:c6 + 1], scale=cw[:, c6, 1:2]),
                   reads=[kh, kcw, kcb], writes=[ka])
            kb.dve(lambda e, h_ap=h_ap, a_ap=a_ap, c6=c6: e.scalar_tensor_tensor(out=a_ap, in0=h_ap[:, 0:n], scalar=cw[:, c6, 0:1], in1=a_ap, op0=ALU.mult, op1=ALU.add),
                   reads=[kh, kcw, ka], writes=[ka])
            kb.dve(lambda e, h_ap=h_ap, a_ap=a_ap, u_ap=u_ap, c6=c6: e.scalar_tensor_tensor(out=u_ap, in0=h_ap[:, 2:n + 2], scalar=cw[:, c6, 2:3], in1=a_ap, op0=ALU.mult, op1=ALU.add),
                   reads=[kh, kcw, ka], writes=[ku])
            for g8 in range((NT + 7) // 8):
                ng = min(8, NT - g8 * 8)
                pb, kpb = self.nb()
                pbb = pb.bitcast(BF16)
                for i in range(ng):
                    tt = g8 * 8 + i
                    kb.pe(lambda e, pbb=pbb, u_ap=u_ap, i=i, tt=tt: e.transpose(pbb[:, i * 128:(i + 1) * 128], u_ap[:, tt * 128:(tt + 1) * 128], self.ident_b),
                          reads=[ku, "ident_b"], writes=[kpb])
                self.evac(g8, lambda e, pbb=pbb, t_ap=t_ap, g8=g8, ng=ng: e.activation(out=t_ap[:, g8 * 8:g8 * 8 + ng, :], in_=pbb[:, 0:ng * 128].rearrange("p (a c) -> p a c", c=128), func=AF.Copy),
                          lambda e, pbb=pbb, t_ap=t_ap, g8=g8, ng=ng: e.tensor_copy(out=t_ap[:, g8 * 8:g8 * 8 + ng, :], in_=pbb[:, 0:ng * 128].rearrange("p (a c) -> p a c", c=128)),
                          reads=[kpb], writes=[kt + ".%d" % g8])
            dst = dests[c6 // 2]
            col0 = si * 256 + (c6 % 2) * 128
            kb.dma(dst[:, col0:col0 + 128].rearrange("(a p) c -> p a c", p=128), t_ap,
                   reads=[kt + ".%d" % g8 for g8 in range((NT + 7) // 8)], writes=["HU_" + nm])
            it += 1
    self.phase_end()


def _hy_h1(self, l, nm, n):
    kb, ar, w = self.kb, self.arena, self.w
    NT = n // 128
    N2 = 2 * n
    CW = min(512, n)
    zT, kzT = ar.alloc([n], F32, "zT")
    kb.dma(zT[0:HY_EMB, :], self.k["hyz_" + nm], writes=[kzT])
    w1s, kw1s = ar.alloc([HY_FFN], F32, "w1s")
    w2s, kw2s = ar.alloc([HY_FFN], F32, "w2s")
    w3s, kw3s = ar.alloc([1024], F32, "w3s")
    kb.dma(w1s[0:HY_EMB, :], w["hy_f_w1"][l], writes=[kw1s])
    kb.dma(w2s[0:HY_FFN, :], w["hy_f_w2"][l], writes=[kw2s])
    kb.dma(w3s[0:HY_FFN, :], w["hy_f_w3"][l], writes=[kw3s])
    par, kpar = ar.alloc([6], F32, "par")
    for ci, nm_ in enumerate(("hy_f_b1", "hy_f_freq1", "hy_f_b2", "hy_f_freq2")):
        kb.dma(par[0:64, ci:ci + 1], w[nm_][l].rearrange("(p o) -> p o", o=1), writes=[kpar])
    kb.dve(lambda e: e.tensor_tensor(out=par[0:64, 4:5], in0=par[0:64, 0:1], in1=par[0:64, 1:2], op=ALU.mult), reads=[kpar], writes=[kpar])
    kb.dve(lambda e: e.tensor_tensor(out=par[0:64, 5:6], in0=par[0:64, 2:3], in1=par[0:64, 3:4], op=ALU.mult), reads=[kpar], writes=[kpar])
    h1T, kh1 = ar.alloc([n], F32, "h1T")
    h2T, kh2 = ar.alloc([n], F32, "h2T")
    arg, karg = ar.alloc([512], F32, "arg")
    kk, kkk = ar.alloc([512], F32, "kk")

    def layer(srcT, ksrc, wS, kwS, K, fcol, fbcol, dstT, kdst):
        for ch in range(n // CW):
            c0 = ch * CW
            pb, kpb = self.nb()
            kb.pe(lambda e, pb=pb, c0=c0: e.matmul(pb[0:64, 0:CW], wS[0:K, 0:64], srcT[0:K, c0:c0 + CW], start=True, stop=True),
                  reads=[ksrc, kwS], writes=[kpb])
            kb.act(lambda e, pb=pb: e.activation(out=arg[0:64, 0:CW], in_=pb[0:64, 0:CW], func=AF.Identity, bias=par[0:64, fbcol:fbcol + 1], scale=par[0:64, fcol:fcol + 1]),
                   reads=[kpb, kpar], writes=[karg])
            kb.dve(lambda e: e.tensor_scalar(out=kk[0:64, 0:CW], in0=arg[0:64, 0:CW], scalar1=1.0 / (2 * math.pi), scalar2=MAGIC, op0=ALU.mult, op1=ALU.add),
                   reads=[karg], writes=[kkk])
            kb.dve(lambda e: e.tensor_scalar(out=kk[0:64, 0:CW], in0=kk[0:64, 0:CW], scalar1=-MAGIC, scalar2=-2 * math.pi, op0=ALU.add, op1=ALU.mult),
                   reads=[kkk], writes=[kkk])
            kb.dve(lambda e: e.tensor_tensor(out=arg[0:64, 0:CW], in0=arg[0:64, 0:CW], in1=kk[0:64, 0:CW], op=ALU.add), reads=[karg, kkk], writes=[karg])
            kb.act(lambda e, c0=c0: e.activation(out=dstT[0:64, c0:c0 + CW], in_=arg[0:64, 0:CW], func=AF.Sin), reads=[karg], writes=[kdst])
    layer(zT, kzT, w1s, kw1s, HY_EMB, 1, 4, h1T, kh1)
    layer(h1T, kh1, w2s, kw2s, HY_FFN, 3, 5, h2T, kh2)
    HP, kHP = ar.alloc([NT, 512], BF16, "HP")
    HM, kHM = ar.alloc([NT, 512], BF16, "HM")
    dec = [ar.alloc([256], F32, "dec") for _ in range(2)]
    tp = [ar.alloc([4, 256], F32, "tp") for _ in range(2)]
    ab = [ar.alloc([4, 256], F32, "ab") for _ in range(2)]
    psZ = self.ps2[3]
    kZ = ["pb6", "pb7"]
    for tt in range(NT):
        pt = self.ps2[tt % 2]
        kpt = ["pb%d" % (2 * (tt % 2)), "pb%d" % (2 * (tt % 2) + 1)]
        d_ap, kd = dec[tt % 2]
        t_ap, ktp = tp[tt % 2]
        a_ap, kab = ab[tt % 2]
        for hf in range(2):
            kb.pe(lambda e, pt=pt, hf=hf, tt=tt: e.matmul(pt[:, hf * 512:(hf + 1) * 512], h2T[0:64, tt * 128:(tt + 1) * 128], w3s[0:64, hf * 512:(hf + 1) * 512], start=True, stop=True),
                  reads=[kh2, kw3s], writes=[kpt[hf]])
        kb.dma(d_ap, self.k["hydec_" + nm][tt * 128:(tt + 1) * 128, :], writes=[kd])
        for q in range(4):
            kb.dve(lambda e, pt=pt, t_ap=t_ap, d_ap=d_ap, q=q: e.tensor_tensor(out=t_ap[:, q, :], in0=pt[:, q * 256:(q + 1) * 256], in1=d_ap, op=ALU.mult),
                   reads=[kpt[q // 2], kd], writes=[ktp])
        if tt == 0:
            for q in (1, 3):
                kb.dve(lambda e, t_ap=t_ap, q=q: e.memset(t_ap[0:1, q, :], 0.0), reads=[ktp], writes=[ktp])
        kb.act(lambda e, t_ap=t_ap, a_ap=a_ap: e.activation(out=a_ap, in_=t_ap, func=AF.Abs), reads=[ktp], writes=[kab])
        for hf in range(2):
            kb.pe(lambda e, a_ap=a_ap, hf=hf, tt=tt: e.matmul(psZ[:, hf * 512:(hf + 1) * 512], self.ones_f, a_ap[:, 2 * hf:2 * hf + 2, :].rearrange("p a c -> p (a c)"),
                                                               start=(tt == 0), stop=(tt == NT - 1)),
                  reads=[kab, "ones_f"], writes=[kZ[hf]])
        for o in range(2):
            kb.pool(lambda e, t_ap=t_ap, o=o, tt=tt: e.tensor_tensor(out=HP[:, tt, o * 256:(o + 1) * 256], in0=t_ap[:, 2 * o, :], in1=t_ap[:, 2 * o + 1, :], op=ALU.add),
                    reads=[ktp], writes=[kHP + ".%d" % tt])
            kb.pool(lambda e, t_ap=t_ap, o=o, tt=tt: e.tensor_tensor(out=HM[:, tt, o * 256:(o + 1) * 256], in0=t_ap[:, 2 * o, :], in1=t_ap[:, 2 * o + 1, :], op=ALU.subtract),
                    reads=[ktp], writes=[kHM + ".%d" % tt])
    zc, kzc = ar.alloc([1024], F32, "zc")
    rz, krz = ar.alloc([512], F32, "rz")
    kb.act(lambda e: e.activation(out=zc, in_=psZ, func=AF.Copy), reads=kZ, writes=[kzc])
    for o in range(2):
        kb.dve(lambda e, o=o: e.tensor_tensor(out=rz[:, o * 256:(o + 1) * 256], in0=zc[:, o * 512:o * 512 + 256], in1=zc[:, o * 512 + 256:(o + 1) * 512], op=ALU.add),
               reads=[kzc], writes=[krz])
    kb.dve(lambda e: e.reciprocal(out=rz, in_=rz), reads=[krz], writes=[krz])
    cf, kcf = ar.alloc([2], F32, "cf")
    kb.dve(lambda e: e.memset(cf, 2.0 / N2), writes=[kcf])
    kb.dve(lambda e: e.memset(cf[0:1, 0:1], 1.0 / N2), reads=[kcf], writes=[kcf])
    Cb = [ar.alloc([NT, 128], BF16, "Cb") for _ in range(2)]
    Sb = [ar.alloc([NT, 128], BF16, "Sb") for _ in range(2)]
    hs = [ar.alloc([2, 512], BF16, "hs") for _ in range(2)]
    HS = getattr(self, "HS_" + nm)
    kHPall = [kHP + ".%d" % tt for tt in range(NT)]
    kHMall = [kHM + ".%d" % tt for tt in range(NT)]
    for ft in range(NT):
        c_ap, kc = Cb[ft % 2]
        s_ap, ks = Sb[ft % 2]
        h_ap, kh = hs[ft % 2]
        kb.dma(c_ap, self.k["dftc_" + nm][ft], writes=[kc])
        kb.dma(s_ap, self.k["dfts_" + nm][ft], writes=[ks])
        pre, kpre = self.nb()
        pim, kpim = self.nb()
        for tt in range(NT):
            kb.pe(lambda e, pre=pre, c_ap=c_ap, tt=tt: e.matmul(pre, c_ap[:, tt, :], HP[:, tt, :], start=(tt == 0), stop=(tt == NT - 1)), reads=[kc, kHPall[tt]], writes=[kpre])
        for tt in range(NT):
            kb.pe(lambda e, pim=pim, s_ap=s_ap, tt=tt: e.matmul(pim, s_ap[:, tt, :], HM[:, tt, :], start=(tt == 0), stop=(tt == NT - 1)), reads=[ks, kHMall[tt]], writes=[kpim])
        ccol = 0 if ft == 0 else 1
        kb.dve(lambda e, pre=pre, h_ap=h_ap, ccol=ccol: e.scalar_tensor_tensor(out=h_ap[:, 0, :], in0=pre, scalar=cf[:, ccol:ccol + 1], in1=rz, op0=ALU.mult, op1=ALU.mult),
               reads=[kpre, kcf, krz], writes=[kh + ".0"])
        kb.dve(lambda e, pim=pim, h_ap=h_ap, ccol=ccol: e.scalar_tensor_tensor(out=h_ap[:, 1, :], in0=pim, scalar=cf[:, ccol:ccol + 1], in1=rz, op0=ALU.mult, op1=ALU.mult),
               reads=[kpim, kcf, krz], writes=[kh + ".1"])
        kb.dma(HS[:, ft].rearrange("r p c -> p r c"), h_ap, reads=[kh + ".0", kh + ".1"], writes=["HS_" + nm])
    pmf, kpmf = ar.alloc([1], F32, "pmf")
    pmc, kpmc = ar.alloc([1], BF16, "pmc")
    kb.dma(pmf, self.k["pm1_" + nm][0:128].rearrange("(p o) -> p o", o=1), writes=[kpmf])
    kb.act(lambda e: e.activation(out=pmc, in_=pmf, func=AF.Copy), reads=[kpmf], writes=[kpmc])
    psn, kpsn = self.nb()
    for tt in range(NT):
        kb.pe(lambda e, tt=tt: e.matmul(psn[0:1, :], pmc[:, 0:1], HP[:, tt, :], start=(tt == 0), stop=(tt == NT - 1)), reads=[kpmc, kHPall[tt]], writes=[kpsn])
    hn, khn = ar.alloc([512], F32, "hn")
    kb.dve(lambda e: e.scalar_tensor_tensor(out=hn[0:1, :], in0=psn[0:1, :], scalar=1.0 / N2, in1=rz[0:1, :], op0=ALU.mult, op1=ALU.mult),
           reads=[kpsn, krz], writes=[khn])
    kb.dma(getattr(self, "HNQ_" + nm), hn[0:1, :], reads=[khn], writes=["HNQ_" + nm])
    self.phase_end()


def _hy_h2(self, l, nm, seqs, n):
    kb, ar, w = self.kb, self.arena, self.w
    NT = n // 128
    HV, HX1, HX2 = getattr(self, "HV_" + nm), getattr(self, "HX1_" + nm), getattr(self, "HX2_" + nm)
    HYO, HS, HNQ = getattr(self, "HYO_" + nm), getattr(self, "HS_" + nm), getattr(self, "HNQ_" + nm)
    U, kU = ar.alloc([NT, 512], BF16, "U")
    Zb, kZb = ar.alloc([NT, 512], BF16, "Zb")
    Y, kY = ar.alloc([2, NT, 512], BF16, "Y")
    kb.dma(U, HV.rearrange("(a p) c -> p a c", p=128), writes=[kU + ".%d" % tt for tt in range(NT)])
    Cb = [ar.alloc([NT, 128], BF16, "Cb") for _ in range(2)]
    Sb = [ar.alloc([NT, 128], BF16, "Sb") for _ in range(2)]
    hsb = [ar.alloc([2, 256], BF16, "hsb") for _ in range(2)]
    bias, kbias = ar.alloc([2, 256], F32, "bias")
    kb.dma(bias, w["hy_bias"][l].rearrange("o c -> (o c)").partition_broadcast(128), writes=[kbias])
    hnq, khnq = ar.alloc([512], F32, "hnq")
    kb.dma(hnq[0:1, :], HNQ, writes=[khnq])
    pmf, kpmf = ar.alloc([1], F32, "pmf")
    pmc, kpmc = ar.alloc([1], BF16, "pmc")
    pmrf, kpmrf = ar.alloc([128], F32, "pmrf")
    pmr, kpmr = ar.alloc([128], BF16, "pmr")
    kb.dma(pmf, self.k["pm1_" + nm][0:128].rearrange("(p o) -> p o", o=1), writes=[kpmf])
    kb.act(lambda e: e.activation(out=pmc, in_=pmf, func=AF.Copy), reads=[kpmf], writes=[kpmc])
    kb.dma(pmrf[0:1, :], self.k["pm1_" + nm][0:128].rearrange("(o t) -> o t", o=1), writes=[kpmrf])
    kb.act(lambda e: e.activation(out=pmr[0:1, :], in_=pmrf[0:1, :], func=AF.Copy), reads=[kpmrf], writes=[kpmr])
    tq = [[ar.alloc([256], F32, "tq") for _ in range(4)] for _ in range(2)]
    ynq, kynq = ar.alloc([512], BF16, "ynq")
    gt = [ar.alloc([512], BF16, "gt") for _ in range(2)]
    tb = [ar.alloc([512], F32, "tb") for _ in range(2)]
    t2b = [ar.alloc([512], F32, "t2b") for _ in range(2)]
    ostg = [ar.alloc([512], F32, "ostg") for _ in range(2)]
    for o in range(2):
        src, ksrc = (U, kU) if o == 0 else (Zb, kZb)
        ksrc_all = [ksrc + ".%d" % tt for tt in range(NT)]
        for ft in range(NT):
            c_ap, kc = Cb[ft % 2]
            s_ap, ks = Sb[ft % 2]
            h_ap, kh = hsb[ft % 2]
            kb.dma(c_ap, self.k["dftc_" + nm][ft], writes=[kc])
            kb.dma(s_ap, self.k["dfts_" + nm][ft], writes=[ks])
            kb.dma(h_ap, HS[:, ft, :, o * 256:(o + 1) * 256].rearrange("r p c -> p r c"), writes=[kh])
            pre, kpre = self.nb()
            pim, kpim = self.nb()
            for tt in range(NT):
                kb.pe(lambda e, pre=pre, c_ap=c_ap, tt=tt, src=src: e.matmul(pre, c_ap[:, tt, :], src[:, tt, :], start=(tt == 0), stop=(tt == NT - 1)),
                      reads=[kc, ksrc_all[tt]], writes=[kpre])
            for tt in range(NT):
                kb.pe(lambda e, pim=pim, s_ap=s_ap, tt=tt, src=src: e.matmul(pim, s_ap[:, tt, :], src[:, tt, :], start=(tt == 0), stop=(tt == NT - 1)),
                      reads=[ks, ksrc_all[tt]], writes=[kpim])
            for b in range(2):
                (t1, k1), (t2, k2), (t3, k3), (t4, k4) = tq[b]
                bs = slice(b * 256, (b + 1) * 256)
                kb.dve(lambda e, pre=pre, h_ap=h_ap, t1=t1, bs=bs: e.tensor_tensor(out=t1, in0=pre[:, bs], in1=h_ap[:, 0, :], op=ALU.mult), reads=[kpre, kh], writes=[k1])
                kb.dve(lambda e, pim=pim, h_ap=h_ap, t2=t2, bs=bs: e.tensor_tensor(out=t2, in0=pim[:, bs], in1=h_ap[:, 1, :], op=ALU.mult), reads=[kpim, kh], writes=[k2])
                kb.dve(lambda e, pre=pre, h_ap=h_ap, t3=t3, bs=bs: e.tensor_tensor(out=t3, in0=pre[:, bs], in1=h_ap[:, 1, :], op=ALU.mult), reads=[kpre, kh], writes=[k3])
                kb.dve(lambda e, pim=pim, h_ap=h_ap, t4=t4, bs=bs: e.tensor_tensor(out=t4, in0=pim[:, bs], in1=h_ap[:, 0, :], op=ALU.mult), reads=[kpim, kh], writes=[k4])
                kb.pool(lambda e, t1=t1, t2=t2, ft=ft, bs=bs: e.tensor_tensor(out=Y[:, 0, ft, bs], in0=t1, in1=t2, op=ALU.subtract), reads=[k1, k2], writes=[kY + ".0.%d" % ft])
                kb.pool(lambda e, t3=t3, t4=t4, ft=ft, bs=bs: e.tensor_tensor(out=Y[:, 1, ft, bs], in0=t3, in1=t4, op=ALU.add), reads=[k3, k4], writes=[kY + ".1.%d" % ft])
        psn, kpsn = self.nb()
        for tt in range(NT):
            kb.pe(lambda e, psn=psn, tt=tt, src=src: e.matmul(psn[0:1, :], pmc[:, 0:1], src[:, tt, :], start=(tt == 0), stop=(tt == NT - 1)), reads=[kpmc, ksrc_all[tt]], writes=[kpsn])
        for b in range(2):
            kb.dve(lambda e, psn=psn, b=b, o=o: e.tensor_tensor(out=ynq[0:1, b * 256:(b + 1) * 256], in0=psn[0:1, b * 256:(b + 1) * 256], in1=hnq[0:1, o * 256:(o + 1) * 256], op=ALU.mult),
                   reads=[kpsn, khnq], writes=[kynq])
        gateD = HX1 if o == 0 else HX2
        kY0 = [kY + ".0.%d" % ft for ft in range(NT)]
        kY1 = [kY + ".1.%d" % ft for ft in range(NT)]
        for j in range(NT):
            c_ap, kc = Cb[j % 2]
            s_ap, ks = Sb[j % 2]
            g_ap, kg = gt[j % 2]
            tb_ap, ktb = tb[j % 2]
            t2_ap, kt2 = t2b[j % 2]
            o_ap, ko = ostg[j % 2]
            kb.dma(c_ap, self.k["dftc_" + nm][j], writes=[kc])
            kb.dma(s_ap, self.k["dfts_" + nm][j], writes=[ks])
            kb.dma(g_ap, gateD[j * 128:(j + 1) * 128, :], writes=[kg])
            py, kpy = self.nb()
            for ft in range(NT):
                kb.pe(lambda e, py=py, c_ap=c_ap, ft=ft: e.matmul(py, c_ap[:, ft, :], Y[:, 0, ft, :], start=(ft == 0), stop=False), reads=[kc, kY0[ft]], writes=[kpy])
                kb.pe(lambda e, py=py, s_ap=s_ap, ft=ft: e.matmul(py, s_ap[:, ft, :], Y[:, 1, ft, :], start=False, stop=False), reads=[ks, kY1[ft]], writes=[kpy])
            kb.pe(lambda e, py=py: e.matmul(py, pmr[0:1, :], ynq[0:1, :], start=False, stop=True), reads=[kpmr, kynq], writes=[kpy])
            for b in range(2):
                kb.pool(lambda e, tb_ap=tb_ap, src=src, j=j, b=b, o=o: e.tensor_tensor(out=tb_ap[:, b * 256:(b + 1) * 256], in0=src[:, j, b * 256:(b + 1) * 256], in1=bias[:, o, :], op=ALU.mult),
                        reads=[ksrc_all[j], kbias], writes=[ktb])
            kb.dve(lambda e, py=py, tb_ap=tb_ap, t2_ap=t2_ap: e.tensor_tensor(out=t2_ap, in0=py, in1=tb_ap, op=ALU.add), reads=[kpy, ktb], writes=[kt2])
            if o == 0:
                kb.pool(lambda e, t2_ap=t2_ap, g_ap=g_ap, j=j: e.tensor_tensor(out=Zb[:, j, :], in0=t2_ap, in1=g_ap, op=ALU.mult), reads=[kt2, kg], writes=[kZb + ".%d" % j])
            else:
                kb.pool(lambda e, t2_ap=t2_ap, g_ap=g_ap, o_ap=o_ap: e.tensor_tensor(out=o_ap, in0=t2_ap, in1=g_ap, op=ALU.mult), reads=[kt2, kg], writes=[ko])
                kb.dma(HYO[j * 128:(j + 1) * 128, :], o_ap, reads=[ko], writes=["HYO"])
    self.phase_end()


def _hyena_phase(self, l):
    last = (l == L - 1)
    groups = [("lat", self.lat, S)] + ([] if last else [("ctx", self.ctx, CL)])
    for nm, seqs, n in groups:
        self.hy_h0(l, nm, seqs, n)
        self.hy_h1(l, nm, n)
        self.hy_h2(l, nm, seqs, n)


Prog.hy_h0 = _hy_h0
Prog.hy_h1 = _hy_h1
Prog.hy_h2 = _hy_h2
Prog.phase_hyena = _hyena_phase


def build_program(nc, dbg=(), scopes=False):
    P = Prog(nc, dbg=dbg)
    kb = P.kb
    kb.scopes = scopes
    kb.phase = "mod"
    P.declare()
    P.phase_mod()
    kb.phase = "t0"
    P.phase_t0()
    for l in range(L):
        kb.phase = "a%d" % l
        P.phase_a(l)
        kb.phase = "pool%d" % l
        P.phase_pool(l)
        kb.phase = "hy%d" % l
        P.phase_hyena(l)
        kb.phase = "attn%d" % l
        P.phase_attn(l)
        kb.phase = "c1_%d" % l
        P.phase_c1(l)
        kb.phase = "c2_%d" % l
        P.phase_c2(l)
    kb.phase = "final"
    P.phase_final()
    kb.emit()
    return P


_PROG_CACHE = {}


def kernel(**inputs):
    shared = _shared_inputs(inputs)
    nc = bass.Bass("TRN2", target_bir_lowering=False)
    P = build_program(nc)
    in_maps = []
    for core in range(NCORES):
        m = _core_inputs(inputs, core, shared)
        in_maps.append({k_: v for k_, v in m.items() if k_ in P.dram})
    res = run_bass_kernel_spmd(nc, in_maps, core_ids=list(range(NCORES)))
    out = np.concatenate([np.asarray(r["y"], dtype=np.float32) for r in res.results], axis=0)
    return out
```

```python
import math
import numpy as np
import ml_dtypes
import concourse.bass as bass
import concourse.mybir as mybir
from concourse.bass_utils import run_bass_kernel_spmd

F32 = mybir.dt.float32
BF16 = mybir.dt.bfloat16
AF = mybir.ActivationFunctionType
ALU = mybir.AluOpType
AX = mybir.AxisListType

import os
SAME_ENGINE_SYNC = os.environ.get("MK_SES", "1") == "1"
N_DMA_SEMS = 24


class _Op:
    __slots__ = ("eng", "fn", "deps", "need_inc", "cnt", "dsem", "dval", "is_dma", "idx", "phase")

    def __init__(self, eng, fn, is_dma):
        self.eng = eng
        self.fn = fn
        self.deps = []
        self.need_inc = False
        self.cnt = 0
        self.dsem = -1
        self.dval = 0
        self.is_dma = is_dma
        self.idx = 0


class KB:
    ENGS = ("pe", "act", "dve", "pool", "sp")

    def __init__(self, nc):
        self.nc = nc
        self.ops = []
        self.last_w = {}
        self.readers = {}
        self.n_dma = 0
        self._bar_start = 0
        self.phase = None
        self.scopes = False

    def _add(self, eng, fn, reads, writes, is_dma):
        op = _Op(eng, fn, is_dma)
        op.idx = len(self.ops)
        op.phase = self.phase
        deps = set()
        for k in reads:
            w = self.last_w.get(k)
            if w is not None:
                deps.add(w)
        for k in writes:
            w = self.last_w.get(k)
            if w is not None:
                deps.add(w)
            for r in self.readers.get(k, ()):
                deps.add(r)
        deps.discard(op.idx)
        op.deps = sorted(deps)
        for k in reads:
            lst = self.readers.setdefault(k, [])
            if not is_dma:
                lst[:] = [r for r in lst if self.ops[r].is_dma or self.ops[r].eng != eng]
            lst.append(op.idx)
        for k in writes:
            self.last_w[k] = op.idx
            self.readers[k] = []
        self.ops.append(op)
        return op

    def pe(self, fn, reads=(), writes=()):
        return self._add("pe", fn, reads, writes, False)

    def act(self, fn, reads=(), writes=()):
        return self._add("act", fn, reads, writes, False)

    def dve(self, fn, reads=(), writes=()):
        return self._add("dve", fn, reads, writes, False)

    def pool(self, fn, reads=(), writes=()):
        return self._add("pool", fn, reads, writes, False)

    def dma(self, out, in_, reads=(), writes=(), q="sp", **kw):
        def fn(e, out=out, in_=in_, kw=kw):
            return e.dma_start(out=out, in_=in_, **kw)
        return self._add(q, fn, reads, writes, True)

    def barrier(self):
        last = {}
        dmas = []
        for op in self.ops[self._bar_start:]:
            if op.is_dma:
                dmas.append(op.idx)
            elif op.fn is not None:
                last[op.eng] = op.idx
        deps = sorted(set(list(last.values()) + dmas))
        for e in self.ENGS:
            op = _Op(e, None, False)
            op.idx = len(self.ops)
            op.phase = self.phase
            op.deps = list(deps)
            self.ops.append(op)
        self._bar_start = len(self.ops)
        self.last_w = {}
        self.readers = {}

    def emit(self):
        nc = self.nc
        ops = self.ops
        for op in ops:
            best = {}
            for d in op.deps:
                dop = ops[d]
                if dop.is_dma or dop.fn is None:
                    continue
                if dop.eng == op.eng and not op.is_dma:
                    if dop.eng == "pe" or not SAME_ENGINE_SYNC:
                        continue
                if d > best.get(dop.eng, -1):
                    best[dop.eng] = d
            for d in best.values():
                ops[d].need_inc = True
        cnt = {e: 0 for e in self.ENGS}
        dma_cnt = [0] * N_DMA_SEMS
        nd = 0
        for op in ops:
            if op.is_dma:
                s = nd % N_DMA_SEMS
                nd += 1
                dma_cnt[s] += 1
                op.dsem = s
                op.dval = 16 * dma_cnt[s]
            elif op.need_inc:
                cnt[op.eng] += 1
                op.cnt = cnt[op.eng]
        per_eng = {e: [] for e in self.ENGS}
        for op in ops:
            per_eng[op.eng].append(op)
        self.stats = {e: len(per_eng[e]) for e in self.ENGS}
        self.stats["incs"] = dict(cnt)

        import contextlib
        with contextlib.ExitStack() as st:
            esem = {e: st.enter_context(nc.semaphore("s_" + e)) for e in self.ENGS}
            dsem = [st.enter_context(nc.semaphore("d%d" % i)) for i in range(N_DMA_SEMS)]
            block = st.enter_context(nc.Block())

            def run(eng_name, e):
                waited_e = {x: 0 for x in self.ENGS}
                waited_d = [0] * N_DMA_SEMS
                cur = None
                for op in per_eng[eng_name]:
                    if self.scopes and op.phase != cur:
                        if cur is not None:
                            nc.pop_named_scope(cur)
                        cur = op.phase
                        if cur is not None:
                            nc.push_named_scope(cur)
                    need_e = {}
                    need_d = {}
                    for d in op.deps:
                        dop = ops[d]
                        if dop.is_dma:
                            if dop.dval > waited_d[dop.dsem]:
                                need_d[dop.dsem] = max(need_d.get(dop.dsem, 0), dop.dval)
                        else:
                            if dop.eng == eng_name and not op.is_dma:
                                if eng_name == "pe" or not SAME_ENGINE_SYNC:
                                    continue
                            if dop.cnt > waited_e[dop.eng]:
                                need_e[dop.eng] = max(need_e.get(dop.eng, 0), dop.cnt)
                    if op.is_dma:
                        prev = op.dval - 16
                        if prev > waited_d[op.dsem]:
                            need_d[op.dsem] = max(need_d.get(op.dsem, 0), prev)
                    for x, v in need_e.items():
                        e.wait_ge(esem[x], v)
                        waited_e[x] = v
                    for s, v in need_d.items():
                        e.wait_ge(dsem[s], v)
                        waited_d[s] = v
                    if op.fn is None:
                        continue
                    ins = op.fn(e)
                    if op.is_dma:
                        ins.then_inc(dsem[op.dsem], 16)
                    elif op.need_inc:
                        ins.then_inc(esem[eng_name], 1)
                if self.scopes and cur is not None:
                    nc.pop_named_scope(cur)
                last = {}
                for op in per_eng[eng_name]:
                    if op.is_dma:
                        last[op.dsem] = max(last.get(op.dsem, 0), op.dval)
                for s, v in last.items():
                    if v > waited_d[s]:
                        e.wait_ge(dsem[s], v)

            @block.sync
            def _(e):
                run("sp", e)

            @block.scalar
            def _(e):
                run("act", e)

            @block.vector
            def _(e):
                run("dve", e)

            @block.gpsimd
            def _(e):
                run("pool", e)

            @block.tensor
            def _(e):
                run("pe", e)


D = 1024
S = 4096
CL = 256
L = 2
NCORES = 8
NBC = 2
NH = 8
NOPE = 64
ROPE = 32
DV = 64
QR = 256
KVR = 128
NIN = 1440
C_KV, C_KR, C_Q, C_POOL, C_HY = 0, 128, 160, 416, 672
DFF = 4096
EPS = 1e-6
SCALE = float((NOPE + ROPE) ** -0.5)
POOL_WINDOWS = (2, 4, 8, 16)
HY_EMB = 33
HY_FFN = 64
MAGIC = 12582912.0
BFNP = ml_dtypes.bfloat16


def _rope_perm():
    perm = np.zeros(32, np.int64)
    for a in range(2):
        for hf in range(2):
            for f in range(8):
                perm[a * 16 + hf * 8 + f] = a * 16 + (1 - hf) * 8 + f
    return perm


def _rope_tables():
    n = S
    rows = n // 64
    r = np.repeat(np.arange(rows), 64).astype(np.float32)
    cidx = np.tile(np.arange(64), rows).astype(np.float32)
    inv = np.power(np.float32(10000.0), -(np.arange(8, dtype=np.float32) / np.float32(8))).astype(np.float32)
    ang = np.stack([r[:, None] * inv, cidx[:, None] * inv], axis=1).astype(np.float32)
    cos = np.cos(ang).astype(np.float32)
    sin = np.sin(ang).astype(np.float32)
    cosT = np.zeros((32, n), np.float32)
    sinT = np.zeros((32, n), np.float32)
    for a in range(2):
        for hf in range(2):
            for f in range(8):
                rr = a * 16 + hf * 8 + f
                cosT[rr] = cos[:, a, f]
                sinT[rr] = (-sin[:, a, f]) if hf == 0 else sin[:, a, f]
    return cosT, sinT


def _pool_bands(n, W):
    blocks = []
    index = {}
    table = {}
    nt = n // W
    for g, w in enumerate(POOL_WINDOWS):
        t = np.arange(n)
        lo = np.clip(t - w // 2, 0, n)
        hi = np.clip(t + w // 2, 0, n)
        for i in range(nt):
            T0 = i * W
            m_lo = max((T0 - w // 2) // 128, 0)
            m_hi = min((T0 + W + w // 2 - 1) // 128, n // 128 - 1)
            for m in range(m_lo, m_hi + 1):
                blk = np.zeros((128, 512), np.float32)
                for tt in range(T0, T0 + W):
                    s0 = max(lo[tt], m * 128)
                    s1 = min(hi[tt], (m + 1) * 128)
                    if s1 > s0:
                        blk[s0 - m * 128:s1 - m * 128, tt - T0] += 1.0 / float(hi[tt] - lo[tt])
                    if m * 128 <= tt < (m + 1) * 128:
                        blk[tt - m * 128, tt - T0] -= 1.0
                if not blk.any():
                    continue
                key = blk.tobytes()
                if key not in index:
                    index[key] = len(blocks)
                    blocks.append(blk)
                table[(g, i, m)] = index[key]
    return np.stack(blocks).astype(BFNP), table


def _dft_blocks(n):
    nt = n // 128
    t = np.arange(n, dtype=np.int64)
    m = (t[:, None] * t[None, :]) % (2 * n)
    ang = m.astype(np.float64) * (2.0 * np.pi / (2 * n))
    Cm = np.cos(ang)
    Sm = -np.sin(ang)

    def tile(M):
        M4 = M.reshape(nt, 128, nt, 128)
        return np.ascontiguousarray(M4.transpose(2, 1, 0, 3)).astype(BFNP)
    return tile(Cm), tile(Sm)


def _hy_consts(n):
    f32 = np.float32
    t = np.linspace(0.0, 1.0, n, dtype=f32)[:, None]
    bands = (HY_EMB - 1) // 2
    freqs = np.linspace(1e-4, bands - 1, bands, dtype=f32)[None, :]
    wpos = (f32(2.0 * math.pi) * np.arange(n, dtype=f32)[:, None] / f32(n)).astype(f32)
    z = np.concatenate([t, np.cos(freqs * wpos), -np.sin(freqs * wpos)], axis=-1).astype(f32)
    deltas = np.abs(np.linspace(math.log(1e-2) / 1.5, math.log(1e-2) / 0.3, 256, dtype=f32))
    dec = np.exp(-t * deltas[None, :]).astype(f32)
    pm1 = np.where(np.arange(n) % 2 == 0, 1.0, -1.0).astype(f32)
    return np.ascontiguousarray(z.T), dec, pm1


_CONST_CACHE = {}


def _host_consts():
    if _CONST_CACHE:
        return _CONST_CACHE
    c = _CONST_CACHE
    c["ident_f"] = np.eye(128, dtype=np.float32)
    cosT, sinT = _rope_tables()
    c["rope_cos"] = cosT
    c["rope_sin"] = sinT
    c["rope_cos_q"] = (cosT * np.float32(SCALE)).astype(np.float32)
    c["rope_sin_q"] = (sinT * np.float32(SCALE)).astype(np.float32)
    bl, tl = _pool_bands(S, 512)
    bc, tc = _pool_bands(CL, 256)
    c["band_lat"] = bl
    c["band_ctx"] = bc
    c["_band_tab_lat"] = tl
    c["_band_tab_ctx"] = tc
    for n, nm in ((S, "lat"), (CL, "ctx")):
        Cb, Sb = _dft_blocks(n)
        c["dftc_" + nm] = Cb
        c["dfts_" + nm] = Sb
        zT, dec, pm1 = _hy_consts(n)
        c["hyz_" + nm] = zT
        c["hydec_" + nm] = dec
        c["pm1_" + nm] = pm1
    return c


class Arena:
    def __init__(self, nc, nbytes):
        self.t = nc.alloc_sbuf_tensor("arena", [128, nbytes // 2], BF16).ap()
        self.cap = nbytes
        self.off = 0
        self.cnt = 0

    def reset(self):
        self.off = 0

    def alloc(self, free_shape, dtype, name="t"):
        esz = 4 if dtype == F32 else 2
        n = 1
        for d_ in free_shape:
            n *= d_
        nb = n * esz
        off = (self.off + 63) // 64 * 64
        assert off + nb <= self.cap, "arena overflow %s %d+%d>%d" % (name, off, nb, self.cap)
        self.off = off + nb
        ap = self.t[:, off // 2:(off + nb) // 2]
        if dtype == F32:
            ap = ap.bitcast(F32)
        if len(free_shape) == 2:
            ap = ap.rearrange("p (a b) -> p a b", b=free_shape[1])
        elif len(free_shape) == 3:
            ap = ap.rearrange("p (a b c) -> p a b c", b=free_shape[1], c=free_shape[2])
        self.cnt += 1
        return ap, "%s#%d" % (name, self.cnt)


class Seq:
    def __init__(self, name, n, v, kind, b):
        self.name = name
        self.n = n
        self.v = v
        self.kind = kind
        self.b = b
        self.W = 512 if n >= 512 else n
        self.nt = n // self.W


class Prog:
    def __init__(self, nc, dbg=()):
        self.nc = nc
        self.kb = KB(nc)
        self.dbg = set(dbg)
        self.dram = {}
        self.consts = _host_consts()

    def din(self, name, shape, dtype=F32):
        t = self.nc.dram_tensor(name, list(shape), dtype, kind="ExternalInput").ap()
        self.dram[name] = t
        return t

    def dscr(self, name, shape, dtype=F32):
        kind = "ExternalOutput" if name in self.dbg else "Internal"
        t = self.nc.dram_tensor(name, list(shape), dtype, kind=kind).ap()
        self.dram[name] = t
        return t

    def declare(self):
        nc = self.nc
        c = self.consts
        self.x_in = self.din("x", [NBC, S, D])
        self.ctx_in = self.din("ctx", [NBC, CL, D])
        self.cv_in = self.din("cv", [3, D])
        self.y_out = nc.dram_tensor("y", [NBC, S, D], F32, kind="ExternalOutput").ap()
        w = {}
        w["w_mod"] = self.din("w_mod", [L, D, 6 * D])
        w["b_mod"] = self.din("b_mod", [L, 6 * D])
        w["g_mix"] = self.din("g_mix", [L, D])
        w["g_mlp"] = self.din("g_mlp", [L, D])
        w["w_in"] = self.din("w_in", [L, D, NIN])
        w["w_in_rot"] = self.din("w_in_rot", [L, D, ROPE])
        w["g_q"] = self.din("g_q", [L, QR])
        w["w_q_up"] = self.din("w_q_up", [L, QR, NH * 96])
        w["w_q_rot"] = self.din("w_q_rot", [L, QR, NH * 96])
        w["g_kv"] = self.din("g_kv", [L, KVR])
        w["w_kv_up"] = self.din("w_kv_up", [L, KVR, NH * 128])
        w["pool_w"] = self.din("pool_w", [L, 4, 64, 64])
        w["pool_scale"] = self.din("pool_scale", [L, 256])
        w["hy_conv_w"] = self.din("hy_conv_w", [L, 3, 768])
        w["hy_conv_b"] = self.din("hy_conv_b", [L, 768])
        w["hy_f_w1"] = self.din("hy_f_w1", [L, HY_EMB, HY_FFN])
        w["hy_f_b1"] = self.din("hy_f_b1", [L, HY_FFN])
        w["hy_f_freq1"] = self.din("hy_f_freq1", [L, HY_FFN])
        w["hy_f_w2"] = self.din("hy_f_w2", [L, HY_FFN, HY_FFN])
        w["hy_f_b2"] = self.din("hy_f_b2", [L, HY_FFN])
        w["hy_f_freq2"] = self.din("hy_f_freq2", [L, HY_FFN])
        w["hy_f_w3"] = self.din("hy_f_w3", [L, HY_FFN, 1024])
        w["hy_bias"] = self.din("hy_bias", [L, 2, 256])
        w["g_out"] = self.din("g_out", [L, D])
        w["w_out"] = self.din("w_out", [L, D, D])
        w["w_mlp1"] = self.din("w_mlp1", [L, D, DFF])
        w["w_mlp2"] = self.din("w_mlp2", [L, DFF, D])
        w["g_final"] = self.din("g_final", [D])
        self.w = w
        k = {}
        for name, arr in c.items():
            if name.startswith("_"):
                continue
            dt_ = BF16 if arr.dtype == BFNP else F32
            k[name] = self.din(name, arr.shape, dt_)
        self.k = k
        self.lat = [Seq("b%d" % b, S, b, "lat", b) for b in range(NBC)]
        self.ctx = [Seq("c%d" % b, CL, 2, "ctx", b) for b in range(NBC)]
        for s in self.lat + self.ctx:
            n = s.n
            s.XT = self.dscr("XT_" + s.name, [8, 128, n])
            s.PKV = self.dscr("PKV_" + s.name, [128, n], BF16)
            s.KR = self.dscr("KR_" + s.name, [32, n], BF16)
            s.PQ = self.dscr("PQ_" + s.name, [2, 128, n], BF16)
            s.POOLU = self.dscr("POOLU_" + s.name, [n, 256], BF16)
            s.HYP = self.dscr("HYP_" + s.name, [6, 128, n + 2])
            s.ATT = self.dscr("ATT_" + s.name, [n, 512])
            s.POOLO = self.dscr("POOLO_" + s.name, [n, 256])
        for nm, n in (("lat", S), ("ctx", CL)):
            setattr(self, "HV_" + nm, self.dscr("HV_" + nm, [n, 512], BF16))
            setattr(self, "HX1_" + nm, self.dscr("HX1_" + nm, [n, 512], BF16))
            setattr(self, "HX2_" + nm, self.dscr("HX2_" + nm, [n, 512], BF16))
            setattr(self, "HYO_" + nm, self.dscr("HYO_" + nm, [n, 512]))
            setattr(self, "HS_" + nm, self.dscr("HS_" + nm, [2, n // 128, 128, 512], BF16))
            setattr(self, "HNQ_" + nm, self.dscr("HNQ_" + nm, [1, 512]))
        A = lambda name, shape, dt_: nc.alloc_sbuf_tensor(name, shape, dt_).ap()
        self.ident_f = A("ident_f_sb", [128, 128], F32)
        self.ident_b = A("ident_b", [128, 128], BF16)
        self.ones_b = A("ones_b", [128, 128], BF16)
        self.ones_f = A("ones_f", [128, 128], F32)
        self.eps = A("eps_t", [128, 1], F32)
        self.modT = A("modT", [128, L * 6 * 8 * 3], F32).rearrange("p (l k j v) -> p l k j v", l=L, k=6, j=8)
        self.ps2 = [nc.alloc_psum_tensor("ps2_%d" % i, [128, 1024], F32).ap() for i in range(4)]
        self.arena = Arena(nc, 204 * 1024)
        kb = self.kb
        kb.dma(self.ident_f, self.k["ident_f"], writes=["ident_f"])
        kb.act(lambda e: e.activation(out=self.ident_b, in_=self.ident_f, func=AF.Copy), reads=["ident_f"], writes=["ident_b"])
        kb.dve(lambda e: e.memset(self.ones_b, 1.0), writes=["ones_b"])
        kb.dve(lambda e: e.memset(self.ones_f, 1.0), writes=["ones_f"])
        kb.dve(lambda e: e.memset(self.eps, EPS), writes=["eps"])

    def bank(self, kidx):
        return self.ps2[kidx // 2][:, (kidx % 2) * 512:(kidx % 2 + 1) * 512], "pb%d" % kidx

    def mod(self, l, kind, v):
        return self.modT[:, l, kind, :, v]

    def phase_end(self):
        self.kb.barrier()
        self.arena.reset()

    def phase_mod(self):
        kb, ar, w = self.kb, self.arena, self.w
        cvs, kcvs = ar.alloc([D], F32, "cvs")
        sil, ksil = ar.alloc([D], F32, "sil")
        silT, ksilT = ar.alloc([8, 3], F32, "silT")
        kb.dma(cvs[0:3, :], self.cv_in, writes=[kcvs])
        kb.act(lambda e: e.activation(out=sil[0:3, :], in_=cvs[0:3, :], func=AF.Silu), reads=[kcvs], writes=[ksil])
        pb, kpb = self.bank(0)
        for j in range(8):
            kb.pe(lambda e, j=j: e.transpose(pb[:, j * 3:(j + 1) * 3], sil[0:3, j * 128:(j + 1) * 128], self.ident_f[0:3, 0:3]),
                  reads=[ksil, "ident_f"], writes=[kpb])
        kb.dve(lambda e: e.tensor_copy(out=silT.rearrange("p j v -> p (j v)"), in_=pb[:, 0:24]), reads=[kpb], writes=[ksilT])
        wbuf = [ar.alloc([8, 512], F32, "wmod") for _ in range(2)]
        modrow, kmodrow = ar.alloc([6 * D], F32, "modrow")
        brow, kbrow = ar.alloc([6 * D], F32, "brow")
        gbc, kgbc = ar.alloc([2, D], F32, "gbc")
        for l in range(L):
            kb.dma(brow[0:3, :], w["b_mod"][l].partition_broadcast(3), writes=[kbrow])
            kb.dma(gbc[0:3, 0, :], w["g_mix"][l].partition_broadcast(3), writes=[kgbc])
            kb.dma(gbc[0:3, 1, :], w["g_mlp"][l].partition_broadcast(3), writes=[kgbc])
            for ncn in range(12):
                wb, kwb = wbuf[ncn % 2]
                kb.dma(wb, w["w_mod"][l][:, ncn * 512:(ncn + 1) * 512].rearrange("(j p) n -> p j n", p=128), writes=[kwb])
                pm, kpm = self.bank(1 + ncn % 2)
                for j in range(8):
                    kb.pe(lambda e, j=j, wb=wb, pm=pm: e.matmul(pm[0:3, :], silT[:, j, :], wb[:, j, :], start=(j == 0), stop=(j == 7)),
                          reads=[ksilT, kwb], writes=[kpm])
                kb.dve(lambda e, pm=pm, ncn=ncn: e.tensor_tensor(out=modrow[0:3, ncn * 512:(ncn + 1) * 512], in0=pm[0:3, :],
                                                                in1=brow[0:3, ncn * 512:(ncn + 1) * 512], op=ALU.add),
                       reads=[kpm, kbrow], writes=[kmodrow])
            for (kind, gi) in ((1, 0), (4, 1)):
                sl = modrow[0:3, kind * D:(kind + 1) * D]
                kb.dve(lambda e, sl=sl, gi=gi: e.scalar_tensor_tensor(out=sl, in0=sl, scalar=1.0, in1=gbc[0:3, gi, :], op0=ALU.add, op1=ALU.mult),
                       reads=[kmodrow, kgbc], writes=[kmodrow])
            pt, kpt = self.bank(3)
            for kind in range(6):
                for j in range(8):
                    col = (kind * 8 + j) * 3
                    kb.pe(lambda e, kind=kind, j=j, col=col: e.transpose(pt[:, col:col + 3], modrow[0:3, kind * D + j * 128: kind * D + (j + 1) * 128],
                                                                        self.ident_f[0:3, 0:3]),
                          reads=[kmodrow, "ident_f"], writes=[kpt])
            kb.dve(lambda e, l=l: e.tensor_copy(out=self.modT[:, l].rearrange("p k j v -> p (k j v)"), in_=pt[:, 0:144]),
                   reads=[kpt], writes=["modT"])
        self.phase_end()

    def nb(self):
        self._nb = (getattr(self, "_nb", -1) + 1) % 8
        return self.bank(self._nb)

    def evac(self, i, fn_act, fn_dve, reads, writes):
        if i % 2 == 0:
            self.kb.act(fn_act, reads=reads, writes=writes)
        else:
            self.kb.dve(fn_dve, reads=reads, writes=writes)

    def phase_t0(self):
        kb, ar = self.kb, self.arena
        for s in self.lat + self.ctx:
            src = self.x_in[s.b] if s.kind == "lat" else self.ctx_in[s.b]
            W, nsub = s.W, s.W // 128
            xin = [ar.alloc([nsub, D], F32, "xin") for _ in range(2)]
            xt = [ar.alloc([8, W], F32, "xt") for _ in range(2)]
            for i in range(s.nt):
                xi, kxi = xin[i % 2]
                xo, kxo = xt[i % 2]
                kb.dma(xi, src[i * W:(i + 1) * W, :].rearrange("(a p) d -> p a d", p=128), writes=[kxi])
                for j in range(8):
                    pb, kpb = self.nb()
                    for sub in range(nsub):
                        kb.pe(lambda e, pb=pb, xi=xi, j=j, sub=sub: e.transpose(pb[:, sub * 128:(sub + 1) * 128], xi[:, sub, j * 128:(j + 1) * 128], self.ident_f),
                              reads=[kxi, "ident_f"], writes=[kpb])
                    self.evac(j, lambda e, pb=pb, xo=xo, j=j, W=W: e.activation(out=xo[:, j, :], in_=pb[:, 0:W], func=AF.Copy),
                              lambda e, pb=pb, xo=xo, j=j, W=W: e.tensor_copy(out=xo[:, j, :], in_=pb[:, 0:W]),
                              reads=[kpb], writes=[kxo + ".%d" % j])
                kb.dma(s.XT[:, :, i * W:(i + 1) * W].rearrange("j p t -> p j t"), xo,
                       reads=[kxo + ".%d" % j for j in range(8)], writes=["XT_" + s.name])
            self.phase_end()

    def fm_rstd(self, chunks, nfeat, W, sq, ksq, rs, krs, extra_sq=None):
        kb = self.kb
        pss, kpss = self.nb()
        n = len(chunks)
        for ci, (ap, key, P) in enumerate(chunks):
            kb.act(lambda e, ap=ap, ci=ci, P=P: e.activation(out=sq[0:P, ci, 0:W], in_=ap, func=AF.Square),
                   reads=[key], writes=[ksq + ".%d" % ci] + (extra_sq(ci) if extra_sq else []))
            kb.pe(lambda e, ci=ci, P=P: e.matmul(pss[:, 0:W], self.ones_b[0:P, :], sq[0:P, ci, 0:W], start=(ci == 0), stop=(ci == n - 1)),
                  reads=[ksq + ".%d" % ci, "ones_b"], writes=[kpss])
        kb.act(lambda e: e.activation(out=rs[:, 0:W], in_=pss[:, 0:W], func=AF.Sqrt, bias=self.eps[:, 0:1], scale=1.0 / nfeat),
               reads=[kpss, "eps"], writes=[krs])
        kb.dve(lambda e: e.reciprocal(out=rs[:, 0:W], in_=rs[:, 0:W]), reads=[krs], writes=[krs])

    def mod_norm(self, xt, kxt, W, gm, sh, sq, ksq, rs, krs, tmp, ktmp, hT, khT, extra=None, extra_sq=None):
        kb = self.kb
        self.fm_rstd([(xt[:, j, 0:W], kxt + ".%d" % j, 128) for j in range(8)], D, W, sq, ksq, rs, krs, extra_sq=extra_sq)
        for j in range(8):
            kb.dve(lambda e, j=j: e.scalar_tensor_tensor(out=tmp[:, j, 0:W], in0=xt[:, j, 0:W], scalar=gm[:, j:j + 1], in1=rs[:, 0:W],
                                                         op0=ALU.mult, op1=ALU.mult),
                   reads=[kxt + ".%d" % j, krs, "modT"], writes=[ktmp + ".%d" % j] + (extra(j) if extra else []))
            kb.act(lambda e, j=j: e.activation(out=hT[:, j, 0:W], in_=tmp[:, j, 0:W], func=AF.Identity, bias=sh[:, j:j + 1], scale=1.0),
                   reads=[ktmp + ".%d" % j, "modT"], writes=[khT + ".%d" % j])

    def load_cast_rows(self, dst, src2d, nj, split=1):
        ap, key = dst
        n = src2d.shape[-1]
        step = n // split
        for j in range(nj):
            for sp_ in range(split):
                self.kb.dma(ap[:, j, sp_ * step:(sp_ + 1) * step], src2d[j * 128:(j + 1) * 128, sp_ * step:(sp_ + 1) * step],
                            writes=[key + ".%d" % j], q="pool")

    def phase_a(self, l):
        kb, ar, w = self.kb, self.arena, self.w
        last = (l == L - 1)
        win = ar.alloc([8, NIN], BF16, "win")
        wrot = ar.alloc([8, ROPE], BF16, "wrot")
        self.load_cast_rows(win, w["w_in"][l], 8)
        self.load_cast_rows(wrot, w["w_in_rot"][l], 8)
        win_ap, kwin = win
        wrot_ap, kwrot = wrot
        kwin_all = [kwin + ".%d" % j for j in range(8)]
        kwrot_all = [kwrot + ".%d" % j for j in range(8)]
        gkv, kgkv = ar.alloc([1], F32, "gkv")
        gq, kgq = ar.alloc([2], F32, "gq")
        kb.dma(gkv, w["g_kv"][l].rearrange("(p o) -> p o", o=1), writes=[kgkv])
        for c_ in range(2):
            kb.dma(gq[:, c_:c_ + 1], w["g_q"][l][c_ * 128:(c_ + 1) * 128].rearrange("(p o) -> p o", o=1), writes=[kgq])
        zt, kzt = ar.alloc([6, 1], F32, "zt")
        kb.dve(lambda e: e.memset(zt, 0.0), writes=[kzt])
        WM = 512
        xts = [ar.alloc([8, WM], F32, "xt") for _ in range(2)]
        sq, ksq = ar.alloc([8, WM], BF16, "sq")
        rs, krs = ar.alloc([WM], F32, "rs")
        tmp, ktmp = ar.alloc([8, WM], F32, "tmp")
        hTs = [ar.alloc([8, WM], BF16, "hT") for _ in range(2)]
        sq2, ksq2 = ar.alloc([2, WM], BF16, "sq2")
        rs2, krs2 = ar.alloc([WM], F32, "rs2")
        pkvn = [ar.alloc([WM], BF16, "pkvn") for _ in range(2)]
        krs_ = [ar.alloc([WM], BF16, "kr") for _ in range(2)]
        pqn = [ar.alloc([2, WM], BF16, "pqn") for _ in range(2)]
        poolu = [ar.alloc([4, 256], BF16, "poolu") for _ in range(2)]
        hyp = [ar.alloc([6, WM], F32, "hyp") for _ in range(2)]
        ropec = [ar.alloc([WM], F32, "ropec") for _ in range(2)]
        ropes = [ar.alloc([WM], F32, "ropes") for _ in range(2)]
        rt1, krt1 = ar.alloc([WM], F32, "rt1")
        rt2, krt2 = ar.alloc([WM], F32, "rt2")
        tiles = []
        for s in self.lat + self.ctx:
            full = not (last and s.kind == "ctx")
            if full:
                for (a0, a1) in ((0, 1), (s.n + 1, s.n + 2)):
                    kb.dma(s.HYP[:, :, a0:a1].rearrange("c p o -> p c o"), zt, reads=[kzt], writes=["HYP_" + s.name], allow_slow_non_contiguous=True)
            for i in range(s.nt):
                tiles.append((s, i, full))

        def front(it):
            s, i, full = tiles[it]
            W = s.W
            gm, sh = self.mod(l, 1, s.v), self.mod(l, 0, s.v)
            xt, kxt = xts[it % 2]
            hT, khT = hTs[it % 2]
            t0 = i * W
            kb.dma(xt[:, :, 0:W], s.XT[:, :, t0:t0 + W].rearrange("j p t -> p j t"), reads=["XT_" + s.name],
                   writes=[kxt + ".%d" % j for j in range(8)])
            self.mod_norm(xt, kxt, W, gm, sh, sq, ksq, rs, krs, tmp, ktmp, hT, khT)

        def back(it):
            s, i, full = tiles[it]
            W, nsub = s.W, s.W // 128
            hT, khT = hTs[it % 2]
            t0 = i * W
            if True:
                khT_all = [khT + ".%d" % j for j in range(8)]

                def proj_fm(col0, ncol, dstps, wt=win_ap, kw=kwin_all, hT=hT, khT_all=khT_all, W=W):
                    for j in range(8):
                        kb.pe(lambda e, j=j: e.matmul(dstps[0:ncol, 0:W], wt[:, j, col0:col0 + ncol], hT[:, j, 0:W], start=(j == 0), stop=(j == 7)),
                              reads=[kw[j], khT_all[j]], writes=[dstps_key[0]])
                pkv, kpkv = self.nb()
                dstps_key = [kpkv]
                proj_fm(C_KV, 128, pkv)
                self.fm_rstd([(pkv[:, 0:W], kpkv, 128)], KVR, W, sq2, ksq2, rs2, krs2)
                o_, ko_ = pkvn[it % 2]
                kb.dve(lambda e, o_=o_, pkv=pkv, W=W: e.scalar_tensor_tensor(out=o_[:, 0:W], in0=pkv[:, 0:W], scalar=gkv[:, 0:1], in1=rs2[:, 0:W],
                                                                            op0=ALU.mult, op1=ALU.mult),
                       reads=[kpkv, krs2, kgkv], writes=[ko_])
                kb.dma(s.PKV[:, t0:t0 + W], o_[:, 0:W], reads=[ko_], writes=["PKV_" + s.name])
                pka, kpka = self.nb()
                dstps_key = [kpka]
                proj_fm(C_KR, ROPE, pka)
                o_, ko_ = krs_[it % 2]
                if s.kind == "lat":
                    pkb, kpkb = self.nb()
                    dstps_key = [kpkb]
                    proj_fm(0, ROPE, pkb, wt=wrot_ap, kw=kwrot_all)
                    rc, krc = ropec[it % 2]
                    rsn, krsn = ropes[it % 2]
                    kb.dma(rc[0:32, 0:W], self.k["rope_cos"][:, t0:t0 + W], writes=[krc])
                    kb.dma(rsn[0:32, 0:W], self.k["rope_sin"][:, t0:t0 + W], writes=[krsn])
                    kb.dve(lambda e, pka=pka, rc=rc, W=W: e.tensor_tensor(out=rt1[0:32, 0:W], in0=pka[0:32, 0:W], in1=rc[0:32, 0:W], op=ALU.mult),
                           reads=[kpka, krc], writes=[krt1])
                    kb.dve(lambda e, pkb=pkb, rsn=rsn, W=W: e.tensor_tensor(out=rt2[0:32, 0:W], in0=pkb[0:32, 0:W], in1=rsn[0:32, 0:W], op=ALU.mult),
                           reads=[kpkb, krsn], writes=[krt2])
                    kb.dve(lambda e, o_=o_, W=W: e.tensor_tensor(out=o_[0:32, 0:W], in0=rt1[0:32, 0:W], in1=rt2[0:32, 0:W], op=ALU.add),
                           reads=[krt1, krt2], writes=[ko_])
                else:
                    kb.act(lambda e, o_=o_, pka=pka, W=W: e.activation(out=o_[0:32, 0:W], in_=pka[0:32, 0:W], func=AF.Copy),
                           reads=[kpka], writes=[ko_])
                kb.dma(s.KR[:, t0:t0 + W], o_[0:32, 0:W], reads=[ko_], writes=["KR_" + s.name])
                if not full:
                    return
                pq = [self.nb() for _ in range(2)]
                for c_ in range(2):
                    dstps_key = [pq[c_][1]]
                    proj_fm(C_Q + c_ * 128, 128, pq[c_][0])
                self.fm_rstd([(pq[c_][0][:, 0:W], pq[c_][1], 128) for c_ in range(2)], QR, W, sq2, ksq2, rs2, krs2)
                o_, ko_ = pqn[it % 2]
                for c_ in range(2):
                    kb.dve(lambda e, o_=o_, c_=c_, pq=pq, W=W: e.scalar_tensor_tensor(out=o_[:, c_, 0:W], in0=pq[c_][0][:, 0:W], scalar=gq[:, c_:c_ + 1],
                                                                                  in1=rs2[:, 0:W], op0=ALU.mult, op1=ALU.mult),
                           reads=[pq[c_][1], krs2, kgq], writes=[ko_ + ".%d" % c_])
                kb.dma(s.PQ[:, :, t0:t0 + W].rearrange("c p t -> p c t"), o_[:, :, 0:W], reads=[ko_ + ".0", ko_ + ".1"], writes=["PQ_" + s.name])
                o_, ko_ = poolu[it % 2]
                for sub in range(nsub):
                    pp, kpp = self.nb()
                    for j in range(8):
                        kb.pe(lambda e, j=j, sub=sub, pp=pp, hT=hT: e.matmul(pp[:, 0:256], hT[:, j, sub * 128:(sub + 1) * 128], win_ap[:, j, C_POOL:C_POOL + 256],
                                                                            start=(j == 0), stop=(j == 7)),
                              reads=[kwin_all[j], khT_all[j]], writes=[kpp])
                    self.evac(sub, lambda e, o_=o_, pp=pp, sub=sub: e.activation(out=o_[:, sub, :], in_=pp[:, 0:256], func=AF.Copy),
                              lambda e, o_=o_, pp=pp, sub=sub: e.tensor_copy(out=o_[:, sub, :], in_=pp[:, 0:256]),
                              reads=[kpp], writes=[ko_ + ".%d" % sub])
                kb.dma(s.POOLU[t0:t0 + W, :].rearrange("(a p) c -> p a c", p=128), o_[:, 0:nsub, :],
                       reads=[ko_ + ".%d" % sub for sub in range(nsub)], writes=["POOLU_" + s.name])
                o_, ko_ = hyp[it % 2]
                for c6 in range(6):
                    ph, kph = self.nb()
                    dstps_key = [kph]
                    proj_fm(C_HY + c6 * 128, 128, ph)
                    self.evac(c6, lambda e, o_=o_, ph=ph, c6=c6, W=W: e.activation(out=o_[:, c6, 0:W], in_=ph[:, 0:W], func=AF.Copy),
                              lambda e, o_=o_, ph=ph, c6=c6, W=W: e.tensor_copy(out=o_[:, c6, 0:W], in_=ph[:, 0:W]),
                              reads=[kph], writes=[ko_ + ".%d" % c6])
                kb.dma(s.HYP[:, :, 1 + t0:1 + t0 + W].rearrange("c p t -> p c t"), o_[:, :, 0:W],
                       reads=[ko_ + ".%d" % c6 for c6 in range(6)], writes=["HYP_" + s.name])

        front(0)
        for it in range(len(tiles)):
            if it + 1 < len(tiles):
                front(it + 1)
            back(it)
        self.phase_end()


def _shared_inputs(inp):
    f = lambda a: np.ascontiguousarray(np.asarray(a, dtype=np.float32))
    sh = {}
    for k_ in ("w_mod", "b_mod", "g_mix", "g_mlp", "w_in", "g_q", "g_kv", "w_kv_up", "pool_w", "pool_scale", "hy_conv_w",
               "hy_conv_b", "hy_f_w1", "hy_f_b1", "hy_f_freq1", "hy_f_w2", "hy_f_b2", "hy_f_freq2", "hy_f_w3", "hy_bias",
               "g_out", "w_out", "w_mlp1", "w_mlp2", "g_final"):
        sh[k_] = f(inp[k_])
    perm = _rope_perm()
    w_in = sh["w_in"]
    sh["w_in_rot"] = np.ascontiguousarray(w_in[:, :, C_KR:C_KR + ROPE][:, :, perm])
    wq = f(inp["w_q_up"]).reshape(L, QR, NH, 96)
    sh["w_q_up"] = np.ascontiguousarray(wq.reshape(L, QR, NH * 96))
    wrot = np.zeros_like(wq)
    wrot[..., NOPE:] = wq[..., NOPE:][..., perm]
    sh["w_q_rot"] = np.ascontiguousarray(wrot.reshape(L, QR, NH * 96))
    for name, arr in _host_consts().items():
        if not name.startswith("_"):
            sh[name] = arr
    return sh


def _core_inputs(inp, core, shared):
    b0 = core * NBC
    m = dict(shared)
    m["x"] = np.ascontiguousarray(np.asarray(inp["x"][b0:b0 + NBC], dtype=np.float32))
    m["ctx"] = np.ascontiguousarray(np.asarray(inp["ctx"][b0:b0 + NBC], dtype=np.float32))
    cv = np.concatenate([np.asarray(inp["c"][b0:b0 + NBC], dtype=np.float32), np.asarray(inp["c_ctx"], dtype=np.float32)[None, :]], axis=0)
    m["cv"] = np.ascontiguousarray(cv)
    return m


def _attn_phase(self, l):
    kb, ar, w = self.kb, self.arena, self.w
    last = (l == L - 1)
    NK = CL + S
    NKT = NK // 128
    wkv = ar.alloc([1, NH * 128], BF16, "wkv")
    self.load_cast_rows(wkv, w["w_kv_up"][l], 1)
    wq = ar.alloc([2, NH * 96], BF16, "wq")
    wqr = ar.alloc([2, NH * 96], BF16, "wqr")
    self.load_cast_rows(wq, w["w_q_up"][l], 2)
    self.load_cast_rows(wqr, w["w_q_rot"][l], 2)
    wkv_ap, kwkv = wkv[0], wkv[1] + ".0"
    wq_ap, wqr_ap = wq[0], wqr[0]
    kwq = [wq[1] + ".0", wq[1] + ".1"]
    kwqr = [wqr[1] + ".0", wqr[1] + ".1"]
    cosq, kcosq = ar.alloc([S], F32, "cosq")
    sinq, ksinq = ar.alloc([S], F32, "sinq")
    kb.dma(cosq[64:96, :], self.k["rope_cos_q"], writes=[kcosq])
    kb.dma(sinq[64:96, :], self.k["rope_sin_q"], writes=[ksinq])
    pkv, kpkv = ar.alloc([NK], BF16, "pkv")
    pq, kpq = ar.alloc([2, S], BF16, "pq")
    pqc, kpqc = ar.alloc([2, CL], BF16, "pqc")
    KT = [ar.alloc([NK], BF16, "KT") for _ in range(2)]
    VA = [ar.alloc([NKT, 128], BF16, "VA") for _ in range(2)]
    QT = [ar.alloc([S], BF16, "QT") for _ in range(2)]
    QTc = [ar.alloc([CL], BF16, "QTc") for _ in range(2)]
    PT = [ar.alloc([1024], BF16, "PT") for _ in range(2)]
    oT = [ar.alloc([512], F32, "oT") for _ in range(2)]
    rc = [ar.alloc([4, 1], F32, "rc") for _ in range(2)]
    stg = [ar.alloc([4, 64], F32, "stg") for _ in range(2)]
    rt1, krt1 = ar.alloc([512], F32, "rt1")
    rt2, krt2 = ar.alloc([512], F32, "rt2")
    for hb in range(2):
        kb.pool(lambda e, hb=hb: e.memset(VA[hb][0][:, :, 64:128], 1.0), writes=[VA[hb][1] + ".ones"])
    misc = [self.bank(6), self.bank(7)]
    mi = [0]

    def mbank():
        mi[0] += 1
        return misc[mi[0] % 2]
    cnt_o = [0]
    for b in range(NBC):
        lat, cx = self.lat[b], self.ctx[b]
        kb.dma(pkv[:, 0:CL], cx.PKV, reads=["PKV_" + cx.name], writes=[kpkv])
        kb.dma(pkv[:, CL:NK], lat.PKV, reads=["PKV_" + lat.name], writes=[kpkv])
        for hb in range(2):
            kb.dma(KT[hb][0][64:96, 0:CL], cx.KR, reads=["KR_" + cx.name], writes=[KT[hb][1] + ".r"])
            kb.dma(KT[hb][0][64:96, CL:NK], lat.KR, reads=["KR_" + lat.name], writes=[KT[hb][1] + ".r"])
        kb.dma(pq, lat.PQ.rearrange("c p t -> p c t"), reads=["PQ_" + lat.name], writes=[kpq])
        if not last:
            kb.dma(pqc, cx.PQ.rearrange("c p t -> p c t"), reads=["PQ_" + cx.name], writes=[kpqc])
        def build_steps(h, lat=lat, cx=cx):
            hb = h % 2
            kt_ap, kkt = KT[hb]
            va_ap, kva = VA[hb]
            steps = []

            def k_chunk(kc):
                k0 = kc * 512
                kw_ = min(512, NK - k0)
                pb, kpb = mbank()
                kb.pe(lambda e: e.matmul(pb[0:64, 0:kw_], wkv_ap[:, 0, h * 128:h * 128 + 64], pkv[:, k0:k0 + kw_], start=True, stop=True),
                      reads=[kwkv, kpkv], writes=[kpb])
                kb.dve(lambda e: e.tensor_copy(out=kt_ap[0:64, k0:k0 + kw_], in_=pb[0:64, 0:kw_]), reads=[kpb], writes=[kkt + ".n"])

            def v_group(g8):
                k0 = g8 * 8
                ng = min(8, NKT - k0)
                pb, kpb = mbank()
                for i in range(ng):
                    kb.pe(lambda e, i=i: e.matmul(pb[:, i * 64:(i + 1) * 64], pkv[:, (k0 + i) * 128:(k0 + i + 1) * 128],
                                                  wkv_ap[:, 0, h * 128 + 64:h * 128 + 128], start=True, stop=True),
                          reads=[kwkv, kpkv], writes=[kpb])
                kb.dve(lambda e: e.tensor_copy(out=va_ap[:, k0:k0 + ng, 0:64], in_=pb[:, 0:ng * 64].rearrange("p (a c) -> p a c", c=64)),
                       reads=[kpb], writes=[kva + ".v"])

            def q_chunk(qc, QW, rope, pq_ap, kpq_, qt_ap, kqt):
                q0 = qc * QW
                pa, kpa = mbank()
                for c_ in range(2):
                    kb.pe(lambda e, c_=c_: e.matmul(pa[0:96, 0:QW], wq_ap[:, c_, h * 96:(h + 1) * 96], pq_ap[:, c_, q0:q0 + QW], start=(c_ == 0), stop=(c_ == 1)),
                          reads=[kwq[c_], kpq_], writes=[kpa])
                kb.dve(lambda e: e.tensor_scalar(out=qt_ap[0:64, q0:q0 + QW], in0=pa[0:64, 0:QW], scalar1=SCALE, scalar2=None, op0=ALU.mult),
                       reads=[kpa], writes=[kqt + ".n"])
                if rope:
                    pb, kpb = mbank()
                    for c_ in range(2):
                        kb.pe(lambda e, c_=c_: e.matmul(pb[0:96, 0:QW], wqr_ap[:, c_, h * 96:(h + 1) * 96], pq_ap[:, c_, q0:q0 + QW], start=(c_ == 0), stop=(c_ == 1)),
                              reads=[kwqr[c_], kpq_], writes=[kpb])
                    kb.dve(lambda e: e.tensor_tensor(out=rt1[64:96, 0:QW], in0=pa[64:96, 0:QW], in1=cosq[64:96, q0:q0 + QW], op=ALU.mult),
                           reads=[kpa, kcosq], writes=[krt1])
                    kb.dve(lambda e: e.tensor_tensor(out=rt2[64:96, 0:QW], in0=pb[64:96, 0:QW], in1=sinq[64:96, q0:q0 + QW], op=ALU.mult),
                           reads=[kpb, ksinq], writes=[krt2])
                    kb.dve(lambda e: e.tensor_tensor(out=qt_ap[64:96, q0:q0 + QW], in0=rt1[64:96, 0:QW], in1=rt2[64:96, 0:QW], op=ALU.add),
                           reads=[krt1, krt2], writes=[kqt + ".r"])
                else:
                    kb.dve(lambda e: e.tensor_scalar(out=qt_ap[64:96, q0:q0 + QW], in0=pa[64:96, 0:QW], scalar1=SCALE, scalar2=None, op0=ALU.mult),
                           reads=[kpa], writes=[kqt + ".r"])
            for kc in range((NK + 511) // 512):
                steps.append(lambda kc=kc: k_chunk(kc))
            for g8 in range((NKT + 7) // 8):
                steps.append(lambda g8=g8: v_group(g8))
            for qc in range(S // 512):
                steps.append(lambda qc=qc: q_chunk(qc, 512, True, pq, kpq, QT[hb][0], QT[hb][1]))
            if not last:
                steps.append(lambda: q_chunk(0, CL, False, pqc, kpqc, QTc[hb][0], QTc[hb][1]))
            return steps

        for st_ in build_steps(0):
            st_()
        for h in range(NH):
            hb = h % 2
            kt_ap, kkt = KT[hb]
            va_ap, kva = VA[hb]
            pending = build_steps(h + 1) if h + 1 < NH else []
            qsets = [(lat, S, NKT, QT[hb])]
            if not last:
                qsets.append((cx, CL, CL // 128, QTc[hb]))
            n_items_total = sum((nq_ // min(512, nq_)) * ((nkt_ + 1) // 2) for (_, nq_, nkt_, _) in qsets)
            every = max(1, (n_items_total - 8) // max(1, len(pending)))
            tick = [0]
            for (qs, nq, nkt, (qt_ap, kqt)) in qsets:
                QW = min(512, nq)
                npair = (nkt + 1) // 2
                items = [(qb, kp) for qb in range(nq // QW) for kp in range(npair)]
                po_of = {}
                for qb in range(nq // QW):
                    cnt_o[0] += 1
                    po_of[qb] = (self.bank(4 + cnt_o[0] % 2), cnt_o[0] % 2)

                def emit_S(ii, nkt=nkt, QW=QW, kt_ap=kt_ap, qt_ap=qt_ap, kkt=kkt, kqt=kqt, items=items):
                    qb, kp = items[ii]
                    q0 = qb * QW
                    ps_s = self.ps2[ii % 2]
                    kps = ["pb%d" % (2 * (ii % 2)), "pb%d" % (2 * (ii % 2) + 1)]
                    pt_ap, kpt = PT[ii % 2]
                    nh = min(2, nkt - 2 * kp)
                    for half in range(nh):
                        kt = 2 * kp + half
                        kb.pe(lambda e, ps_s=ps_s, half=half, kt=kt, q0=q0:
                              e.matmul(ps_s[:, half * 512:half * 512 + QW], kt_ap[0:96, kt * 128:(kt + 1) * 128], qt_ap[0:96, q0:q0 + QW], start=True, stop=True),
                              reads=[kkt + ".n", kkt + ".r", kqt + ".n", kqt + ".r"], writes=[kps[half]])
                    if QW == 512:
                        kb.act(lambda e, ps_s=ps_s, pt_ap=pt_ap, nh=nh: e.activation(out=pt_ap[:, 0:nh * 512], in_=ps_s[:, 0:nh * 512], func=AF.Exp),
                               reads=kps[0:nh], writes=[kpt])
                    else:
                        for half in range(nh):
                            kb.act(lambda e, ps_s=ps_s, pt_ap=pt_ap, half=half: e.activation(out=pt_ap[:, half * 512:half * 512 + QW], in_=ps_s[:, half * 512:half * 512 + QW], func=AF.Exp),
                                   reads=[kps[half]], writes=[kpt])

                def emit_PV(ii, nkt=nkt, QW=QW, va_ap=va_ap, kva=kva, qs=qs, h=h, npair=npair, items=items, po_of=po_of):
                    qb, kp = items[ii]
                    q0 = qb * QW
                    (po, kpo), par = po_of[qb]
                    pt_ap, kpt = PT[ii % 2]
                    nh = min(2, nkt - 2 * kp)
                    for half in range(nh):
                        kt = 2 * kp + half
                        kb.pe(lambda e, po=po, pt_ap=pt_ap, half=half, kt=kt:
                              e.matmul(po[:, 0:QW], va_ap[:, kt, :], pt_ap[:, half * 512:half * 512 + QW], start=(kt == 0), stop=(kt == nkt - 1)),
                              reads=[kva + ".v", kva + ".ones", kpt], writes=[kpo])
                    if kp != npair - 1:
                        return
                    o_ap, ko = oT[par]
                    r_ap, kr_ = rc[par]
                    s_ap, ks_ = stg[par]
                    nsub = QW // 128
                    kb.dve(lambda e, o_ap=o_ap, po=po: e.tensor_copy(out=o_ap[:, 0:QW], in_=po[:, 0:QW]), reads=[kpo], writes=[ko])
                    ptr, kptr = mbank()
                    for sub in range(nsub):
                        kb.pe(lambda e, ptr=ptr, o_ap=o_ap, sub=sub: e.transpose(ptr[:, sub * 128:(sub + 1) * 128], o_ap[:, sub * 128:(sub + 1) * 128], self.ident_f),
                              reads=[ko, "ident_f"], writes=[kptr])
                    kb.dve(lambda e, ptr=ptr, r_ap=r_ap, nsub=nsub: e.reciprocal(out=r_ap[:, 0:nsub, :], in_=ptr[:, 0:nsub * 128].rearrange("p (a c) -> p a c", c=128)[:, :, 64:65]),
                           reads=[kptr], writes=[kr_])
                    for sub in range(nsub):
                        kb.dve(lambda e, ptr=ptr, s_ap=s_ap, r_ap=r_ap, sub=sub: e.tensor_scalar(out=s_ap[:, sub, :], in0=ptr[:, sub * 128:sub * 128 + 64], scalar1=r_ap[:, sub, :], scalar2=None, op0=ALU.mult),
                               reads=[kptr, kr_], writes=[ks_ + ".%d" % sub])
                    kb.dma(qs.ATT[q0:q0 + QW, h * 64:(h + 1) * 64].rearrange("(a p) c -> p a c", p=128), s_ap[:, 0:nsub, :],
                           reads=[ks_ + ".%d" % sub for sub in range(nsub)], writes=["ATT_" + qs.name])

                emit_S(0)
                for ii in range(len(items)):
                    if ii + 1 < len(items):
                        emit_S(ii + 1)
                    emit_PV(ii)
                    tick[0] += 1
                    if pending and tick[0] % every == 0:
                        pending.pop(0)()
            while pending:
                pending.pop(0)()
    self.phase_end()


Prog.phase_attn = _attn_phase


def _pool_phase(self, l):
    kb, ar, w = self.kb, self.arena, self.w
    last = (l == L - 1)
    pw = ar.alloc([4, 64], BF16, "pw")
    kb.dma(pw[0][0:64, :, :], w["pool_w"][l].rearrange("g i o -> i g o"), writes=[pw[1]], q="pool")
    psc, kpsc = ar.alloc([256], F32, "psc")
    kb.dma(psc, w["pool_scale"][l].partition_broadcast(128), writes=[kpsc])
    groups = [("lat", self.lat)] + ([] if last else [("ctx", self.ctx)])
    for nm, seqs in groups:
        bandc = self.k["band_" + nm]
        tab = self.consts["_band_tab_" + nm]
        nblk = bandc.shape[0]
        band, kband = ar.alloc([nblk, 512], BF16, "band")
        kb.dma(band, bandc.rearrange("b p t -> p b t"), writes=[kband])
        n = seqs[0].n
        NT = n // 128
        u, ku = ar.alloc([NT, 256], BF16, "u")
        dg = [ar.alloc([4, 512], BF16, "dg") for _ in range(2)]
        stg = [ar.alloc([4, 256], F32, "pstg") for _ in range(2)]
        it = 0
        for s in seqs:
            W, nsub = s.W, s.W // 128
            kb.dma(u, s.POOLU.rearrange("(a p) c -> p a c", p=128), reads=["POOLU_" + s.name], writes=[ku])
            for i in range(s.nt):
                d_ap, kd = dg[it % 2]
                s_ap, ks = stg[it % 2]
                for g in range(4):
                    ms = sorted(m for (g_, i_, m) in tab if g_ == g and i_ == i)
                    pb, kpb = self.nb()
                    for mi_, m in enumerate(ms):
                        bi = tab[(g, i, m)]
                        kb.pe(lambda e, pb=pb, g=g, m=m, bi=bi, W=W, mi_=mi_, nm_=len(ms), u=u, band=band: e.matmul(pb[0:64, 0:W], u[:, m, g * 64:(g + 1) * 64], band[:, bi, 0:W],
                                                                                            start=(mi_ == 0), stop=(mi_ == nm_ - 1)),
                              reads=[ku, kband], writes=[kpb])
                    self.evac(g, lambda e, pb=pb, d_ap=d_ap, g=g, W=W: e.activation(out=d_ap[0:64, g, 0:W], in_=pb[0:64, 0:W], func=AF.Copy),
                              lambda e, pb=pb, d_ap=d_ap, g=g, W=W: e.tensor_copy(out=d_ap[0:64, g, 0:W], in_=pb[0:64, 0:W]),
                              reads=[kpb], writes=[kd + ".%d" % g])
                for sub in range(nsub):
                    pb, kpb = self.nb()
                    for g in range(4):
                        kb.pe(lambda e, pb=pb, g=g, sub=sub, d_ap=d_ap: e.matmul(pb[:, g * 64:(g + 1) * 64], d_ap[0:64, g, sub * 128:(sub + 1) * 128], pw[0][0:64, g, :],
                                                                                start=True, stop=True),
                              reads=[kd + ".%d" % g, pw[1]], writes=[kpb])
                    kb.dve(lambda e, pb=pb, s_ap=s_ap, sub=sub: e.tensor_tensor(out=s_ap[:, sub, :], in0=pb[:, 0:256], in1=psc, op=ALU.mult),
                           reads=[kpb, kpsc], writes=[ks + ".%d" % sub])
                kb.dma(s.POOLO[i * W:(i + 1) * W, :].rearrange("(a p) c -> p a c", p=128), s_ap[:, 0:nsub, :],
                       reads=[ks + ".%d" % sub for sub in range(nsub)], writes=["POOLO_" + s.name])
                it += 1
    self.phase_end()


def _c1_phase(self, l):
    kb, ar, w = self.kb, self.arena, self.w
    last = (l == L - 1)
    wout = ar.alloc([8, D], BF16, "wout")
    self.load_cast_rows(wout, w["w_out"][l], 8)
    wo_ap, kwo = wout
    gout, kgout = ar.alloc([D], F32, "gout")
    kb.dma(gout, w["g_out"][l].partition_broadcast(128), writes=[kgout])
    xts = [ar.alloc([8, 512], F32, "xt") for _ in range(2)]
    att = [ar.alloc([4, 512], F32, "att") for _ in range(2)]
    pl = [ar.alloc([4, 256], F32, "pl") for _ in range(2)]
    hy = [ar.alloc([4, 256], F32, "hy") for _ in range(2)]
    junk, kjunk = ar.alloc([512], BF16, "junk")
    ss = [ar.alloc([3, 4], F32, "ss") for _ in range(2)]
    mrg = [ar.alloc([4, D], BF16, "mrg") for _ in range(2)]
    mT = [ar.alloc([8, 512], BF16, "mT") for _ in range(2)]
    seqs = self.lat + ([] if last else self.ctx)
    GR = ((0, 512, 0), (512, 256, 1), (768, 256, 2))
    it = 0
    for s in seqs:
        W, nsub = s.W, s.W // 128
        g1 = self.mod(l, 2, s.v)
        HYO = self.HYO_lat if s.kind == "lat" else self.HYO_ctx
        for i in range(s.nt):
            t0 = i * W
            xt, kxt = xts[it % 2]
            a_ap, ka = att[it % 2]
            p_ap, kp = pl[it % 2]
            h_ap, kh = hy[it % 2]
            ss_ap, kss = ss[it % 2]
            m_ap, km = mrg[it % 2]
            t_ap, kt = mT[it % 2]
            kb.dma(xt[:, :, 0:W], s.XT[:, :, t0:t0 + W].rearrange("j p t -> p j t"), reads=["XT_" + s.name], writes=[kxt + ".%d" % j for j in range(8)])
            kb.dma(a_ap[:, 0:nsub, :], s.ATT[t0:t0 + W, :].rearrange("(a p) c -> p a c", p=128), reads=["ATT_" + s.name], writes=[ka])
            kb.dma(p_ap[:, 0:nsub, :], s.POOLO[t0:t0 + W, :].rearrange("(a p) c -> p a c", p=128), reads=["POOLO_" + s.name], writes=[kp])
            kb.dma(h_ap[:, 0:nsub, :], HYO[t0:t0 + W, s.b * 256:(s.b + 1) * 256].rearrange("(a p) c -> p a c", p=128), reads=["HYO"], writes=[kh])
            kb.dve(lambda e, ss_ap=ss_ap: e.memset(ss_ap, 0.0), writes=[kss])
            srcs = ((a_ap, ka), (p_ap, kp), (h_ap, kh))
            for sub in range(nsub):
                for (c0, ng, gi) in GR:
                    src, ksrc = srcs[gi]
                    kb.act(lambda e, src=src, sub=sub, ng=ng, gi=gi, ss_ap=ss_ap: e.activation(out=junk[:, 0:ng], in_=src[:, sub, :], func=AF.Square,
                                                                                            accum_out=ss_ap[:, gi, sub:sub + 1]),
                           reads=[ksrc, kss], writes=[kss, kjunk])
            for (c0, ng, gi) in GR:
                kb.act(lambda e, ss_ap=ss_ap, gi=gi, ng=ng, nsub=nsub: e.activation(out=ss_ap[:, gi, 0:nsub], in_=ss_ap[:, gi, 0:nsub], func=AF.Sqrt,
                                                                                bias=self.eps[:, 0:1], scale=1.0 / ng),
                       reads=[kss, "eps"], writes=[kss])
            kb.dve(lambda e, ss_ap=ss_ap: e.reciprocal(out=ss_ap, in_=ss_ap), reads=[kss], writes=[kss])
            for sub in range(nsub):
                for (c0, ng, gi) in GR:
                    src, ksrc = srcs[gi]
                    kb.dve(lambda e, src=src, sub=sub, c0=c0, ng=ng, gi=gi, ss_ap=ss_ap, m_ap=m_ap:
                           e.scalar_tensor_tensor(out=m_ap[:, sub, c0:c0 + ng], in0=src[:, sub, :], scalar=ss_ap[:, gi, sub:sub + 1], in1=gout[:, c0:c0 + ng],
                                                  op0=ALU.mult, op1=ALU.mult),
                           reads=[ksrc, kss, kgout], writes=[km + ".%d" % sub])
                pb, kpb = self.nb()
                pbb = pb.bitcast(BF16)
                for j in range(8):
                    kb.pe(lambda e, pbb=pbb, m_ap=m_ap, sub=sub, j=j: e.transpose(pbb[:, j * 128:(j + 1) * 128], m_ap[:, sub, j * 128:(j + 1) * 128], self.ident_b),
                          reads=[km + ".%d" % sub, "ident_b"], writes=[kpb])
                self.evac(sub, lambda e, pbb=pbb, t_ap=t_ap, sub=sub: e.activation(out=t_ap[:, :, sub * 128:(sub + 1) * 128], in_=pbb.rearrange("p (j t) -> p j t", t=128), func=AF.Copy),
                          lambda e, pbb=pbb, t_ap=t_ap, sub=sub: e.tensor_copy(out=t_ap[:, :, sub * 128:(sub + 1) * 128], in_=pbb.rearrange("p (j t) -> p j t", t=128)),
                          reads=[kpb], writes=[kt + ".%d" % sub])
            for oc in range(8):
                pb, kpb = self.nb()
                for k_ in range(8):
                    kb.pe(lambda e, pb=pb, k_=k_, oc=oc, t_ap=t_ap, W=W: e.matmul(pb[:, 0:W], wo_ap[:, k_, oc * 128:(oc + 1) * 128], t_ap[:, k_, 0:W], start=(k_ == 0), stop=(k_ == 7)),
                          reads=[kwo + ".%d" % k_] + [kt + ".%d" % sub for sub in range(nsub)], writes=[kpb])
                kb.dve(lambda e, pb=pb, xt=xt, oc=oc, W=W, g1=g1: e.scalar_tensor_tensor(out=xt[:, oc, 0:W], in0=pb[:, 0:W], scalar=g1[:, oc:oc + 1], in1=xt[:, oc, 0:W],
                                                                                    op0=ALU.mult, op1=ALU.add),
                       reads=[kpb, kxt + ".%d" % oc, "modT"], writes=[kxt + ".%d" % oc])
            kb.dma(s.XT[:, :, t0:t0 + W].rearrange("j p t -> p j t"), xt[:, :, 0:W], reads=[kxt + ".%d" % j for j in range(8)], writes=["XT_" + s.name])
            it += 1
    self.phase_end()


def _c2_phase(self, l):
    kb, ar, w = self.kb, self.arena, self.w
    last = (l == L - 1)
    w1 = ar.alloc([8, DFF], BF16, "w1")
    w2 = ar.alloc([32, D], BF16, "w2")
    self.load_cast_rows(w1, w["w_mlp1"][l], 8, split=2)
    self.load_cast_rows(w2, w["w_mlp2"][l], 32)
    w1_ap, kw1 = w1
    w2_ap, kw2 = w2
    xts = [ar.alloc([8, 512], F32, "xt")] * 2
    sq, ksq = ar.alloc([8, 512], BF16, "sq")
    rs, krs = ar.alloc([512], F32, "rs")
    hT, khT = ar.alloc([8, 512], BF16, "hT")
    hid, khid = ar.alloc([32, 512], BF16, "hid")
    tmp = hid[:, 0:16, :].rearrange("p a b -> p (a b)").bitcast(F32).rearrange("p (a b) -> p a b", b=512)
    ktmp = khid + ".tmp"
    rl = [ar.alloc([512], F32, "rl") for _ in range(2)]
    seqs = self.lat + ([] if last else self.ctx)
    itc = [0]
    for s in seqs:
        W = s.W
        gm, sh, g2 = self.mod(l, 4, s.v), self.mod(l, 3, s.v), self.mod(l, 5, s.v)
        for i in range(s.nt):
            t0 = i * W
            xt, kxt = xts[itc[0] % 2]
            itc[0] += 1
            kb.dma(xt[:, :, 0:W], s.XT[:, :, t0:t0 + W].rearrange("j p t -> p j t"), reads=["XT_" + s.name], writes=[kxt + ".%d" % j for j in range(8)])
            self.mod_norm(xt, kxt, W, gm, sh, sq, ksq, rs, krs, tmp, ktmp, hT, khT,
                          extra=lambda j: [khid + ".%d" % (2 * j), khid + ".%d" % (2 * j + 1)])
            khT_all = [khT + ".%d" % j for j in range(8)]
            for hc in range(32):
                pb, kpb = self.nb()
                for j in range(8):
                    kb.pe(lambda e, pb=pb, j=j, hc=hc, W=W: e.matmul(pb[:, 0:W], w1_ap[:, j, hc * 128:(hc + 1) * 128], hT[:, j, 0:W], start=(j == 0), stop=(j == 7)),
                          reads=[kw1 + ".%d" % j, khT_all[j]], writes=[kpb])
                r_ap, kr_ = rl[hc % 2]
                kb.act(lambda e, pb=pb, r_ap=r_ap, W=W: e.activation(out=r_ap[:, 0:W], in_=pb[:, 0:W], func=AF.Relu), reads=[kpb], writes=[kr_])
                kb.dve(lambda e, r_ap=r_ap, hc=hc, W=W: e.tensor_tensor(out=hid[:, hc, 0:W], in0=r_ap[:, 0:W], in1=r_ap[:, 0:W], op=ALU.mult),
                       reads=[kr_], writes=[khid + ".%d" % hc])
            for oc in range(8):
                pb, kpb = self.nb()
                for hc in range(32):
                    kb.pe(lambda e, pb=pb, hc=hc, oc=oc, W=W: e.matmul(pb[:, 0:W], w2_ap[:, hc, oc * 128:(oc + 1) * 128], hid[:, hc, 0:W], start=(hc == 0), stop=(hc == 31)),
                          reads=[kw2 + ".%d" % hc, khid + ".%d" % hc], writes=[kpb])
                kb.dve(lambda e, pb=pb, oc=oc, W=W, g2=g2, xt=xt: e.scalar_tensor_tensor(out=xt[:, oc, 0:W], in0=pb[:, 0:W], scalar=g2[:, oc:oc + 1], in1=xt[:, oc, 0:W],
                                                                             op0=ALU.mult, op1=ALU.add),
                       reads=[kpb, kxt + ".%d" % oc, "modT"], writes=[kxt + ".%d" % oc])
            kb.dma(s.XT[:, :, t0:t0 + W].rearrange("j p t -> p j t"), xt[:, :, 0:W], reads=[kxt + ".%d" % j for j in range(8)], writes=["XT_" + s.name])
    self.phase_end()


def _final_phase(self):
    kb, ar, w = self.kb, self.arena, self.w
    gf, kgf = ar.alloc([8], F32, "gf")
    for j in range(8):
        kb.dma(gf[:, j:j + 1], w["g_final"][j * 128:(j + 1) * 128].rearrange("(p o) -> p o", o=1), writes=[kgf])
    xts = [ar.alloc([8, 512], F32, "xt") for _ in range(2)]
    sq, ksq = ar.alloc([8, 512], BF16, "sq")
    rs, krs = ar.alloc([512], F32, "rs")
    xn, kxn = ar.alloc([8, 512], F32, "xn")
    yts = [ar.alloc([4, D], F32, "yt") for _ in range(2)]
    it = 0
    for s in self.lat:
        W = 512
        for i in range(s.nt):
            t0 = i * W
            xt, kxt = xts[it % 2]
            yt, kyt = yts[it % 2]
            kb.dma(xt, s.XT[:, :, t0:t0 + W].rearrange("j p t -> p j t"), reads=["XT_" + s.name], writes=[kxt + ".%d" % j for j in range(8)])
            self.fm_rstd([(xt[:, j, :], kxt + ".%d" % j, 128) for j in range(8)], D, W, sq, ksq, rs, krs)
            for j in range(8):
                kb.dve(lambda e, xt=xt, j=j: e.scalar_tensor_tensor(out=xn[:, j, :], in0=xt[:, j, :], scalar=gf[:, j:j + 1], in1=rs, op0=ALU.mult, op1=ALU.mult),
                       reads=[kxt + ".%d" % j, krs, kgf], writes=[kxn + ".%d" % j])
            for sub in range(4):
                pp = self.ps2[sub % 2]
                kpp = ["pb%d" % (2 * (sub % 2)), "pb%d" % (2 * (sub % 2) + 1)]
                for j in range(8):
                    kb.pe(lambda e, pp=pp, j=j, sub=sub: e.transpose(pp[:, j * 128:(j + 1) * 128], xn[:, j, sub * 128:(sub + 1) * 128], self.ident_f),
                          reads=[kxn + ".%d" % j, "ident_f"], writes=[kpp[j // 4]])
                self.evac(sub, lambda e, pp=pp, yt=yt, sub=sub: e.activation(out=yt[:, sub, :], in_=pp, func=AF.Copy),
                          lambda e, pp=pp, yt=yt, sub=sub: e.tensor_copy(out=yt[:, sub, :], in_=pp),
                          reads=kpp, writes=[kyt + ".%d" % sub])
            kb.dma(self.y_out[s.b][t0:t0 + W, :].rearrange("(a p) d -> p a d", p=128), yt, reads=[kyt + ".%d" % sub for sub in range(4)], writes=["y"])
            it += 1
    self.phase_end()


Prog.phase_pool = _pool_phase
Prog.phase_c1 = _c1_phase
Prog.phase_c2 = _c2_phase
Prog.phase_final = _final_phase


def _hy_h0(self, l, nm, seqs, n):
    kb, ar, w = self.kb, self.arena, self.w
    NT = n // 128
    cw, kcw = ar.alloc([6, 3], F32, "cw")
    cb, kcb = ar.alloc([6], F32, "cb")
    for c6 in range(6):
        for k_ in range(3):
            kb.dma(cw[:, c6, k_:k_ + 1], w["hy_conv_w"][l][k_, c6 * 128:(c6 + 1) * 128].rearrange("(p o) -> p o", o=1), writes=[kcw])
        kb.dma(cb[:, c6:c6 + 1], w["hy_conv_b"][l][c6 * 128:(c6 + 1) * 128].rearrange("(p o) -> p o", o=1), writes=[kcb])
    hp = [ar.alloc([n + 2], F32, "hp") for _ in range(2)]
    acc = [ar.alloc([n], F32, "acc") for _ in range(2)]
    ucb = [ar.alloc([n], BF16, "ucb") for _ in range(2)]
    tm = [ar.alloc([NT, 128], BF16, "tm") for _ in range(2)]
    dests = (getattr(self, "HV_" + nm), getattr(self, "HX1_" + nm), getattr(self, "HX2_" + nm))
    it = 0
    for si, s in enumerate(seqs):
        for c6 in range(6):
            h_ap, kh = hp[it % 2]
            a_ap, ka = acc[it % 2]
            u_ap, ku = ucb[it % 2]
            t_ap, kt = tm[it % 2]
            kb.dma(h_ap, s.HYP[c6], reads=["HYP_" + s.name], writes=[kh])
            kb.act(lambda e, h_ap=h_ap, a_ap=a_ap, c6=c6: e.activation(out=a_ap, in_=h_ap[:, 1:n + 1], func=AF.Identity, bias=cb[:, c6:c6 + 1], scale=cw[:, c6, 1:2]),
                   reads=[kh, kcw, kcb], writes=[ka])
            kb.dve(lambda e, h_ap=h_ap, a_ap=a_ap, c6=c6: e.scalar_tensor_tensor(out=a_ap, in0=h_ap[:, 0:n], scalar=cw[:, c6, 0:1], in1=a_ap, op0=ALU.mult, op1=ALU.add),
                   reads=[kh, kcw, ka], writes=[ka])
            kb.dve(lambda e, h_ap=h_ap, a_ap=a_ap, u_ap=u_ap, c6=c6: e.scalar_tensor_tensor(out=u_ap, in0=h_ap[:, 2:n + 2], scalar=cw[:, c6, 2:3], in1=a_ap, op0=ALU.mult, op1=ALU.add),
                   reads=[kh, kcw, ka], writes=[ku])
            for g8 in range((NT + 7) // 8):
                ng = min(8, NT - g8 * 8)
                pb, kpb = self.nb()
                pbb = pb.bitcast(BF16)
                for i in range(ng):
                    tt = g8 * 8 + i
                    kb.pe(lambda e, pbb=pbb, u_ap=u_ap, i=i, tt=tt: e.transpose(pbb[:, i * 128:(i + 1) * 128], u_ap[:, tt * 128:(tt + 1) * 128], self.ident_b),
                          reads=[ku, "ident_b"], writes=[kpb])
                self.evac(g8, lambda e, pbb=pbb, t_ap=t_ap, g8=g8, ng=ng: e.activation(out=t_ap[:, g8 * 8:g8 * 8 + ng, :], in_=pbb[:, 0:ng * 128].rearrange("p (a c) -> p a c", c=128), func=AF.Copy),
                          lambda e, pbb=pbb, t_ap=t_ap, g8=g8, ng=ng: e.tensor_copy(out=t_ap[:, g8 * 8:g8 * 8 + ng, :], in_=pbb[:, 0:ng * 128].rearrange("p (a c) -> p a c", c=128)),
                          reads=[kpb], writes=[kt + ".%d" % g8])
            dst = dests[c6 // 2]
            col0 = si * 256 + (c6 % 2) * 128
            kb.dma(dst[:, col0:col0 + 128].rearrange("(a p) c -> p a c", p=128), t_ap,
                   reads=[kt + ".%d" % g8 for g8 in range((NT + 7) // 8)], writes=["HU_" + nm])
            it += 1
    self.phase_end()


def _hy_h1(self, l, nm, n):
    kb, ar, w = self.kb, self.arena, self.w
    NT = n // 128
    N2 = 2 * n
    CW = min(512, n)
    zT, kzT = ar.alloc([n], F32, "zT")
    kb.dma(zT[0:HY_EMB, :], self.k["hyz_" + nm], writes=[kzT])
    w1s, kw1s = ar.alloc([HY_FFN], F32, "w1s")
    w2s, kw2s = ar.alloc([HY_FFN], F32, "w2s")
    w3s, kw3s = ar.alloc([1024], F32, "w3s")
    kb.dma(w1s[0:HY_EMB, :], w["hy_f_w1"][l], writes=[kw1s])
    kb.dma(w2s[0:HY_FFN, :], w["hy_f_w2"][l], writes=[kw2s])
    kb.dma(w3s[0:HY_FFN, :], w["hy_f_w3"][l], writes=[kw3s])
    par, kpar = ar.alloc([6], F32, "par")
    for ci, nm_ in enumerate(("hy_f_b1", "hy_f_freq1", "hy_f_b2", "hy_f_freq2")):
        kb.dma(par[0:64, ci:ci + 1], w[nm_][l].rearrange("(p o) -> p o", o=1), writes=[kpar])
    kb.dve(lambda e: e.tensor_tensor(out=par[0:64, 4:5], in0=par[0:64, 0:1], in1=par[0:64, 1:2], op=ALU.mult), reads=[kpar], writes=[kpar])
    kb.dve(lambda e: e.tensor_tensor(out=par[0:64, 5:6], in0=par[0:64, 2:3], in1=par[0:64, 3:4], op=ALU.mult), reads=[kpar], writes=[kpar])
    h1T, kh1 = ar.alloc([n], F32, "h1T")
    h2T, kh2 = ar.alloc([n], F32, "h2T")
    arg, karg = ar.alloc([512], F32, "arg")
    kk, kkk = ar.alloc([512], F32, "kk")

    def layer(srcT, ksrc, wS, kwS, K, fcol, fbcol, dstT, kdst):
        for ch in range(n // CW):
            c0 = ch * CW
            pb, kpb = self.nb()
            kb.pe(lambda e, pb=pb, c0=c0: e.matmul(pb[0:64, 0:CW], wS[0:K, 0:64], srcT[0:K, c0:c0 + CW], start=True, stop=True),
                  reads=[ksrc, kwS], writes=[kpb])
            kb.act(lambda e, pb=pb: e.activation(out=arg[0:64, 0:CW], in_=pb[0:64, 0:CW], func=AF.Identity, bias=par[0:64, fbcol:fbcol + 1], scale=par[0:64, fcol:fcol + 1]),
                   reads=[kpb, kpar], writes=[karg])
            kb.dve(lambda e: e.tensor_scalar(out=kk[0:64, 0:CW], in0=arg[0:64, 0:CW], scalar1=1.0 / (2 * math.pi), scalar2=MAGIC, op0=ALU.mult, op1=ALU.add),
                   reads=[karg], writes=[kkk])
            kb.dve(lambda e: e.tensor_scalar(out=kk[0:64, 0:CW], in0=kk[0:64, 0:CW], scalar1=-MAGIC, scalar2=-2 * math.pi, op0=ALU.add, op1=ALU.mult),
                   reads=[kkk], writes=[kkk])
            kb.dve(lambda e: e.tensor_tensor(out=arg[0:64, 0:CW], in0=arg[0:64, 0:CW], in1=kk[0:64, 0:CW], op=ALU.add), reads=[karg, kkk], writes=[karg])
            kb.act(lambda e, c0=c0: e.activation(out=dstT[0:64, c0:c0 + CW], in_=arg[0:64, 0:CW], func=AF.Sin), reads=[karg], writes=[kdst])
    layer(zT, kzT, w1s, kw1s, HY_EMB, 1, 4, h1T, kh1)
    layer(h1T, kh1, w2s, kw2s, HY_FFN, 3, 5, h2T, kh2)
    HP, kHP = ar.alloc([NT, 512], BF16, "HP")
    HM, kHM = ar.alloc([NT, 512], BF16, "HM")
    dec = [ar.alloc([256], F32, "dec") for _ in range(2)]
    tp = [ar.alloc([4, 256], F32, "tp") for _ in range(2)]
    ab = [ar.alloc([4, 256], F32, "ab") for _ in range(2)]
    psZ = self.ps2[3]
    kZ = ["pb6", "pb7"]
    for tt in range(NT):
        pt = self.ps2[tt % 2]
        kpt = ["pb%d" % (2 * (tt % 2)), "pb%d" % (2 * (tt % 2) + 1)]
        d_ap, kd = dec[tt % 2]
        t_ap, ktp = tp[tt % 2]
        a_ap, kab = ab[tt % 2]
        for hf in range(2):
            kb.pe(lambda e, pt=pt, hf=hf, tt=tt: e.matmul(pt[:, hf * 512:(hf + 1) * 512], h2T[0:64, tt * 128:(tt + 1) * 128], w3s[0:64, hf * 512:(hf + 1) * 512], start=True, stop=True),
                  reads=[kh2, kw3s], writes=[kpt[hf]])
        kb.dma(d_ap, self.k["hydec_" + nm][tt * 128:(tt + 1) * 128, :], writes=[kd])
        for q in range(4):
            kb.dve(lambda e, pt=pt, t_ap=t_ap, d_ap=d_ap, q=q: e.tensor_tensor(out=t_ap[:, q, :], in0=pt[:, q * 256:(q + 1) * 256], in1=d_ap, op=ALU.mult),
                   reads=[kpt[q // 2], kd], writes=[ktp])
        if tt == 0:
            for q in (1, 3):
                kb.dve(lambda e, t_ap=t_ap, q=q: e.memset(t_ap[0:1, q, :], 0.0), reads=[ktp], writes=[ktp])
        kb.act(lambda e, t_ap=t_ap, a_ap=a_ap: e.activation(out=a_ap, in_=t_ap, func=AF.Abs), reads=[ktp], writes=[kab])
        for hf in range(2):
            kb.pe(lambda e, a_ap=a_ap, hf=hf, tt=tt: e.matmul(psZ[:, hf * 512:(hf + 1) * 512], self.ones_f, a_ap[:, 2 * hf:2 * hf + 2, :].rearrange("p a c -> p (a c)"),
                                                               start=(tt == 0), stop=(tt == NT - 1)),
                  reads=[kab, "ones_f"], writes=[kZ[hf]])
        for o in range(2):
            kb.pool(lambda e, t_ap=t_ap, o=o, tt=tt: e.tensor_tensor(out=HP[:, tt, o * 256:(o + 1) * 256], in0=t_ap[:, 2 * o, :], in1=t_ap[:, 2 * o + 1, :], op=ALU.add),
                    reads=[ktp], writes=[kHP + ".%d" % tt])
            kb.pool(lambda e, t_ap=t_ap, o=o, tt=tt: e.tensor_tensor(out=HM[:, tt, o * 256:(o + 1) * 256], in0=t_ap[:, 2 * o, :], in1=t_ap[:, 2 * o + 1, :], op=ALU.subtract),
                    reads=[ktp], writes=[kHM + ".%d" % tt])
    zc, kzc = ar.alloc([1024], F32, "zc")
    rz, krz = ar.alloc([512], F32, "rz")
    kb.act(lambda e: e.activation(out=zc, in_=psZ, func=AF.Copy), reads=kZ, writes=[kzc])
    for o in range(2):
        kb.dve(lambda e, o=o: e.tensor_tensor(out=rz[:, o * 256:(o + 1) * 256], in0=zc[:, o * 512:o * 512 + 256], in1=zc[:, o * 512 + 256:(o + 1) * 512], op=ALU.add),
               reads=[kzc], writes=[krz])
    kb.dve(lambda e: e.reciprocal(out=rz, in_=rz), reads=[krz], writes=[krz])
    cf, kcf = ar.alloc([2], F32, "cf")
    kb.dve(lambda e: e.memset(cf, 2.0 / N2), writes=[kcf])
    kb.dve(lambda e: e.memset(cf[0:1, 0:1], 1.0 / N2), reads=[kcf], writes=[kcf])
    Cb = [ar.alloc([NT, 128], BF16, "Cb") for _ in range(2)]
    Sb = [ar.alloc([NT, 128], BF16, "Sb") for _ in range(2)]
    hs = [ar.alloc([2, 512], BF16, "hs") for _ in range(2)]
    HS = getattr(self, "HS_" + nm)
    kHPall = [kHP + ".%d" % tt for tt in range(NT)]
    kHMall = [kHM + ".%d" % tt for tt in range(NT)]
    for ft in range(NT):
        c_ap, kc = Cb[ft % 2]
        s_ap, ks = Sb[ft % 2]
        h_ap, kh = hs[ft % 2]
        kb.dma(c_ap, self.k["dftc_" + nm][ft], writes=[kc])
        kb.dma(s_ap, self.k["dfts_" + nm][ft], writes=[ks])
        pre, kpre = self.nb()
        pim, kpim = self.nb()
        for tt in range(NT):
            kb.pe(lambda e, pre=pre, c_ap=c_ap, tt=tt: e.matmul(pre, c_ap[:, tt, :], HP[:, tt, :], start=(tt == 0), stop=(tt == NT - 1)), reads=[kc, kHPall[tt]], writes=[kpre])
        for tt in range(NT):
            kb.pe(lambda e, pim=pim, s_ap=s_ap, tt=tt: e.matmul(pim, s_ap[:, tt, :], HM[:, tt, :], start=(tt == 0), stop=(tt == NT - 1)), reads=[ks, kHMall[tt]], writes=[kpim])
        ccol = 0 if ft == 0 else 1
        kb.dve(lambda e, pre=pre, h_ap=h_ap, ccol=ccol: e.scalar_tensor_tensor(out=h_ap[:, 0, :], in0=pre, scalar=cf[:, ccol:ccol + 1], in1=rz, op0=ALU.mult, op1=ALU.mult),
               reads=[kpre, kcf, krz], writes=[kh + ".0"])
        kb.dve(lambda e, pim=pim, h_ap=h_ap, ccol=ccol: e.scalar_tensor_tensor(out=h_ap[:, 1, :], in0=pim, scalar=cf[:, ccol:ccol + 1], in1=rz, op0=ALU.mult, op1=ALU.mult),
               reads=[kpim, kcf, krz], writes=[kh + ".1"])
        kb.dma(HS[:, ft].rearrange("r p c -> p r c"), h_ap, reads=[kh + ".0", kh + ".1"], writes=["HS_" + nm])
    pmf, kpmf = ar.alloc([1], F32, "pmf")
    pmc, kpmc = ar.alloc([1], BF16, "pmc")
    kb.dma(pmf, self.k["pm1_" + nm][0:128].rearrange("(p o) -> p o", o=1), writes=[kpmf])
    kb.act(lambda e: e.activation(out=pmc, in_=pmf, func=AF.Copy), reads=[kpmf], writes=[kpmc])
    psn, kpsn = self.nb()
    for tt in range(NT):
        kb.pe(lambda e, tt=tt: e.matmul(psn[0:1, :], pmc[:, 0:1], HP[:, tt, :], start=(tt == 0), stop=(tt == NT - 1)), reads=[kpmc, kHPall[tt]], writes=[kpsn])
    hn, khn = ar.alloc([512], F32, "hn")
    kb.dve(lambda e: e.scalar_tensor_tensor(out=hn[0:1, :], in0=psn[0:1, :], scalar=1.0 / N2, in1=rz[0:1, :], op0=ALU.mult, op1=ALU.mult),
           reads=[kpsn, krz], writes=[khn])
    kb.dma(getattr(self, "HNQ_" + nm), hn[0:1, :], reads=[khn], writes=["HNQ_" + nm])
    self.phase_end()


def _hy_h2(self, l, nm, seqs, n):
    kb, ar, w = self.kb, self.arena, self.w
    NT = n // 128
    HV, HX1, HX2 = getattr(self, "HV_" + nm), getattr(self, "HX1_" + nm), getattr(self, "HX2_" + nm)
    HYO, HS, HNQ = getattr(self, "HYO_" + nm), getattr(self, "HS_" + nm), getattr(self, "HNQ_" + nm)
    U, kU = ar.alloc([NT, 512], BF16, "U")
    Zb, kZb = ar.alloc([NT, 512], BF16, "Zb")
    Y, kY = ar.alloc([2, NT, 512], BF16, "Y")
    kb.dma(U, HV.rearrange("(a p) c -> p a c", p=128), writes=[kU + ".%d" % tt for tt in range(NT)])
    Cb = [ar.alloc([NT, 128], BF16, "Cb") for _ in range(2)]
    Sb = [ar.alloc([NT, 128], BF16, "Sb") for _ in range(2)]
    hsb = [ar.alloc([2, 256], BF16, "hsb") for _ in range(2)]
    bias, kbias = ar.alloc([2, 256], F32, "bias")
    kb.dma(bias, w["hy_bias"][l].rearrange("o c -> (o c)").partition_broadcast(128), writes=[kbias])
    hnq, khnq = ar.alloc([512], F32, "hnq")
    kb.dma(hnq[0:1, :], HNQ, writes=[khnq])
    pmf, kpmf = ar.alloc([1], F32, "pmf")
    pmc, kpmc = ar.alloc([1], BF16, "pmc")
    pmrf, kpmrf = ar.alloc([128], F32, "pmrf")
    pmr, kpmr = ar.alloc([128], BF16, "pmr")
    kb.dma(pmf, self.k["pm1_" + nm][0:128].rearrange("(p o) -> p o", o=1), writes=[kpmf])
    kb.act(lambda e: e.activation(out=pmc, in_=pmf, func=AF.Copy), reads=[kpmf], writes=[kpmc])
    kb.dma(pmrf[0:1, :], self.k["pm1_" + nm][0:128].rearrange("(o t) -> o t", o=1), writes=[kpmrf])
    kb.act(lambda e: e.activation(out=pmr[0:1, :], in_=pmrf[0:1, :], func=AF.Copy), reads=[kpmrf], writes=[kpmr])
    tq = [[ar.alloc([256], F32, "tq") for _ in range(4)] for _ in range(2)]
    ynq, kynq = ar.alloc([512], BF16, "ynq")
    gt = [ar.alloc([512], BF16, "gt") for _ in range(2)]
    tb = [ar.alloc([512], F32, "tb") for _ in range(2)]
    t2b = [ar.alloc([512], F32, "t2b") for _ in range(2)]
    ostg = [ar.alloc([512], F32, "ostg") for _ in range(2)]
    for o in range(2):
        src, ksrc = (U, kU) if o == 0 else (Zb, kZb)
        ksrc_all = [ksrc + ".%d" % tt for tt in range(NT)]
        for ft in range(NT):
            c_ap, kc = Cb[ft % 2]
            s_ap, ks = Sb[ft % 2]
            h_ap, kh = hsb[ft % 2]
            kb.dma(c_ap, self.k["dftc_" + nm][ft], writes=[kc])
            kb.dma(s_ap, self.k["dfts_" + nm][ft], writes=[ks])
            kb.dma(h_ap, HS[:, ft, :, o * 256:(o + 1) * 256].rearrange("r p c -> p r c"), writes=[kh])
            pre, kpre = self.nb()
            pim, kpim = self.nb()
            for tt in range(NT):
                kb.pe(lambda e, pre=pre, c_ap=c_ap, tt=tt, src=src: e.matmul(pre, c_ap[:, tt, :], src[:, tt, :], start=(tt == 0), stop=(tt == NT - 1)),
                      reads=[kc, ksrc_all[tt]], writes=[kpre])
            for tt in range(NT):
                kb.pe(lambda e, pim=pim, s_ap=s_ap, tt=tt, src=src: e.matmul(pim, s_ap[:, tt, :], src[:, tt, :], start=(tt == 0), stop=(tt == NT - 1)),
                      reads=[ks, ksrc_all[tt]], writes=[kpim])
            for b in range(2):
                (t1, k1), (t2, k2), (t3, k3), (t4, k4) = tq[b]
                bs = slice(b * 256, (b + 1) * 256)
                kb.dve(lambda e, pre=pre, h_ap=h_ap, t1=t1, bs=bs: e.tensor_tensor(out=t1, in0=pre[:, bs], in1=h_ap[:, 0, :], op=ALU.mult), reads=[kpre, kh], writes=[k1])
                kb.dve(lambda e, pim=pim, h_ap=h_ap, t2=t2, bs=bs: e.tensor_tensor(out=t2, in0=pim[:, bs], in1=h_ap[:, 1, :], op=ALU.mult), reads=[kpim, kh], writes=[k2])
                kb.dve(lambda e, pre=pre, h_ap=h_ap, t3=t3, bs=bs: e.tensor_tensor(out=t3, in0=pre[:, bs], in1=h_ap[:, 1, :], op=ALU.mult), reads=[kpre, kh], writes=[k3])
                kb.dve(lambda e, pim=pim, h_ap=h_ap, t4=t4, bs=bs: e.tensor_tensor(out=t4, in0=pim[:, bs], in1=h_ap[:, 0, :], op=ALU.mult), reads=[kpim, kh], writes=[k4])
                kb.pool(lambda e, t1=t1, t2=t2, ft=ft, bs=bs: e.tensor_tensor(out=Y[:, 0, ft, bs], in0=t1, in1=t2, op=ALU.subtract), reads=[k1, k2], writes=[kY + ".0.%d" % ft])
                kb.pool(lambda e, t3=t3, t4=t4, ft=ft, bs=bs: e.tensor_tensor(out=Y[:, 1, ft, bs], in0=t3, in1=t4, op=ALU.add), reads=[k3, k4], writes=[kY + ".1.%d" % ft])
        psn, kpsn = self.nb()
        for tt in range(NT):
            kb.pe(lambda e, psn=psn, tt=tt, src=src: e.matmul(psn[0:1, :], pmc[:, 0:1], src[:, tt, :], start=(tt == 0), stop=(tt == NT - 1)), reads=[kpmc, ksrc_all[tt]], writes=[kpsn])
        for b in range(2):
            kb.dve(lambda e, psn=psn, b=b, o=o: e.tensor_tensor(out=ynq[0:1, b * 256:(b + 1) * 256], in0=psn[0:1, b * 256:(b + 1) * 256], in1=hnq[0:1, o * 256:(o + 1) * 256], op=ALU.mult),
                   reads=[kpsn, khnq], writes=[kynq])
        gateD = HX1 if o == 0 else HX2
        kY0 = [kY + ".0.%d" % ft for ft in range(NT)]
        kY1 = [kY + ".1.%d" % ft for ft in range(NT)]
        for j in range(NT):
            c_ap, kc = Cb[j % 2]
            s_ap, ks = Sb[j % 2]
            g_ap, kg = gt[j % 2]
            tb_ap, ktb = tb[j % 2]
            t2_ap, kt2 = t2b[j % 2]
            o_ap, ko = ostg[j % 2]
            kb.dma(c_ap, self.k["dftc_" + nm][j], writes=[kc])
            kb.dma(s_ap, self.k["dfts_" + nm][j], writes=[ks])
            kb.dma(g_ap, gateD[j * 128:(j + 1) * 128, :], writes=[kg])
            py, kpy = self.nb()
            for ft in range(NT):
                kb.pe(lambda e, py=py, c_ap=c_ap, ft=ft: e.matmul(py, c_ap[:, ft, :], Y[:, 0, ft, :], start=(ft == 0), stop=False), reads=[kc, kY0[ft]], writes=[kpy])
                kb.pe(lambda e, py=py, s_ap=s_ap, ft=ft: e.matmul(py, s_ap[:, ft, :], Y[:, 1, ft, :], start=False, stop=False), reads=[ks, kY1[ft]], writes=[kpy])
            kb.pe(lambda e, py=py: e.matmul(py, pmr[0:1, :], ynq[0:1, :], start=False, stop=True), reads=[kpmr, kynq], writes=[kpy])
            for b in range(2):
                kb.pool(lambda e, tb_ap=tb_ap, src=src, j=j, b=b, o=o: e.tensor_tensor(out=tb_ap[:, b * 256:(b + 1) * 256], in0=src[:, j, b * 256:(b + 1) * 256], in1=bias[:, o, :], op=ALU.mult),
                        reads=[ksrc_all[j], kbias], writes=[ktb])
            kb.dve(lambda e, py=py, tb_ap=tb_ap, t2_ap=t2_ap: e.tensor_tensor(out=t2_ap, in0=py, in1=tb_ap, op=ALU.add), reads=[kpy, ktb], writes=[kt2])
            if o == 0:
                kb.pool(lambda e, t2_ap=t2_ap, g_ap=g_ap, j=j: e.tensor_tensor(out=Zb[:, j, :], in0=t2_ap, in1=g_ap, op=ALU.mult), reads=[kt2, kg], writes=[kZb + ".%d" % j])
            else:
                kb.pool(lambda e, t2_ap=t2_ap, g_ap=g_ap, o_ap=o_ap: e.tensor_tensor(out=o_ap, in0=t2_ap, in1=g_ap, op=ALU.mult), reads=[kt2, kg], writes=[ko])
                kb.dma(HYO[j * 128:(j + 1) * 128, :], o_ap, reads=[ko], writes=["HYO"])
    self.phase_end()


def _hyena_phase(self, l):
    last = (l == L - 1)
    groups = [("lat", self.lat, S)] + ([] if last else [("ctx", self.ctx, CL)])
    for nm, seqs, n in groups:
        self.hy_h0(l, nm, seqs, n)
        self.hy_h1(l, nm, n)
        self.hy_h2(l, nm, seqs, n)


Prog.hy_h0 = _hy_h0
Prog.hy_h1 = _hy_h1
Prog.hy_h2 = _hy_h2
Prog.phase_hyena = _hyena_phase


def build_program(nc, dbg=(), scopes=False):
    P = Prog(nc, dbg=dbg)
    kb = P.kb
    kb.scopes = scopes
    kb.phase = "mod"
    P.declare()
    P.phase_mod()
    kb.phase = "t0"
    P.phase_t0()
    for l in range(L):
        kb.phase = "a%d" % l
        P.phase_a(l)
        kb.phase = "pool%d" % l
        P.phase_pool(l)
        kb.phase = "hy%d" % l
        P.phase_hyena(l)
        kb.phase = "attn%d" % l
        P.phase_attn(l)
        kb.phase = "c1_%d" % l
        P.phase_c1(l)
        kb.phase = "c2_%d" % l
        P.phase_c2(l)
    kb.phase = "final"
    P.phase_final()
    kb.emit()
    return P


_PROG_CACHE = {}


def kernel(**inputs):
    shared = _shared_inputs(inputs)
    nc = bass.Bass("TRN2", target_bir_lowering=False)
    P = build_program(nc)
    in_maps = []
    for core in range(NCORES):
        m = _core_inputs(inputs, core, shared)
        in_maps.append({k_: v for k_, v in m.items() if k_ in P.dram})
    res = run_bass_kernel_spmd(nc, in_maps, core_ids=list(range(NCORES)))
    out = np.concatenate([np.asarray(r["y"], dtype=np.float32) for r in res.results], axis=0)
    return out
```

```python
import math
import numpy as np
import ml_dtypes
import concourse.bass as bass
import concourse.mybir as mybir
from concourse.bass_utils import run_bass_kernel_spmd

F32 = mybir.dt.float32
BF16 = mybir.dt.bfloat16
AF = mybir.ActivationFunctionType
ALU = mybir.AluOpType
AX = mybir.AxisListType

import os
SAME_ENGINE_SYNC = os.environ.get("MK_SES", "1") == "1"
N_DMA_SEMS = 24


class _Op:
    __slots__ = ("eng", "fn", "deps", "need_inc", "cnt", "dsem", "dval", "is_dma", "idx", "phase")

    def __init__(self, eng, fn, is_dma):
        self.eng = eng
        self.fn = fn
        self.deps = []
        self.need_inc = False
        self.cnt = 0
        self.dsem = -1
        self.dval = 0
        self.is_dma = is_dma
        self.idx = 0


class KB:
    ENGS = ("pe", "act", "dve", "pool", "sp")

    def __init__(self, nc):
        self.nc = nc
        self.ops = []
        self.last_w = {}
        self.readers = {}
        self.n_dma = 0
        self._bar_start = 0
        self.phase = None
        self.scopes = False

    def _add(self, eng, fn, reads, writes, is_dma):
        op = _Op(eng, fn, is_dma)
        op.idx = len(self.ops)
        op.phase = self.phase
        deps = set()
        for k in reads:
            w = self.last_w.get(k)
            if w is not None:
                deps.add(w)
        for k in writes:
            w = self.last_w.get(k)
            if w is not None:
                deps.add(w)
            for r in self.readers.get(k, ()):
                deps.add(r)
        deps.discard(op.idx)
        op.deps = sorted(deps)
        for k in reads:
            lst = self.readers.setdefault(k, [])
            if not is_dma:
                lst[:] = [r for r in lst if self.ops[r].is_dma or self.ops[r].eng != eng]
            lst.append(op.idx)
        for k in writes:
            self.last_w[k] = op.idx
            self.readers[k] = []
        self.ops.append(op)
        return op

    def pe(self, fn, reads=(), writes=()):
        return self._add("pe", fn, reads, writes, False)

    def act(self, fn, reads=(), writes=()):
        return self._add("act", fn, reads, writes, False)

    def dve(self, fn, reads=(), writes=()):
        return self._add("dve", fn, reads, writes, False)

    def pool(self, fn, reads=(), writes=()):
        return self._add("pool", fn, reads, writes, False)

    def dma(self, out, in_, reads=(), writes=(), q="sp", **kw):
        def fn(e, out=out, in_=in_, kw=kw):
            return e.dma_start(out=out, in_=in_, **kw)
        return self._add(q, fn, reads, writes, True)

    def barrier(self):
        last = {}
        dmas = []
        for op in self.ops[self._bar_start:]:
            if op.is_dma:
                dmas.append(op.idx)
            elif op.fn is not None:
                last[op.eng] = op.idx
        deps = sorted(set(list(last.values()) + dmas))
        for e in self.ENGS:
            op = _Op(e, None, False)
            op.idx = len(self.ops)
            op.phase = self.phase
            op.deps = list(deps)
            self.ops.append(op)
        self._bar_start = len(self.ops)
        self.last_w = {}
        self.readers = {}

    def emit(self):
        nc = self.nc
        ops = self.ops
        for op in ops:
            best = {}
            for d in op.deps:
                dop = ops[d]
                if dop.is_dma or dop.fn is None:
                    continue
                if dop.eng == op.eng and not op.is_dma:
                    if dop.eng == "pe" or not SAME_ENGINE_SYNC:
                        continue
                if d > best.get(dop.eng, -1):
                    best[dop.eng] = d
            for d in best.values():
                ops[d].need_inc = True
        cnt = {e: 0 for e in self.ENGS}
        dma_cnt = [0] * N_DMA_SEMS
        nd = 0
        for op in ops:
            if op.is_dma:
                s = nd % N_DMA_SEMS
                nd += 1
                dma_cnt[s] += 1
                op.dsem = s
                op.dval = 16 * dma_cnt[s]
            elif op.need_inc:
                cnt[op.eng] += 1
                op.cnt = cnt[op.eng]
        per_eng = {e: [] for e in self.ENGS}
        for op in ops:
            per_eng[op.eng].append(op)
        self.stats = {e: len(per_eng[e]) for e in self.ENGS}
        self.stats["incs"] = dict(cnt)

        import contextlib
        with contextlib.ExitStack() as st:
            esem = {e: st.enter_context(nc.semaphore("s_" + e)) for e in self.ENGS}
            dsem = [st.enter_context(nc.semaphore("d%d" % i)) for i in range(N_DMA_SEMS)]
            block = st.enter_context(nc.Block())

            def run(eng_name, e):
                waited_e = {x: 0 for x in self.ENGS}
                waited_d = [0] * N_DMA_SEMS
                cur = None
                for op in per_eng[eng_name]:
                    if self.scopes and op.phase != cur:
                        if cur is not None:
                            nc.pop_named_scope(cur)
                        cur = op.phase
                        if cur is not None:
                            nc.push_named_scope(cur)
                    need_e = {}
                    need_d = {}
                    for d in op.deps:
                        dop = ops[d]
                        if dop.is_dma:
                            if dop.dval > waited_d[dop.dsem]:
                                need_d[dop.dsem] = max(need_d.get(dop.dsem, 0), dop.dval)
                        else:
                            if dop.eng == eng_name and not op.is_dma:
                                if eng_name == "pe" or not SAME_ENGINE_SYNC:
                                    continue
                            if dop.cnt > waited_e[dop.eng]:
                                need_e[dop.eng] = max(need_e.get(dop.eng, 0), dop.cnt)
                    if op.is_dma:
                        prev = op.dval - 16
                        if prev > waited_d[op.dsem]:
                            need_d[op.dsem] = max(need_d.get(op.dsem, 0), prev)
                    for x, v in need_e.items():
                        e.wait_ge(esem[x], v)
                        waited_e[x] = v
                    for s, v in need_d.items():
                        e.wait_ge(dsem[s], v)
                        waited_d[s] = v
                    if op.fn is None:
                        continue
                    ins = op.fn(e)
                    if op.is_dma:
                        ins.then_inc(dsem[op.dsem], 16)
                    elif op.need_inc:
                        ins.then_inc(esem[eng_name], 1)
                if self.scopes and cur is not None:
                    nc.pop_named_scope(cur)
                last = {}
                for op in per_eng[eng_name]:
                    if op.is_dma:
                        last[op.dsem] = max(last.get(op.dsem, 0), op.dval)
                for s, v in last.items():
                    if v > waited_d[s]:
                        e.wait_ge(dsem[s], v)

            @block.sync
            def _(e):
                run("sp", e)

            @block.scalar
            def _(e):
                run("act", e)

            @block.vector
            def _(e):
                run("dve", e)

            @block.gpsimd
            def _(e):
                run("pool", e)

            @block.tensor
            def _(e):
                run("pe", e)


D = 1024
S = 4096
CL = 256
L = 2
NCORES = 8
NBC = 2
NH = 8
NOPE = 64
ROPE = 32
DV = 64
QR = 256
KVR = 128
NIN = 1440
C_KV, C_KR, C_Q, C_POOL, C_HY = 0, 128, 160, 416, 672
DFF = 4096
EPS = 1e-6
SCALE = float((NOPE + ROPE) ** -0.5)
POOL_WINDOWS = (2, 4, 8, 16)
HY_EMB = 33
HY_FFN = 64
MAGIC = 12582912.0
BFNP = ml_dtypes.bfloat16


def _rope_perm():
    perm = np.zeros(32, np.int64)
    for a in range(2):
        for hf in range(2):
            for f in range(8):
                perm[a * 16 + hf * 8 + f] = a * 16 + (1 - hf) * 8 + f
    return perm


def _rope_tables():
    n = S
    rows = n // 64
    r = np.repeat(np.arange(rows), 64).astype(np.float32)
    cidx = np.tile(np.arange(64), rows).astype(np.float32)
    inv = np.power(np.float32(10000.0), -(np.arange(8, dtype=np.float32) / np.float32(8))).astype(np.float32)
    ang = np.stack([r[:, None] * inv, cidx[:, None] * inv], axis=1).astype(np.float32)
    cos = np.cos(ang).astype(np.float32)
    sin = np.sin(ang).astype(np.float32)
    cosT = np.zeros((32, n), np.float32)
    sinT = np.zeros((32, n), np.float32)
    for a in range(2):
        for hf in range(2):
            for f in range(8):
                rr = a * 16 + hf * 8 + f
                cosT[rr] = cos[:, a, f]
                sinT[rr] = (-sin[:, a, f]) if hf == 0 else sin[:, a, f]
    return cosT, sinT


def _pool_bands(n, W):
    blocks = []
    index = {}
    table = {}
    nt = n // W
    for g, w in enumerate(POOL_WINDOWS):
        t = np.arange(n)
        lo = np.clip(t - w // 2, 0, n)
        hi = np.clip(t + w // 2, 0, n)
        for i in range(nt):
            T0 = i * W
            m_lo = max((T0 - w // 2) // 128, 0)
            m_hi = min((T0 + W + w // 2 - 1) // 128, n // 128 - 1)
            for m in range(m_lo, m_hi + 1):
                blk = np.zeros((128, 512), np.float32)
                for tt in range(T0, T0 + W):
                    s0 = max(lo[tt], m * 128)
                    s1 = min(hi[tt], (m + 1) * 128)
                    if s1 > s0:
                        blk[s0 - m * 128:s1 - m * 128, tt - T0] += 1.0 / float(hi[tt] - lo[tt])
                    if m * 128 <= tt < (m + 1) * 128:
                        blk[tt - m * 128, tt - T0] -= 1.0
                if not blk.any():
                    continue
                key = blk.tobytes()
                if key not in index:
                    index[key] = len(blocks)
                    blocks.append(blk)
                table[(g, i, m)] = index[key]
    return np.stack(blocks).astype(BFNP), table


def _dft_blocks(n):
    nt = n // 128
    t = np.arange(n, dtype=np.int64)
    m = (t[:, None] * t[None, :]) % (2 * n)
    ang = m.astype(np.float64) * (2.0 * np.pi / (2 * n))
    Cm = np.cos(ang)
    Sm = -np.sin(ang)

    def tile(M):
        M4 = M.reshape(nt, 128, nt, 128)
        return np.ascontiguousarray(M4.transpose(2, 1, 0, 3)).astype(BFNP)
    return tile(Cm), tile(Sm)


def _hy_consts(n):
    f32 = np.float32
    t = np.linspace(0.0, 1.0, n, dtype=f32)[:, None]
    bands = (HY_EMB - 1) // 2
    freqs = np.linspace(1e-4, bands - 1, bands, dtype=f32)[None, :]
    wpos = (f32(2.0 * math.pi) * np.arange(n, dtype=f32)[:, None] / f32(n)).astype(f32)
    z = np.concatenate([t, np.cos(freqs * wpos), -np.sin(freqs * wpos)], axis=-1).astype(f32)
    deltas = np.abs(np.linspace(math.log(1e-2) / 1.5, math.log(1e-2) / 0.3, 256, dtype=f32))
    dec = np.exp(-t * deltas[None, :]).astype(f32)
    pm1 = np.where(np.arange(n) % 2 == 0, 1.0, -1.0).astype(f32)
    return np.ascontiguousarray(z.T), dec, pm1


_CONST_CACHE = {}


def _host_consts():
    if _CONST_CACHE:
        return _CONST_CACHE
    c = _CONST_CACHE
    c["ident_f"] = np.eye(128, dtype=np.float32)
    cosT, sinT = _rope_tables()
    c["rope_cos"] = cosT
    c["rope_sin"] = sinT
    c["rope_cos_q"] = (cosT * np.float32(SCALE)).astype(np.float32)
    c["rope_sin_q"] = (sinT * np.float32(SCALE)).astype(np.float32)
    bl, tl = _pool_bands(S, 512)
    bc, tc = _pool_bands(CL, 256)
    c["band_lat"] = bl
    c["band_ctx"] = bc
    c["_band_tab_lat"] = tl
    c["_band_tab_ctx"] = tc
    for n, nm in ((S, "lat"), (CL, "ctx")):
        Cb, Sb = _dft_blocks(n)
        c["dftc_" + nm] = Cb
        c["dfts_" + nm] = Sb
        zT, dec, pm1 = _hy_consts(n)
        c["hyz_" + nm] = zT
        c["hydec_" + nm] = dec
        c["pm1_" + nm] = pm1
    return c


class Arena:
    def __init__(self, nc, nbytes):
        self.t = nc.alloc_sbuf_tensor("arena", [128, nbytes // 2], BF16).ap()
        self.cap = nbytes
        self.off = 0
        self.cnt = 0

    def reset(self):
        self.off = 0

    def alloc(self, free_shape, dtype, name="t"):
        esz = 4 if dtype == F32 else 2
        n = 1
        for d_ in free_shape:
            n *= d_
        nb = n * esz
        off = (self.off + 63) // 64 * 64
        assert off + nb <= self.cap, "arena overflow %s %d+%d>%d" % (name, off, nb, self.cap)
        self.off = off + nb
        ap = self.t[:, off // 2:(off + nb) // 2]
        if dtype == F32:
            ap = ap.bitcast(F32)
        if len(free_shape) == 2:
            ap = ap.rearrange("p (a b) -> p a b", b=free_shape[1])
        elif len(free_shape) == 3:
            ap = ap.rearrange("p (a b c) -> p a b c", b=free_shape[1], c=free_shape[2])
        self.cnt += 1
        return ap, "%s#%d" % (name, self.cnt)


class Seq:
    def __init__(self, name, n, v, kind, b):
        self.name = name
        self.n = n
        self.v = v
        self.kind = kind
        self.b = b
        self.W = 512 if n >= 512 else n
        self.nt = n // self.W


class Prog:
    def __init__(self, nc, dbg=()):
        self.nc = nc
        self.kb = KB(nc)
        self.dbg = set(dbg)
        self.dram = {}
        self.consts = _host_consts()

    def din(self, name, shape, dtype=F32):
        t = self.nc.dram_tensor(name, list(shape), dtype, kind="ExternalInput").ap()
        self.dram[name] = t
        return t

    def dscr(self, name, shape, dtype=F32):
        kind = "ExternalOutput" if name in self.dbg else "Internal"
        t = self.nc.dram_tensor(name, list(shape), dtype, kind=kind).ap()
        self.dram[name] = t
        return t

    def declare(self):
        nc = self.nc
        c = self.consts
        self.x_in = self.din("x", [NBC, S, D])
        self.ctx_in = self.din("ctx", [NBC, CL, D])
        self.cv_in = self.din("cv", [3, D])
        self.y_out = nc.dram_tensor("y", [NBC, S, D], F32, kind="ExternalOutput").ap()
        w = {}
        w["w_mod"] = self.din("w_mod", [L, D, 6 * D])
        w["b_mod"] = self.din("b_mod", [L, 6 * D])
        w["g_mix"] = self.din("g_mix", [L, D])
        w["g_mlp"] = self.din("g_mlp", [L, D])
        w["w_in"] = self.din("w_in", [L, D, NIN])
        w["w_in_rot"] = self.din("w_in_rot", [L, D, ROPE])
        w["g_q"] = self.din("g_q", [L, QR])
        w["w_q_up"] = self.din("w_q_up", [L, QR, NH * 96])
        w["w_q_rot"] = self.din("w_q_rot", [L, QR, NH * 96])
        w["g_kv"] = self.din("g_kv", [L, KVR])
        w["w_kv_up"] = self.din("w_kv_up", [L, KVR, NH * 128])
        w["pool_w"] = self.din("pool_w", [L, 4, 64, 64])
        w["pool_scale"] = self.din("pool_scale", [L, 256])
        w["hy_conv_w"] = self.din("hy_conv_w", [L, 3, 768])
        w["hy_conv_b"] = self.din("hy_conv_b", [L, 768])
        w["hy_f_w1"] = self.din("hy_f_w1", [L, HY_EMB, HY_FFN])
        w["hy_f_b1"] = self.din("hy_f_b1", [L, HY_FFN])
        w["hy_f_freq1"] = self.din("hy_f_freq1", [L, HY_FFN])
        w["hy_f_w2"] = self.din("hy_f_w2", [L, HY_FFN, HY_FFN])
        w["hy_f_b2"] = self.din("hy_f_b2", [L, HY_FFN])
        w["hy_f_freq2"] = self.din("hy_f_freq2", [L, HY_FFN])
        w["hy_f_w3"] = self.din("hy_f_w3", [L, HY_FFN, 1024])
        w["hy_bias"] = self.din("hy_bias", [L, 2, 256])
        w["g_out"] = self.din("g_out", [L, D])
        w["w_out"] = self.din("w_out", [L, D, D])
        w["w_mlp1"] = self.din("w_mlp1", [L, D, DFF])
        w["w_mlp2"] = self.din("w_mlp2", [L, DFF, D])
        w["g_final"] = self.din("g_final", [D])
        self.w = w
        k = {}
        for name, arr in c.items():
            if name.startswith("_"):
                continue
            dt_ = BF16 if arr.dtype == BFNP else F32
            k[name] = self.din(name, arr.shape, dt_)
        self.k = k
        self.lat = [Seq("b%d" % b, S, b, "lat", b) for b in range(NBC)]
        self.ctx = [Seq("c%d" % b, CL, 2, "ctx", b) for b in range(NBC)]
        for s in self.lat + self.ctx:
            n = s.n
            s.XT = self.dscr("XT_" + s.name, [8, 128, n])
            s.PKV = self.dscr("PKV_" + s.name, [128, n], BF16)
            s.KR = self.dscr("KR_" + s.name, [32, n], BF16)
            s.PQ = self.dscr("PQ_" + s.name, [2, 128, n], BF16)
            s.POOLU = self.dscr("POOLU_" + s.name, [n, 256], BF16)
            s.HYP = self.dscr("HYP_" + s.name, [6, 128, n + 2])
            s.ATT = self.dscr("ATT_" + s.name, [n, 512])
            s.POOLO = self.dscr("POOLO_" + s.name, [n, 256])
        for nm, n in (("lat", S), ("ctx", CL)):
            setattr(self, "HV_" + nm, self.dscr("HV_" + nm, [n, 512], BF16))
            setattr(self, "HX1_" + nm, self.dscr("HX1_" + nm, [n, 512], BF16))
            setattr(self, "HX2_" + nm, self.dscr("HX2_" + nm, [n, 512], BF16))
            setattr(self, "HYO_" + nm, self.dscr("HYO_" + nm, [n, 512]))
            setattr(self, "HS_" + nm, self.dscr("HS_" + nm, [2, n // 128, 128, 512], BF16))
            setattr(self, "HNQ_" + nm, self.dscr("HNQ_" + nm, [1, 512]))
        A = lambda name, shape, dt_: nc.alloc_sbuf_tensor(name, shape, dt_).ap()
        self.ident_f = A("ident_f_sb", [128, 128], F32)
        self.ident_b = A("ident_b", [128, 128], BF16)
        self.ones_b = A("ones_b", [128, 128], BF16)
        self.ones_f = A("ones_f", [128, 128], F32)
        self.eps = A("eps_t", [128, 1], F32)
        self.modT = A("modT", [128, L * 6 * 8 * 3], F32).rearrange("p (l k j v) -> p l k j v", l=L, k=6, j=8)
        self.ps2 = [nc.alloc_psum_tensor("ps2_%d" % i, [128, 1024], F32).ap() for i in range(4)]
        self.arena = Arena(nc, 205 * 1024)
        kb = self.kb
        kb.dma(self.ident_f, self.k["ident_f"], writes=["ident_f"])
        kb.act(lambda e: e.activation(out=self.ident_b, in_=self.ident_f, func=AF.Copy), reads=["ident_f"], writes=["ident_b"])
        kb.dve(lambda e: e.memset(self.ones_b, 1.0), writes=["ones_b"])
        kb.dve(lambda e: e.memset(self.ones_f, 1.0), writes=["ones_f"])
        kb.dve(lambda e: e.memset(self.eps, EPS), writes=["eps"])

    def bank(self, kidx):
        return self.ps2[kidx // 2][:, (kidx % 2) * 512:(kidx % 2 + 1) * 512], "pb%d" % kidx

    def mod(self, l, kind, v):
        return self.modT[:, l, kind, :, v]

    def phase_end(self):
        self.kb.barrier()
        self.arena.reset()

    def phase_mod(self):
        kb, ar, w = self.kb, self.arena, self.w
        cvs, kcvs = ar.alloc([D], F32, "cvs")
        sil, ksil = ar.alloc([D], F32, "sil")
        silT, ksilT = ar.alloc([8, 3], F32, "silT")
        kb.dma(cvs[0:3, :], self.cv_in, writes=[kcvs])
        kb.act(lambda e: e.activation(out=sil[0:3, :], in_=cvs[0:3, :], func=AF.Silu), reads=[kcvs], writes=[ksil])
        pb, kpb = self.bank(0)
        for j in range(8):
            kb.pe(lambda e, j=j: e.transpose(pb[:, j * 3:(j + 1) * 3], sil[0:3, j * 128:(j + 1) * 128], self.ident_f[0:3, 0:3]),
                  reads=[ksil, "ident_f"], writes=[kpb])
        kb.dve(lambda e: e.tensor_copy(out=silT.rearrange("p j v -> p (j v)"), in_=pb[:, 0:24]), reads=[kpb], writes=[ksilT])
        wbuf = [ar.alloc([8, 512], F32, "wmod") for _ in range(2)]
        modrow, kmodrow = ar.alloc([6 * D], F32, "modrow")
        brow, kbrow = ar.alloc([6 * D], F32, "brow")
        gbc, kgbc = ar.alloc([2, D], F32, "gbc")
        for l in range(L):
            kb.dma(brow[0:3, :], w["b_mod"][l].partition_broadcast(3), writes=[kbrow])
            kb.dma(gbc[0:3, 0, :], w["g_mix"][l].partition_broadcast(3), writes=[kgbc])
            kb.dma(gbc[0:3, 1, :], w["g_mlp"][l].partition_broadcast(3), writes=[kgbc])
            for ncn in range(12):
                wb, kwb = wbuf[ncn % 2]
                kb.dma(wb, w["w_mod"][l][:, ncn * 512:(ncn + 1) * 512].rearrange("(j p) n -> p j n", p=128), writes=[kwb])
                pm, kpm = self.bank(1 + ncn % 2)
                for j in range(8):
                    kb.pe(lambda e, j=j, wb=wb, pm=pm: e.matmul(pm[0:3, :], silT[:, j, :], wb[:, j, :], start=(j == 0), stop=(j == 7)),
                          reads=[ksilT, kwb], writes=[kpm])
                kb.dve(lambda e, pm=pm, ncn=ncn: e.tensor_tensor(out=modrow[0:3, ncn * 512:(ncn + 1) * 512], in0=pm[0:3, :],
                                                                in1=brow[0:3, ncn * 512:(ncn + 1) * 512], op=ALU.add),
                       reads=[kpm, kbrow], writes=[kmodrow])
            for (kind, gi) in ((1, 0), (4, 1)):
                sl = modrow[0:3, kind * D:(kind + 1) * D]
                kb.dve(lambda e, sl=sl, gi=gi: e.scalar_tensor_tensor(out=sl, in0=sl, scalar=1.0, in1=gbc[0:3, gi, :], op0=ALU.add, op1=ALU.mult),
                       reads=[kmodrow, kgbc], writes=[kmodrow])
            pt, kpt = self.bank(3)
            for kind in range(6):
                for j in range(8):
                    col = (kind * 8 + j) * 3
                    kb.pe(lambda e, kind=kind, j=j, col=col: e.transpose(pt[:, col:col + 3], modrow[0:3, kind * D + j * 128: kind * D + (j + 1) * 128],
                                                                        self.ident_f[0:3, 0:3]),
                          reads=[kmodrow, "ident_f"], writes=[kpt])
            kb.dve(lambda e, l=l: e.tensor_copy(out=self.modT[:, l].rearrange("p k j v -> p (k j v)"), in_=pt[:, 0:144]),
                   reads=[kpt], writes=["modT"])
        self.phase_end()

    def nb(self):
        self._nb = (getattr(self, "_nb", -1) + 1) % 8
        return self.bank(self._nb)

    def evac(self, i, fn_act, fn_dve, reads, writes):
        if i % 2 == 0:
            self.kb.act(fn_act, reads=reads, writes=writes)
        else:
            self.kb.dve(fn_dve, reads=reads, writes=writes)

    def phase_t0(self):
        kb, ar = self.kb, self.arena
        for s in self.lat + self.ctx:
            src = self.x_in[s.b] if s.kind == "lat" else self.ctx_in[s.b]
            W, nsub = s.W, s.W // 128
            xin = [ar.alloc([nsub, D], F32, "xin") for _ in range(2)]
            xt = [ar.alloc([8, W], F32, "xt") for _ in range(2)]
            for i in range(s.nt):
                xi, kxi = xin[i % 2]
                xo, kxo = xt[i % 2]
                kb.dma(xi, src[i * W:(i + 1) * W, :].rearrange("(a p) d -> p a d", p=128), writes=[kxi])
                for j in range(8):
                    pb, kpb = self.nb()
                    for sub in range(nsub):
                        kb.pe(lambda e, pb=pb, xi=xi, j=j, sub=sub: e.transpose(pb[:, sub * 128:(sub + 1) * 128], xi[:, sub, j * 128:(j + 1) * 128], self.ident_f),
                              reads=[kxi, "ident_f"], writes=[kpb])
                    self.evac(j, lambda e, pb=pb, xo=xo, j=j, W=W: e.activation(out=xo[:, j, :], in_=pb[:, 0:W], func=AF.Copy),
                              lambda e, pb=pb, xo=xo, j=j, W=W: e.tensor_copy(out=xo[:, j, :], in_=pb[:, 0:W]),
                              reads=[kpb], writes=[kxo + ".%d" % j])
                kb.dma(s.XT[:, :, i * W:(i + 1) * W].rearrange("j p t -> p j t"), xo,
                       reads=[kxo + ".%d" % j for j in range(8)], writes=["XT_" + s.name])
            self.phase_end()

    def fm_rstd(self, chunks, nfeat, W, sq, ksq, rs, krs, extra_sq=None):
        kb = self.kb
        pss, kpss = self.nb()
        n = len(chunks)
        for ci, (ap, key, P) in enumerate(chunks):
            kb.act(lambda e, ap=ap, ci=ci, P=P: e.activation(out=sq[0:P, ci, 0:W], in_=ap, func=AF.Square),
                   reads=[key], writes=[ksq + ".%d" % ci] + (extra_sq(ci) if extra_sq else []))
            kb.pe(lambda e, ci=ci, P=P: e.matmul(pss[:, 0:W], self.ones_b[0:P, :], sq[0:P, ci, 0:W], start=(ci == 0), stop=(ci == n - 1)),
                  reads=[ksq + ".%d" % ci, "ones_b"], writes=[kpss])
        kb.act(lambda e: e.activation(out=rs[:, 0:W], in_=pss[:, 0:W], func=AF.Sqrt, bias=self.eps[:, 0:1], scale=1.0 / nfeat),
               reads=[kpss, "eps"], writes=[krs])
        kb.dve(lambda e: e.reciprocal(out=rs[:, 0:W], in_=rs[:, 0:W]), reads=[krs], writes=[krs])

    def mod_norm(self, xt, kxt, W, gm, sh, sq, ksq, rs, krs, tmp, ktmp, hT, khT, extra=None, extra_sq=None):
        kb = self.kb
        self.fm_rstd([(xt[:, j, 0:W], kxt + ".%d" % j, 128) for j in range(8)], D, W, sq, ksq, rs, krs, extra_sq=extra_sq)
        for j in range(8):
            kb.dve(lambda e, j=j: e.scalar_tensor_tensor(out=tmp[:, j, 0:W], in0=xt[:, j, 0:W], scalar=gm[:, j:j + 1], in1=rs[:, 0:W],
                                                         op0=ALU.mult, op1=ALU.mult),
                   reads=[kxt + ".%d" % j, krs, "modT"], writes=[ktmp + ".%d" % j] + (extra(j) if extra else []))
            kb.act(lambda e, j=j: e.activation(out=hT[:, j, 0:W], in_=tmp[:, j, 0:W], func=AF.Identity, bias=sh[:, j:j + 1], scale=1.0),
                   reads=[ktmp + ".%d" % j, "modT"], writes=[khT + ".%d" % j])

    def load_cast_rows(self, dst, src2d, nj, split=1):
        ap, key = dst
        n = src2d.shape[-1]
        step = n // split
        for j in range(nj):
            for sp_ in range(split):
                self.kb.dma(ap[:, j, sp_ * step:(sp_ + 1) * step], src2d[j * 128:(j + 1) * 128, sp_ * step:(sp_ + 1) * step],
                            writes=[key + ".%d" % j], q="pool")

    def phase_a(self, l):
        kb, ar, w = self.kb, self.arena, self.w
        last = (l == L - 1)
        win = ar.alloc([8, NIN], BF16, "win")
        wrot = ar.alloc([8, ROPE], BF16, "wrot")
        self.load_cast_rows(win, w["w_in"][l], 8)
        self.load_cast_rows(wrot, w["w_in_rot"][l], 8)
        win_ap, kwin = win
        wrot_ap, kwrot = wrot
        kwin_all = [kwin + ".%d" % j for j in range(8)]
        kwrot_all = [kwrot + ".%d" % j for j in range(8)]
        gkv, kgkv = ar.alloc([1], F32, "gkv")
        gq, kgq = ar.alloc([2], F32, "gq")
        kb.dma(gkv, w["g_kv"][l].rearrange("(p o) -> p o", o=1), writes=[kgkv])
        for c_ in range(2):
            kb.dma(gq[:, c_:c_ + 1], w["g_q"][l][c_ * 128:(c_ + 1) * 128].rearrange("(p o) -> p o", o=1), writes=[kgq])
        zt, kzt = ar.alloc([6, 1], F32, "zt")
        kb.dve(lambda e: e.memset(zt, 0.0), writes=[kzt])
        WM = 512
        xts = [ar.alloc([8, WM], F32, "xt") for _ in range(2)]
        sq, ksq = ar.alloc([8, WM], BF16, "sq")
        rs, krs = ar.alloc([WM], F32, "rs")
        tmp, ktmp = ar.alloc([8, WM], F32, "tmp")
        hTs = [ar.alloc([8, WM], BF16, "hT") for _ in range(2)]
        sq2, ksq2 = ar.alloc([2, WM], BF16, "sq2")
        rs2, krs2 = ar.alloc([WM], F32, "rs2")
        pkvn = [ar.alloc([WM], BF16, "pkvn") for _ in range(2)]
        krs_ = [ar.alloc([WM], BF16, "kr") for _ in range(2)]
        pqn = [ar.alloc([2, WM], BF16, "pqn") for _ in range(2)]
        poolu = [ar.alloc([4, 256], BF16, "poolu") for _ in range(2)]
        hyp = [ar.alloc([6, WM], F32, "hyp") for _ in range(2)]
        ropec = [ar.alloc([WM], F32, "ropec") for _ in range(2)]
        ropes = [ar.alloc([WM], F32, "ropes") for _ in range(2)]
        rt1, krt1 = ar.alloc([WM], F32, "rt1")
        rt2, krt2 = ar.alloc([WM], F32, "rt2")
        tiles = []
        for s in self.lat + self.ctx:
            full = not (last and s.kind == "ctx")
            if full:
                for (a0, a1) in ((0, 1), (s.n + 1, s.n + 2)):
                    kb.dma(s.HYP[:, :, a0:a1].rearrange("c p o -> p c o"), zt, reads=[kzt], writes=["HYP_" + s.name], allow_slow_non_contiguous=True)
            for i in range(s.nt):
                tiles.append((s, i, full))

        def front(it):
            s, i, full = tiles[it]
            W = s.W
            gm, sh = self.mod(l, 1, s.v), self.mod(l, 0, s.v)
            xt, kxt = xts[it % 2]
            hT, khT = hTs[it % 2]
            t0 = i * W
            kb.dma(xt[:, :, 0:W], s.XT[:, :, t0:t0 + W].rearrange("j p t -> p j t"), reads=["XT_" + s.name],
                   writes=[kxt + ".%d" % j for j in range(8)])
            self.mod_norm(xt, kxt, W, gm, sh, sq, ksq, rs, krs, tmp, ktmp, hT, khT)

        def back(it):
            s, i, full = tiles[it]
            W, nsub = s.W, s.W // 128
            hT, khT = hTs[it % 2]
            t0 = i * W
            if True:
                khT_all = [khT + ".%d" % j for j in range(8)]

                def proj_fm(col0, ncol, dstps, wt=win_ap, kw=kwin_all, hT=hT, khT_all=khT_all, W=W):
                    for j in range(8):
                        kb.pe(lambda e, j=j: e.matmul(dstps[0:ncol, 0:W], wt[:, j, col0:col0 + ncol], hT[:, j, 0:W], start=(j == 0), stop=(j == 7)),
                              reads=[kw[j], khT_all[j]], writes=[dstps_key[0]])
                pkv, kpkv = self.nb()
                dstps_key = [kpkv]
                proj_fm(C_KV, 128, pkv)
                self.fm_rstd([(pkv[:, 0:W], kpkv, 128)], KVR, W, sq2, ksq2, rs2, krs2)
                o_, ko_ = pkvn[it % 2]
                kb.dve(lambda e, o_=o_, pkv=pkv, W=W: e.scalar_tensor_tensor(out=o_[:, 0:W], in0=pkv[:, 0:W], scalar=gkv[:, 0:1], in1=rs2[:, 0:W],
                                                                            op0=ALU.mult, op1=ALU.mult),
                       reads=[kpkv, krs2, kgkv], writes=[ko_])
                kb.dma(s.PKV[:, t0:t0 + W], o_[:, 0:W], reads=[ko_], writes=["PKV_" + s.name])
                pka, kpka = self.nb()
                dstps_key = [kpka]
                proj_fm(C_KR, ROPE, pka)
                o_, ko_ = krs_[it % 2]
                if s.kind == "lat":
                    pkb, kpkb = self.nb()
                    dstps_key = [kpkb]
                    proj_fm(0, ROPE, pkb, wt=wrot_ap, kw=kwrot_all)
                    rc, krc = ropec[it % 2]
                    rsn, krsn = ropes[it % 2]
                    kb.dma(rc[0:32, 0:W], self.k["rope_cos"][:, t0:t0 + W], writes=[krc])
                    kb.dma(rsn[0:32, 0:W], self.k["rope_sin"][:, t0:t0 + W], writes=[krsn])
                    kb.dve(lambda e, pka=pka, rc=rc, W=W: e.tensor_tensor(out=rt1[0:32, 0:W], in0=pka[0:32, 0:W], in1=rc[0:32, 0:W], op=ALU.mult),
                           reads=[kpka, krc], writes=[krt1])
                    kb.dve(lambda e, pkb=pkb, rsn=rsn, W=W: e.tensor_tensor(out=rt2[0:32, 0:W], in0=pkb[0:32, 0:W], in1=rsn[0:32, 0:W], op=ALU.mult),
                           reads=[kpkb, krsn], writes=[krt2])
                    kb.dve(lambda e, o_=o_, W=W: e.tensor_tensor(out=o_[0:32, 0:W], in0=rt1[0:32, 0:W], in1=rt2[0:32, 0:W], op=ALU.add),
                           reads=[krt1, krt2], writes=[ko_])
                else:
                    kb.act(lambda e, o_=o_, pka=pka, W=W: e.activation(out=o_[0:32, 0:W], in_=pka[0:32, 0:W], func=AF.Copy),
                           reads=[kpka], writes=[ko_])
                kb.dma(s.KR[:, t0:t0 + W], o_[0:32, 0:W], reads=[ko_], writes=["KR_" + s.name])
                if not full:
                    return
                pq = [self.nb() for _ in range(2)]
                for c_ in range(2):
                    dstps_key = [pq[c_][1]]
                    proj_fm(C_Q + c_ * 128, 128, pq[c_][0])
                self.fm_rstd([(pq[c_][0][:, 0:W], pq[c_][1], 128) for c_ in range(2)], QR, W, sq2, ksq2, rs2, krs2)
                o_, ko_ = pqn[it % 2]
                for c_ in range(2):
                    kb.dve(lambda e, o_=o_, c_=c_, pq=pq, W=W: e.scalar_tensor_tensor(out=o_[:, c_, 0:W], in0=pq[c_][0][:, 0:W], scalar=gq[:, c_:c_ + 1],
                                                                                  in1=rs2[:, 0:W], op0=ALU.mult, op1=ALU.mult),
                           reads=[pq[c_][1], krs2, kgq], writes=[ko_ + ".%d" % c_])
                kb.dma(s.PQ[:, :, t0:t0 + W].rearrange("c p t -> p c t"), o_[:, :, 0:W], reads=[ko_ + ".0", ko_ + ".1"], writes=["PQ_" + s.name])
                o_, ko_ = poolu[it % 2]
                for sub in range(nsub):
                    pp, kpp = self.nb()
                    for j in range(8):
                        kb.pe(lambda e, j=j, sub=sub, pp=pp, hT=hT: e.matmul(pp[:, 0:256], hT[:, j, sub * 128:(sub + 1) * 128], win_ap[:, j, C_POOL:C_POOL + 256],
                                                                            start=(j == 0), stop=(j == 7)),
                              reads=[kwin_all[j], khT_all[j]], writes=[kpp])
                    self.evac(sub, lambda e, o_=o_, pp=pp, sub=sub: e.activation(out=o_[:, sub, :], in_=pp[:, 0:256], func=AF.Copy),
                              lambda e, o_=o_, pp=pp, sub=sub: e.tensor_copy(out=o_[:, sub, :], in_=pp[:, 0:256]),
                              reads=[kpp], writes=[ko_ + ".%d" % sub])
                kb.dma(s.POOLU[t0:t0 + W, :].rearrange("(a p) c -> p a c", p=128), o_[:, 0:nsub, :],
                       reads=[ko_ + ".%d" % sub for sub in range(nsub)], writes=["POOLU_" + s.name])
                o_, ko_ = hyp[it % 2]
                for c6 in range(6):
                    ph, kph = self.nb()
                    dstps_key = [kph]
                    proj_fm(C_HY + c6 * 128, 128, ph)
                    self.evac(c6, lambda e, o_=o_, ph=ph, c6=c6, W=W: e.activation(out=o_[:, c6, 0:W], in_=ph[:, 0:W], func=AF.Copy),
                              lambda e, o_=o_, ph=ph, c6=c6, W=W: e.tensor_copy(out=o_[:, c6, 0:W], in_=ph[:, 0:W]),
                              reads=[kph], writes=[ko_ + ".%d" % c6])
                kb.dma(s.HYP[:, :, 1 + t0:1 + t0 + W].rearrange("c p t -> p c t"), o_[:, :, 0:W],
                       reads=[ko_ + ".%d" % c6 for c6 in range(6)], writes=["HYP_" + s.name])

        front(0)
        for it in range(len(tiles)):
            if it + 1 < len(tiles):
                front(it + 1)
            back(it)
        self.phase_end()


def _shared_inputs(inp):
    f = lambda a: np.ascontiguousarray(np.asarray(a, dtype=np.float32))
    sh = {}
    for k_ in ("w_mod", "b_mod", "g_mix", "g_mlp", "w_in", "g_q", "g_kv", "w_kv_up", "pool_w", "pool_scale", "hy_conv_w",
               "hy_conv_b", "hy_f_w1", "hy_f_b1", "hy_f_freq1", "hy_f_w2", "hy_f_b2", "hy_f_freq2", "hy_f_w3", "hy_bias",
               "g_out", "w_out", "w_mlp1", "w_mlp2", "g_final"):
        sh[k_] = f(inp[k_])
    perm = _rope_perm()
    w_in = sh["w_in"]
    sh["w_in_rot"] = np.ascontiguousarray(w_in[:, :, C_KR:C_KR + ROPE][:, :, perm])
    wq = f(inp["w_q_up"]).reshape(L, QR, NH, 96)
    sh["w_q_up"] = np.ascontiguousarray(wq.reshape(L, QR, NH * 96))
    wrot = np.zeros_like(wq)
    wrot[..., NOPE:] = wq[..., NOPE:][..., perm]
    sh["w_q_rot"] = np.ascontiguousarray(wrot.reshape(L, QR, NH * 96))
    for name, arr in _host_consts().items():
        if not name.startswith("_"):
            sh[name] = arr
    return sh


def _core_inputs(inp, core, shared):
    b0 = core * NBC
    m = dict(shared)
    m["x"] = np.ascontiguousarray(np.asarray(inp["x"][b0:b0 + NBC], dtype=np.float32))
    m["ctx"] = np.ascontiguousarray(np.asarray(inp["ctx"][b0:b0 + NBC], dtype=np.float32))
    cv = np.concatenate([np.asarray(inp["c"][b0:b0 + NBC], dtype=np.float32), np.asarray(inp["c_ctx"], dtype=np.float32)[None, :]], axis=0)
    m["cv"] = np.ascontiguousarray(cv)
    return m


def _attn_phase(self, l):
    kb, ar, w = self.kb, self.arena, self.w
    last = (l == L - 1)
    NK = CL + S
    NKT = NK // 128
    wkv = ar.alloc([1, NH * 128], BF16, "wkv")
    self.load_cast_rows(wkv, w["w_kv_up"][l], 1)
    wq = ar.alloc([2, NH * 96], BF16, "wq")
    wqr = ar.alloc([2, NH * 96], BF16, "wqr")
    self.load_cast_rows(wq, w["w_q_up"][l], 2)
    self.load_cast_rows(wqr, w["w_q_rot"][l], 2)
    wkv_ap, kwkv = wkv[0], wkv[1] + ".0"
    wq_ap, wqr_ap = wq[0], wqr[0]
    kwq = [wq[1] + ".0", wq[1] + ".1"]
    kwqr = [wqr[1] + ".0", wqr[1] + ".1"]
    cosq, kcosq = ar.alloc([S], F32, "cosq")
    sinq, ksinq = ar.alloc([S], F32, "sinq")
    kb.dma(cosq[64:96, :], self.k["rope_cos_q"], writes=[kcosq])
    kb.dma(sinq[64:96, :], self.k["rope_sin_q"], writes=[ksinq])
    pkv, kpkv = ar.alloc([NK], BF16, "pkv")
    pq, kpq = ar.alloc([2, S], BF16, "pq")
    pqc, kpqc = ar.alloc([2, CL], BF16, "pqc")
    KT = [ar.alloc([NK], BF16, "KT") for _ in range(2)]
    VA = [ar.alloc([NKT, 128], BF16, "VA") for _ in range(2)]
    QT = [ar.alloc([S], BF16, "QT") for _ in range(2)]
    QTc = [ar.alloc([CL], BF16, "QTc") for _ in range(2)]
    PT = [ar.alloc([1024], BF16, "PT") for _ in range(2)]
    oT = [ar.alloc([512], F32, "oT") for _ in range(2)]
    rc = [ar.alloc([4, 1], F32, "rc") for _ in range(2)]
    stg = [ar.alloc([4, 64], F32, "stg") for _ in range(2)]
    rt1, krt1 = ar.alloc([512], F32, "rt1")
    rt2, krt2 = ar.alloc([512], F32, "rt2")
    for hb in range(2):
        kb.pool(lambda e, hb=hb: e.memset(VA[hb][0][:, :, 64:128], 1.0), writes=[VA[hb][1] + ".ones"])
    misc = [self.bank(6), self.bank(7)]
    mi = [0]

    def mbank():
        mi[0] += 1
        return misc[mi[0] % 2]
    cnt_o = [0]
    for b in range(NBC):
        lat, cx = self.lat[b], self.ctx[b]
        kb.dma(pkv[:, 0:CL], cx.PKV, reads=["PKV_" + cx.name], writes=[kpkv])
        kb.dma(pkv[:, CL:NK], lat.PKV, reads=["PKV_" + lat.name], writes=[kpkv])
        for hb in range(2):
            kb.dma(KT[hb][0][64:96, 0:CL], cx.KR, reads=["KR_" + cx.name], writes=[KT[hb][1] + ".r"])
            kb.dma(KT[hb][0][64:96, CL:NK], lat.KR, reads=["KR_" + lat.name], writes=[KT[hb][1] + ".r"])
        kb.dma(pq, lat.PQ.rearrange("c p t -> p c t"), reads=["PQ_" + lat.name], writes=[kpq])
        if not last:
            kb.dma(pqc, cx.PQ.rearrange("c p t -> p c t"), reads=["PQ_" + cx.name], writes=[kpqc])
        def build_steps(h, lat=lat, cx=cx):
            hb = h % 2
            kt_ap, kkt = KT[hb]
            va_ap, kva = VA[hb]
            steps = []

            def k_chunk(kc):
                k0 = kc * 512
                kw_ = min(512, NK - k0)
                pb, kpb = mbank()
                kb.pe(lambda e: e.matmul(pb[0:64, 0:kw_], wkv_ap[:, 0, h * 128:h * 128 + 64], pkv[:, k0:k0 + kw_], start=True, stop=True),
                      reads=[kwkv, kpkv], writes=[kpb])
                kb.dve(lambda e: e.tensor_copy(out=kt_ap[0:64, k0:k0 + kw_], in_=pb[0:64, 0:kw_]), reads=[kpb], writes=[kkt + ".n"])

            def v_group(g8):
                k0 = g8 * 8
                ng = min(8, NKT - k0)
                pb, kpb = mbank()
                for i in range(ng):
                    kb.pe(lambda e, i=i: e.matmul(pb[:, i * 64:(i + 1) * 64], pkv[:, (k0 + i) * 128:(k0 + i + 1) * 128],
                                                  wkv_ap[:, 0, h * 128 + 64:h * 128 + 128], start=True, stop=True),
                          reads=[kwkv, kpkv], writes=[kpb])
                kb.dve(lambda e: e.tensor_copy(out=va_ap[:, k0:k0 + ng, 0:64], in_=pb[:, 0:ng * 64].rearrange("p (a c) -> p a c", c=64)),
                       reads=[kpb], writes=[kva + ".v"])

            def q_chunk(qc, QW, rope, pq_ap, kpq_, qt_ap, kqt):
                q0 = qc * QW
                pa, kpa = mbank()
                for c_ in range(2):
                    kb.pe(lambda e, c_=c_: e.matmul(pa[0:96, 0:QW], wq_ap[:, c_, h * 96:(h + 1) * 96], pq_ap[:, c_, q0:q0 + QW], start=(c_ == 0), stop=(c_ == 1)),
                          reads=[kwq[c_], kpq_], writes=[kpa])
                kb.dve(lambda e: e.tensor_scalar(out=qt_ap[0:64, q0:q0 + QW], in0=pa[0:64, 0:QW], scalar1=SCALE, scalar2=None, op0=ALU.mult),
                       reads=[kpa], writes=[kqt + ".n"])
                if rope:
                    pb, kpb = mbank()
                    for c_ in range(2):
                        kb.pe(lambda e, c_=c_: e.matmul(pb[0:96, 0:QW], wqr_ap[:, c_, h * 96:(h + 1) * 96], pq_ap[:, c_, q0:q0 + QW], start=(c_ == 0), stop=(c_ == 1)),
                              reads=[kwqr[c_], kpq_], writes=[kpb])
                    kb.dve(lambda e: e.tensor_tensor(out=rt1[64:96, 0:QW], in0=pa[64:96, 0:QW], in1=cosq[64:96, q0:q0 + QW], op=ALU.mult),
                           reads=[kpa, kcosq], writes=[krt1])
                    kb.dve(lambda e: e.tensor_tensor(out=rt2[64:96, 0:QW], in0=pb[64:96, 0:QW], in1=sinq[64:96, q0:q0 + QW], op=ALU.mult),
                           reads=[kpb, ksinq], writes=[krt2])
                    kb.dve(lambda e: e.tensor_tensor(out=qt_ap[64:96, q0:q0 + QW], in0=rt1[64:96, 0:QW], in1=rt2[64:96, 0:QW], op=ALU.add),
                           reads=[krt1, krt2], writes=[kqt + ".r"])
                else:
                    kb.dve(lambda e: e.tensor_scalar(out=qt_ap[64:96, q0:q0 + QW], in0=pa[64:96, 0:QW], scalar1=SCALE, scalar2=None, op0=ALU.mult),
                           reads=[kpa], writes=[kqt + ".r"])
            for kc in range((NK + 511) // 512):
                steps.append(lambda kc=kc: k_chunk(kc))
            for g8 in range((NKT + 7) // 8):
                steps.append(lambda g8=g8: v_group(g8))
            for qc in range(S // 512):
                steps.append(lambda qc=qc: q_chunk(qc, 512, True, pq, kpq, QT[hb][0], QT[hb][1]))
            if not last:
                steps.append(lambda: q_chunk(0, CL, False, pqc, kpqc, QTc[hb][0], QTc[hb][1]))
            return steps

        for st_ in build_steps(0):
            st_()
        for h in range(NH):
            hb = h % 2
            kt_ap, kkt = KT[hb]
            va_ap, kva = VA[hb]
            pending = build_steps(h + 1) if h + 1 < NH else []
            qsets = [(lat, S, NKT, QT[hb])]
            if not last:
                qsets.append((cx, CL, CL // 128, QTc[hb]))
            n_items_total = sum((nq_ // min(512, nq_)) * ((nkt_ + 1) // 2) for (_, nq_, nkt_, _) in qsets)
            every = max(1, (n_items_total - 8) // max(1, len(pending)))
            tick = [0]
            for (qs, nq, nkt, (qt_ap, kqt)) in qsets:
                QW = min(512, nq)
                npair = (nkt + 1) // 2
                items = [(qb, kp) for qb in range(nq // QW) for kp in range(npair)]
                po_of = {}
                for qb in range(nq // QW):
                    cnt_o[0] += 1
                    po_of[qb] = (self.bank(4 + cnt_o[0] % 2), cnt_o[0] % 2)

                def emit_S(ii, nkt=nkt, QW=QW, kt_ap=kt_ap, qt_ap=qt_ap, kkt=kkt, kqt=kqt, items=items):
                    qb, kp = items[ii]
                    q0 = qb * QW
                    ps_s = self.ps2[ii % 2]
                    kps = ["pb%d" % (2 * (ii % 2)), "pb%d" % (2 * (ii % 2) + 1)]
                    pt_ap, kpt = PT[ii % 2]
                    nh = min(2, nkt - 2 * kp)
                    for half in range(nh):
                        kt = 2 * kp + half
                        kb.pe(lambda e, ps_s=ps_s, half=half, kt=kt, q0=q0:
                              e.matmul(ps_s[:, half * 512:half * 512 + QW], kt_ap[0:96, kt * 128:(kt + 1) * 128], qt_ap[0:96, q0:q0 + QW], start=True, stop=True),
                              reads=[kkt + ".n", kkt + ".r", kqt + ".n", kqt + ".r"], writes=[kps[half]])
                    if QW == 512:
                        kb.act(lambda e, ps_s=ps_s, pt_ap=pt_ap, nh=nh: e.activation(out=pt_ap[:, 0:nh * 512], in_=ps_s[:, 0:nh * 512], func=AF.Exp),
                               reads=kps[0:nh], writes=[kpt])
                    else:
                        for half in range(nh):
                            kb.act(lambda e, ps_s=ps_s, pt_ap=pt_ap, half=half: e.activation(out=pt_ap[:, half * 512:half * 512 + QW], in_=ps_s[:, half * 512:half * 512 + QW], func=AF.Exp),
                                   reads=[kps[half]], writes=[kpt])

                def emit_PV(ii, nkt=nkt, QW=QW, va_ap=va_ap, kva=kva, qs=qs, h=h, npair=npair, items=items, po_of=po_of):
                    qb, kp = items[ii]
                    q0 = qb * QW
                    (po, kpo), par = po_of[qb]
                    pt_ap, kpt = PT[ii % 2]
                    nh = min(2, nkt - 2 * kp)
                    for half in range(nh):
                        kt = 2 * kp + half
                        kb.pe(lambda e, po=po, pt_ap=pt_ap, half=half, kt=kt:
                              e.matmul(po[:, 0:QW], va_ap[:, kt, :], pt_ap[:, half * 512:half * 512 + QW], start=(kt == 0), stop=(kt == nkt - 1)),
                              reads=[kva + ".v", kva + ".ones", kpt], writes=[kpo])
                    if kp != npair - 1:
                        return
                    o_ap, ko = oT[par]
                    r_ap, kr_ = rc[par]
                    s_ap, ks_ = stg[par]
                    nsub = QW // 128
                    kb.dve(lambda e, o_ap=o_ap, po=po: e.tensor_copy(out=o_ap[:, 0:QW], in_=po[:, 0:QW]), reads=[kpo], writes=[ko])
                    ptr, kptr = mbank()
                    for sub in range(nsub):
                        kb.pe(lambda e, ptr=ptr, o_ap=o_ap, sub=sub: e.transpose(ptr[:, sub * 128:(sub + 1) * 128], o_ap[:, sub * 128:(sub + 1) * 128], self.ident_f),
                              reads=[ko, "ident_f"], writes=[kptr])
                    kb.dve(lambda e, ptr=ptr, r_ap=r_ap, nsub=nsub: e.reciprocal(out=r_ap[:, 0:nsub, :], in_=ptr[:, 0:nsub * 128].rearrange("p (a c) -> p a c", c=128)[:, :, 64:65]),
                           reads=[kptr], writes=[kr_])
                    for sub in range(nsub):
                        kb.dve(lambda e, ptr=ptr, s_ap=s_ap, r_ap=r_ap, sub=sub: e.tensor_scalar(out=s_ap[:, sub, :], in0=ptr[:, sub * 128:sub * 128 + 64], scalar1=r_ap[:, sub, :], scalar2=None, op0=ALU.mult),
                               reads=[kptr, kr_], writes=[ks_ + ".%d" % sub])
                    kb.dma(qs.ATT[q0:q0 + QW, h * 64:(h + 1) * 64].rearrange("(a p) c -> p a c", p=128), s_ap[:, 0:nsub, :],
                           reads=[ks_ + ".%d" % sub for sub in range(nsub)], writes=["ATT_" + qs.name])

                emit_S(0)
                for ii in range(len(items)):
                    if ii + 1 < len(items):
                        emit_S(ii + 1)
                    emit_PV(ii)
                    tick[0] += 1
                    if pending and tick[0] % every == 0:
                        pending.pop(0)()
            while pending:
                pending.pop(0)()
    self.phase_end()


Prog.phase_attn = _attn_phase


def _pool_phase(self, l):
    kb, ar, w = self.kb, self.arena, self.w
    last = (l == L - 1)
    pw = ar.alloc([4, 64], BF16, "pw")
    kb.dma(pw[0][0:64, :, :], w["pool_w"][l].rearrange("g i o -> i g o"), writes=[pw[1]], q="pool")
    psc, kpsc = ar.alloc([256], F32, "psc")
    kb.dma(psc, w["pool_scale"][l].partition_broadcast(128), writes=[kpsc])
    groups = [("lat", self.lat)] + ([] if last else [("ctx", self.ctx)])
    for nm, seqs in groups:
        bandc = self.k["band_" + nm]
        tab = self.consts["_band_tab_" + nm]
        nblk = bandc.shape[0]
        band, kband = ar.alloc([nblk, 512], BF16, "band")
        kb.dma(band, bandc.rearrange("b p t -> p b t"), writes=[kband])
        n = seqs[0].n
        NT = n // 128
        u, ku = ar.alloc([NT, 256], BF16, "u")
        dg = [ar.alloc([4, 512], BF16, "dg") for _ in range(2)]
        stg = [ar.alloc([4, 256], F32, "pstg") for _ in range(2)]
        it = 0
        for s in seqs:
            W, nsub = s.W, s.W // 128
            kb.dma(u, s.POOLU.rearrange("(a p) c -> p a c", p=128), reads=["POOLU_" + s.name], writes=[ku])
            for i in range(s.nt):
                d_ap, kd = dg[it % 2]
                s_ap, ks = stg[it % 2]
                for g in range(4):
                    ms = sorted(m for (g_, i_, m) in tab if g_ == g and i_ == i)
                    pb, kpb = self.nb()
                    for mi_, m in enumerate(ms):
                        bi = tab[(g, i, m)]
                        kb.pe(lambda e, pb=pb, g=g, m=m, bi=bi, W=W, mi_=mi_, nm_=len(ms), u=u, band=band: e.matmul(pb[0:64, 0:W], u[:, m, g * 64:(g + 1) * 64], band[:, bi, 0:W],
                                                                                            start=(mi_ == 0), stop=(mi_ == nm_ - 1)),
                              reads=[ku, kband], writes=[kpb])
                    self.evac(g, lambda e, pb=pb, d_ap=d_ap, g=g, W=W: e.activation(out=d_ap[0:64, g, 0:W], in_=pb[0:64, 0:W], func=AF.Copy),
                              lambda e, pb=pb, d_ap=d_ap, g=g, W=W: e.tensor_copy(out=d_ap[0:64, g, 0:W], in_=pb[0:64, 0:W]),
                              reads=[kpb], writes=[kd + ".%d" % g])
                for sub in range(nsub):
                    pb, kpb = self.nb()
                    for g in range(4):
                        kb.pe(lambda e, pb=pb, g=g, sub=sub, d_ap=d_ap: e.matmul(pb[:, g * 64:(g + 1) * 64], d_ap[0:64, g, sub * 128:(sub + 1) * 128], pw[0][0:64, g, :],
                                                                                start=True, stop=True),
                              reads=[kd + ".%d" % g, pw[1]], writes=[kpb])
                    kb.dve(lambda e, pb=pb, s_ap=s_ap, sub=sub: e.tensor_tensor(out=s_ap[:, sub, :], in0=pb[:, 0:256], in1=psc, op=ALU.mult),
                           reads=[kpb, kpsc], writes=[ks + ".%d" % sub])
                kb.dma(s.POOLO[i * W:(i + 1) * W, :].rearrange("(a p) c -> p a c", p=128), s_ap[:, 0:nsub, :],
                       reads=[ks + ".%d" % sub for sub in range(nsub)], writes=["POOLO_" + s.name])
                it += 1
    self.phase_end()


def _c1_phase(self, l):
    kb, ar, w = self.kb, self.arena, self.w
    last = (l == L - 1)
    wout = ar.alloc([8, D], BF16, "wout")
    self.load_cast_rows(wout, w["w_out"][l], 8)
    wo_ap, kwo = wout
    gout, kgout = ar.alloc([D], F32, "gout")
    kb.dma(gout, w["g_out"][l].partition_broadcast(128), writes=[kgout])
    xts = [ar.alloc([8, 512], F32, "xt") for _ in range(2)]
    att = [ar.alloc([4, 512], F32, "att") for _ in range(2)]
    pl = [ar.alloc([4, 256], F32, "pl") for _ in range(2)]
    hy = [ar.alloc([4, 256], F32, "hy") for _ in range(2)]
    junk, kjunk = ar.alloc([512], BF16, "junk")
    ss = [ar.alloc([3, 4], F32, "ss") for _ in range(2)]
    mrg = [ar.alloc([4, D], BF16, "mrg") for _ in range(2)]
    mT = [ar.alloc([8, 512], BF16, "mT") for _ in range(2)]
    seqs = self.lat + ([] if last else self.ctx)
    GR = ((0, 512, 0), (512, 256, 1), (768, 256, 2))
    tiles = [(s, i) for s in seqs for i in range(s.nt)]

    def front(it):
        s, i = tiles[it]
        W, nsub = s.W, s.W // 128
        HYO = self.HYO_lat if s.kind == "lat" else self.HYO_ctx
        t0 = i * W
        xt, kxt = xts[it % 2]
        a_ap, ka = att[it % 2]
        p_ap, kp = pl[it % 2]
        h_ap, kh = hy[it % 2]
        ss_ap, kss = ss[it % 2]
        m_ap, km = mrg[it % 2]
        t_ap, kt = mT[it % 2]
        kb.dma(xt[:, :, 0:W], s.XT[:, :, t0:t0 + W].rearrange("j p t -> p j t"), reads=["XT_%s.%d" % (s.name, i)], writes=[kxt + ".%d" % j for j in range(8)])
        kb.dma(a_ap[:, 0:nsub, :], s.ATT[t0:t0 + W, :].rearrange("(a p) c -> p a c", p=128), reads=["ATT_" + s.name], writes=[ka])
        kb.dma(p_ap[:, 0:nsub, :], s.POOLO[t0:t0 + W, :].rearrange("(a p) c -> p a c", p=128), reads=["POOLO_" + s.name], writes=[kp])
        kb.dma(h_ap[:, 0:nsub, :], HYO[t0:t0 + W, s.b * 256:(s.b + 1) * 256].rearrange("(a p) c -> p a c", p=128), reads=["HYO"], writes=[kh])
        kb.dve(lambda e: e.memset(ss_ap, 0.0), writes=[kss])
        srcs = ((a_ap, ka), (p_ap, kp), (h_ap, kh))
        for sub in range(nsub):
            for (c0, ng, gi) in GR:
                src, ksrc = srcs[gi]
                kb.act(lambda e, src=src, sub=sub, ng=ng, gi=gi: e.activation(out=junk[:, 0:ng], in_=src[:, sub, :], func=AF.Square,
                                                                            accum_out=ss_ap[:, gi, sub:sub + 1]),
                       reads=[ksrc, kss], writes=[kss, kjunk])
        for (c0, ng, gi) in GR:
            kb.act(lambda e, gi=gi, ng=ng: e.activation(out=ss_ap[:, gi, 0:nsub], in_=ss_ap[:, gi, 0:nsub], func=AF.Sqrt,
                                                        bias=self.eps[:, 0:1], scale=1.0 / ng),
                   reads=[kss, "eps"], writes=[kss])
        kb.dve(lambda e: e.reciprocal(out=ss_ap, in_=ss_ap), reads=[kss], writes=[kss])
        for sub in range(nsub):
            for (c0, ng, gi) in GR:
                src, ksrc = srcs[gi]
                kb.dve(lambda e, src=src, sub=sub, c0=c0, ng=ng, gi=gi:
                       e.scalar_tensor_tensor(out=m_ap[:, sub, c0:c0 + ng], in0=src[:, sub, :], scalar=ss_ap[:, gi, sub:sub + 1], in1=gout[:, c0:c0 + ng],
                                              op0=ALU.mult, op1=ALU.mult),
                       reads=[ksrc, kss, kgout], writes=[km + ".%d" % sub])
            pb, kpb = self.nb()
            pbb = pb.bitcast(BF16)
            for j in range(8):
                kb.pe(lambda e, pbb=pbb, sub=sub, j=j: e.transpose(pbb[:, j * 128:(j + 1) * 128], m_ap[:, sub, j * 128:(j + 1) * 128], self.ident_b),
                      reads=[km + ".%d" % sub, "ident_b"], writes=[kpb])
            self.evac(sub, lambda e, pbb=pbb, sub=sub: e.activation(out=t_ap[:, :, sub * 128:(sub + 1) * 128], in_=pbb.rearrange("p (j t) -> p j t", t=128), func=AF.Copy),
                      lambda e, pbb=pbb, sub=sub: e.tensor_copy(out=t_ap[:, :, sub * 128:(sub + 1) * 128], in_=pbb.rearrange("p (j t) -> p j t", t=128)),
                      reads=[kpb], writes=[kt + ".%d" % sub])

    def back(it):
        s, i = tiles[it]
        W, nsub = s.W, s.W // 128
        g1 = self.mod(l, 2, s.v)
        t0 = i * W
        xt, kxt = xts[it % 2]
        t_ap, kt = mT[it % 2]
        for oc in range(8):
            pb, kpb = self.nb()
            for k_ in range(8):
                kb.pe(lambda e, pb=pb, k_=k_, oc=oc: e.matmul(pb[:, 0:W], wo_ap[:, k_, oc * 128:(oc + 1) * 128], t_ap[:, k_, 0:W], start=(k_ == 0), stop=(k_ == 7)),
                      reads=[kwo + ".%d" % k_] + [kt + ".%d" % sub for sub in range(nsub)], writes=[kpb])
            kb.dve(lambda e, pb=pb, oc=oc: e.scalar_tensor_tensor(out=xt[:, oc, 0:W], in0=pb[:, 0:W], scalar=g1[:, oc:oc + 1], in1=xt[:, oc, 0:W],
                                                                op0=ALU.mult, op1=ALU.add),
                   reads=[kpb, kxt + ".%d" % oc, "modT"], writes=[kxt + ".%d" % oc])
        kb.dma(s.XT[:, :, t0:t0 + W].rearrange("j p t -> p j t"), xt[:, :, 0:W], reads=[kxt + ".%d" % j for j in range(8)], writes=["XT_%s.%d" % (s.name, i)])

    front(0)
    for it in range(len(tiles)):
        if it + 1 < len(tiles):
            front(it + 1)
        back(it)
    self.phase_end()


def _c2_phase(self, l):
    kb, ar, w = self.kb, self.arena, self.w
    last = (l == L - 1)
    w1 = ar.alloc([8, DFF], BF16, "w1")
    w2 = ar.alloc([32, D], BF16, "w2")
    self.load_cast_rows(w1, w["w_mlp1"][l], 8, split=2)
    self.load_cast_rows(w2, w["w_mlp2"][l], 32)
    w1_ap, kw1 = w1
    w2_ap, kw2 = w2
    xts = [ar.alloc([8, 512], F32, "xt") for _ in range(2)]
    rs, krs = ar.alloc([512], F32, "rs")
    hT, khT = ar.alloc([8, 512], BF16, "hT")
    hid, khid = ar.alloc([32, 512], BF16, "hid")
    tmp = hid[:, 0:16, :].rearrange("p a b -> p (a b)").bitcast(F32).rearrange("p (a b) -> p a b", b=512)
    ktmp = khid + ".tmp"
    sq = hid[:, 16:24, :]
    ksq = khid + ".sq"
    rl = [ar.alloc([512], F32, "rl")] * 2
    seqs = self.lat + ([] if last else self.ctx)
    tiles = [(s, i) for s in seqs for i in range(s.nt)]

    def load(it):
        s, i = tiles[it]
        W = s.W
        xt, kxt = xts[it % 2]
        kb.dma(xt[:, :, 0:W], s.XT[:, :, i * W:(i + 1) * W].rearrange("j p t -> p j t"), reads=["XT_%s.%d" % (s.name, i)], writes=[kxt + ".%d" % j for j in range(8)])

    load(0)
    for it in range(len(tiles)):
        s, i = tiles[it]
        W = s.W
        gm, sh, g2 = self.mod(l, 4, s.v), self.mod(l, 3, s.v), self.mod(l, 5, s.v)
        t0 = i * W
        xt, kxt = xts[it % 2]
        if it + 1 < len(tiles):
            load(it + 1)
        self.mod_norm(xt, kxt, W, gm, sh, sq, ksq, rs, krs, tmp, ktmp, hT, khT,
                      extra=lambda j: [khid + ".%d" % (2 * j), khid + ".%d" % (2 * j + 1)],
                      extra_sq=lambda ci: [khid + ".%d" % (16 + ci)])
        khT_all = [khT + ".%d" % j for j in range(8)]
        for hc in range(32):
            pb, kpb = self.nb()
            for j in range(8):
                kb.pe(lambda e, pb=pb, j=j, hc=hc, W=W: e.matmul(pb[:, 0:W], w1_ap[:, j, hc * 128:(hc + 1) * 128], hT[:, j, 0:W], start=(j == 0), stop=(j == 7)),
                      reads=[kw1 + ".%d" % j, khT_all[j]], writes=[kpb])
            r_ap, kr_ = rl[hc % 2]
            kb.act(lambda e, pb=pb, r_ap=r_ap, W=W: e.activation(out=r_ap[:, 0:W], in_=pb[:, 0:W], func=AF.Relu), reads=[kpb], writes=[kr_])
            kb.dve(lambda e, r_ap=r_ap, hc=hc, W=W: e.tensor_tensor(out=hid[:, hc, 0:W], in0=r_ap[:, 0:W], in1=r_ap[:, 0:W], op=ALU.mult),
                   reads=[kr_], writes=[khid + ".%d" % hc])
        for oc in range(8):
            pb, kpb = self.nb()
            for hc in range(32):
                kb.pe(lambda e, pb=pb, hc=hc, oc=oc, W=W: e.matmul(pb[:, 0:W], w2_ap[:, hc, oc * 128:(oc + 1) * 128], hid[:, hc, 0:W], start=(hc == 0), stop=(hc == 31)),
                      reads=[kw2 + ".%d" % hc, khid + ".%d" % hc], writes=[kpb])
            kb.dve(lambda e, pb=pb, oc=oc, W=W, g2=g2, xt=xt: e.scalar_tensor_tensor(out=xt[:, oc, 0:W], in0=pb[:, 0:W], scalar=g2[:, oc:oc + 1], in1=xt[:, oc, 0:W],
                                                                                op0=ALU.mult, op1=ALU.add),
                   reads=[kpb, kxt + ".%d" % oc, "modT"], writes=[kxt + ".%d" % oc])
        kb.dma(s.XT[:, :, t0:t0 + W].rearrange("j p t -> p j t"), xt[:, :, 0:W], reads=[kxt + ".%d" % j for j in range(8)], writes=["XT_%s.%d" % (s.name, i)])
    self.phase_end()


def _final_phase(self):
    kb, ar, w = self.kb, self.arena, self.w
    gf, kgf = ar.alloc([8], F32, "gf")
    for j in range(8):
        kb.dma(gf[:, j:j + 1], w["g_final"][j * 128:(j + 1) * 128].rearrange("(p o) -> p o", o=1), writes=[kgf])
    xts = [ar.alloc([8, 512], F32, "xt") for _ in range(2)]
    sq, ksq = ar.alloc([8, 512], BF16, "sq")
    rs, krs = ar.alloc([512], F32, "rs")
    xn, kxn = ar.alloc([8, 512], F32, "xn")
    yts = [ar.alloc([4, D], F32, "yt") for _ in range(2)]
    it = 0
    for s in self.lat:
        W = 512
        for i in range(s.nt):
            t0 = i * W
            xt, kxt = xts[it % 2]
            yt, kyt = yts[it % 2]
            kb.dma(xt, s.XT[:, :, t0:t0 + W].rearrange("j p t -> p j t"), reads=["XT_" + s.name], writes=[kxt + ".%d" % j for j in range(8)])
            self.fm_rstd([(xt[:, j, :], kxt + ".%d" % j, 128) for j in range(8)], D, W, sq, ksq, rs, krs)
            for j in range(8):
                kb.dve(lambda e, xt=xt, j=j: e.scalar_tensor_tensor(out=xn[:, j, :], in0=xt[:, j, :], scalar=gf[:, j:j + 1], in1=rs, op0=ALU.mult, op1=ALU.mult),
                       reads=[kxt + ".%d" % j, krs, kgf], writes=[kxn + ".%d" % j])
            for sub in range(4):
                pp = self.ps2[sub % 2]
                kpp = ["pb%d" % (2 * (sub % 2)), "pb%d" % (2 * (sub % 2) + 1)]
                for j in range(8):
                    kb.pe(lambda e, pp=pp, j=j, sub=sub: e.transpose(pp[:, j * 128:(j + 1) * 128], xn[:, j, sub * 128:(sub + 1) * 128], self.ident_f),
                          reads=[kxn + ".%d" % j, "ident_f"], writes=[kpp[j // 4]])
                self.evac(sub, lambda e, pp=pp, yt=yt, sub=sub: e.activation(out=yt[:, sub, :], in_=pp, func=AF.Copy),
                          lambda e, pp=pp, yt=yt, sub=sub: e.tensor_copy(out=yt[:, sub, :], in_=pp),
                          reads=kpp, writes=[kyt + ".%d" % sub])
            kb.dma(self.y_out[s.b][t0:t0 + W, :].rearrange("(a p) d -> p a d", p=128), yt, reads=[kyt + ".%d" % sub for sub in range(4)], writes=["y"])
            it += 1
    self.phase_end()


Prog.phase_pool = _pool_phase
Prog.phase_c1 = _c1_phase
Prog.phase_c2 = _c2_phase
Prog.phase_final = _final_phase


def _hy_h0(self, l, nm, seqs, n):
    kb, ar, w = self.kb, self.arena, self.w
    NT = n // 128
    cw, kcw = ar.alloc([6, 3], F32, "cw")
    cb, kcb = ar.alloc([6], F32, "cb")
    for c6 in range(6):
        for k_ in range(3):
            kb.dma(cw[:, c6, k_:k_ + 1], w["hy_conv_w"][l][k_, c6 * 128:(c6 + 1) * 128].rearrange("(p o) -> p o", o=1), writes=[kcw])
        kb.dma(cb[:, c6:c6 + 1], w["hy_conv_b"][l][c6 * 128:(c6 + 1) * 128].rearrange("(p o) -> p o", o=1), writes=[kcb])
    hp = [ar.alloc([n + 2], F32, "hp") for _ in range(2)]
    acc = [ar.alloc([n], F32, "acc") for _ in range(2)]
    ucb = [ar.alloc([n], BF16, "ucb") for _ in range(2)]
    tm = [ar.alloc([NT, 128], BF16, "tm") for _ in range(2)]
    dests = (getattr(self, "HV_" + nm), getattr(self, "HX1_" + nm), getattr(self, "HX2_" + nm))
    it = 0
    for si, s in enumerate(seqs):
        for c6 in range(6):
            h_ap, kh = hp[it % 2]
            a_ap, ka = acc[it % 2]
            u_ap, ku = ucb[it % 2]
            t_ap, kt = tm[it % 2]
            kb.dma(h_ap, s.HYP[c6], reads=["HYP_" + s.name], writes=[kh])
            kb.act(lambda e, h_ap=h_ap, a_ap=a_ap, c6=c6: e.activation(out=a_ap, in_=h_ap[:, 1:n + 1], func=AF.Identity, bias=cb[:, c6:c6 + 1], scale=cw[:, c6, 1:2]),
                   reads=[kh, kcw, kcb], writes=[ka])
            kb.dve(lambda e, h_ap=h_ap, a_ap=a_ap, c6=c6: e.scalar_tensor_tensor(out=a_ap, in0=h_ap[:, 0:n], scalar=cw[:, c6, 0:1], in1=a_ap, op0=ALU.mult, op1=ALU.add),
                   reads=[kh, kcw, ka], writes=[ka])
            kb.dve(lambda e, h_ap=h_ap, a_ap=a_ap, u_ap=u_ap, c6=c6: e.scalar_tensor_tensor(out=u_ap, in0=h_ap[:, 2:n + 2], scalar=cw[:, c6, 2:3], in1=a_ap, op0=ALU.mult, op1=ALU.add),
                   reads=[kh, kcw, ka], writes=[ku])
            for g8 in range((NT + 7) // 8):
                ng = min(8, NT - g8 * 8)
                pb, kpb = self.nb()
                pbb = pb.bitcast(BF16)
                for i in range(ng):
                    tt = g8 * 8 + i
                    kb.pe(lambda e, pbb=pbb, u_ap=u_ap, i=i, tt=tt: e.transpose(pbb[:, i * 128:(i + 1) * 128], u_ap[:, tt * 128:(tt + 1) * 128], self.ident_b),
                          reads=[ku, "ident_b"], writes=[kpb])
                self.evac(g8, lambda e, pbb=pbb, t_ap=t_ap, g8=g8, ng=ng: e.activation(out=t_ap[:, g8 * 8:g8 * 8 + ng, :], in_=pbb[:, 0:ng * 128].rearrange("p (a c) -> p a c", c=128), func=AF.Copy),
                          lambda e, pbb=pbb, t_ap=t_ap, g8=g8, ng=ng: e.tensor_copy(out=t_ap[:, g8 * 8:g8 * 8 + ng, :], in_=pbb[:, 0:ng * 128].rearrange("p (a c) -> p a c", c=128)),
                          reads=[kpb], writes=[kt + ".%d" % g8])
            dst = dests[c6 // 2]
            col0 = si * 256 + (c6 % 2) * 128
            kb.dma(dst[:, col0:col0 + 128].rearrange("(a p) c -> p a c", p=128), t_ap,
                   reads=[kt + ".%d" % g8 for g8 in range((NT + 7) // 8)], writes=["HU_" + nm])
            it += 1
    self.phase_end()


def _hy_h1(self, l, nm, n):
    kb, ar, w = self.kb, self.arena, self.w
    NT = n // 128
    N2 = 2 * n
    CW = min(512, n)
    zT, kzT = ar.alloc([n], F32, "zT")
    kb.dma(zT[0:HY_EMB, :], self.k["hyz_" + nm], writes=[kzT])
    w1s, kw1s = ar.alloc([HY_FFN], F32, "w1s")
    w2s, kw2s = ar.alloc([HY_FFN], F32, "w2s")
    w3s, kw3s = ar.alloc([1024], F32, "w3s")
    kb.dma(w1s[0:HY_EMB, :], w["hy_f_w1"][l], writes=[kw1s])
    kb.dma(w2s[0:HY_FFN, :], w["hy_f_w2"][l], writes=[kw2s])
    kb.dma(w3s[0:HY_FFN, :], w["hy_f_w3"][l], writes=[kw3s])
    par, kpar = ar.alloc([6], F32, "par")
    for ci, nm_ in enumerate(("hy_f_b1", "hy_f_freq1", "hy_f_b2", "hy_f_freq2")):
        kb.dma(par[0:64, ci:ci + 1], w[nm_][l].rearrange("(p o) -> p o", o=1), writes=[kpar])
    kb.dve(lambda e: e.tensor_tensor(out=par[0:64, 4:5], in0=par[0:64, 0:1], in1=par[0:64, 1:2], op=ALU.mult), reads=[kpar], writes=[kpar])
    kb.dve(lambda e: e.tensor_tensor(out=par[0:64, 5:6], in0=par[0:64, 2:3], in1=par[0:64, 3:4], op=ALU.mult), reads=[kpar], writes=[kpar])
    h1T, kh1 = ar.alloc([n], F32, "h1T")
    h2T, kh2 = ar.alloc([n], F32, "h2T")
    arg, karg = ar.alloc([512], F32, "arg")
    kk, kkk = ar.alloc([512], F32, "kk")

    def layer(srcT, ksrc, wS, kwS, K, fcol, fbcol, dstT, kdst):
        for ch in range(n // CW):
            c0 = ch * CW
            pb, kpb = self.nb()
            kb.pe(lambda e, pb=pb, c0=c0: e.matmul(pb[0:64, 0:CW], wS[0:K, 0:64], srcT[0:K, c0:c0 + CW], start=True, stop=True),
                  reads=[ksrc, kwS], writes=[kpb])
            kb.act(lambda e, pb=pb: e.activation(out=arg[0:64, 0:CW], in_=pb[0:64, 0:CW], func=AF.Identity, bias=par[0:64, fbcol:fbcol + 1], scale=par[0:64, fcol:fcol + 1]),
                   reads=[kpb, kpar], writes=[karg])
            kb.dve(lambda e: e.tensor_scalar(out=kk[0:64, 0:CW], in0=arg[0:64, 0:CW], scalar1=1.0 / (2 * math.pi), scalar2=MAGIC, op0=ALU.mult, op1=ALU.add),
                   reads=[karg], writes=[kkk])
            kb.dve(lambda e: e.tensor_scalar(out=kk[0:64, 0:CW], in0=kk[0:64, 0:CW], scalar1=-MAGIC, scalar2=-2 * math.pi, op0=ALU.add, op1=ALU.mult),
                   reads=[kkk], writes=[kkk])
            kb.dve(lambda e: e.tensor_tensor(out=arg[0:64, 0:CW], in0=arg[0:64, 0:CW], in1=kk[0:64, 0:CW], op=ALU.add), reads=[karg, kkk], writes=[karg])
            kb.act(lambda e, c0=c0: e.activation(out=dstT[0:64, c0:c0 + CW], in_=arg[0:64, 0:CW], func=AF.Sin), reads=[karg], writes=[kdst])
    layer(zT, kzT, w1s, kw1s, HY_EMB, 1, 4, h1T, kh1)
    layer(h1T, kh1, w2s, kw2s, HY_FFN, 3, 5, h2T, kh2)
    HP, kHP = ar.alloc([NT, 512], BF16, "HP")
    HM, kHM = ar.alloc([NT, 512], BF16, "HM")
    dec = [ar.alloc([256], F32, "dec") for _ in range(2)]
    tp = [ar.alloc([4, 256], F32, "tp") for _ in range(2)]
    ab = [ar.alloc([4, 256], F32, "ab") for _ in range(2)]
    psZ = self.ps2[3]
    kZ = ["pb6", "pb7"]
    for tt in range(NT):
        pt = self.ps2[tt % 2]
        kpt = ["pb%d" % (2 * (tt % 2)), "pb%d" % (2 * (tt % 2) + 1)]
        d_ap, kd = dec[tt % 2]
        t_ap, ktp = tp[tt % 2]
        a_ap, kab = ab[tt % 2]
        for hf in range(2):
            kb.pe(lambda e, pt=pt, hf=hf, tt=tt: e.matmul(pt[:, hf * 512:(hf + 1) * 512], h2T[0:64, tt * 128:(tt + 1) * 128], w3s[0:64, hf * 512:(hf + 1) * 512], start=True, stop=True),
                  reads=[kh2, kw3s], writes=[kpt[hf]])
        kb.dma(d_ap, self.k["hydec_" + nm][tt * 128:(tt + 1) * 128, :], writes=[kd])
        for q in range(4):
            kb.dve(lambda e, pt=pt, t_ap=t_ap, d_ap=d_ap, q=q: e.tensor_tensor(out=t_ap[:, q, :], in0=pt[:, q * 256:(q + 1) * 256], in1=d_ap, op=ALU.mult),
                   reads=[kpt[q // 2], kd], writes=[ktp])
        if tt == 0:
            for q in (1, 3):
                kb.dve(lambda e, t_ap=t_ap, q=q: e.memset(t_ap[0:1, q, :], 0.0), reads=[ktp], writes=[ktp])
        kb.act(lambda e, t_ap=t_ap, a_ap=a_ap: e.activation(out=a_ap, in_=t_ap, func=AF.Abs), reads=[ktp], writes=[kab])
        for hf in range(2):
            kb.pe(lambda e, a_ap=a_ap, hf=hf, tt=tt: e.matmul(psZ[:, hf * 512:(hf + 1) * 512], self.ones_f, a_ap[:, 2 * hf:2 * hf + 2, :].rearrange("p a c -> p (a c)"),
                                                               start=(tt == 0), stop=(tt == NT - 1)),
                  reads=[kab, "ones_f"], writes=[kZ[hf]])
        for o in range(2):
            kb.pool(lambda e, t_ap=t_ap, o=o, tt=tt: e.tensor_tensor(out=HP[:, tt, o * 256:(o + 1) * 256], in0=t_ap[:, 2 * o, :], in1=t_ap[:, 2 * o + 1, :], op=ALU.add),
                    reads=[ktp], writes=[kHP + ".%d" % tt])
            kb.pool(lambda e, t_ap=t_ap, o=o, tt=tt: e.tensor_tensor(out=HM[:, tt, o * 256:(o + 1) * 256], in0=t_ap[:, 2 * o, :], in1=t_ap[:, 2 * o + 1, :], op=ALU.subtract),
                    reads=[ktp], writes=[kHM + ".%d" % tt])
    zc, kzc = ar.alloc([1024], F32, "zc")
    rz, krz = ar.alloc([512], F32, "rz")
    kb.act(lambda e: e.activation(out=zc, in_=psZ, func=AF.Copy), reads=kZ, writes=[kzc])
    for o in range(2):
        kb.dve(lambda e, o=o: e.tensor_tensor(out=rz[:, o * 256:(o + 1) * 256], in0=zc[:, o * 512:o * 512 + 256], in1=zc[:, o * 512 + 256:(o + 1) * 512], op=ALU.add),
               reads=[kzc], writes=[krz])
    kb.dve(lambda e: e.reciprocal(out=rz, in_=rz), reads=[krz], writes=[krz])
    cf, kcf = ar.alloc([2], F32, "cf")
    kb.dve(lambda e: e.memset(cf, 2.0 / N2), writes=[kcf])
    kb.dve(lambda e: e.memset(cf[0:1, 0:1], 1.0 / N2), reads=[kcf], writes=[kcf])
    Cb = [ar.alloc([NT, 128], BF16, "Cb") for _ in range(2)]
    Sb = [ar.alloc([NT, 128], BF16, "Sb") for _ in range(2)]
    hs = [ar.alloc([2, 512], BF16, "hs") for _ in range(2)]
    HS = getattr(self, "HS_" + nm)
    kHPall = [kHP + ".%d" % tt for tt in range(NT)]
    kHMall = [kHM + ".%d" % tt for tt in range(NT)]
    for ft in range(NT):
        c_ap, kc = Cb[ft % 2]
        s_ap, ks = Sb[ft % 2]
        h_ap, kh = hs[ft % 2]
        kb.dma(c_ap, self.k["dftc_" + nm][ft], writes=[kc])
        kb.dma(s_ap, self.k["dfts_" + nm][ft], writes=[ks])
        pre, kpre = self.nb()
        pim, kpim = self.nb()
        for tt in range(NT):
            kb.pe(lambda e, pre=pre, c_ap=c_ap, tt=tt: e.matmul(pre, c_ap[:, tt, :], HP[:, tt, :], start=(tt == 0), stop=(tt == NT - 1)), reads=[kc, kHPall[tt]], writes=[kpre])
        for tt in range(NT):
            kb.pe(lambda e, pim=pim, s_ap=s_ap, tt=tt: e.matmul(pim, s_ap[:, tt, :], HM[:, tt, :], start=(tt == 0), stop=(tt == NT - 1)), reads=[ks, kHMall[tt]], writes=[kpim])
        ccol = 0 if ft == 0 else 1
        kb.dve(lambda e, pre=pre, h_ap=h_ap, ccol=ccol: e.scalar_tensor_tensor(out=h_ap[:, 0, :], in0=pre, scalar=cf[:, ccol:ccol + 1], in1=rz, op0=ALU.mult, op1=ALU.mult),
               reads=[kpre, kcf, krz], writes=[kh + ".0"])
        kb.dve(lambda e, pim=pim, h_ap=h_ap, ccol=ccol: e.scalar_tensor_tensor(out=h_ap[:, 1, :], in0=pim, scalar=cf[:, ccol:ccol + 1], in1=rz, op0=ALU.mult, op1=ALU.mult),
               reads=[kpim, kcf, krz], writes=[kh + ".1"])
        kb.dma(HS[:, ft].rearrange("r p c -> p r c"), h_ap, reads=[kh + ".0", kh + ".1"], writes=["HS_" + nm])
    pmf, kpmf = ar.alloc([1], F32, "pmf")
    pmc, kpmc = ar.alloc([1], BF16, "pmc")
    kb.dma(pmf, self.k["pm1_" + nm][0:128].rearrange("(p o) -> p o", o=1), writes=[kpmf])
    kb.act(lambda e: e.activation(out=pmc, in_=pmf, func=AF.Copy), reads=[kpmf], writes=[kpmc])
    psn, kpsn = self.nb()
    for tt in range(NT):
        kb.pe(lambda e, tt=tt: e.matmul(psn[0:1, :], pmc[:, 0:1], HP[:, tt, :], start=(tt == 0), stop=(tt == NT - 1)), reads=[kpmc, kHPall[tt]], writes=[kpsn])
    hn, khn = ar.alloc([512], F32, "hn")
    kb.dve(lambda e: e.scalar_tensor_tensor(out=hn[0:1, :], in0=psn[0:1, :], scalar=1.0 / N2, in1=rz[0:1, :], op0=ALU.mult, op1=ALU.mult),
           reads=[kpsn, krz], writes=[khn])
    kb.dma(getattr(self, "HNQ_" + nm), hn[0:1, :], reads=[khn], writes=["HNQ_" + nm])
    self.phase_end()


def _hy_h2(self, l, nm, seqs, n):
    kb, ar, w = self.kb, self.arena, self.w
    NT = n // 128
    HV, HX1, HX2 = getattr(self, "HV_" + nm), getattr(self, "HX1_" + nm), getattr(self, "HX2_" + nm)
    HYO, HS, HNQ = getattr(self, "HYO_" + nm), getattr(self, "HS_" + nm), getattr(self, "HNQ_" + nm)
    U, kU = ar.alloc([NT, 512], BF16, "U")
    Zb, kZb = ar.alloc([NT, 512], BF16, "Zb")
    Y, kY = ar.alloc([2, NT, 512], BF16, "Y")
    kb.dma(U, HV.rearrange("(a p) c -> p a c", p=128), writes=[kU + ".%d" % tt for tt in range(NT)])
    Cb = [ar.alloc([NT, 128], BF16, "Cb") for _ in range(2)]
    Sb = [ar.alloc([NT, 128], BF16, "Sb") for _ in range(2)]
    hsb = [ar.alloc([2, 256], BF16, "hsb") for _ in range(2)]
    bias, kbias = ar.alloc([2, 256], F32, "bias")
    kb.dma(bias, w["hy_bias"][l].rearrange("o c -> (o c)").partition_broadcast(128), writes=[kbias])
    hnq, khnq = ar.alloc([512], F32, "hnq")
    kb.dma(hnq[0:1, :], HNQ, writes=[khnq])
    pmf, kpmf = ar.alloc([1], F32, "pmf")
    pmc, kpmc = ar.alloc([1], BF16, "pmc")
    pmrf, kpmrf = ar.alloc([128], F32, "pmrf")
    pmr, kpmr = ar.alloc([128], BF16, "pmr")
    kb.dma(pmf, self.k["pm1_" + nm][0:128].rearrange("(p o) -> p o", o=1), writes=[kpmf])
    kb.act(lambda e: e.activation(out=pmc, in_=pmf, func=AF.Copy), reads=[kpmf], writes=[kpmc])
    kb.dma(pmrf[0:1, :], self.k["pm1_" + nm][0:128].rearrange("(o t) -> o t", o=1), writes=[kpmrf])
    kb.act(lambda e: e.activation(out=pmr[0:1, :], in_=pmrf[0:1, :], func=AF.Copy), reads=[kpmrf], writes=[kpmr])
    tq = [[ar.alloc([256], F32, "tq") for _ in range(4)] for _ in range(2)]
    ynq, kynq = ar.alloc([512], BF16, "ynq")
    gt = [ar.alloc([512], BF16, "gt") for _ in range(2)]
    tb = [ar.alloc([512], F32, "tb") for _ in range(2)]
    t2b = [ar.alloc([512], F32, "t2b") for _ in range(2)]
    ostg = [ar.alloc([512], F32, "ostg") for _ in range(2)]
    for o in range(2):
        src, ksrc = (U, kU) if o == 0 else (Zb, kZb)
        ksrc_all = [ksrc + ".%d" % tt for tt in range(NT)]
        for ft in range(NT):
            c_ap, kc = Cb[ft % 2]
            s_ap, ks = Sb[ft % 2]
            h_ap, kh = hsb[ft % 2]
            kb.dma(c_ap, self.k["dftc_" + nm][ft], writes=[kc])
            kb.dma(s_ap, self.k["dfts_" + nm][ft], writes=[ks])
            kb.dma(h_ap, HS[:, ft, :, o * 256:(o + 1) * 256].rearrange("r p c -> p r c"), writes=[kh])
            pre, kpre = self.nb()
            pim, kpim = self.nb()
            for tt in range(NT):
                kb.pe(lambda e, pre=pre, c_ap=c_ap, tt=tt, src=src: e.matmul(pre, c_ap[:, tt, :], src[:, tt, :], start=(tt == 0), stop=(tt == NT - 1)),
                      reads=[kc, ksrc_all[tt]], writes=[kpre])
            for tt in range(NT):
                kb.pe(lambda e, pim=pim, s_ap=s_ap, tt=tt, src=src: e.matmul(pim, s_ap[:, tt, :], src[:, tt, :], start=(tt == 0), stop=(tt == NT - 1)),
                      reads=[ks, ksrc_all[tt]], writes=[kpim])
            for b in range(2):
                (t1, k1), (t2, k2), (t3, k3), (t4, k4) = tq[b]
                bs = slice(b * 256, (b + 1) * 256)
                kb.dve(lambda e, pre=pre, h_ap=h_ap, t1=t1, bs=bs: e.tensor_tensor(out=t1, in0=pre[:, bs], in1=h_ap[:, 0, :], op=ALU.mult), reads=[kpre, kh], writes=[k1])
                kb.dve(lambda e, pim=pim, h_ap=h_ap, t2=t2, bs=bs: e.tensor_tensor(out=t2, in0=pim[:, bs], in1=h_ap[:, 1, :], op=ALU.mult), reads=[kpim, kh], writes=[k2])
                kb.dve(lambda e, pre=pre, h_ap=h_ap, t3=t3, bs=bs: e.tensor_tensor(out=t3, in0=pre[:, bs], in1=h_ap[:, 1, :], op=ALU.mult), reads=[kpre, kh], writes=[k3])
                kb.dve(lambda e, pim=pim, h_ap=h_ap, t4=t4, bs=bs: e.tensor_tensor(out=t4, in0=pim[:, bs], in1=h_ap[:, 0, :], op=ALU.mult), reads=[kpim, kh], writes=[k4])
                kb.pool(lambda e, t1=t1, t2=t2, ft=ft, bs=bs: e.tensor_tensor(out=Y[:, 0, ft, bs], in0=t1, in1=t2, op=ALU.subtract), reads=[k1, k2], writes=[kY + ".0.%d" % ft])
                kb.pool(lambda e, t3=t3, t4=t4, ft=ft, bs=bs: e.tensor_tensor(out=Y[:, 1, ft, bs], in0=t3, in1=t4, op=ALU.add), reads=[k3, k4], writes=[kY + ".1.%d" % ft])
        psn, kpsn = self.nb()
        for tt in range(NT):
            kb.pe(lambda e, psn=psn, tt=tt, src=src: e.matmul(psn[0:1, :], pmc[:, 0:1], src[:, tt, :], start=(tt == 0), stop=(tt == NT - 1)), reads=[kpmc, ksrc_all[tt]], writes=[kpsn])
        for b in range(2):
            kb.dve(lambda e, psn=psn, b=b, o=o: e.tensor_tensor(out=ynq[0:1, b * 256:(b + 1) * 256], in0=psn[0:1, b * 256:(b + 1) * 256], in1=hnq[0:1, o * 256:(o + 1) * 256], op=ALU.mult),
                   reads=[kpsn, khnq], writes=[kynq])
        gateD = HX1 if o == 0 else HX2
        kY0 = [kY + ".0.%d" % ft for ft in range(NT)]
        kY1 = [kY + ".1.%d" % ft for ft in range(NT)]
        for j in range(NT):
            c_ap, kc = Cb[j % 2]
            s_ap, ks = Sb[j % 2]
            g_ap, kg = gt[j % 2]
            tb_ap, ktb = tb[j % 2]
            t2_ap, kt2 = t2b[j % 2]
            o_ap, ko = ostg[j % 2]
            kb.dma(c_ap, self.k["dftc_" + nm][j], writes=[kc])
            kb.dma(s_ap, self.k["dfts_" + nm][j], writes=[ks])
            kb.dma(g_ap, gateD[j * 128:(j + 1) * 128, :], writes=[kg])
            py, kpy = self.nb()
            for ft in range(NT):
                kb.pe(lambda e, py=py, c_ap=c_ap, ft=ft: e.matmul(py, c_ap[:, ft, :], Y[:, 0, ft, :], start=(ft == 0), stop=False), reads=[kc, kY0[ft]], writes=[kpy])
                kb.pe(lambda e, py=py, s_ap=s_ap, ft=ft: e.matmul(py, s_ap[:, ft, :], Y[:, 1, ft, :], start=False, stop=False), reads=[ks, kY1[ft]], writes=[kpy])
            kb.pe(lambda e, py=py: e.matmul(py, pmr[0:1, :], ynq[0:1, :], start=False, stop=True), reads=[kpmr, kynq], writes=[kpy])
            for b in range(2):
                kb.pool(lambda e, tb_ap=tb_ap, src=src, j=j, b=b, o=o: e.tensor_tensor(out=tb_ap[:, b * 256:(b + 1) * 256], in0=src[:, j, b * 256:(b + 1) * 256], in1=bias[:, o, :], op=ALU.mult),
                        reads=[ksrc_all[j], kbias], writes=[ktb])
            kb.dve(lambda e, py=py, tb_ap=tb_ap, t2_ap=t2_ap: e.tensor_tensor(out=t2_ap, in0=py, in1=tb_ap, op=ALU.add), reads=[kpy, ktb], writes=[kt2])
            if o == 0:
                kb.pool(lambda e, t2_ap=t2_ap, g_ap=g_ap, j=j: e.tensor_tensor(out=Zb[:, j, :], in0=t2_ap, in1=g_ap, op=ALU.mult), reads=[kt2, kg], writes=[kZb + ".%d" % j])
            else:
                kb.pool(lambda e, t2_ap=t2_ap, g_ap=g_ap, o_ap=o_ap: e.tensor_tensor(out=o_ap, in0=t2_ap, in1=g_ap, op=ALU.mult), reads=[kt2, kg], writes=[ko])
                kb.dma(HYO[j * 128:(j + 1) * 128, :], o_ap, reads=[ko], writes=["HYO"])
    self.phase_end()


def _hyena_phase(self, l):
    last = (l == L - 1)
    groups = [("lat", self.lat, S)] + ([] if last else [("ctx", self.ctx, CL)])
    for nm, seqs, n in groups:
        self.hy_h0(l, nm, seqs, n)
        self.hy_h1(l, nm, n)
        self.hy_h2(l, nm, seqs, n)


Prog.hy_h0 = _hy_h0
Prog.hy_h1 = _hy_h1
Prog.hy_h2 = _hy_h2
Prog.phase_hyena = _hyena_phase


def build_program(nc, dbg=(), scopes=False):
    P = Prog(nc, dbg=dbg)
    kb = P.kb
    kb.scopes = scopes
    kb.phase = "mod"
    P.declare()
    P.phase_mod()
    kb.phase = "t0"
    P.phase_t0()
    for l in range(L):
        kb.phase = "a%d" % l
        P.phase_a(l)
        kb.phase = "pool%d" % l
        P.phase_pool(l)
        kb.phase = "hy%d" % l
        P.phase_hyena(l)
        kb.phase = "attn%d" % l
        P.phase_attn(l)
        kb.phase = "c1_%d" % l
        P.phase_c1(l)
        kb.phase = "c2_%d" % l
        P.phase_c2(l)
    kb.phase = "final"
    P.phase_final()
    kb.emit()
    return P


_PROG_CACHE = {}


def kernel(**inputs):
    shared = _shared_inputs(inputs)
    nc = bass.Bass("TRN2", target_bir_lowering=False)
    P = build_program(nc)
    in_maps = []
    for core in range(NCORES):
        m = _core_inputs(inputs, core, shared)
        in_maps.append({k_: v for k_, v in m.items() if k_ in P.dram})
    res = run_bass_kernel_spmd(nc, in_maps, core_ids=list(range(NCORES)))
    out = np.concatenate([np.asarray(r["y"], dtype=np.float32) for r in res.results], axis=0)
    return out
```

```python
import math
import numpy as np
import ml_dtypes
import concourse.bass as bass
import concourse.mybir as mybir
from concourse.bass_utils import run_bass_kernel_spmd

F32 = mybir.dt.float32
BF16 = mybir.dt.bfloat16
AF = mybir.ActivationFunctionType
ALU = mybir.AluOpType
AX = mybir.AxisListType

import os
SAME_ENGINE_SYNC = os.environ.get("MK_SES", "1") == "1"
N_DMA_SEMS = 24


class _Op:
    __slots__ = ("eng", "fn", "deps", "need_inc", "cnt", "dsem", "dval", "is_dma", "idx", "phase")

    def __init__(self, eng, fn, is_dma):
        self.eng = eng
        self.fn = fn
        self.deps = []
        self.need_inc = False
        self.cnt = 0
        self.dsem = -1
        self.dval = 0
        self.is_dma = is_dma
        self.idx = 0


class KB:
    ENGS = ("pe", "act", "dve", "pool", "sp")

    def __init__(self, nc):
        self.nc = nc
        self.ops = []
        self.last_w = {}
        self.readers = {}
        self.n_dma = 0
        self._bar_start = 0
        self.phase = None
        self.scopes = False

    def _add(self, eng, fn, reads, writes, is_dma):
        op = _Op(eng, fn, is_dma)
        op.idx = len(self.ops)
        op.phase = self.phase
        deps = set()
        for k in reads:
            w = self.last_w.get(k)
            if w is not None:
                deps.add(w)
        for k in writes:
            w = self.last_w.get(k)
            if w is not None:
                deps.add(w)
            for r in self.readers.get(k, ()):
                deps.add(r)
        deps.discard(op.idx)
        op.deps = sorted(deps)
        for k in reads:
            lst = self.readers.setdefault(k, [])
            if not is_dma:
                lst[:] = [r for r in lst if self.ops[r].is_dma or self.ops[r].eng != eng]
            lst.append(op.idx)
        for k in writes:
            self.last_w[k] = op.idx
            self.readers[k] = []
        self.ops.append(op)
        return op

    def pe(self, fn, reads=(), writes=()):
        return self._add("pe", fn, reads, writes, False)

    def act(self, fn, reads=(), writes=()):
        return self._add("act", fn, reads, writes, False)

    def dve(self, fn, reads=(), writes=()):
        return self._add("dve", fn, reads, writes, False)

    def pool(self, fn, reads=(), writes=()):
        return self._add("pool", fn, reads, writes, False)

    def dma(self, out, in_, reads=(), writes=(), q="sp", **kw):
        def fn(e, out=out, in_=in_, kw=kw):
            return e.dma_start(out=out, in_=in_, **kw)
        return self._add(q, fn, reads, writes, True)

    def barrier(self):
        last = {}
        dmas = []
        for op in self.ops[self._bar_start:]:
            if op.is_dma:
                dmas.append(op.idx)
            elif op.fn is not None:
                last[op.eng] = op.idx
        deps = sorted(set(list(last.values()) + dmas))
        for e in self.ENGS:
            op = _Op(e, None, False)
            op.idx = len(self.ops)
            op.phase = self.phase
            op.deps = list(deps)
            self.ops.append(op)
        self._bar_start = len(self.ops)
        self.last_w = {}
        self.readers = {}

    def emit(self):
        nc = self.nc
        ops = self.ops
        for op in ops:
            best = {}
            for d in op.deps:
                dop = ops[d]
                if dop.is_dma or dop.fn is None:
                    continue
                if dop.eng == op.eng and not op.is_dma:
                    if dop.eng == "pe" or not SAME_ENGINE_SYNC:
                        continue
                if d > best.get(dop.eng, -1):
                    best[dop.eng] = d
            for d in best.values():
                ops[d].need_inc = True
        cnt = {e: 0 for e in self.ENGS}
        dma_cnt = [0] * N_DMA_SEMS
        nd = 0
        for op in ops:
            if op.is_dma:
                s = nd % N_DMA_SEMS
                nd += 1
                dma_cnt[s] += 1
                op.dsem = s
                op.dval = 16 * dma_cnt[s]
            elif op.need_inc:
                cnt[op.eng] += 1
                op.cnt = cnt[op.eng]
        per_eng = {e: [] for e in self.ENGS}
        for op in ops:
            per_eng[op.eng].append(op)
        self.stats = {e: len(per_eng[e]) for e in self.ENGS}
        self.stats["incs"] = dict(cnt)

        import contextlib
        with contextlib.ExitStack() as st:
            esem = {e: st.enter_context(nc.semaphore("s_" + e)) for e in self.ENGS}
            dsem = [st.enter_context(nc.semaphore("d%d" % i)) for i in range(N_DMA_SEMS)]
            block = st.enter_context(nc.Block())

            def run(eng_name, e):
                waited_e = {x: 0 for x in self.ENGS}
                waited_d = [0] * N_DMA_SEMS
                cur = None
                for op in per_eng[eng_name]:
                    if self.scopes and op.phase != cur:
                        if cur is not None:
                            nc.pop_named_scope(cur)
                        cur = op.phase
                        if cur is not None:
                            nc.push_named_scope(cur)
                    need_e = {}
                    need_d = {}
                    for d in op.deps:
                        dop = ops[d]
                        if dop.is_dma:
                            if dop.dval > waited_d[dop.dsem]:
                                need_d[dop.dsem] = max(need_d.get(dop.dsem, 0), dop.dval)
                        else:
                            if dop.eng == eng_name and not op.is_dma:
                                if eng_name == "pe" or not SAME_ENGINE_SYNC:
                                    continue
                            if dop.cnt > waited_e[dop.eng]:
                                need_e[dop.eng] = max(need_e.get(dop.eng, 0), dop.cnt)
                    if op.is_dma:
                        prev = op.dval - 16
                        if prev > waited_d[op.dsem]:
                            need_d[op.dsem] = max(need_d.get(op.dsem, 0), prev)
                    for x, v in need_e.items():
                        e.wait_ge(esem[x], v)
                        waited_e[x] = v
                    for s, v in need_d.items():
                        e.wait_ge(dsem[s], v)
                        waited_d[s] = v
                    if op.fn is None:
                        continue
                    ins = op.fn(e)
                    if op.is_dma:
                        ins.then_inc(dsem[op.dsem], 16)
                    elif op.need_inc:
                        ins.then_inc(esem[eng_name], 1)
                if self.scopes and cur is not None:
                    nc.pop_named_scope(cur)
                last = {}
                for op in per_eng[eng_name]:
                    if op.is_dma:
                        last[op.dsem] = max(last.get(op.dsem, 0), op.dval)
                for s, v in last.items():
                    if v > waited_d[s]:
                        e.wait_ge(dsem[s], v)

            @block.sync
            def _(e):
                run("sp", e)

            @block.scalar
            def _(e):
                run("act", e)

            @block.vector
            def _(e):
                run("dve", e)

            @block.gpsimd
            def _(e):
                run("pool", e)

            @block.tensor
            def _(e):
                run("pe", e)


D = 1024
S = 4096
CL = 256
L = 2
NCORES = 8
NBC = 2
NH = 8
NOPE = 64
ROPE = 32
DV = 64
QR = 256
KVR = 128
NIN = 1440
C_KV, C_KR, C_Q, C_POOL, C_HY = 0, 128, 160, 416, 672
DFF = 4096
EPS = 1e-6
SCALE = float((NOPE + ROPE) ** -0.5)
POOL_WINDOWS = (2, 4, 8, 16)
HY_EMB = 33
HY_FFN = 64
MAGIC = 12582912.0
BFNP = ml_dtypes.bfloat16


def _rope_perm():
    perm = np.zeros(32, np.int64)
    for a in range(2):
        for hf in range(2):
            for f in range(8):
                perm[a * 16 + hf * 8 + f] = a * 16 + (1 - hf) * 8 + f
    return perm


def _rope_tables():
    n = S
    rows = n // 64
    r = np.repeat(np.arange(rows), 64).astype(np.float32)
    cidx = np.tile(np.arange(64), rows).astype(np.float32)
    inv = np.power(np.float32(10000.0), -(np.arange(8, dtype=np.float32) / np.float32(8))).astype(np.float32)
    ang = np.stack([r[:, None] * inv, cidx[:, None] * inv], axis=1).astype(np.float32)
    cos = np.cos(ang).astype(np.float32)
    sin = np.sin(ang).astype(np.float32)
    cosT = np.zeros((32, n), np.float32)
    sinT = np.zeros((32, n), np.float32)
    for a in range(2):
        for hf in range(2):
            for f in range(8):
                rr = a * 16 + hf * 8 + f
                cosT[rr] = cos[:, a, f]
                sinT[rr] = (-sin[:, a, f]) if hf == 0 else sin[:, a, f]
    return cosT, sinT


def _pool_bands(n, W):
    blocks = []
    index = {}
    table = {}
    nt = n // W
    for g, w in enumerate(POOL_WINDOWS):
        t = np.arange(n)
        lo = np.clip(t - w // 2, 0, n)
        hi = np.clip(t + w // 2, 0, n)
        for i in range(nt):
            T0 = i * W
            m_lo = max((T0 - w // 2) // 128, 0)
            m_hi = min((T0 + W + w // 2 - 1) // 128, n // 128 - 1)
            for m in range(m_lo, m_hi + 1):
                blk = np.zeros((128, 512), np.float32)
                for tt in range(T0, T0 + W):
                    s0 = max(lo[tt], m * 128)
                    s1 = min(hi[tt], (m + 1) * 128)
                    if s1 > s0:
                        blk[s0 - m * 128:s1 - m * 128, tt - T0] += 1.0 / float(hi[tt] - lo[tt])
                    if m * 128 <= tt < (m + 1) * 128:
                        blk[tt - m * 128, tt - T0] -= 1.0
                if not blk.any():
                    continue
                key = blk.tobytes()
                if key not in index:
                    index[key] = len(blocks)
                    blocks.append(blk)
                table[(g, i, m)] = index[key]
    return np.stack(blocks).astype(BFNP), table


def _dft_blocks(n):
    nt = n // 128
    t = np.arange(n, dtype=np.int64)
    m = (t[:, None] * t[None, :]) % (2 * n)
    ang = m.astype(np.float64) * (2.0 * np.pi / (2 * n))
    Cm = np.cos(ang)
    Sm = -np.sin(ang)

    def tile(M):
        M4 = M.reshape(nt, 128, nt, 128)
        return np.ascontiguousarray(M4.transpose(2, 1, 0, 3)).astype(BFNP)
    return tile(Cm), tile(Sm)


def _hy_consts(n):
    f32 = np.float32
    t = np.linspace(0.0, 1.0, n, dtype=f32)[:, None]
    bands = (HY_EMB - 1) // 2
    freqs = np.linspace(1e-4, bands - 1, bands, dtype=f32)[None, :]
    wpos = (f32(2.0 * math.pi) * np.arange(n, dtype=f32)[:, None] / f32(n)).astype(f32)
    z = np.concatenate([t, np.cos(freqs * wpos), -np.sin(freqs * wpos)], axis=-1).astype(f32)
    deltas = np.abs(np.linspace(math.log(1e-2) / 1.5, math.log(1e-2) / 0.3, 256, dtype=f32))
    dec = np.exp(-t * deltas[None, :]).astype(f32)
    pm1 = np.where(np.arange(n) % 2 == 0, 1.0, -1.0).astype(f32)
    return np.ascontiguousarray(z.T), dec, pm1


_CONST_CACHE = {}


def _host_consts():
    if _CONST_CACHE:
        return _CONST_CACHE
    c = _CONST_CACHE
    c["ident_f"] = np.eye(128, dtype=np.float32)
    cosT, sinT = _rope_tables()
    c["rope_cos"] = cosT
    c["rope_sin"] = sinT
    c["rope_cos_q"] = (cosT * np.float32(SCALE)).astype(np.float32)
    c["rope_sin_q"] = (sinT * np.float32(SCALE)).astype(np.float32)
    bl, tl = _pool_bands(S, 512)
    bc, tc = _pool_bands(CL, 256)
    c["band_lat"] = bl
    c["band_ctx"] = bc
    c["_band_tab_lat"] = tl
    c["_band_tab_ctx"] = tc
    for n, nm in ((S, "lat"), (CL, "ctx")):
        Cb, Sb = _dft_blocks(n)
        c["dftc_" + nm] = Cb
        c["dfts_" + nm] = Sb
        zT, dec, pm1 = _hy_consts(n)
        c["hyz_" + nm] = zT
        c["hydec_" + nm] = dec
        c["pm1_" + nm] = pm1
    return c


class Arena:
    def __init__(self, nc, nbytes):
        self.t = nc.alloc_sbuf_tensor("arena", [128, nbytes // 2], BF16).ap()
        self.cap = nbytes
        self.off = 0
        self.cnt = 0

    def reset(self):
        self.off = 0

    def alloc(self, free_shape, dtype, name="t"):
        esz = 4 if dtype == F32 else 2
        n = 1
        for d_ in free_shape:
            n *= d_
        nb = n * esz
        off = (self.off + 63) // 64 * 64
        assert off + nb <= self.cap, "arena overflow %s %d+%d>%d" % (name, off, nb, self.cap)
        self.off = off + nb
        ap = self.t[:, off // 2:(off + nb) // 2]
        if dtype == F32:
            ap = ap.bitcast(F32)
        if len(free_shape) == 2:
            ap = ap.rearrange("p (a b) -> p a b", b=free_shape[1])
        elif len(free_shape) == 3:
            ap = ap.rearrange("p (a b c) -> p a b c", b=free_shape[1], c=free_shape[2])
        self.cnt += 1
        return ap, "%s#%d" % (name, self.cnt)


class Seq:
    def __init__(self, name, n, v, kind, b):
        self.name = name
        self.n = n
        self.v = v
        self.kind = kind
        self.b = b
        self.W = 512 if n >= 512 else n
        self.nt = n // self.W


class Prog:
    def __init__(self, nc, dbg=()):
        self.nc = nc
        self.kb = KB(nc)
        self.dbg = set(dbg)
        self.dram = {}
        self.consts = _host_consts()

    def din(self, name, shape, dtype=F32):
        t = self.nc.dram_tensor(name, list(shape), dtype, kind="ExternalInput").ap()
        self.dram[name] = t
        return t

    def dscr(self, name, shape, dtype=F32):
        kind = "ExternalOutput" if name in self.dbg else "Internal"
        t = self.nc.dram_tensor(name, list(shape), dtype, kind=kind).ap()
        self.dram[name] = t
        return t

    def declare(self):
        nc = self.nc
        c = self.consts
        self.x_in = self.din("x", [NBC, S, D])
        self.ctx_in = self.din("ctx", [NBC, CL, D])
        self.cv_in = self.din("cv", [3, D])
        self.y_out = nc.dram_tensor("y", [NBC, S, D], F32, kind="ExternalOutput").ap()
        w = {}
        w["w_mod"] = self.din("w_mod", [L, D, 6 * D])
        w["b_mod"] = self.din("b_mod", [L, 6 * D])
        w["g_mix"] = self.din("g_mix", [L, D])
        w["g_mlp"] = self.din("g_mlp", [L, D])
        w["w_in"] = self.din("w_in", [L, D, NIN])
        w["w_in_rot"] = self.din("w_in_rot", [L, D, ROPE])
        w["g_q"] = self.din("g_q", [L, QR])
        w["w_q_up"] = self.din("w_q_up", [L, QR, NH * 96])
        w["w_q_rot"] = self.din("w_q_rot", [L, QR, NH * 96])
        w["g_kv"] = self.din("g_kv", [L, KVR])
        w["w_kv_up"] = self.din("w_kv_up", [L, KVR, NH * 128])
        w["pool_w"] = self.din("pool_w", [L, 4, 64, 64])
        w["pool_scale"] = self.din("pool_scale", [L, 256])
        w["hy_conv_w"] = self.din("hy_conv_w", [L, 3, 768])
        w["hy_conv_b"] = self.din("hy_conv_b", [L, 768])
        w["hy_f_w1"] = self.din("hy_f_w1", [L, HY_EMB, HY_FFN])
        w["hy_f_b1"] = self.din("hy_f_b1", [L, HY_FFN])
        w["hy_f_freq1"] = self.din("hy_f_freq1", [L, HY_FFN])
        w["hy_f_w2"] = self.din("hy_f_w2", [L, HY_FFN, HY_FFN])
        w["hy_f_b2"] = self.din("hy_f_b2", [L, HY_FFN])
        w["hy_f_freq2"] = self.din("hy_f_freq2", [L, HY_FFN])
        w["hy_f_w3"] = self.din("hy_f_w3", [L, HY_FFN, 1024])
        w["hy_bias"] = self.din("hy_bias", [L, 2, 256])
        w["g_out"] = self.din("g_out", [L, D])
        w["w_out"] = self.din("w_out", [L, D, D])
        w["w_mlp1"] = self.din("w_mlp1", [L, D, DFF])
        w["w_mlp2"] = self.din("w_mlp2", [L, DFF, D])
        w["g_final"] = self.din("g_final", [D])
        self.w = w
        k = {}
        for name, arr in c.items():
            if name.startswith("_"):
                continue
            dt_ = BF16 if arr.dtype == BFNP else F32
            k[name] = self.din(name, arr.shape, dt_)
        self.k = k
        self.lat = [Seq("b%d" % b, S, b, "lat", b) for b in range(NBC)]
        self.ctx = [Seq("c%d" % b, CL, 2, "ctx", b) for b in range(NBC)]
        for s in self.lat + self.ctx:
            n = s.n
            s.XT = self.dscr("XT_" + s.name, [8, 128, n])
            s.PKV = self.dscr("PKV_" + s.name, [128, n], BF16)
            s.KR = self.dscr("KR_" + s.name, [32, n], BF16)
            s.PQ = self.dscr("PQ_" + s.name, [2, 128, n], BF16)
            s.POOLU = self.dscr("POOLU_" + s.name, [n, 256], BF16)
            s.HYP = self.dscr("HYP_" + s.name, [6, 128, n + 2])
            s.ATT = self.dscr("ATT_" + s.name, [n, 512])
            s.POOLO = self.dscr("POOLO_" + s.name, [n, 256])
        for nm, n in (("lat", S), ("ctx", CL)):
            setattr(self, "HV_" + nm, self.dscr("HV_" + nm, [n, 512], BF16))
            setattr(self, "HX1_" + nm, self.dscr("HX1_" + nm, [n, 512], BF16))
            setattr(self, "HX2_" + nm, self.dscr("HX2_" + nm, [n, 512], BF16))
            setattr(self, "HYO_" + nm, self.dscr("HYO_" + nm, [n, 512]))
            setattr(self, "HS_" + nm, self.dscr("HS_" + nm, [2, n // 128, 128, 512], BF16))
            setattr(self, "HNQ_" + nm, self.dscr("HNQ_" + nm, [1, 512]))
        A = lambda name, shape, dt_: nc.alloc_sbuf_tensor(name, shape, dt_).ap()
        self.ident_f = A("ident_f_sb", [128, 128], F32)
        self.ident_b = A("ident_b", [128, 128], BF16)
        self.ones_b = A("ones_b", [128, 128], BF16)
        self.ones_f = A("ones_f", [128, 128], F32)
        self.eps = A("eps_t", [128, 1], F32)
        self.modT = A("modT", [128, L * 6 * 8 * 3], F32).rearrange("p (l k j v) -> p l k j v", l=L, k=6, j=8)
        self.ps2 = [nc.alloc_psum_tensor("ps2_%d" % i, [128, 1024], F32).ap() for i in range(4)]
        self.arena = Arena(nc, 205 * 1024)
        kb = self.kb
        kb.dma(self.ident_f, self.k["ident_f"], writes=["ident_f"])
        kb.act(lambda e: e.activation(out=self.ident_b, in_=self.ident_f, func=AF.Copy), reads=["ident_f"], writes=["ident_b"])
        kb.dve(lambda e: e.memset(self.ones_b, 1.0), writes=["ones_b"])
        kb.dve(lambda e: e.memset(self.ones_f, 1.0), writes=["ones_f"])
        kb.dve(lambda e: e.memset(self.eps, EPS), writes=["eps"])

    def bank(self, kidx):
        return self.ps2[kidx // 2][:, (kidx % 2) * 512:(kidx % 2 + 1) * 512], "pb%d" % kidx

    def mod(self, l, kind, v):
        return self.modT[:, l, kind, :, v]

    def phase_end(self):
        self.kb.barrier()
        self.arena.reset()

    def phase_mod(self):
        kb, ar, w = self.kb, self.arena, self.w
        cvs, kcvs = ar.alloc([D], F32, "cvs")
        sil, ksil = ar.alloc([D], F32, "sil")
        silT, ksilT = ar.alloc([8, 3], F32, "silT")
        kb.dma(cvs[0:3, :], self.cv_in, writes=[kcvs])
        kb.act(lambda e: e.activation(out=sil[0:3, :], in_=cvs[0:3, :], func=AF.Silu), reads=[kcvs], writes=[ksil])
        pb, kpb = self.bank(0)
        for j in range(8):
            kb.pe(lambda e, j=j: e.transpose(pb[:, j * 3:(j + 1) * 3], sil[0:3, j * 128:(j + 1) * 128], self.ident_f[0:3, 0:3]),
                  reads=[ksil, "ident_f"], writes=[kpb])
        kb.dve(lambda e: e.tensor_copy(out=silT.rearrange("p j v -> p (j v)"), in_=pb[:, 0:24]), reads=[kpb], writes=[ksilT])
        wbuf = [ar.alloc([8, 512], F32, "wmod") for _ in range(2)]
        modrow, kmodrow = ar.alloc([6 * D], F32, "modrow")
        brow, kbrow = ar.alloc([6 * D], F32, "brow")
        gbc, kgbc = ar.alloc([2, D], F32, "gbc")
        for l in range(L):
            kb.dma(brow[0:3, :], w["b_mod"][l].partition_broadcast(3), writes=[kbrow])
            kb.dma(gbc[0:3, 0, :], w["g_mix"][l].partition_broadcast(3), writes=[kgbc])
            kb.dma(gbc[0:3, 1, :], w["g_mlp"][l].partition_broadcast(3), writes=[kgbc])
            for ncn in range(12):
                wb, kwb = wbuf[ncn % 2]
                kb.dma(wb, w["w_mod"][l][:, ncn * 512:(ncn + 1) * 512].rearrange("(j p) n -> p j n", p=128), writes=[kwb])
                pm, kpm = self.bank(1 + ncn % 2)
                for j in range(8):
                    kb.pe(lambda e, j=j, wb=wb, pm=pm: e.matmul(pm[0:3, :], silT[:, j, :], wb[:, j, :], start=(j == 0), stop=(j == 7)),
                          reads=[ksilT, kwb], writes=[kpm])
                kb.dve(lambda e, pm=pm, ncn=ncn: e.tensor_tensor(out=modrow[0:3, ncn * 512:(ncn + 1) * 512], in0=pm[0:3, :],
                                                                in1=brow[0:3, ncn * 512:(ncn + 1) * 512], op=ALU.add),
                       reads=[kpm, kbrow], writes=[kmodrow])
            for (kind, gi) in ((1, 0), (4, 1)):
                sl = modrow[0:3, kind * D:(kind + 1) * D]
                kb.dve(lambda e, sl=sl, gi=gi: e.scalar_tensor_tensor(out=sl, in0=sl, scalar=1.0, in1=gbc[0:3, gi, :], op0=ALU.add, op1=ALU.mult),
                       reads=[kmodrow, kgbc], writes=[kmodrow])
            pt, kpt = self.bank(3)
            for kind in range(6):
                for j in range(8):
                    col = (kind * 8 + j) * 3
                    kb.pe(lambda e, kind=kind, j=j, col=col: e.transpose(pt[:, col:col + 3], modrow[0:3, kind * D + j * 128: kind * D + (j + 1) * 128],
                                                                        self.ident_f[0:3, 0:3]),
                          reads=[kmodrow, "ident_f"], writes=[kpt])
            kb.dve(lambda e, l=l: e.tensor_copy(out=self.modT[:, l].rearrange("p k j v -> p (k j v)"), in_=pt[:, 0:144]),
                   reads=[kpt], writes=["modT"])
        self.phase_end()

    def nb(self):
        self._nb = (getattr(self, "_nb", -1) + 1) % 8
        return self.bank(self._nb)

    def evac(self, i, fn_act, fn_dve, reads, writes):
        if i % 2 == 0:
            self.kb.act(fn_act, reads=reads, writes=writes)
        else:
            self.kb.dve(fn_dve, reads=reads, writes=writes)

    def phase_t0(self):
        kb, ar = self.kb, self.arena
        for s in self.lat + self.ctx:
            src = self.x_in[s.b] if s.kind == "lat" else self.ctx_in[s.b]
            W, nsub = s.W, s.W // 128
            xin = [ar.alloc([nsub, D], F32, "xin") for _ in range(2)]
            xt = [ar.alloc([8, W], F32, "xt") for _ in range(2)]
            for i in range(s.nt):
                xi, kxi = xin[i % 2]
                xo, kxo = xt[i % 2]
                kb.dma(xi, src[i * W:(i + 1) * W, :].rearrange("(a p) d -> p a d", p=128), writes=[kxi])
                for j in range(8):
                    pb, kpb = self.nb()
                    for sub in range(nsub):
                        kb.pe(lambda e, pb=pb, xi=xi, j=j, sub=sub: e.transpose(pb[:, sub * 128:(sub + 1) * 128], xi[:, sub, j * 128:(j + 1) * 128], self.ident_f),
                              reads=[kxi, "ident_f"], writes=[kpb])
                    self.evac(j, lambda e, pb=pb, xo=xo, j=j, W=W: e.activation(out=xo[:, j, :], in_=pb[:, 0:W], func=AF.Copy),
                              lambda e, pb=pb, xo=xo, j=j, W=W: e.tensor_copy(out=xo[:, j, :], in_=pb[:, 0:W]),
                              reads=[kpb], writes=[kxo + ".%d" % j])
                kb.dma(s.XT[:, :, i * W:(i + 1) * W].rearrange("j p t -> p j t"), xo,
                       reads=[kxo + ".%d" % j for j in range(8)], writes=["XT_" + s.name])
            self.phase_end()

    def fm_rstd(self, chunks, nfeat, W, sq, ksq, rs, krs, extra_sq=None):
        kb = self.kb
        pss, kpss = self.nb()
        n = len(chunks)
        for ci, (ap, key, P) in enumerate(chunks):
            kb.act(lambda e, ap=ap, ci=ci, P=P: e.activation(out=sq[0:P, ci, 0:W], in_=ap, func=AF.Square),
                   reads=[key], writes=[ksq + ".%d" % ci] + (extra_sq(ci) if extra_sq else []))
            kb.pe(lambda e, ci=ci, P=P: e.matmul(pss[:, 0:W], self.ones_b[0:P, :], sq[0:P, ci, 0:W], start=(ci == 0), stop=(ci == n - 1)),
                  reads=[ksq + ".%d" % ci, "ones_b"], writes=[kpss])
        kb.act(lambda e: e.activation(out=rs[:, 0:W], in_=pss[:, 0:W], func=AF.Sqrt, bias=self.eps[:, 0:1], scale=1.0 / nfeat),
               reads=[kpss, "eps"], writes=[krs])
        kb.dve(lambda e: e.reciprocal(out=rs[:, 0:W], in_=rs[:, 0:W]), reads=[krs], writes=[krs])

    def mod_norm(self, xt, kxt, W, gm, sh, sq, ksq, rs, krs, tmp, ktmp, hT, khT, extra=None, extra_sq=None):
        kb = self.kb
        self.fm_rstd([(xt[:, j, 0:W], kxt + ".%d" % j, 128) for j in range(8)], D, W, sq, ksq, rs, krs, extra_sq=extra_sq)
        for j in range(8):
            kb.dve(lambda e, j=j: e.scalar_tensor_tensor(out=tmp[:, j, 0:W], in0=xt[:, j, 0:W], scalar=gm[:, j:j + 1], in1=rs[:, 0:W],
                                                         op0=ALU.mult, op1=ALU.mult),
                   reads=[kxt + ".%d" % j, krs, "modT"], writes=[ktmp + ".%d" % j] + (extra(j) if extra else []))
            kb.act(lambda e, j=j: e.activation(out=hT[:, j, 0:W], in_=tmp[:, j, 0:W], func=AF.Identity, bias=sh[:, j:j + 1], scale=1.0),
                   reads=[ktmp + ".%d" % j, "modT"], writes=[khT + ".%d" % j])

    def load_cast_rows(self, dst, src2d, nj, split=1):
        ap, key = dst
        n = src2d.shape[-1]
        step = n // split
        for j in range(nj):
            for sp_ in range(split):
                self.kb.dma(ap[:, j, sp_ * step:(sp_ + 1) * step], src2d[j * 128:(j + 1) * 128, sp_ * step:(sp_ + 1) * step],
                            writes=[key + ".%d" % j], q="pool")

    def phase_a(self, l):
        kb, ar, w = self.kb, self.arena, self.w
        last = (l == L - 1)
        win = ar.alloc([8, NIN], BF16, "win")
        wrot = ar.alloc([8, ROPE], BF16, "wrot")
        self.load_cast_rows(win, w["w_in"][l], 8)
        self.load_cast_rows(wrot, w["w_in_rot"][l], 8)
        win_ap, kwin = win
        wrot_ap, kwrot = wrot
        kwin_all = [kwin + ".%d" % j for j in range(8)]
        kwrot_all = [kwrot + ".%d" % j for j in range(8)]
        gkv, kgkv = ar.alloc([1], F32, "gkv")
        gq, kgq = ar.alloc([2], F32, "gq")
        kb.dma(gkv, w["g_kv"][l].rearrange("(p o) -> p o", o=1), writes=[kgkv])
        for c_ in range(2):
            kb.dma(gq[:, c_:c_ + 1], w["g_q"][l][c_ * 128:(c_ + 1) * 128].rearrange("(p o) -> p o", o=1), writes=[kgq])
        zt, kzt = ar.alloc([6, 1], F32, "zt")
        kb.dve(lambda e: e.memset(zt, 0.0), writes=[kzt])
        WM = 512
        xts = [ar.alloc([8, WM], F32, "xt") for _ in range(2)]
        sq, ksq = ar.alloc([8, WM], BF16, "sq")
        rs, krs = ar.alloc([WM], F32, "rs")
        tmp, ktmp = ar.alloc([8, WM], F32, "tmp")
        hTs = [ar.alloc([8, WM], BF16, "hT") for _ in range(2)]
        sq2, ksq2 = ar.alloc([2, WM], BF16, "sq2")
        rs2, krs2 = ar.alloc([WM], F32, "rs2")
        pkvn = [ar.alloc([WM], BF16, "pkvn") for _ in range(2)]
        krs_ = [ar.alloc([WM], BF16, "kr") for _ in range(2)]
        pqn = [ar.alloc([2, WM], BF16, "pqn") for _ in range(2)]
        poolu = [ar.alloc([4, 256], BF16, "poolu") for _ in range(2)]
        hyp = [ar.alloc([6, WM], F32, "hyp") for _ in range(2)]
        ropec = [ar.alloc([WM], F32, "ropec") for _ in range(2)]
        ropes = [ar.alloc([WM], F32, "ropes") for _ in range(2)]
        rt1, krt1 = ar.alloc([WM], F32, "rt1")
        rt2, krt2 = ar.alloc([WM], F32, "rt2")
        tiles = []
        for s in self.lat + self.ctx:
            full = not (last and s.kind == "ctx")
            if full:
                for (a0, a1) in ((0, 1), (s.n + 1, s.n + 2)):
                    kb.dma(s.HYP[:, :, a0:a1].rearrange("c p o -> p c o"), zt, reads=[kzt], writes=["HYP_" + s.name], allow_slow_non_contiguous=True)
            for i in range(s.nt):
                tiles.append((s, i, full))

        def front(it):
            s, i, full = tiles[it]
            W = s.W
            gm, sh = self.mod(l, 1, s.v), self.mod(l, 0, s.v)
            xt, kxt = xts[it % 2]
            hT, khT = hTs[it % 2]
            t0 = i * W
            kb.dma(xt[:, :, 0:W], s.XT[:, :, t0:t0 + W].rearrange("j p t -> p j t"), reads=["XT_" + s.name],
                   writes=[kxt + ".%d" % j for j in range(8)])
            self.mod_norm(xt, kxt, W, gm, sh, sq, ksq, rs, krs, tmp, ktmp, hT, khT)

        def back(it):
            s, i, full = tiles[it]
            W, nsub = s.W, s.W // 128
            hT, khT = hTs[it % 2]
            t0 = i * W
            if True:
                khT_all = [khT + ".%d" % j for j in range(8)]

                def proj_fm(col0, ncol, dstps, wt=win_ap, kw=kwin_all, hT=hT, khT_all=khT_all, W=W):
                    for j in range(8):
                        kb.pe(lambda e, j=j: e.matmul(dstps[0:ncol, 0:W], wt[:, j, col0:col0 + ncol], hT[:, j, 0:W], start=(j == 0), stop=(j == 7)),
                              reads=[kw[j], khT_all[j]], writes=[dstps_key[0]])
                pkv, kpkv = self.nb()
                dstps_key = [kpkv]
                proj_fm(C_KV, 128, pkv)
                self.fm_rstd([(pkv[:, 0:W], kpkv, 128)], KVR, W, sq2, ksq2, rs2, krs2)
                o_, ko_ = pkvn[it % 2]
                kb.dve(lambda e, o_=o_, pkv=pkv, W=W: e.scalar_tensor_tensor(out=o_[:, 0:W], in0=pkv[:, 0:W], scalar=gkv[:, 0:1], in1=rs2[:, 0:W],
                                                                            op0=ALU.mult, op1=ALU.mult),
                       reads=[kpkv, krs2, kgkv], writes=[ko_])
                kb.dma(s.PKV[:, t0:t0 + W], o_[:, 0:W], reads=[ko_], writes=["PKV_" + s.name])
                pka, kpka = self.nb()
                dstps_key = [kpka]
                proj_fm(C_KR, ROPE, pka)
                o_, ko_ = krs_[it % 2]
                if s.kind == "lat":
                    pkb, kpkb = self.nb()
                    dstps_key = [kpkb]
                    proj_fm(0, ROPE, pkb, wt=wrot_ap, kw=kwrot_all)
                    rc, krc = ropec[it % 2]
                    rsn, krsn = ropes[it % 2]
                    kb.dma(rc[0:32, 0:W], self.k["rope_cos"][:, t0:t0 + W], writes=[krc])
                    kb.dma(rsn[0:32, 0:W], self.k["rope_sin"][:, t0:t0 + W], writes=[krsn])
                    kb.dve(lambda e, pka=pka, rc=rc, W=W: e.tensor_tensor(out=rt1[0:32, 0:W], in0=pka[0:32, 0:W], in1=rc[0:32, 0:W], op=ALU.mult),
                           reads=[kpka, krc], writes=[krt1])
                    kb.dve(lambda e, pkb=pkb, rsn=rsn, W=W: e.tensor_tensor(out=rt2[0:32, 0:W], in0=pkb[0:32, 0:W], in1=rsn[0:32, 0:W], op=ALU.mult),
                           reads=[kpkb, krsn], writes=[krt2])
                    kb.dve(lambda e, o_=o_, W=W: e.tensor_tensor(out=o_[0:32, 0:W], in0=rt1[0:32, 0:W], in1=rt2[0:32, 0:W], op=ALU.add),
                           reads=[krt1, krt2], writes=[ko_])
                else:
                    kb.act(lambda e, o_=o_, pka=pka, W=W: e.activation(out=o_[0:32, 0:W], in_=pka[0:32, 0:W], func=AF.Copy),
                           reads=[kpka], writes=[ko_])
                kb.dma(s.KR[:, t0:t0 + W], o_[0:32, 0:W], reads=[ko_], writes=["KR_" + s.name])
                if not full:
                    return
                pq = [self.nb() for _ in range(2)]
                for c_ in range(2):
                    dstps_key = [pq[c_][1]]
                    proj_fm(C_Q + c_ * 128, 128, pq[c_][0])
                self.fm_rstd([(pq[c_][0][:, 0:W], pq[c_][1], 128) for c_ in range(2)], QR, W, sq2, ksq2, rs2, krs2)
                o_, ko_ = pqn[it % 2]
                for c_ in range(2):
                    kb.dve(lambda e, o_=o_, c_=c_, pq=pq, W=W: e.scalar_tensor_tensor(out=o_[:, c_, 0:W], in0=pq[c_][0][:, 0:W], scalar=gq[:, c_:c_ + 1],
                                                                                  in1=rs2[:, 0:W], op0=ALU.mult, op1=ALU.mult),
                           reads=[pq[c_][1], krs2, kgq], writes=[ko_ + ".%d" % c_])
                kb.dma(s.PQ[:, :, t0:t0 + W].rearrange("c p t -> p c t"), o_[:, :, 0:W], reads=[ko_ + ".0", ko_ + ".1"], writes=["PQ_" + s.name])
                o_, ko_ = poolu[it % 2]
                for sub in range(nsub):
                    pp, kpp = self.nb()
                    for j in range(8):
                        kb.pe(lambda e, j=j, sub=sub, pp=pp, hT=hT: e.matmul(pp[:, 0:256], hT[:, j, sub * 128:(sub + 1) * 128], win_ap[:, j, C_POOL:C_POOL + 256],
                                                                            start=(j == 0), stop=(j == 7)),
                              reads=[kwin_all[j], khT_all[j]], writes=[kpp])
                    self.evac(sub, lambda e, o_=o_, pp=pp, sub=sub: e.activation(out=o_[:, sub, :], in_=pp[:, 0:256], func=AF.Copy),
                              lambda e, o_=o_, pp=pp, sub=sub: e.tensor_copy(out=o_[:, sub, :], in_=pp[:, 0:256]),
                              reads=[kpp], writes=[ko_ + ".%d" % sub])
                kb.dma(s.POOLU[t0:t0 + W, :].rearrange("(a p) c -> p a c", p=128), o_[:, 0:nsub, :],
                       reads=[ko_ + ".%d" % sub for sub in range(nsub)], writes=["POOLU_" + s.name])
                o_, ko_ = hyp[it % 2]
                for c6 in range(6):
                    ph, kph = self.nb()
                    dstps_key = [kph]
                    proj_fm(C_HY + c6 * 128, 128, ph)
                    self.evac(c6, lambda e, o_=o_, ph=ph, c6=c6, W=W: e.activation(out=o_[:, c6, 0:W], in_=ph[:, 0:W], func=AF.Copy),
                              lambda e, o_=o_, ph=ph, c6=c6, W=W: e.tensor_copy(out=o_[:, c6, 0:W], in_=ph[:, 0:W]),
                              reads=[kph], writes=[ko_ + ".%d" % c6])
                kb.dma(s.HYP[:, :, 1 + t0:1 + t0 + W].rearrange("c p t -> p c t"), o_[:, :, 0:W],
                       reads=[ko_ + ".%d" % c6 for c6 in range(6)], writes=["HYP_" + s.name])

        front(0)
        for it in range(len(tiles)):
            if it + 1 < len(tiles):
                front(it + 1)
            back(it)
        self.phase_end()


def _shared_inputs(inp):
    f = lambda a: np.ascontiguousarray(np.asarray(a, dtype=np.float32))
    sh = {}
    for k_ in ("w_mod", "b_mod", "g_mix", "g_mlp", "w_in", "g_q", "g_kv", "w_kv_up", "pool_w", "pool_scale", "hy_conv_w",
               "hy_conv_b", "hy_f_w1", "hy_f_b1", "hy_f_freq1", "hy_f_w2", "hy_f_b2", "hy_f_freq2", "hy_f_w3", "hy_bias",
               "g_out", "w_out", "w_mlp1", "w_mlp2", "g_final"):
        sh[k_] = f(inp[k_])
    perm = _rope_perm()
    w_in = sh["w_in"]
    sh["w_in_rot"] = np.ascontiguousarray(w_in[:, :, C_KR:C_KR + ROPE][:, :, perm])
    wq = f(inp["w_q_up"]).reshape(L, QR, NH, 96)
    sh["w_q_up"] = np.ascontiguousarray(wq.reshape(L, QR, NH * 96))
    wrot = np.zeros_like(wq)
    wrot[..., NOPE:] = wq[..., NOPE:][..., perm]
    sh["w_q_rot"] = np.ascontiguousarray(wrot.reshape(L, QR, NH * 96))
    for name, arr in _host_consts().items():
        if not name.startswith("_"):
            sh[name] = arr
    return sh


def _core_inputs(inp, core, shared):
    b0 = core * NBC
    m = dict(shared)
    m["x"] = np.ascontiguousarray(np.asarray(inp["x"][b0:b0 + NBC], dtype=np.float32))
    m["ctx"] = np.ascontiguousarray(np.asarray(inp["ctx"][b0:b0 + NBC], dtype=np.float32))
    cv = np.concatenate([np.asarray(inp["c"][b0:b0 + NBC], dtype=np.float32), np.asarray(inp["c_ctx"], dtype=np.float32)[None, :]], axis=0)
    m["cv"] = np.ascontiguousarray(cv)
    return m


def _attn_phase(self, l):
    kb, ar, w = self.kb, self.arena, self.w
    last = (l == L - 1)
    NK = CL + S
    NKT = NK // 128
    wkv = ar.alloc([1, NH * 128], BF16, "wkv")
    self.load_cast_rows(wkv, w["w_kv_up"][l], 1)
    wq = ar.alloc([2, NH * 96], BF16, "wq")
    wqr = ar.alloc([2, NH * 96], BF16, "wqr")
    self.load_cast_rows(wq, w["w_q_up"][l], 2)
    self.load_cast_rows(wqr, w["w_q_rot"][l], 2)
    wkv_ap, kwkv = wkv[0], wkv[1] + ".0"
    wq_ap, wqr_ap = wq[0], wqr[0]
    kwq = [wq[1] + ".0", wq[1] + ".1"]
    kwqr = [wqr[1] + ".0", wqr[1] + ".1"]
    cosq, kcosq = ar.alloc([S], F32, "cosq")
    sinq, ksinq = ar.alloc([S], F32, "sinq")
    kb.dma(cosq[64:96, :], self.k["rope_cos_q"], writes=[kcosq])
    kb.dma(sinq[64:96, :], self.k["rope_sin_q"], writes=[ksinq])
    pkv_b = [ar.alloc([NK], BF16, "pkv") for _ in range(2)]
    pq_b = [ar.alloc([2, S], BF16, "pq") for _ in range(2)]
    pqc_b = [ar.alloc([2, CL], BF16, "pqc") for _ in range(2)]
    KT = [ar.alloc([NK], BF16, "KT") for _ in range(2)]
    VA = [ar.alloc([NKT, 128], BF16, "VA") for _ in range(2)]
    QT = [ar.alloc([S], BF16, "QT") for _ in range(2)]
    QTc = [ar.alloc([CL], BF16, "QTc") for _ in range(2)]
    PT = [ar.alloc([1024], BF16, "PT") for _ in range(2)]
    oT = [ar.alloc([512], F32, "oT") for _ in range(2)]
    rc = [ar.alloc([4, 1], F32, "rc") for _ in range(2)]
    stg = [ar.alloc([4, 64], F32, "stg") for _ in range(2)]
    rt1, krt1 = ar.alloc([512], F32, "rt1")
    rt2, krt2 = ar.alloc([512], F32, "rt2")
    for hb in range(2):
        kb.pool(lambda e, hb=hb: e.memset(VA[hb][0][:, :, 64:128], 1.0), writes=[VA[hb][1] + ".ones"])
    misc = [self.bank(6), self.bank(7)]
    mi = [0]

    def mbank():
        mi[0] += 1
        return misc[mi[0] % 2]
    cnt_o = [0]

    def batch_loads(b):
        lat, cx = self.lat[b], self.ctx[b]
        (pkv, kpkv), (pq, kpq), (pqc, kpqc) = pkv_b[b % 2], pq_b[b % 2], pqc_b[b % 2]
        kb.dma(pkv[:, 0:CL], cx.PKV, reads=["PKV_" + cx.name], writes=[kpkv])
        kb.dma(pkv[:, CL:NK], lat.PKV, reads=["PKV_" + lat.name], writes=[kpkv])
        kb.dma(pq, lat.PQ.rearrange("c p t -> p c t"), reads=["PQ_" + lat.name], writes=[kpq])
        if not last:
            kb.dma(pqc, cx.PQ.rearrange("c p t -> p c t"), reads=["PQ_" + cx.name], writes=[kpqc])

    def kr_load(b, hb):
        lat, cx = self.lat[b], self.ctx[b]
        kb.dma(KT[hb][0][64:96, 0:CL], cx.KR, reads=["KR_" + cx.name], writes=[KT[hb][1] + ".r"])
        kb.dma(KT[hb][0][64:96, CL:NK], lat.KR, reads=["KR_" + lat.name], writes=[KT[hb][1] + ".r"])

    def build_steps(b, h):
        lat, cx = self.lat[b], self.ctx[b]
        (pkv, kpkv), (pq, kpq), (pqc, kpqc) = pkv_b[b % 2], pq_b[b % 2], pqc_b[b % 2]
        hb = h % 2
        kt_ap, kkt = KT[hb]
        va_ap, kva = VA[hb]
        steps = []

        def k_chunk(kc):
            k0 = kc * 512
            kw_ = min(512, NK - k0)
            pb, kpb = mbank()
            kb.pe(lambda e: e.matmul(pb[0:64, 0:kw_], wkv_ap[:, 0, h * 128:h * 128 + 64], pkv[:, k0:k0 + kw_], start=True, stop=True),
                  reads=[kwkv, kpkv], writes=[kpb])
            kb.dve(lambda e: e.tensor_copy(out=kt_ap[0:64, k0:k0 + kw_], in_=pb[0:64, 0:kw_]), reads=[kpb], writes=[kkt + ".n"])

        def v_group(g8):
            k0 = g8 * 8
            ng = min(8, NKT - k0)
            pb, kpb = mbank()
            for i in range(ng):
                kb.pe(lambda e, i=i: e.matmul(pb[:, i * 64:(i + 1) * 64], pkv[:, (k0 + i) * 128:(k0 + i + 1) * 128],
                                              wkv_ap[:, 0, h * 128 + 64:h * 128 + 128], start=True, stop=True),
                      reads=[kwkv, kpkv], writes=[kpb])
            kb.dve(lambda e: e.tensor_copy(out=va_ap[:, k0:k0 + ng, 0:64], in_=pb[:, 0:ng * 64].rearrange("p (a c) -> p a c", c=64)),
                   reads=[kpb], writes=[kva + ".v"])

        def q_chunk(qc, QW, rope, pq_ap, kpq_, qt_ap, kqt):
            q0 = qc * QW
            pa, kpa = mbank()
            for c_ in range(2):
                kb.pe(lambda e, c_=c_: e.matmul(pa[0:96, 0:QW], wq_ap[:, c_, h * 96:(h + 1) * 96], pq_ap[:, c_, q0:q0 + QW], start=(c_ == 0), stop=(c_ == 1)),
                      reads=[kwq[c_], kpq_], writes=[kpa])
            kb.dve(lambda e: e.tensor_scalar(out=qt_ap[0:64, q0:q0 + QW], in0=pa[0:64, 0:QW], scalar1=SCALE, scalar2=None, op0=ALU.mult),
                   reads=[kpa], writes=[kqt + ".n"])
            if rope:
                pb, kpb = mbank()
                for c_ in range(2):
                    kb.pe(lambda e, c_=c_: e.matmul(pb[0:96, 0:QW], wqr_ap[:, c_, h * 96:(h + 1) * 96], pq_ap[:, c_, q0:q0 + QW], start=(c_ == 0), stop=(c_ == 1)),
                          reads=[kwqr[c_], kpq_], writes=[kpb])
                kb.dve(lambda e: e.tensor_tensor(out=rt1[64:96, 0:QW], in0=pa[64:96, 0:QW], in1=cosq[64:96, q0:q0 + QW], op=ALU.mult),
                       reads=[kpa, kcosq], writes=[krt1])
                kb.dve(lambda e: e.tensor_tensor(out=rt2[64:96, 0:QW], in0=pb[64:96, 0:QW], in1=sinq[64:96, q0:q0 + QW], op=ALU.mult),
                       reads=[kpb, ksinq], writes=[krt2])
                kb.dve(lambda e: e.tensor_tensor(out=qt_ap[64:96, q0:q0 + QW], in0=rt1[64:96, 0:QW], in1=rt2[64:96, 0:QW], op=ALU.add),
                       reads=[krt1, krt2], writes=[kqt + ".r"])
            else:
                kb.dve(lambda e: e.tensor_scalar(out=qt_ap[64:96, q0:q0 + QW], in0=pa[64:96, 0:QW], scalar1=SCALE, scalar2=None, op0=ALU.mult),
                       reads=[kpa], writes=[kqt + ".r"])
        for kc in range((NK + 511) // 512):
            steps.append(lambda kc=kc: k_chunk(kc))
        for g8 in range((NKT + 7) // 8):
            steps.append(lambda g8=g8: v_group(g8))
        for qc in range(S // 512):
            steps.append(lambda qc=qc: q_chunk(qc, 512, True, pq, kpq, QT[hb][0], QT[hb][1]))
        if not last:
            steps.append(lambda: q_chunk(0, CL, False, pqc, kpqc, QTc[hb][0], QTc[hb][1]))
        return steps


    batch_loads(0)
    kr_load(0, 0)
    kr_load(0, 1)
    for b in range(NBC):
        lat, cx = self.lat[b], self.ctx[b]
        (pkv, kpkv), (pq, kpq), (pqc, kpqc) = pkv_b[b % 2], pq_b[b % 2], pqc_b[b % 2]

        if b == 0:
            for st_ in build_steps(0, 0):
                st_()
        for h in range(NH):
            hb = h % 2
            kt_ap, kkt = KT[hb]
            va_ap, kva = VA[hb]
            if h == 1 and b + 1 < NBC:
                batch_loads(b + 1)
            if h + 1 < NH:
                pending = build_steps(b, h + 1)
            elif b + 1 < NBC:
                kr_load(b + 1, 0)
                pending = build_steps(b + 1, 0)
            else:
                pending = []
            qsets = [(lat, S, NKT, QT[hb])]
            if not last:
                qsets.append((cx, CL, CL // 128, QTc[hb]))
            n_items_total = sum((nq_ // min(512, nq_)) * ((nkt_ + 1) // 2) for (_, nq_, nkt_, _) in qsets)
            every = max(1, (n_items_total - 8) // max(1, len(pending)))
            tick = [0]
            for (qs, nq, nkt, (qt_ap, kqt)) in qsets:
                QW = min(512, nq)
                npair = (nkt + 1) // 2
                items = [(qb, kp) for qb in range(nq // QW) for kp in range(npair)]
                po_of = {}
                for qb in range(nq // QW):
                    cnt_o[0] += 1
                    po_of[qb] = (self.bank(4 + cnt_o[0] % 2), cnt_o[0] % 2)

                def emit_S(ii, nkt=nkt, QW=QW, kt_ap=kt_ap, qt_ap=qt_ap, kkt=kkt, kqt=kqt, items=items):
                    qb, kp = items[ii]
                    q0 = qb * QW
                    ps_s = self.ps2[ii % 2]
                    kps = ["pb%d" % (2 * (ii % 2)), "pb%d" % (2 * (ii % 2) + 1)]
                    pt_ap, kpt = PT[ii % 2]
                    nh = min(2, nkt - 2 * kp)
                    for half in range(nh):
                        kt = 2 * kp + half
                        kb.pe(lambda e, ps_s=ps_s, half=half, kt=kt, q0=q0:
                              e.matmul(ps_s[:, half * 512:half * 512 + QW], kt_ap[0:96, kt * 128:(kt + 1) * 128], qt_ap[0:96, q0:q0 + QW], start=True, stop=True),
                              reads=[kkt + ".n", kkt + ".r", kqt + ".n", kqt + ".r"], writes=[kps[half]])
                    if QW == 512:
                        kb.act(lambda e, ps_s=ps_s, pt_ap=pt_ap, nh=nh: e.activation(out=pt_ap[:, 0:nh * 512], in_=ps_s[:, 0:nh * 512], func=AF.Exp),
                               reads=kps[0:nh], writes=[kpt])
                    else:
                        for half in range(nh):
                            kb.act(lambda e, ps_s=ps_s, pt_ap=pt_ap, half=half: e.activation(out=pt_ap[:, half * 512:half * 512 + QW], in_=ps_s[:, half * 512:half * 512 + QW], func=AF.Exp),
                                   reads=[kps[half]], writes=[kpt])

                def emit_PV(ii, nkt=nkt, QW=QW, va_ap=va_ap, kva=kva, qs=qs, h=h, npair=npair, items=items, po_of=po_of):
                    qb, kp = items[ii]
                    q0 = qb * QW
                    (po, kpo), par = po_of[qb]
                    pt_ap, kpt = PT[ii % 2]
                    nh = min(2, nkt - 2 * kp)
                    for half in range(nh):
                        kt = 2 * kp + half
                        kb.pe(lambda e, po=po, pt_ap=pt_ap, half=half, kt=kt:
                              e.matmul(po[:, 0:QW], va_ap[:, kt, :], pt_ap[:, half * 512:half * 512 + QW], start=(kt == 0), stop=(kt == nkt - 1)),
                              reads=[kva + ".v", kva + ".ones", kpt], writes=[kpo])
                    if kp != npair - 1:
                        return
                    o_ap, ko = oT[par]
                    r_ap, kr_ = rc[par]
                    s_ap, ks_ = stg[par]
                    nsub = QW // 128
                    kb.dve(lambda e, o_ap=o_ap, po=po: e.tensor_copy(out=o_ap[:, 0:QW], in_=po[:, 0:QW]), reads=[kpo], writes=[ko])
                    ptr, kptr = mbank()
                    for sub in range(nsub):
                        kb.pe(lambda e, ptr=ptr, o_ap=o_ap, sub=sub: e.transpose(ptr[:, sub * 128:(sub + 1) * 128], o_ap[:, sub * 128:(sub + 1) * 128], self.ident_f),
                              reads=[ko, "ident_f"], writes=[kptr])
                    kb.dve(lambda e, ptr=ptr, r_ap=r_ap, nsub=nsub: e.reciprocal(out=r_ap[:, 0:nsub, :], in_=ptr[:, 0:nsub * 128].rearrange("p (a c) -> p a c", c=128)[:, :, 64:65]),
                           reads=[kptr], writes=[kr_])
                    for sub in range(nsub):
                        kb.dve(lambda e, ptr=ptr, s_ap=s_ap, r_ap=r_ap, sub=sub: e.tensor_scalar(out=s_ap[:, sub, :], in0=ptr[:, sub * 128:sub * 128 + 64], scalar1=r_ap[:, sub, :], scalar2=None, op0=ALU.mult),
                               reads=[kptr, kr_], writes=[ks_ + ".%d" % sub])
                    kb.dma(qs.ATT[q0:q0 + QW, h * 64:(h + 1) * 64].rearrange("(a p) c -> p a c", p=128), s_ap[:, 0:nsub, :],
                           reads=[ks_ + ".%d" % sub for sub in range(nsub)], writes=["ATT_" + qs.name])

                emit_S(0)
                for ii in range(len(items)):
                    if ii + 1 < len(items):
                        emit_S(ii + 1)
                    emit_PV(ii)
                    tick[0] += 1
                    if pending and tick[0] % every == 0:
                        pending.pop(0)()
            while pending:
                pending.pop(0)()
            if h == NH - 1 and b + 1 < NBC:
                kr_load(b + 1, 1)
    self.phase_end()


Prog.phase_attn = _attn_phase


def _pool_phase(self, l):
    kb, ar, w = self.kb, self.arena, self.w
    last = (l == L - 1)
    pw = ar.alloc([4, 64], BF16, "pw")
    kb.dma(pw[0][0:64, :, :], w["pool_w"][l].rearrange("g i o -> i g o"), writes=[pw[1]], q="pool")
    psc, kpsc = ar.alloc([256], F32, "psc")
    kb.dma(psc, w["pool_scale"][l].partition_broadcast(128), writes=[kpsc])
    groups = [("lat", self.lat)] + ([] if last else [("ctx", self.ctx)])
    for nm, seqs in groups:
        bandc = self.k["band_" + nm]
        tab = self.consts["_band_tab_" + nm]
        nblk = bandc.shape[0]
        band, kband = ar.alloc([nblk, 512], BF16, "band")
        kb.dma(band, bandc.rearrange("b p t -> p b t"), writes=[kband])
        n = seqs[0].n
        NT = n // 128
        u, ku = ar.alloc([NT, 256], BF16, "u")
        dg = [ar.alloc([4, 512], BF16, "dg") for _ in range(2)]
        stg = [ar.alloc([4, 256], F32, "pstg") for _ in range(2)]
        it = 0
        for s in seqs:
            W, nsub = s.W, s.W // 128
            kb.dma(u, s.POOLU.rearrange("(a p) c -> p a c", p=128), reads=["POOLU_" + s.name], writes=[ku])
            for i in range(s.nt):
                d_ap, kd = dg[it % 2]
                s_ap, ks = stg[it % 2]
                for g in range(4):
                    ms = sorted(m for (g_, i_, m) in tab if g_ == g and i_ == i)
                    pb, kpb = self.nb()
                    for mi_, m in enumerate(ms):
                        bi = tab[(g, i, m)]
                        kb.pe(lambda e, pb=pb, g=g, m=m, bi=bi, W=W, mi_=mi_, nm_=len(ms), u=u, band=band: e.matmul(pb[0:64, 0:W], u[:, m, g * 64:(g + 1) * 64], band[:, bi, 0:W],
                                                                                            start=(mi_ == 0), stop=(mi_ == nm_ - 1)),
                              reads=[ku, kband], writes=[kpb])
                    self.evac(g, lambda e, pb=pb, d_ap=d_ap, g=g, W=W: e.activation(out=d_ap[0:64, g, 0:W], in_=pb[0:64, 0:W], func=AF.Copy),
                              lambda e, pb=pb, d_ap=d_ap, g=g, W=W: e.tensor_copy(out=d_ap[0:64, g, 0:W], in_=pb[0:64, 0:W]),
                              reads=[kpb], writes=[kd + ".%d" % g])
                for sub in range(nsub):
                    pb, kpb = self.nb()
                    for g in range(4):
                        kb.pe(lambda e, pb=pb, g=g, sub=sub, d_ap=d_ap: e.matmul(pb[:, g * 64:(g + 1) * 64], d_ap[0:64, g, sub * 128:(sub + 1) * 128], pw[0][0:64, g, :],
                                                                                start=True, stop=True),
                              reads=[kd + ".%d" % g, pw[1]], writes=[kpb])
                    kb.dve(lambda e, pb=pb, s_ap=s_ap, sub=sub: e.tensor_tensor(out=s_ap[:, sub, :], in0=pb[:, 0:256], in1=psc, op=ALU.mult),
                           reads=[kpb, kpsc], writes=[ks + ".%d" % sub])
                kb.dma(s.POOLO[i * W:(i + 1) * W, :].rearrange("(a p) c -> p a c", p=128), s_ap[:, 0:nsub, :],
                       reads=[ks + ".%d" % sub for sub in range(nsub)], writes=["POOLO_" + s.name])
                it += 1
    self.phase_end()


def _c1_phase(self, l):
    kb, ar, w = self.kb, self.arena, self.w
    last = (l == L - 1)
    wout = ar.alloc([8, D], BF16, "wout")
    self.load_cast_rows(wout, w["w_out"][l], 8)
    wo_ap, kwo = wout
    gout, kgout = ar.alloc([D], F32, "gout")
    kb.dma(gout, w["g_out"][l].partition_broadcast(128), writes=[kgout])
    xts = [ar.alloc([8, 512], F32, "xt") for _ in range(2)]
    att = [ar.alloc([4, 512], F32, "att") for _ in range(2)]
    pl = [ar.alloc([4, 256], F32, "pl") for _ in range(2)]
    hy = [ar.alloc([4, 256], F32, "hy") for _ in range(2)]
    junk, kjunk = ar.alloc([512], BF16, "junk")
    ss = [ar.alloc([3, 4], F32, "ss") for _ in range(2)]
    mrg = [ar.alloc([4, D], BF16, "mrg") for _ in range(2)]
    mT = [ar.alloc([8, 512], BF16, "mT") for _ in range(2)]
    seqs = self.lat + ([] if last else self.ctx)
    GR = ((0, 512, 0), (512, 256, 1), (768, 256, 2))
    tiles = [(s, i) for s in seqs for i in range(s.nt)]

    def front(it):
        s, i = tiles[it]
        W, nsub = s.W, s.W // 128
        HYO = self.HYO_lat if s.kind == "lat" else self.HYO_ctx
        t0 = i * W
        xt, kxt = xts[it % 2]
        a_ap, ka = att[it % 2]
        p_ap, kp = pl[it % 2]
        h_ap, kh = hy[it % 2]
        ss_ap, kss = ss[it % 2]
        m_ap, km = mrg[it % 2]
        t_ap, kt = mT[it % 2]
        kb.dma(xt[:, :, 0:W], s.XT[:, :, t0:t0 + W].rearrange("j p t -> p j t"), reads=["XT_%s.%d" % (s.name, i)], writes=[kxt + ".%d" % j for j in range(8)])
        kb.dma(a_ap[:, 0:nsub, :], s.ATT[t0:t0 + W, :].rearrange("(a p) c -> p a c", p=128), reads=["ATT_" + s.name], writes=[ka])
        kb.dma(p_ap[:, 0:nsub, :], s.POOLO[t0:t0 + W, :].rearrange("(a p) c -> p a c", p=128), reads=["POOLO_" + s.name], writes=[kp])
        kb.dma(h_ap[:, 0:nsub, :], HYO[t0:t0 + W, s.b * 256:(s.b + 1) * 256].rearrange("(a p) c -> p a c", p=128), reads=["HYO"], writes=[kh])
        kb.dve(lambda e: e.memset(ss_ap, 0.0), writes=[kss])
        srcs = ((a_ap, ka), (p_ap, kp), (h_ap, kh))
        for sub in range(nsub):
            for (c0, ng, gi) in GR:
                src, ksrc = srcs[gi]
                kb.act(lambda e, src=src, sub=sub, ng=ng, gi=gi: e.activation(out=junk[:, 0:ng], in_=src[:, sub, :], func=AF.Square,
                                                                            accum_out=ss_ap[:, gi, sub:sub + 1]),
                       reads=[ksrc, kss], writes=[kss, kjunk])
        for (c0, ng, gi) in GR:
            kb.act(lambda e, gi=gi, ng=ng: e.activation(out=ss_ap[:, gi, 0:nsub], in_=ss_ap[:, gi, 0:nsub], func=AF.Sqrt,
                                                        bias=self.eps[:, 0:1], scale=1.0 / ng),
                   reads=[kss, "eps"], writes=[kss])
        kb.dve(lambda e: e.reciprocal(out=ss_ap, in_=ss_ap), reads=[kss], writes=[kss])
        for sub in range(nsub):
            for (c0, ng, gi) in GR:
                src, ksrc = srcs[gi]
                kb.dve(lambda e, src=src, sub=sub, c0=c0, ng=ng, gi=gi:
                       e.scalar_tensor_tensor(out=m_ap[:, sub, c0:c0 + ng], in0=src[:, sub, :], scalar=ss_ap[:, gi, sub:sub + 1], in1=gout[:, c0:c0 + ng],
                                              op0=ALU.mult, op1=ALU.mult),
                       reads=[ksrc, kss, kgout], writes=[km + ".%d" % sub])
            pb, kpb = self.nb()
            pbb = pb.bitcast(BF16)
            for j in range(8):
                kb.pe(lambda e, pbb=pbb, sub=sub, j=j: e.transpose(pbb[:, j * 128:(j + 1) * 128], m_ap[:, sub, j * 128:(j + 1) * 128], self.ident_b),
                      reads=[km + ".%d" % sub, "ident_b"], writes=[kpb])
            self.evac(sub, lambda e, pbb=pbb, sub=sub: e.activation(out=t_ap[:, :, sub * 128:(sub + 1) * 128], in_=pbb.rearrange("p (j t) -> p j t", t=128), func=AF.Copy),
                      lambda e, pbb=pbb, sub=sub: e.tensor_copy(out=t_ap[:, :, sub * 128:(sub + 1) * 128], in_=pbb.rearrange("p (j t) -> p j t", t=128)),
                      reads=[kpb], writes=[kt + ".%d" % sub])

    def back(it):
        s, i = tiles[it]
        W, nsub = s.W, s.W // 128
        g1 = self.mod(l, 2, s.v)
        t0 = i * W
        xt, kxt = xts[it % 2]
        t_ap, kt = mT[it % 2]
        for oc in range(8):
            pb, kpb = self.nb()
            for k_ in range(8):
                kb.pe(lambda e, pb=pb, k_=k_, oc=oc: e.matmul(pb[:, 0:W], wo_ap[:, k_, oc * 128:(oc + 1) * 128], t_ap[:, k_, 0:W], start=(k_ == 0), stop=(k_ == 7)),
                      reads=[kwo + ".%d" % k_] + [kt + ".%d" % sub for sub in range(nsub)], writes=[kpb])
            kb.dve(lambda e, pb=pb, oc=oc: e.scalar_tensor_tensor(out=xt[:, oc, 0:W], in0=pb[:, 0:W], scalar=g1[:, oc:oc + 1], in1=xt[:, oc, 0:W],
                                                                op0=ALU.mult, op1=ALU.add),
                   reads=[kpb, kxt + ".%d" % oc, "modT"], writes=[kxt + ".%d" % oc])
        kb.dma(s.XT[:, :, t0:t0 + W].rearrange("j p t -> p j t"), xt[:, :, 0:W], reads=[kxt + ".%d" % j for j in range(8)], writes=["XT_%s.%d" % (s.name, i)])

    front(0)
    for it in range(len(tiles)):
        if it + 1 < len(tiles):
            front(it + 1)
        back(it)
    self.phase_end()


def _c2_phase(self, l):
    kb, ar, w = self.kb, self.arena, self.w
    last = (l == L - 1)
    w1 = ar.alloc([8, DFF], BF16, "w1")
    w2 = ar.alloc([32, D], BF16, "w2")
    self.load_cast_rows(w1, w["w_mlp1"][l], 8, split=2)
    self.load_cast_rows(w2, w["w_mlp2"][l], 32)
    w1_ap, kw1 = w1
    w2_ap, kw2 = w2
    xts = [ar.alloc([8, 512], F32, "xt") for _ in range(2)]
    rs, krs = ar.alloc([512], F32, "rs")
    hT, khT = ar.alloc([8, 512], BF16, "hT")
    hid, khid = ar.alloc([32, 512], BF16, "hid")
    tmp = hid[:, 0:16, :].rearrange("p a b -> p (a b)").bitcast(F32).rearrange("p (a b) -> p a b", b=512)
    ktmp = khid + ".tmp"
    sq = hid[:, 16:24, :]
    ksq = khid + ".sq"
    rl = [ar.alloc([512], F32, "rl")] * 2
    seqs = self.lat + ([] if last else self.ctx)
    tiles = [(s, i) for s in seqs for i in range(s.nt)]

    def load(it):
        s, i = tiles[it]
        W = s.W
        xt, kxt = xts[it % 2]
        kb.dma(xt[:, :, 0:W], s.XT[:, :, i * W:(i + 1) * W].rearrange("j p t -> p j t"), reads=["XT_%s.%d" % (s.name, i)], writes=[kxt + ".%d" % j for j in range(8)])

    load(0)
    for it in range(len(tiles)):
        s, i = tiles[it]
        W = s.W
        gm, sh, g2 = self.mod(l, 4, s.v), self.mod(l, 3, s.v), self.mod(l, 5, s.v)
        t0 = i * W
        xt, kxt = xts[it % 2]
        if it + 1 < len(tiles):
            load(it + 1)
        self.mod_norm(xt, kxt, W, gm, sh, sq, ksq, rs, krs, tmp, ktmp, hT, khT,
                      extra=lambda j: [khid + ".%d" % (2 * j), khid + ".%d" % (2 * j + 1)],
                      extra_sq=lambda ci: [khid + ".%d" % (16 + ci)])
        khT_all = [khT + ".%d" % j for j in range(8)]
        for hc in range(32):
            pb, kpb = self.nb()
            for j in range(8):
                kb.pe(lambda e, pb=pb, j=j, hc=hc, W=W: e.matmul(pb[:, 0:W], w1_ap[:, j, hc * 128:(hc + 1) * 128], hT[:, j, 0:W], start=(j == 0), stop=(j == 7)),
                      reads=[kw1 + ".%d" % j, khT_all[j]], writes=[kpb])
            r_ap, kr_ = rl[hc % 2]
            kb.act(lambda e, pb=pb, r_ap=r_ap, W=W: e.activation(out=r_ap[:, 0:W], in_=pb[:, 0:W], func=AF.Relu), reads=[kpb], writes=[kr_])
            kb.dve(lambda e, r_ap=r_ap, hc=hc, W=W: e.tensor_tensor(out=hid[:, hc, 0:W], in0=r_ap[:, 0:W], in1=r_ap[:, 0:W], op=ALU.mult),
                   reads=[kr_], writes=[khid + ".%d" % hc])
        for oc in range(8):
            pb, kpb = self.nb()
            for hc in range(32):
                kb.pe(lambda e, pb=pb, hc=hc, oc=oc, W=W: e.matmul(pb[:, 0:W], w2_ap[:, hc, oc * 128:(oc + 1) * 128], hid[:, hc, 0:W], start=(hc == 0), stop=(hc == 31)),
                      reads=[kw2 + ".%d" % hc, khid + ".%d" % hc], writes=[kpb])
            kb.dve(lambda e, pb=pb, oc=oc, W=W, g2=g2, xt=xt: e.scalar_tensor_tensor(out=xt[:, oc, 0:W], in0=pb[:, 0:W], scalar=g2[:, oc:oc + 1], in1=xt[:, oc, 0:W],
                                                                                op0=ALU.mult, op1=ALU.add),
                   reads=[kpb, kxt + ".%d" % oc, "modT"], writes=[kxt + ".%d" % oc])
        kb.dma(s.XT[:, :, t0:t0 + W].rearrange("j p t -> p j t"), xt[:, :, 0:W], reads=[kxt + ".%d" % j for j in range(8)], writes=["XT_%s.%d" % (s.name, i)])
    self.phase_end()


def _final_phase(self):
    kb, ar, w = self.kb, self.arena, self.w
    gf, kgf = ar.alloc([8], F32, "gf")
    for j in range(8):
        kb.dma(gf[:, j:j + 1], w["g_final"][j * 128:(j + 1) * 128].rearrange("(p o) -> p o", o=1), writes=[kgf])
    xts = [ar.alloc([8, 512], F32, "xt") for _ in range(2)]
    sq, ksq = ar.alloc([8, 512], BF16, "sq")
    rs, krs = ar.alloc([512], F32, "rs")
    xn, kxn = ar.alloc([8, 512], F32, "xn")
    yts = [ar.alloc([4, D], F32, "yt") for _ in range(2)]
    it = 0
    for s in self.lat:
        W = 512
        for i in range(s.nt):
            t0 = i * W
            xt, kxt = xts[it % 2]
            yt, kyt = yts[it % 2]
            kb.dma(xt, s.XT[:, :, t0:t0 + W].rearrange("j p t -> p j t"), reads=["XT_" + s.name], writes=[kxt + ".%d" % j for j in range(8)])
            self.fm_rstd([(xt[:, j, :], kxt + ".%d" % j, 128) for j in range(8)], D, W, sq, ksq, rs, krs)
            for j in range(8):
                kb.dve(lambda e, xt=xt, j=j: e.scalar_tensor_tensor(out=xn[:, j, :], in0=xt[:, j, :], scalar=gf[:, j:j + 1], in1=rs, op0=ALU.mult, op1=ALU.mult),
                       reads=[kxt + ".%d" % j, krs, kgf], writes=[kxn + ".%d" % j])
            for sub in range(4):
                pp = self.ps2[sub % 2]
                kpp = ["pb%d" % (2 * (sub % 2)), "pb%d" % (2 * (sub % 2) + 1)]
                for j in range(8):
                    kb.pe(lambda e, pp=pp, j=j, sub=sub: e.transpose(pp[:, j * 128:(j + 1) * 128], xn[:, j, sub * 128:(sub + 1) * 128], self.ident_f),
                          reads=[kxn + ".%d" % j, "ident_f"], writes=[kpp[j // 4]])
                self.evac(sub, lambda e, pp=pp, yt=yt, sub=sub: e.activation(out=yt[:, sub, :], in_=pp, func=AF.Copy),
                          lambda e, pp=pp, yt=yt, sub=sub: e.tensor_copy(out=yt[:, sub, :], in_=pp),
                          reads=kpp, writes=[kyt + ".%d" % sub])
            kb.dma(self.y_out[s.b][t0:t0 + W, :].rearrange("(a p) d -> p a d", p=128), yt, reads=[kyt + ".%d" % sub for sub in range(4)], writes=["y"])
            it += 1
    self.phase_end()


Prog.phase_pool = _pool_phase
Prog.phase_c1 = _c1_phase
Prog.phase_c2 = _c2_phase
Prog.phase_final = _final_phase


def _hy_h0(self, l, nm, seqs, n):
    kb, ar, w = self.kb, self.arena, self.w
    NT = n // 128
    cw, kcw = ar.alloc([6, 3], F32, "cw")
    cb, kcb = ar.alloc([6], F32, "cb")
    for c6 in range(6):
        for k_ in range(3):
            kb.dma(cw[:, c6, k_:k_ + 1], w["hy_conv_w"][l][k_, c6 * 128:(c6 + 1) * 128].rearrange("(p o) -> p o", o=1), writes=[kcw])
        kb.dma(cb[:, c6:c6 + 1], w["hy_conv_b"][l][c6 * 128:(c6 + 1) * 128].rearrange("(p o) -> p o", o=1), writes=[kcb])
    hp = [ar.alloc([n + 2], F32, "hp") for _ in range(2)]
    acc = [ar.alloc([n], F32, "acc") for _ in range(2)]
    ucb = [ar.alloc([n], BF16, "ucb") for _ in range(2)]
    tm = [ar.alloc([NT, 128], BF16, "tm") for _ in range(2)]
    dests = (getattr(self, "HV_" + nm), getattr(self, "HX1_" + nm), getattr(self, "HX2_" + nm))
    it = 0
    for si, s in enumerate(seqs):
        for c6 in range(6):
            h_ap, kh = hp[it % 2]
            a_ap, ka = acc[it % 2]
            u_ap, ku = ucb[it % 2]
            t_ap, kt = tm[it % 2]
            kb.dma(h_ap, s.HYP[c6], reads=["HYP_" + s.name], writes=[kh])
            kb.act(lambda e, h_ap=h_ap, a_ap=a_ap, c6=c6: e.activation(out=a_ap, in_=h_ap[:, 1:n + 1], func=AF.Identity, bias=cb[:, c6:c6 + 1], scale=cw[:, c6, 1:2]),
                   reads=[kh, kcw, kcb], writes=[ka])
            kb.dve(lambda e, h_ap=h_ap, a_ap=a_ap, c6=c6: e.scalar_tensor_tensor(out=a_ap, in0=h_ap[:, 0:n], scalar=cw[:, c6, 0:1], in1=a_ap, op0=ALU.mult, op1=ALU.add),
                   reads=[kh, kcw, ka], writes=[ka])
            kb.dve(lambda e, h_ap=h_ap, a_ap=a_ap, u_ap=u_ap, c6=c6: e.scalar_tensor_tensor(out=u_ap, in0=h_ap[:, 2:n + 2], scalar=cw[:, c6, 2:3], in1=a_ap, op0=ALU.mult, op1=ALU.add),
                   reads=[kh, kcw, ka], writes=[ku])
            for g8 in range((NT + 7) // 8):
                ng = min(8, NT - g8 * 8)
                pb, kpb = self.nb()
                pbb = pb.bitcast(BF16)
                for i in range(ng):
                    tt = g8 * 8 + i
                    kb.pe(lambda e, pbb=pbb, u_ap=u_ap, i=i, tt=tt: e.transpose(pbb[:, i * 128:(i + 1) * 128], u_ap[:, tt * 128:(tt + 1) * 128], self.ident_b),
                          reads=[ku, "ident_b"], writes=[kpb])
                self.evac(g8, lambda e, pbb=pbb, t_ap=t_ap, g8=g8, ng=ng: e.activation(out=t_ap[:, g8 * 8:g8 * 8 + ng, :], in_=pbb[:, 0:ng * 128].rearrange("p (a c) -> p a c", c=128), func=AF.Copy),
                          lambda e, pbb=pbb, t_ap=t_ap, g8=g8, ng=ng: e.tensor_copy(out=t_ap[:, g8 * 8:g8 * 8 + ng, :], in_=pbb[:, 0:ng * 128].rearrange("p (a c) -> p a c", c=128)),
                          reads=[kpb], writes=[kt + ".%d" % g8])
            dst = dests[c6 // 2]
            col0 = si * 256 + (c6 % 2) * 128
            kb.dma(dst[:, col0:col0 + 128].rearrange("(a p) c -> p a c", p=128), t_ap,
                   reads=[kt + ".%d" % g8 for g8 in range((NT + 7) // 8)], writes=["HU_" + nm])
            it += 1
    self.phase_end()


def _hy_h1(self, l, nm, n):
    kb, ar, w = self.kb, self.arena, self.w
    NT = n // 128
    N2 = 2 * n
    CW = min(512, n)
    zT, kzT = ar.alloc([n], F32, "zT")
    kb.dma(zT[0:HY_EMB, :], self.k["hyz_" + nm], writes=[kzT])
    w1s, kw1s = ar.alloc([HY_FFN], F32, "w1s")
    w2s, kw2s = ar.alloc([HY_FFN], F32, "w2s")
    w3s, kw3s = ar.alloc([1024], F32, "w3s")
    kb.dma(w1s[0:HY_EMB, :], w["hy_f_w1"][l], writes=[kw1s])
    kb.dma(w2s[0:HY_FFN, :], w["hy_f_w2"][l], writes=[kw2s])
    kb.dma(w3s[0:HY_FFN, :], w["hy_f_w3"][l], writes=[kw3s])
    par, kpar = ar.alloc([6], F32, "par")
    for ci, nm_ in enumerate(("hy_f_b1", "hy_f_freq1", "hy_f_b2", "hy_f_freq2")):
        kb.dma(par[0:64, ci:ci + 1], w[nm_][l].rearrange("(p o) -> p o", o=1), writes=[kpar])
    kb.dve(lambda e: e.tensor_tensor(out=par[0:64, 4:5], in0=par[0:64, 0:1], in1=par[0:64, 1:2], op=ALU.mult), reads=[kpar], writes=[kpar])
    kb.dve(lambda e: e.tensor_tensor(out=par[0:64, 5:6], in0=par[0:64, 2:3], in1=par[0:64, 3:4], op=ALU.mult), reads=[kpar], writes=[kpar])
    h1T, kh1 = ar.alloc([n], F32, "h1T")
    h2T, kh2 = ar.alloc([n], F32, "h2T")
    arg, karg = ar.alloc([512], F32, "arg")
    kk, kkk = ar.alloc([512], F32, "kk")

    def layer(srcT, ksrc, wS, kwS, K, fcol, fbcol, dstT, kdst):
        for ch in range(n // CW):
            c0 = ch * CW
            pb, kpb = self.nb()
            kb.pe(lambda e, pb=pb, c0=c0: e.matmul(pb[0:64, 0:CW], wS[0:K, 0:64], srcT[0:K, c0:c0 + CW], start=True, stop=True),
                  reads=[ksrc, kwS], writes=[kpb])
            kb.act(lambda e, pb=pb: e.activation(out=arg[0:64, 0:CW], in_=pb[0:64, 0:CW], func=AF.Identity, bias=par[0:64, fbcol:fbcol + 1], scale=par[0:64, fcol:fcol + 1]),
                   reads=[kpb, kpar], writes=[karg])
            kb.dve(lambda e: e.tensor_scalar(out=kk[0:64, 0:CW], in0=arg[0:64, 0:CW], scalar1=1.0 / (2 * math.pi), scalar2=MAGIC, op0=ALU.mult, op1=ALU.add),
                   reads=[karg], writes=[kkk])
            kb.dve(lambda e: e.tensor_scalar(out=kk[0:64, 0:CW], in0=kk[0:64, 0:CW], scalar1=-MAGIC, scalar2=-2 * math.pi, op0=ALU.add, op1=ALU.mult),
                   reads=[kkk], writes=[kkk])
            kb.dve(lambda e: e.tensor_tensor(out=arg[0:64, 0:CW], in0=arg[0:64, 0:CW], in1=kk[0:64, 0:CW], op=ALU.add), reads=[karg, kkk], writes=[karg])
            kb.act(lambda e, c0=c0: e.activation(out=dstT[0:64, c0:c0 + CW], in_=arg[0:64, 0:CW], func=AF.Sin), reads=[karg], writes=[kdst])
    layer(zT, kzT, w1s, kw1s, HY_EMB, 1, 4, h1T, kh1)
    layer(h1T, kh1, w2s, kw2s, HY_FFN, 3, 5, h2T, kh2)
    HP, kHP = ar.alloc([NT, 512], BF16, "HP")
    HM, kHM = ar.alloc([NT, 512], BF16, "HM")
    dec = [ar.alloc([256], F32, "dec") for _ in range(2)]
    tp = [ar.alloc([4, 256], F32, "tp") for _ in range(2)]
    ab = [ar.alloc([4, 256], F32, "ab") for _ in range(2)]
    psZ = self.ps2[3]
    kZ = ["pb6", "pb7"]
    for tt in range(NT):
        pt = self.ps2[tt % 2]
        kpt = ["pb%d" % (2 * (tt % 2)), "pb%d" % (2 * (tt % 2) + 1)]
        d_ap, kd = dec[tt % 2]
        t_ap, ktp = tp[tt % 2]
        a_ap, kab = ab[tt % 2]
        for hf in range(2):
            kb.pe(lambda e, pt=pt, hf=hf, tt=tt: e.matmul(pt[:, hf * 512:(hf + 1) * 512], h2T[0:64, tt * 128:(tt + 1) * 128], w3s[0:64, hf * 512:(hf + 1) * 512], start=True, stop=True),
                  reads=[kh2, kw3s], writes=[kpt[hf]])
        kb.dma(d_ap, self.k["hydec_" + nm][tt * 128:(tt + 1) * 128, :], writes=[kd])
        for q in range(4):
            kb.dve(lambda e, pt=pt, t_ap=t_ap, d_ap=d_ap, q=q: e.tensor_tensor(out=t_ap[:, q, :], in0=pt[:, q * 256:(q + 1) * 256], in1=d_ap, op=ALU.mult),
                   reads=[kpt[q // 2], kd], writes=[ktp])
        if tt == 0:
            for q in (1, 3):
                kb.dve(lambda e, t_ap=t_ap, q=q: e.memset(t_ap[0:1, q, :], 0.0), reads=[ktp], writes=[ktp])
        kb.act(lambda e, t_ap=t_ap, a_ap=a_ap: e.activation(out=a_ap, in_=t_ap, func=AF.Abs), reads=[ktp], writes=[kab])
        for hf in range(2):
            kb.pe(lambda e, a_ap=a_ap, hf=hf, tt=tt: e.matmul(psZ[:, hf * 512:(hf + 1) * 512], self.ones_f, a_ap[:, 2 * hf:2 * hf + 2, :].rearrange("p a c -> p (a c)"),
                                                               start=(tt == 0), stop=(tt == NT - 1)),
                  reads=[kab, "ones_f"], writes=[kZ[hf]])
        for o in range(2):
            kb.dve(lambda e, t_ap=t_ap, o=o, tt=tt: e.tensor_tensor(out=HP[:, tt, o * 256:(o + 1) * 256], in0=t_ap[:, 2 * o, :], in1=t_ap[:, 2 * o + 1, :], op=ALU.add),
                   reads=[ktp], writes=[kHP + ".%d" % tt])
            kb.pool(lambda e, t_ap=t_ap, o=o, tt=tt: e.tensor_tensor(out=HM[:, tt, o * 256:(o + 1) * 256], in0=t_ap[:, 2 * o, :], in1=t_ap[:, 2 * o + 1, :], op=ALU.subtract),
                    reads=[ktp], writes=[kHM + ".%d" % tt])
    zc, kzc = ar.alloc([1024], F32, "zc")
    rz, krz = ar.alloc([512], F32, "rz")
    kb.act(lambda e: e.activation(out=zc, in_=psZ, func=AF.Copy), reads=kZ, writes=[kzc])
    for o in range(2):
        kb.dve(lambda e, o=o: e.tensor_tensor(out=rz[:, o * 256:(o + 1) * 256], in0=zc[:, o * 512:o * 512 + 256], in1=zc[:, o * 512 + 256:(o + 1) * 512], op=ALU.add),
               reads=[kzc], writes=[krz])
    kb.dve(lambda e: e.reciprocal(out=rz, in_=rz), reads=[krz], writes=[krz])
    cf, kcf = ar.alloc([2], F32, "cf")
    kb.dve(lambda e: e.memset(cf, 2.0 / N2), writes=[kcf])
    kb.dve(lambda e: e.memset(cf[0:1, 0:1], 1.0 / N2), reads=[kcf], writes=[kcf])
    Cb = [ar.alloc([NT, 128], BF16, "Cb") for _ in range(2)]
    Sb = [ar.alloc([NT, 128], BF16, "Sb") for _ in range(2)]
    hs = [ar.alloc([2, 512], BF16, "hs") for _ in range(2)]
    HS = getattr(self, "HS_" + nm)
    kHPall = [kHP + ".%d" % tt for tt in range(NT)]
    kHMall = [kHM + ".%d" % tt for tt in range(NT)]
    for ft in range(NT):
        c_ap, kc = Cb[ft % 2]
        s_ap, ks = Sb[ft % 2]
        h_ap, kh = hs[ft % 2]
        kb.dma(c_ap, self.k["dftc_" + nm][ft], writes=[kc])
        kb.dma(s_ap, self.k["dfts_" + nm][ft], writes=[ks])
        pre, kpre = self.nb()
        pim, kpim = self.nb()
        for tt in range(NT):
            kb.pe(lambda e, pre=pre, c_ap=c_ap, tt=tt: e.matmul(pre, c_ap[:, tt, :], HP[:, tt, :], start=(tt == 0), stop=(tt == NT - 1)), reads=[kc, kHPall[tt]], writes=[kpre])
        for tt in range(NT):
            kb.pe(lambda e, pim=pim, s_ap=s_ap, tt=tt: e.matmul(pim, s_ap[:, tt, :], HM[:, tt, :], start=(tt == 0), stop=(tt == NT - 1)), reads=[ks, kHMall[tt]], writes=[kpim])
        ccol = 0 if ft == 0 else 1
        kb.dve(lambda e, pre=pre, h_ap=h_ap, ccol=ccol: e.scalar_tensor_tensor(out=h_ap[:, 0, :], in0=pre, scalar=cf[:, ccol:ccol + 1], in1=rz, op0=ALU.mult, op1=ALU.mult),
               reads=[kpre, kcf, krz], writes=[kh + ".0"])
        kb.dve(lambda e, pim=pim, h_ap=h_ap, ccol=ccol: e.scalar_tensor_tensor(out=h_ap[:, 1, :], in0=pim, scalar=cf[:, ccol:ccol + 1], in1=rz, op0=ALU.mult, op1=ALU.mult),
               reads=[kpim, kcf, krz], writes=[kh + ".1"])
        kb.dma(HS[:, ft].rearrange("r p c -> p r c"), h_ap, reads=[kh + ".0", kh + ".1"], writes=["HS_" + nm])
    pmf, kpmf = ar.alloc([1], F32, "pmf")
    pmc, kpmc = ar.alloc([1], BF16, "pmc")
    kb.dma(pmf, self.k["pm1_" + nm][0:128].rearrange("(p o) -> p o", o=1), writes=[kpmf])
    kb.act(lambda e: e.activation(out=pmc, in_=pmf, func=AF.Copy), reads=[kpmf], writes=[kpmc])
    psn, kpsn = self.nb()
    for tt in range(NT):
        kb.pe(lambda e, tt=tt: e.matmul(psn[0:1, :], pmc[:, 0:1], HP[:, tt, :], start=(tt == 0), stop=(tt == NT - 1)), reads=[kpmc, kHPall[tt]], writes=[kpsn])
    hn, khn = ar.alloc([512], F32, "hn")
    kb.dve(lambda e: e.scalar_tensor_tensor(out=hn[0:1, :], in0=psn[0:1, :], scalar=1.0 / N2, in1=rz[0:1, :], op0=ALU.mult, op1=ALU.mult),
           reads=[kpsn, krz], writes=[khn])
    kb.dma(getattr(self, "HNQ_" + nm), hn[0:1, :], reads=[khn], writes=["HNQ_" + nm])
    self.phase_end()


def _hy_h2(self, l, nm, seqs, n):
    kb, ar, w = self.kb, self.arena, self.w
    NT = n // 128
    HV, HX1, HX2 = getattr(self, "HV_" + nm), getattr(self, "HX1_" + nm), getattr(self, "HX2_" + nm)
    HYO, HS, HNQ = getattr(self, "HYO_" + nm), getattr(self, "HS_" + nm), getattr(self, "HNQ_" + nm)
    U, kU = ar.alloc([NT, 512], BF16, "U")
    Zb, kZb = ar.alloc([NT, 512], BF16, "Zb")
    Y, kY = ar.alloc([2, NT, 512], BF16, "Y")
    kb.dma(U, HV.rearrange("(a p) c -> p a c", p=128), writes=[kU + ".%d" % tt for tt in range(NT)])
    Cb = [ar.alloc([NT, 128], BF16, "Cb") for _ in range(2)]
    Sb = [ar.alloc([NT, 128], BF16, "Sb") for _ in range(2)]
    hsb = [ar.alloc([2, 256], BF16, "hsb") for _ in range(2)]
    bias, kbias = ar.alloc([2, 256], F32, "bias")
    kb.dma(bias, w["hy_bias"][l].rearrange("o c -> (o c)").partition_broadcast(128), writes=[kbias])
    hnq, khnq = ar.alloc([512], F32, "hnq")
    kb.dma(hnq[0:1, :], HNQ, writes=[khnq])
    pmf, kpmf = ar.alloc([1], F32, "pmf")
    pmc, kpmc = ar.alloc([1], BF16, "pmc")
    pmrf, kpmrf = ar.alloc([128], F32, "pmrf")
    pmr, kpmr = ar.alloc([128], BF16, "pmr")
    kb.dma(pmf, self.k["pm1_" + nm][0:128].rearrange("(p o) -> p o", o=1), writes=[kpmf])
    kb.act(lambda e: e.activation(out=pmc, in_=pmf, func=AF.Copy), reads=[kpmf], writes=[kpmc])
    kb.dma(pmrf[0:1, :], self.k["pm1_" + nm][0:128].rearrange("(o t) -> o t", o=1), writes=[kpmrf])
    kb.act(lambda e: e.activation(out=pmr[0:1, :], in_=pmrf[0:1, :], func=AF.Copy), reads=[kpmrf], writes=[kpmr])
    tq = [[ar.alloc([256], F32, "tq") for _ in range(4)] for _ in range(2)]
    ynq, kynq = ar.alloc([512], BF16, "ynq")
    gt = [ar.alloc([512], BF16, "gt") for _ in range(2)]
    tb = [ar.alloc([512], F32, "tb") for _ in range(2)]
    t2b = [ar.alloc([512], F32, "t2b") for _ in range(2)]
    ostg = [ar.alloc([512], F32, "ostg") for _ in range(2)]
    for o in range(2):
        src, ksrc = (U, kU) if o == 0 else (Zb, kZb)
        ksrc_all = [ksrc + ".%d" % tt for tt in range(NT)]
        for ft in range(NT):
            c_ap, kc = Cb[ft % 2]
            s_ap, ks = Sb[ft % 2]
            h_ap, kh = hsb[ft % 2]
            kb.dma(c_ap, self.k["dftc_" + nm][ft], writes=[kc])
            kb.dma(s_ap, self.k["dfts_" + nm][ft], writes=[ks])
            kb.dma(h_ap, HS[:, ft, :, o * 256:(o + 1) * 256].rearrange("r p c -> p r c"), writes=[kh])
            pre, kpre = self.nb()
            pim, kpim = self.nb()
            for tt in range(NT):
                kb.pe(lambda e, pre=pre, c_ap=c_ap, tt=tt, src=src: e.matmul(pre, c_ap[:, tt, :], src[:, tt, :], start=(tt == 0), stop=(tt == NT - 1)),
                      reads=[kc, ksrc_all[tt]], writes=[kpre])
            for tt in range(NT):
                kb.pe(lambda e, pim=pim, s_ap=s_ap, tt=tt, src=src: e.matmul(pim, s_ap[:, tt, :], src[:, tt, :], start=(tt == 0), stop=(tt == NT - 1)),
                      reads=[ks, ksrc_all[tt]], writes=[kpim])
            for b in range(2):
                (t1, k1), (t2, k2), (t3, k3), (t4, k4) = tq[b]
                bs = slice(b * 256, (b + 1) * 256)
                kb.dve(lambda e, pre=pre, h_ap=h_ap, t1=t1, bs=bs: e.tensor_tensor(out=t1, in0=pre[:, bs], in1=h_ap[:, 0, :], op=ALU.mult), reads=[kpre, kh], writes=[k1])
                kb.dve(lambda e, pim=pim, h_ap=h_ap, t2=t2, bs=bs: e.tensor_tensor(out=t2, in0=pim[:, bs], in1=h_ap[:, 1, :], op=ALU.mult), reads=[kpim, kh], writes=[k2])
                kb.dve(lambda e, pre=pre, h_ap=h_ap, t3=t3, bs=bs: e.tensor_tensor(out=t3, in0=pre[:, bs], in1=h_ap[:, 1, :], op=ALU.mult), reads=[kpre, kh], writes=[k3])
                kb.dve(lambda e, pim=pim, h_ap=h_ap, t4=t4, bs=bs: e.tensor_tensor(out=t4, in0=pim[:, bs], in1=h_ap[:, 0, :], op=ALU.mult), reads=[kpim, kh], writes=[k4])
                kb.pool(lambda e, t1=t1, t2=t2, ft=ft, bs=bs: e.tensor_tensor(out=Y[:, 0, ft, bs], in0=t1, in1=t2, op=ALU.subtract), reads=[k1, k2], writes=[kY + ".0.%d" % ft])
                kb.pool(lambda e, t3=t3, t4=t4, ft=ft, bs=bs: e.tensor_tensor(out=Y[:, 1, ft, bs], in0=t3, in1=t4, op=ALU.add), reads=[k3, k4], writes=[kY + ".1.%d" % ft])
        psn, kpsn = self.nb()
        for tt in range(NT):
            kb.pe(lambda e, psn=psn, tt=tt, src=src: e.matmul(psn[0:1, :], pmc[:, 0:1], src[:, tt, :], start=(tt == 0), stop=(tt == NT - 1)), reads=[kpmc, ksrc_all[tt]], writes=[kpsn])
        for b in range(2):
            kb.dve(lambda e, psn=psn, b=b, o=o: e.tensor_tensor(out=ynq[0:1, b * 256:(b + 1) * 256], in0=psn[0:1, b * 256:(b + 1) * 256], in1=hnq[0:1, o * 256:(o + 1) * 256], op=ALU.mult),
                   reads=[kpsn, khnq], writes=[kynq])
        gateD = HX1 if o == 0 else HX2
        kY0 = [kY + ".0.%d" % ft for ft in range(NT)]
        kY1 = [kY + ".1.%d" % ft for ft in range(NT)]
        for j in range(NT):
            c_ap, kc = Cb[j % 2]
            s_ap, ks = Sb[j % 2]
            g_ap, kg = gt[j % 2]
            tb_ap, ktb = tb[j % 2]
            t2_ap, kt2 = t2b[j % 2]
            o_ap, ko = ostg[j % 2]
            kb.dma(c_ap, self.k["dftc_" + nm][j], writes=[kc])
            kb.dma(s_ap, self.k["dfts_" + nm][j], writes=[ks])
            kb.dma(g_ap, gateD[j * 128:(j + 1) * 128, :], writes=[kg])
            py, kpy = self.nb()
            for ft in range(NT):
                kb.pe(lambda e, py=py, c_ap=c_ap, ft=ft: e.matmul(py, c_ap[:, ft, :], Y[:, 0, ft, :], start=(ft == 0), stop=False), reads=[kc, kY0[ft]], writes=[kpy])
                kb.pe(lambda e, py=py, s_ap=s_ap, ft=ft: e.matmul(py, s_ap[:, ft, :], Y[:, 1, ft, :], start=False, stop=False), reads=[ks, kY1[ft]], writes=[kpy])
            kb.pe(lambda e, py=py: e.matmul(py, pmr[0:1, :], ynq[0:1, :], start=False, stop=True), reads=[kpmr, kynq], writes=[kpy])
            for b in range(2):
                kb.pool(lambda e, tb_ap=tb_ap, src=src, j=j, b=b, o=o: e.tensor_tensor(out=tb_ap[:, b * 256:(b + 1) * 256], in0=src[:, j, b * 256:(b + 1) * 256], in1=bias[:, o, :], op=ALU.mult),
                        reads=[ksrc_all[j], kbias], writes=[ktb])
            kb.dve(lambda e, py=py, tb_ap=tb_ap, t2_ap=t2_ap: e.tensor_tensor(out=t2_ap, in0=py, in1=tb_ap, op=ALU.add), reads=[kpy, ktb], writes=[kt2])
            if o == 0:
                kb.pool(lambda e, t2_ap=t2_ap, g_ap=g_ap, j=j: e.tensor_tensor(out=Zb[:, j, :], in0=t2_ap, in1=g_ap, op=ALU.mult), reads=[kt2, kg], writes=[kZb + ".%d" % j])
            else:
                kb.pool(lambda e, t2_ap=t2_ap, g_ap=g_ap, o_ap=o_ap: e.tensor_tensor(out=o_ap, in0=t2_ap, in1=g_ap, op=ALU.mult), reads=[kt2, kg], writes=[ko])
                kb.dma(HYO[j * 128:(j + 1) * 128, :], o_ap, reads=[ko], writes=["HYO"])
    self.phase_end()


def _hyena_phase(self, l):
    last = (l == L - 1)
    groups = [("lat", self.lat, S)] + ([] if last else [("ctx", self.ctx, CL)])
    for nm, seqs, n in groups:
        self.hy_h0(l, nm, seqs, n)
        self.hy_h1(l, nm, n)
        self.hy_h2(l, nm, seqs, n)


Prog.hy_h0 = _hy_h0
Prog.hy_h1 = _hy_h1
Prog.hy_h2 = _hy_h2
Prog.phase_hyena = _hyena_phase


def build_program(nc, dbg=(), scopes=False):
    P = Prog(nc, dbg=dbg)
    kb = P.kb
    kb.scopes = scopes
    kb.phase = "mod"
    P.declare()
    P.phase_mod()
    kb.phase = "t0"
    P.phase_t0()
    for l in range(L):
        kb.phase = "a%d" % l
        P.phase_a(l)
        kb.phase = "pool%d" % l
        P.phase_pool(l)
        kb.phase = "hy%d" % l
        P.phase_hyena(l)
        kb.phase = "attn%d" % l
        P.phase_attn(l)
        kb.phase = "c1_%d" % l
        P.phase_c1(l)
        kb.phase = "c2_%d" % l
        P.phase_c2(l)
    kb.phase = "final"
    P.phase_final()
    kb.emit()
    return P


_PROG_CACHE = {}


def kernel(**inputs):
    shared = _shared_inputs(inputs)
    nc = bass.Bass("TRN2", target_bir_lowering=False)
    P = build_program(nc)
    in_maps = []
    for core in range(NCORES):
        m = _core_inputs(inputs, core, shared)
        in_maps.append({k_: v for k_, v in m.items() if k_ in P.dram})
    res = run_bass_kernel_spmd(nc, in_maps, core_ids=list(range(NCORES)))
    out = np.concatenate([np.asarray(r["y"], dtype=np.float32) for r in res.results], axis=0)
    return out
```

```python
import math
import numpy as np
import ml_dtypes
import concourse.bass as bass
import concourse.mybir as mybir
from concourse.bass_utils import run_bass_kernel_spmd

F32 = mybir.dt.float32
BF16 = mybir.dt.bfloat16
AF = mybir.ActivationFunctionType
ALU = mybir.AluOpType
AX = mybir.AxisListType

import os
SAME_ENGINE_SYNC = os.environ.get("MK_SES", "1") == "1"
N_DMA_SEMS = 24


class _Op:
    __slots__ = ("eng", "fn", "deps", "need_inc", "cnt", "dsem", "dval", "is_dma", "idx", "phase")

    def __init__(self, eng, fn, is_dma):
        self.eng = eng
        self.fn = fn
        self.deps = []
        self.need_inc = False
        self.cnt = 0
        self.dsem = -1
        self.dval = 0
        self.is_dma = is_dma
        self.idx = 0


class KB:
    ENGS = ("pe", "act", "dve", "pool", "sp")

    def __init__(self, nc):
        self.nc = nc
        self.ops = []
        self.last_w = {}
        self.readers = {}
        self.n_dma = 0
        self._bar_start = 0
        self.phase = None
        self.scopes = False

    def _add(self, eng, fn, reads, writes, is_dma):
        op = _Op(eng, fn, is_dma)
        op.idx = len(self.ops)
        op.phase = self.phase
        deps = set()
        for k in reads:
            w = self.last_w.get(k)
            if w is not None:
                deps.add(w)
        for k in writes:
            w = self.last_w.get(k)
            if w is not None:
                deps.add(w)
            for r in self.readers.get(k, ()):
                deps.add(r)
        deps.discard(op.idx)
        op.deps = sorted(deps)
        for k in reads:
            lst = self.readers.setdefault(k, [])
            if not is_dma:
                lst[:] = [r for r in lst if self.ops[r].is_dma or self.ops[r].eng != eng]
            lst.append(op.idx)
        for k in writes:
            self.last_w[k] = op.idx
            self.readers[k] = []
        self.ops.append(op)
        return op

    def pe(self, fn, reads=(), writes=()):
        return self._add("pe", fn, reads, writes, False)

    def act(self, fn, reads=(), writes=()):
        return self._add("act", fn, reads, writes, False)

    def dve(self, fn, reads=(), writes=()):
        return self._add("dve", fn, reads, writes, False)

    def pool(self, fn, reads=(), writes=()):
        return self._add("pool", fn, reads, writes, False)

    def dma(self, out, in_, reads=(), writes=(), q="sp", **kw):
        def fn(e, out=out, in_=in_, kw=kw):
            return e.dma_start(out=out, in_=in_, **kw)
        return self._add(q, fn, reads, writes, True)

    def barrier(self):
        last = {}
        dmas = []
        for op in self.ops[self._bar_start:]:
            if op.is_dma:
                dmas.append(op.idx)
            elif op.fn is not None:
                last[op.eng] = op.idx
        deps = sorted(set(list(last.values()) + dmas))
        for e in self.ENGS:
            op = _Op(e, None, False)
            op.idx = len(self.ops)
            op.phase = self.phase
            op.deps = list(deps)
            self.ops.append(op)
        self._bar_start = len(self.ops)
        self.last_w = {}
        self.readers = {}

    def emit(self):
        nc = self.nc
        ops = self.ops
        for op in ops:
            best = {}
            for d in op.deps:
                dop = ops[d]
                if dop.is_dma or dop.fn is None:
                    continue
                if dop.eng == op.eng and not op.is_dma:
                    if dop.eng == "pe" or not SAME_ENGINE_SYNC:
                        continue
                if d > best.get(dop.eng, -1):
                    best[dop.eng] = d
            for d in best.values():
                ops[d].need_inc = True
        cnt = {e: 0 for e in self.ENGS}
        dma_cnt = [0] * N_DMA_SEMS
        nd = 0
        for op in ops:
            if op.is_dma:
                s = nd % N_DMA_SEMS
                nd += 1
                dma_cnt[s] += 1
                op.dsem = s
                op.dval = 16 * dma_cnt[s]
            elif op.need_inc:
                cnt[op.eng] += 1
                op.cnt = cnt[op.eng]
        per_eng = {e: [] for e in self.ENGS}
        for op in ops:
            per_eng[op.eng].append(op)
        self.stats = {e: len(per_eng[e]) for e in self.ENGS}
        self.stats["incs"] = dict(cnt)

        import contextlib
        with contextlib.ExitStack() as st:
            esem = {e: st.enter_context(nc.semaphore("s_" + e)) for e in self.ENGS}
            dsem = [st.enter_context(nc.semaphore("d%d" % i)) for i in range(N_DMA_SEMS)]
            block = st.enter_context(nc.Block())

            def run(eng_name, e):
                waited_e = {x: 0 for x in self.ENGS}
                waited_d = [0] * N_DMA_SEMS
                cur = None
                for op in per_eng[eng_name]:
                    if self.scopes and op.phase != cur:
                        if cur is not None:
                            nc.pop_named_scope(cur)
                        cur = op.phase
                        if cur is not None:
                            nc.push_named_scope(cur)
                    need_e = {}
                    need_d = {}
                    for d in op.deps:
                        dop = ops[d]
                        if dop.is_dma:
                            if dop.dval > waited_d[dop.dsem]:
                                need_d[dop.dsem] = max(need_d.get(dop.dsem, 0), dop.dval)
                        else:
                            if dop.eng == eng_name and not op.is_dma:
                                if eng_name == "pe" or not SAME_ENGINE_SYNC:
                                    continue
                            if dop.cnt > waited_e[dop.eng]:
                                need_e[dop.eng] = max(need_e.get(dop.eng, 0), dop.cnt)
                    if op.is_dma:
                        prev = op.dval - 16
                        if prev > waited_d[op.dsem]:
                            need_d[op.dsem] = max(need_d.get(op.dsem, 0), prev)
                    for x, v in need_e.items():
                        e.wait_ge(esem[x], v)
                        waited_e[x] = v
                    for s, v in need_d.items():
                        e.wait_ge(dsem[s], v)
                        waited_d[s] = v
                    if op.fn is None:
                        continue
                    ins = op.fn(e)
                    if op.is_dma:
                        ins.then_inc(dsem[op.dsem], 16)
                    elif op.need_inc:
                        ins.then_inc(esem[eng_name], 1)
                if self.scopes and cur is not None:
                    nc.pop_named_scope(cur)
                last = {}
                for op in per_eng[eng_name]:
                    if op.is_dma:
                        last[op.dsem] = max(last.get(op.dsem, 0), op.dval)
                for s, v in last.items():
                    if v > waited_d[s]:
                        e.wait_ge(dsem[s], v)

            @block.sync
            def _(e):
                run("sp", e)

            @block.scalar
            def _(e):
                run("act", e)

            @block.vector
            def _(e):
                run("dve", e)

            @block.gpsimd
            def _(e):
                run("pool", e)

            @block.tensor
            def _(e):
                run("pe", e)


D = 1024
S = 4096
CL = 256
L = 2
NCORES = 8
NBC = 2
NH = 8
NOPE = 64
ROPE = 32
DV = 64
QR = 256
KVR = 128
NIN = 1440
C_KV, C_KR, C_Q, C_POOL, C_HY = 0, 128, 160, 416, 672
DFF = 4096
EPS = 1e-6
SCALE = float((NOPE + ROPE) ** -0.5)
POOL_WINDOWS = (2, 4, 8, 16)
HY_EMB = 33
HY_FFN = 64
MAGIC = 12582912.0
BFNP = ml_dtypes.bfloat16


def _rope_perm():
    perm = np.zeros(32, np.int64)
    for a in range(2):
        for hf in range(2):
            for f in range(8):
                perm[a * 16 + hf * 8 + f] = a * 16 + (1 - hf) * 8 + f
    return perm


def _rope_tables():
    n = S
    rows = n // 64
    r = np.repeat(np.arange(rows), 64).astype(np.float32)
    cidx = np.tile(np.arange(64), rows).astype(np.float32)
    inv = np.power(np.float32(10000.0), -(np.arange(8, dtype=np.float32) / np.float32(8))).astype(np.float32)
    ang = np.stack([r[:, None] * inv, cidx[:, None] * inv], axis=1).astype(np.float32)
    cos = np.cos(ang).astype(np.float32)
    sin = np.sin(ang).astype(np.float32)
    cosT = np.zeros((32, n), np.float32)
    sinT = np.zeros((32, n), np.float32)
    for a in range(2):
        for hf in range(2):
            for f in range(8):
                rr = a * 16 + hf * 8 + f
                cosT[rr] = cos[:, a, f]
                sinT[rr] = (-sin[:, a, f]) if hf == 0 else sin[:, a, f]
    return cosT, sinT


def _pool_bands(n, W):
    blocks = []
    index = {}
    table = {}
    nt = n // W
    for g, w in enumerate(POOL_WINDOWS):
        t = np.arange(n)
        lo = np.clip(t - w // 2, 0, n)
        hi = np.clip(t + w // 2, 0, n)
        for i in range(nt):
            T0 = i * W
            m_lo = max((T0 - w // 2) // 128, 0)
            m_hi = min((T0 + W + w // 2 - 1) // 128, n // 128 - 1)
            for m in range(m_lo, m_hi + 1):
                blk = np.zeros((128, 512), np.float32)
                for tt in range(T0, T0 + W):
                    s0 = max(lo[tt], m * 128)
                    s1 = min(hi[tt], (m + 1) * 128)
                    if s1 > s0:
                        blk[s0 - m * 128:s1 - m * 128, tt - T0] += 1.0 / float(hi[tt] - lo[tt])
                    if m * 128 <= tt < (m + 1) * 128:
                        blk[tt - m * 128, tt - T0] -= 1.0
                if not blk.any():
                    continue
                key = blk.tobytes()
                if key not in index:
                    index[key] = len(blocks)
                    blocks.append(blk)
                table[(g, i, m)] = index[key]
    return np.stack(blocks).astype(BFNP), table


def _dft_blocks(n):
    nt = n // 128
    t = np.arange(n, dtype=np.int64)
    m = (t[:, None] * t[None, :]) % (2 * n)
    ang = m.astype(np.float64) * (2.0 * np.pi / (2 * n))
    Cm = np.cos(ang)
    Sm = -np.sin(ang)

    def tile(M):
        M4 = M.reshape(nt, 128, nt, 128)
        return np.ascontiguousarray(M4.transpose(2, 1, 0, 3)).astype(BFNP)
    return tile(Cm), tile(Sm)


def _hy_consts(n):
    f32 = np.float32
    t = np.linspace(0.0, 1.0, n, dtype=f32)[:, None]
    bands = (HY_EMB - 1) // 2
    freqs = np.linspace(1e-4, bands - 1, bands, dtype=f32)[None, :]
    wpos = (f32(2.0 * math.pi) * np.arange(n, dtype=f32)[:, None] / f32(n)).astype(f32)
    z = np.concatenate([t, np.cos(freqs * wpos), -np.sin(freqs * wpos)], axis=-1).astype(f32)
    deltas = np.abs(np.linspace(math.log(1e-2) / 1.5, math.log(1e-2) / 0.3, 256, dtype=f32))
    dec = np.exp(-t * deltas[None, :]).astype(f32)
    pm1 = np.where(np.arange(n) % 2 == 0, 1.0, -1.0).astype(f32)
    return np.ascontiguousarray(z.T), dec, pm1


_CONST_CACHE = {}


def _host_consts():
    if _CONST_CACHE:
        return _CONST_CACHE
    c = _CONST_CACHE
    c["ident_f"] = np.eye(128, dtype=np.float32)
    cosT, sinT = _rope_tables()
    c["rope_cos"] = cosT
    c["rope_sin"] = sinT
    c["rope_cos_q"] = (cosT * np.float32(SCALE)).astype(np.float32)
    c["rope_sin_q"] = (sinT * np.float32(SCALE)).astype(np.float32)
    bl, tl = _pool_bands(S, 512)
    bc, tc = _pool_bands(CL, 256)
    c["band_lat"] = bl
    c["band_ctx"] = bc
    c["_band_tab_lat"] = tl
    c["_band_tab_ctx"] = tc
    for n, nm in ((S, "lat"), (CL, "ctx")):
        Cb, Sb = _dft_blocks(n)
        c["dftc_" + nm] = Cb
        c["dfts_" + nm] = Sb
        zT, dec, pm1 = _hy_consts(n)
        c["hyz_" + nm] = zT
        c["hydec_" + nm] = dec
        c["pm1_" + nm] = pm1
    return c


class Arena:
    def __init__(self, nc, nbytes):
        self.t = nc.alloc_sbuf_tensor("arena", [128, nbytes // 2], BF16).ap()
        self.cap = nbytes
        self.off = 0
        self.cnt = 0

    def reset(self):
        self.off = 0

    def alloc(self, free_shape, dtype, name="t"):
        esz = 4 if dtype == F32 else 2
        n = 1
        for d_ in free_shape:
            n *= d_
        nb = n * esz
        off = (self.off + 63) // 64 * 64
        assert off + nb <= self.cap, "arena overflow %s %d+%d>%d" % (name, off, nb, self.cap)
        self.off = off + nb
        ap = self.t[:, off // 2:(off + nb) // 2]
        if dtype == F32:
            ap = ap.bitcast(F32)
        if len(free_shape) == 2:
            ap = ap.rearrange("p (a b) -> p a b", b=free_shape[1])
        elif len(free_shape) == 3:
            ap = ap.rearrange("p (a b c) -> p a b c", b=free_shape[1], c=free_shape[2])
        self.cnt += 1
        return ap, "%s#%d" % (name, self.cnt)


class Seq:
    def __init__(self, name, n, v, kind, b):
        self.name = name
        self.n = n
        self.v = v
        self.kind = kind
        self.b = b
        self.W = 512 if n >= 512 else n
        self.nt = n // self.W


class Prog:
    def __init__(self, nc, dbg=()):
        self.nc = nc
        self.kb = KB(nc)
        self.dbg = set(dbg)
        self.dram = {}
        self.consts = _host_consts()

    def din(self, name, shape, dtype=F32):
        t = self.nc.dram_tensor(name, list(shape), dtype, kind="ExternalInput").ap()
        self.dram[name] = t
        return t

    def dscr(self, name, shape, dtype=F32):
        kind = "ExternalOutput" if name in self.dbg else "Internal"
        t = self.nc.dram_tensor(name, list(shape), dtype, kind=kind).ap()
        self.dram[name] = t
        return t

    def declare(self):
        nc = self.nc
        c = self.consts
        self.x_in = self.din("x", [NBC, S, D])
        self.ctx_in = self.din("ctx", [NBC, CL, D])
        self.cv_in = self.din("cv", [3, D])
        self.y_out = nc.dram_tensor("y", [NBC, S, D], F32, kind="ExternalOutput").ap()
        w = {}
        w["w_mod"] = self.din("w_mod", [L, D, 6 * D])
        w["b_mod"] = self.din("b_mod", [L, 6 * D])
        w["g_mix"] = self.din("g_mix", [L, D])
        w["g_mlp"] = self.din("g_mlp", [L, D])
        w["w_in"] = self.din("w_in", [L, D, NIN])
        w["w_in_rot"] = self.din("w_in_rot", [L, D, ROPE])
        w["g_q"] = self.din("g_q", [L, QR])
        w["w_q_up"] = self.din("w_q_up", [L, QR, NH * 96])
        w["w_q_rot"] = self.din("w_q_rot", [L, QR, NH * 96])
        w["g_kv"] = self.din("g_kv", [L, KVR])
        w["w_kv_up"] = self.din("w_kv_up", [L, KVR, NH * 128])
        w["pool_w"] = self.din("pool_w", [L, 4, 64, 64])
        w["pool_scale"] = self.din("pool_scale", [L, 256])
        w["hy_conv_w"] = self.din("hy_conv_w", [L, 3, 768])
        w["hy_conv_b"] = self.din("hy_conv_b", [L, 768])
        w["hy_f_w1"] = self.din("hy_f_w1", [L, HY_EMB, HY_FFN])
        w["hy_f_b1"] = self.din("hy_f_b1", [L, HY_FFN])
        w["hy_f_freq1"] = self.din("hy_f_freq1", [L, HY_FFN])
        w["hy_f_w2"] = self.din("hy_f_w2", [L, HY_FFN, HY_FFN])
        w["hy_f_b2"] = self.din("hy_f_b2", [L, HY_FFN])
        w["hy_f_freq2"] = self.din("hy_f_freq2", [L, HY_FFN])
        w["hy_f_w3"] = self.din("hy_f_w3", [L, HY_FFN, 1024])
        w["hy_bias"] = self.din("hy_bias", [L, 2, 256])
        w["g_out"] = self.din("g_out", [L, D])
        w["w_out"] = self.din("w_out", [L, D, D])
        w["w_mlp1"] = self.din("w_mlp1", [L, D, DFF])
        w["w_mlp2"] = self.din("w_mlp2", [L, DFF, D])
        w["g_final"] = self.din("g_final", [D])
        self.w = w
        k = {}
        for name, arr in c.items():
            if name.startswith("_"):
                continue
            dt_ = BF16 if arr.dtype == BFNP else F32
            k[name] = self.din(name, arr.shape, dt_)
        self.k = k
        self.lat = [Seq("b%d" % b, S, b, "lat", b) for b in range(NBC)]
        self.ctx = [Seq("c%d" % b, CL, 2, "ctx", b) for b in range(NBC)]
        for s in self.lat + self.ctx:
            n = s.n
            s.XT = self.dscr("XT_" + s.name, [8, 128, n])
            s.PKV = self.dscr("PKV_" + s.name, [128, n], BF16)
            s.KR = self.dscr("KR_" + s.name, [32, n], BF16)
            s.PQ = self.dscr("PQ_" + s.name, [2, 128, n], BF16)
            s.POOLU = self.dscr("POOLU_" + s.name, [n, 256], BF16)
            s.HYP = self.dscr("HYP_" + s.name, [6, 128, n + 2])
            s.ATT = self.dscr("ATT_" + s.name, [n, 512])
            s.POOLO = self.dscr("POOLO_" + s.name, [n, 256])
        for nm, n in (("lat", S), ("ctx", CL)):
            setattr(self, "HV_" + nm, self.dscr("HV_" + nm, [n, 512], BF16))
            setattr(self, "HX1_" + nm, self.dscr("HX1_" + nm, [n, 512], BF16))
            setattr(self, "HX2_" + nm, self.dscr("HX2_" + nm, [n, 512], BF16))
            setattr(self, "HYO_" + nm, self.dscr("HYO_" + nm, [n, 512]))
            setattr(self, "HS_" + nm, self.dscr("HS_" + nm, [2, n // 128, 128, 512], BF16))
            setattr(self, "HNQ_" + nm, self.dscr("HNQ_" + nm, [1, 512]))
        A = lambda name, shape, dt_: nc.alloc_sbuf_tensor(name, shape, dt_).ap()
        self.ident_f = A("ident_f_sb", [128, 128], F32)
        self.ident_b = A("ident_b", [128, 128], BF16)
        self.ones_b = A("ones_b", [128, 128], BF16)
        self.ones_f = A("ones_f", [128, 128], F32)
        self.eps = A("eps_t", [128, 1], F32)
        self.modT = A("modT", [128, L * 6 * 8 * 3], F32).rearrange("p (l k j v) -> p l k j v", l=L, k=6, j=8)
        self.ps2 = [nc.alloc_psum_tensor("ps2_%d" % i, [128, 1024], F32).ap() for i in range(4)]
        self.arena = Arena(nc, 205 * 1024)
        kb = self.kb
        kb.dma(self.ident_f, self.k["ident_f"], writes=["ident_f"])
        kb.act(lambda e: e.activation(out=self.ident_b, in_=self.ident_f, func=AF.Copy), reads=["ident_f"], writes=["ident_b"])
        kb.dve(lambda e: e.memset(self.ones_b, 1.0), writes=["ones_b"])
        kb.dve(lambda e: e.memset(self.ones_f, 1.0), writes=["ones_f"])
        kb.dve(lambda e: e.memset(self.eps, EPS), writes=["eps"])

    def bank(self, kidx):
        return self.ps2[kidx // 2][:, (kidx % 2) * 512:(kidx % 2 + 1) * 512], "pb%d" % kidx

    def mod(self, l, kind, v):
        return self.modT[:, l, kind, :, v]

    def phase_end(self):
        self.kb.barrier()
        self.arena.reset()

    def phase_mod(self):
        kb, ar, w = self.kb, self.arena, self.w
        cvs, kcvs = ar.alloc([D], F32, "cvs")
        sil, ksil = ar.alloc([D], F32, "sil")
        silT, ksilT = ar.alloc([8, 3], F32, "silT")
        kb.dma(cvs[0:3, :], self.cv_in, writes=[kcvs])
        kb.act(lambda e: e.activation(out=sil[0:3, :], in_=cvs[0:3, :], func=AF.Silu), reads=[kcvs], writes=[ksil])
        pb, kpb = self.bank(0)
        for j in range(8):
            kb.pe(lambda e, j=j: e.transpose(pb[:, j * 3:(j + 1) * 3], sil[0:3, j * 128:(j + 1) * 128], self.ident_f[0:3, 0:3]),
                  reads=[ksil, "ident_f"], writes=[kpb])
        kb.dve(lambda e: e.tensor_copy(out=silT.rearrange("p j v -> p (j v)"), in_=pb[:, 0:24]), reads=[kpb], writes=[ksilT])
        wbuf = [ar.alloc([8, 512], F32, "wmod") for _ in range(2)]
        modrow, kmodrow = ar.alloc([6 * D], F32, "modrow")
        brow, kbrow = ar.alloc([6 * D], F32, "brow")
        gbc, kgbc = ar.alloc([2, D], F32, "gbc")
        for l in range(L):
            kb.dma(brow[0:3, :], w["b_mod"][l].partition_broadcast(3), writes=[kbrow])
            kb.dma(gbc[0:3, 0, :], w["g_mix"][l].partition_broadcast(3), writes=[kgbc])
            kb.dma(gbc[0:3, 1, :], w["g_mlp"][l].partition_broadcast(3), writes=[kgbc])
            for ncn in range(12):
                wb, kwb = wbuf[ncn % 2]
                kb.dma(wb, w["w_mod"][l][:, ncn * 512:(ncn + 1) * 512].rearrange("(j p) n -> p j n", p=128), writes=[kwb])
                pm, kpm = self.bank(1 + ncn % 2)
                for j in range(8):
                    kb.pe(lambda e, j=j, wb=wb, pm=pm: e.matmul(pm[0:3, :], silT[:, j, :], wb[:, j, :], start=(j == 0), stop=(j == 7)),
                          reads=[ksilT, kwb], writes=[kpm])
                kb.dve(lambda e, pm=pm, ncn=ncn: e.tensor_tensor(out=modrow[0:3, ncn * 512:(ncn + 1) * 512], in0=pm[0:3, :],
                                                                in1=brow[0:3, ncn * 512:(ncn + 1) * 512], op=ALU.add),
                       reads=[kpm, kbrow], writes=[kmodrow])
            for (kind, gi) in ((1, 0), (4, 1)):
                sl = modrow[0:3, kind * D:(kind + 1) * D]
                kb.dve(lambda e, sl=sl, gi=gi: e.scalar_tensor_tensor(out=sl, in0=sl, scalar=1.0, in1=gbc[0:3, gi, :], op0=ALU.add, op1=ALU.mult),
                       reads=[kmodrow, kgbc], writes=[kmodrow])
            pt, kpt = self.bank(3)
            for kind in range(6):
                for j in range(8):
                    col = (kind * 8 + j) * 3
                    kb.pe(lambda e, kind=kind, j=j, col=col: e.transpose(pt[:, col:col + 3], modrow[0:3, kind * D + j * 128: kind * D + (j + 1) * 128],
                                                                        self.ident_f[0:3, 0:3]),
                          reads=[kmodrow, "ident_f"], writes=[kpt])
            kb.dve(lambda e, l=l: e.tensor_copy(out=self.modT[:, l].rearrange("p k j v -> p (k j v)"), in_=pt[:, 0:144]),
                   reads=[kpt], writes=["modT"])
        self.phase_end()

    def nb(self):
        self._nb = (getattr(self, "_nb", -1) + 1) % 8
        return self.bank(self._nb)

    def evac(self, i, fn_act, fn_dve, reads, writes):
        if i % 2 == 0:
            self.kb.act(fn_act, reads=reads, writes=writes)
        else:
            self.kb.dve(fn_dve, reads=reads, writes=writes)

    def phase_t0(self):
        kb, ar = self.kb, self.arena
        for s in self.lat + self.ctx:
            src = self.x_in[s.b] if s.kind == "lat" else self.ctx_in[s.b]
            W, nsub = s.W, s.W // 128
            xin = [ar.alloc([nsub, D], F32, "xin") for _ in range(2)]
            xt = [ar.alloc([8, W], F32, "xt") for _ in range(2)]
            for i in range(s.nt):
                xi, kxi = xin[i % 2]
                xo, kxo = xt[i % 2]
                kb.dma(xi, src[i * W:(i + 1) * W, :].rearrange("(a p) d -> p a d", p=128), writes=[kxi])
                for j in range(8):
                    pb, kpb = self.nb()
                    for sub in range(nsub):
                        kb.pe(lambda e, pb=pb, xi=xi, j=j, sub=sub: e.transpose(pb[:, sub * 128:(sub + 1) * 128], xi[:, sub, j * 128:(j + 1) * 128], self.ident_f),
                              reads=[kxi, "ident_f"], writes=[kpb])
                    self.evac(j, lambda e, pb=pb, xo=xo, j=j, W=W: e.activation(out=xo[:, j, :], in_=pb[:, 0:W], func=AF.Copy),
                              lambda e, pb=pb, xo=xo, j=j, W=W: e.tensor_copy(out=xo[:, j, :], in_=pb[:, 0:W]),
                              reads=[kpb], writes=[kxo + ".%d" % j])
                kb.dma(s.XT[:, :, i * W:(i + 1) * W].rearrange("j p t -> p j t"), xo,
                       reads=[kxo + ".%d" % j for j in range(8)], writes=["XT_" + s.name])
            self.phase_end()

    def fm_rstd(self, chunks, nfeat, W, sq, ksq, rs, krs, extra_sq=None):
        kb = self.kb
        pss, kpss = self.nb()
        n = len(chunks)
        for ci, (ap, key, P) in enumerate(chunks):
            kb.act(lambda e, ap=ap, ci=ci, P=P: e.activation(out=sq[0:P, ci, 0:W], in_=ap, func=AF.Square),
                   reads=[key], writes=[ksq + ".%d" % ci] + (extra_sq(ci) if extra_sq else []))
            kb.pe(lambda e, ci=ci, P=P: e.matmul(pss[:, 0:W], self.ones_b[0:P, :], sq[0:P, ci, 0:W], start=(ci == 0), stop=(ci == n - 1)),
                  reads=[ksq + ".%d" % ci, "ones_b"], writes=[kpss])
        kb.act(lambda e: e.activation(out=rs[:, 0:W], in_=pss[:, 0:W], func=AF.Sqrt, bias=self.eps[:, 0:1], scale=1.0 / nfeat),
               reads=[kpss, "eps"], writes=[krs])
        kb.dve(lambda e: e.reciprocal(out=rs[:, 0:W], in_=rs[:, 0:W]), reads=[krs], writes=[krs])

    def mod_norm(self, xt, kxt, W, gm, sh, sq, ksq, rs, krs, tmp, ktmp, hT, khT, extra=None, extra_sq=None):
        kb = self.kb
        self.fm_rstd([(xt[:, j, 0:W], kxt + ".%d" % j, 128) for j in range(8)], D, W, sq, ksq, rs, krs, extra_sq=extra_sq)
        for j in range(8):
            kb.dve(lambda e, j=j: e.scalar_tensor_tensor(out=tmp[:, j, 0:W], in0=xt[:, j, 0:W], scalar=gm[:, j:j + 1], in1=rs[:, 0:W],
                                                         op0=ALU.mult, op1=ALU.mult),
                   reads=[kxt + ".%d" % j, krs, "modT"], writes=[ktmp + ".%d" % j] + (extra(j) if extra else []))
            kb.act(lambda e, j=j: e.activation(out=hT[:, j, 0:W], in_=tmp[:, j, 0:W], func=AF.Identity, bias=sh[:, j:j + 1], scale=1.0),
                   reads=[ktmp + ".%d" % j, "modT"], writes=[khT + ".%d" % j])

    def load_cast_rows(self, dst, src2d, nj, split=1):
        ap, key = dst
        n = src2d.shape[-1]
        step = n // split
        for j in range(nj):
            for sp_ in range(split):
                self.kb.dma(ap[:, j, sp_ * step:(sp_ + 1) * step], src2d[j * 128:(j + 1) * 128, sp_ * step:(sp_ + 1) * step],
                            writes=[key + ".%d" % j], q="pool")

    def phase_a(self, l):
        kb, ar, w = self.kb, self.arena, self.w
        last = (l == L - 1)
        win = ar.alloc([8, NIN], BF16, "win")
        wrot = ar.alloc([8, ROPE], BF16, "wrot")
        self.load_cast_rows(win, w["w_in"][l], 8)
        self.load_cast_rows(wrot, w["w_in_rot"][l], 8)
        win_ap, kwin = win
        wrot_ap, kwrot = wrot
        kwin_all = [kwin + ".%d" % j for j in range(8)]
        kwrot_all = [kwrot + ".%d" % j for j in range(8)]
        gkv, kgkv = ar.alloc([1], F32, "gkv")
        gq, kgq = ar.alloc([2], F32, "gq")
        kb.dma(gkv, w["g_kv"][l].rearrange("(p o) -> p o", o=1), writes=[kgkv])
        for c_ in range(2):
            kb.dma(gq[:, c_:c_ + 1], w["g_q"][l][c_ * 128:(c_ + 1) * 128].rearrange("(p o) -> p o", o=1), writes=[kgq])
        zt, kzt = ar.alloc([6, 1], F32, "zt")
        kb.dve(lambda e: e.memset(zt, 0.0), writes=[kzt])
        WM = 512
        xts = [ar.alloc([8, WM], F32, "xt") for _ in range(2)]
        sq, ksq = ar.alloc([8, WM], BF16, "sq")
        rs, krs = ar.alloc([WM], F32, "rs")
        tmp, ktmp = ar.alloc([8, WM], F32, "tmp")
        hTs = [ar.alloc([8, WM], BF16, "hT") for _ in range(2)]
        sq2, ksq2 = ar.alloc([2, WM], BF16, "sq2")
        rs2, krs2 = ar.alloc([WM], F32, "rs2")
        pkvn = [ar.alloc([WM], BF16, "pkvn") for _ in range(2)]
        krs_ = [ar.alloc([WM], BF16, "kr") for _ in range(2)]
        pqn = [ar.alloc([2, WM], BF16, "pqn") for _ in range(2)]
        poolu = [ar.alloc([4, 256], BF16, "poolu") for _ in range(2)]
        hyp = [ar.alloc([6, WM], F32, "hyp") for _ in range(2)]
        ropec = [ar.alloc([WM], F32, "ropec") for _ in range(2)]
        ropes = [ar.alloc([WM], F32, "ropes") for _ in range(2)]
        rt1, krt1 = ar.alloc([WM], F32, "rt1")
        rt2, krt2 = ar.alloc([WM], F32, "rt2")
        tiles = []
        for s in self.lat + self.ctx:
            full = not (last and s.kind == "ctx")
            if full:
                for (a0, a1) in ((0, 1), (s.n + 1, s.n + 2)):
                    kb.dma(s.HYP[:, :, a0:a1].rearrange("c p o -> p c o"), zt, reads=[kzt], writes=["HYP_" + s.name], allow_slow_non_contiguous=True)
            for i in range(s.nt):
                tiles.append((s, i, full))

        def front(it):
            s, i, full = tiles[it]
            W = s.W
            gm, sh = self.mod(l, 1, s.v), self.mod(l, 0, s.v)
            xt, kxt = xts[it % 2]
            hT, khT = hTs[it % 2]
            t0 = i * W
            kb.dma(xt[:, :, 0:W], s.XT[:, :, t0:t0 + W].rearrange("j p t -> p j t"), reads=["XT_" + s.name],
                   writes=[kxt + ".%d" % j for j in range(8)])
            self.mod_norm(xt, kxt, W, gm, sh, sq, ksq, rs, krs, tmp, ktmp, hT, khT)

        def back(it):
            s, i, full = tiles[it]
            W, nsub = s.W, s.W // 128
            hT, khT = hTs[it % 2]
            t0 = i * W
            if True:
                khT_all = [khT + ".%d" % j for j in range(8)]

                def proj_fm(col0, ncol, dstps, wt=win_ap, kw=kwin_all, hT=hT, khT_all=khT_all, W=W):
                    for j in range(8):
                        kb.pe(lambda e, j=j: e.matmul(dstps[0:ncol, 0:W], wt[:, j, col0:col0 + ncol], hT[:, j, 0:W], start=(j == 0), stop=(j == 7)),
                              reads=[kw[j], khT_all[j]], writes=[dstps_key[0]])
                pkv, kpkv = self.nb()
                dstps_key = [kpkv]
                proj_fm(C_KV, 128, pkv)
                self.fm_rstd([(pkv[:, 0:W], kpkv, 128)], KVR, W, sq2, ksq2, rs2, krs2)
                o_, ko_ = pkvn[it % 2]
                kb.dve(lambda e, o_=o_, pkv=pkv, W=W: e.scalar_tensor_tensor(out=o_[:, 0:W], in0=pkv[:, 0:W], scalar=gkv[:, 0:1], in1=rs2[:, 0:W],
                                                                            op0=ALU.mult, op1=ALU.mult),
                       reads=[kpkv, krs2, kgkv], writes=[ko_])
                kb.dma(s.PKV[:, t0:t0 + W], o_[:, 0:W], reads=[ko_], writes=["PKV_" + s.name])
                pka, kpka = self.nb()
                dstps_key = [kpka]
                proj_fm(C_KR, ROPE, pka)
                o_, ko_ = krs_[it % 2]
                if s.kind == "lat":
                    pkb, kpkb = self.nb()
                    dstps_key = [kpkb]
                    proj_fm(0, ROPE, pkb, wt=wrot_ap, kw=kwrot_all)
                    rc, krc = ropec[it % 2]
                    rsn, krsn = ropes[it % 2]
                    kb.dma(rc[0:32, 0:W], self.k["rope_cos"][:, t0:t0 + W], writes=[krc])
                    kb.dma(rsn[0:32, 0:W], self.k["rope_sin"][:, t0:t0 + W], writes=[krsn])
                    kb.dve(lambda e, pka=pka, rc=rc, W=W: e.tensor_tensor(out=rt1[0:32, 0:W], in0=pka[0:32, 0:W], in1=rc[0:32, 0:W], op=ALU.mult),
                           reads=[kpka, krc], writes=[krt1])
                    kb.dve(lambda e, pkb=pkb, rsn=rsn, W=W: e.tensor_tensor(out=rt2[0:32, 0:W], in0=pkb[0:32, 0:W], in1=rsn[0:32, 0:W], op=ALU.mult),
                           reads=[kpkb, krsn], writes=[krt2])
                    kb.dve(lambda e, o_=o_, W=W: e.tensor_tensor(out=o_[0:32, 0:W], in0=rt1[0:32, 0:W], in1=rt2[0:32, 0:W], op=ALU.add),
                           reads=[krt1, krt2], writes=[ko_])
                else:
                    kb.act(lambda e, o_=o_, pka=pka, W=W: e.activation(out=o_[0:32, 0:W], in_=pka[0:32, 0:W], func=AF.Copy),
                           reads=[kpka], writes=[ko_])
                kb.dma(s.KR[:, t0:t0 + W], o_[0:32, 0:W], reads=[ko_], writes=["KR_" + s.name])
                if not full:
                    return
                pq = [self.nb() for _ in range(2)]
                for c_ in range(2):
                    dstps_key = [pq[c_][1]]
                    proj_fm(C_Q + c_ * 128, 128, pq[c_][0])
                self.fm_rstd([(pq[c_][0][:, 0:W], pq[c_][1], 128) for c_ in range(2)], QR, W, sq2, ksq2, rs2, krs2)
                o_, ko_ = pqn[it % 2]
                for c_ in range(2):
                    kb.dve(lambda e, o_=o_, c_=c_, pq=pq, W=W: e.scalar_tensor_tensor(out=o_[:, c_, 0:W], in0=pq[c_][0][:, 0:W], scalar=gq[:, c_:c_ + 1],
                                                                                  in1=rs2[:, 0:W], op0=ALU.mult, op1=ALU.mult),
                           reads=[pq[c_][1], krs2, kgq], writes=[ko_ + ".%d" % c_])
                kb.dma(s.PQ[:, :, t0:t0 + W].rearrange("c p t -> p c t"), o_[:, :, 0:W], reads=[ko_ + ".0", ko_ + ".1"], writes=["PQ_" + s.name])
                o_, ko_ = poolu[it % 2]
                for sub in range(nsub):
                    pp, kpp = self.nb()
                    for j in range(8):
                        kb.pe(lambda e, j=j, sub=sub, pp=pp, hT=hT: e.matmul(pp[:, 0:256], hT[:, j, sub * 128:(sub + 1) * 128], win_ap[:, j, C_POOL:C_POOL + 256],
                                                                            start=(j == 0), stop=(j == 7)),
                              reads=[kwin_all[j], khT_all[j]], writes=[kpp])
                    self.evac(sub, lambda e, o_=o_, pp=pp, sub=sub: e.activation(out=o_[:, sub, :], in_=pp[:, 0:256], func=AF.Copy),
                              lambda e, o_=o_, pp=pp, sub=sub: e.tensor_copy(out=o_[:, sub, :], in_=pp[:, 0:256]),
                              reads=[kpp], writes=[ko_ + ".%d" % sub])
                kb.dma(s.POOLU[t0:t0 + W, :].rearrange("(a p) c -> p a c", p=128), o_[:, 0:nsub, :],
                       reads=[ko_ + ".%d" % sub for sub in range(nsub)], writes=["POOLU_" + s.name])
                o_, ko_ = hyp[it % 2]
                for c6 in range(6):
                    ph, kph = self.nb()
                    dstps_key = [kph]
                    proj_fm(C_HY + c6 * 128, 128, ph)
                    self.evac(c6, lambda e, o_=o_, ph=ph, c6=c6, W=W: e.activation(out=o_[:, c6, 0:W], in_=ph[:, 0:W], func=AF.Copy),
                              lambda e, o_=o_, ph=ph, c6=c6, W=W: e.tensor_copy(out=o_[:, c6, 0:W], in_=ph[:, 0:W]),
                              reads=[kph], writes=[ko_ + ".%d" % c6])
                kb.dma(s.HYP[:, :, 1 + t0:1 + t0 + W].rearrange("c p t -> p c t"), o_[:, :, 0:W],
                       reads=[ko_ + ".%d" % c6 for c6 in range(6)], writes=["HYP_" + s.name])

        front(0)
        for it in range(len(tiles)):
            if it + 1 < len(tiles):
                front(it + 1)
            back(it)
        self.phase_end()


def _shared_inputs(inp):
    f = lambda a: np.ascontiguousarray(np.asarray(a, dtype=np.float32))
    sh = {}
    for k_ in ("w_mod", "b_mod", "g_mix", "g_mlp", "w_in", "g_q", "g_kv", "w_kv_up", "pool_w", "pool_scale", "hy_conv_w",
               "hy_conv_b", "hy_f_w1", "hy_f_b1", "hy_f_freq1", "hy_f_w2", "hy_f_b2", "hy_f_freq2", "hy_f_w3", "hy_bias",
               "g_out", "w_out", "w_mlp1", "w_mlp2", "g_final"):
        sh[k_] = f(inp[k_])
    perm = _rope_perm()
    w_in = sh["w_in"]
    sh["w_in_rot"] = np.ascontiguousarray(w_in[:, :, C_KR:C_KR + ROPE][:, :, perm])
    wq = f(inp["w_q_up"]).reshape(L, QR, NH, 96)
    sh["w_q_up"] = np.ascontiguousarray(wq.reshape(L, QR, NH * 96))
    wrot = np.zeros_like(wq)
    wrot[..., NOPE:] = wq[..., NOPE:][..., perm]
    sh["w_q_rot"] = np.ascontiguousarray(wrot.reshape(L, QR, NH * 96))
    for name, arr in _host_consts().items():
        if not name.startswith("_"):
            sh[name] = arr
    return sh


def _core_inputs(inp, core, shared):
    b0 = core * NBC
    m = dict(shared)
    m["x"] = np.ascontiguousarray(np.asarray(inp["x"][b0:b0 + NBC], dtype=np.float32))
    m["ctx"] = np.ascontiguousarray(np.asarray(inp["ctx"][b0:b0 + NBC], dtype=np.float32))
    cv = np.concatenate([np.asarray(inp["c"][b0:b0 + NBC], dtype=np.float32), np.asarray(inp["c_ctx"], dtype=np.float32)[None, :]], axis=0)
    m["cv"] = np.ascontiguousarray(cv)
    return m


def _attn_phase(self, l):
    kb, ar, w = self.kb, self.arena, self.w
    last = (l == L - 1)
    NK = CL + S
    NKT = NK // 128
    wkv = ar.alloc([1, NH * 128], BF16, "wkv")
    self.load_cast_rows(wkv, w["w_kv_up"][l], 1)
    wq = ar.alloc([2, NH * 96], BF16, "wq")
    wqr = ar.alloc([2, NH * 96], BF16, "wqr")
    self.load_cast_rows(wq, w["w_q_up"][l], 2)
    self.load_cast_rows(wqr, w["w_q_rot"][l], 2)
    wkv_ap, kwkv = wkv[0], wkv[1] + ".0"
    wq_ap, wqr_ap = wq[0], wqr[0]
    kwq = [wq[1] + ".0", wq[1] + ".1"]
    kwqr = [wqr[1] + ".0", wqr[1] + ".1"]
    cosq, kcosq = ar.alloc([S], F32, "cosq")
    sinq, ksinq = ar.alloc([S], F32, "sinq")
    kb.dma(cosq[64:96, :], self.k["rope_cos_q"], writes=[kcosq])
    kb.dma(sinq[64:96, :], self.k["rope_sin_q"], writes=[ksinq])
    pkv_b = [ar.alloc([NK], BF16, "pkv") for _ in range(2)]
    pq_b = [ar.alloc([2, S], BF16, "pq") for _ in range(2)]
    pqc_b = [ar.alloc([2, CL], BF16, "pqc") for _ in range(2)]
    KT = [ar.alloc([NK], BF16, "KT") for _ in range(2)]
    VA = [ar.alloc([NKT, 128], BF16, "VA") for _ in range(2)]
    QT = [ar.alloc([S], BF16, "QT") for _ in range(2)]
    QTc = [ar.alloc([CL], BF16, "QTc") for _ in range(2)]
    PT = [ar.alloc([1024], BF16, "PT") for _ in range(2)]
    oT = [ar.alloc([512], F32, "oT") for _ in range(2)]
    rc = [ar.alloc([4, 1], F32, "rc") for _ in range(2)]
    stg = [ar.alloc([4, 64], F32, "stg") for _ in range(2)]
    rt1, krt1 = ar.alloc([512], F32, "rt1")
    rt2, krt2 = ar.alloc([512], F32, "rt2")
    for hb in range(2):
        kb.pool(lambda e, hb=hb: e.memset(VA[hb][0][:, :, 64:128], 1.0), writes=[VA[hb][1] + ".ones"])
    misc = [self.bank(6), self.bank(7)]
    mi = [0]

    def mbank():
        mi[0] += 1
        return misc[mi[0] % 2]
    cnt_o = [0]

    def batch_loads(b):
        lat, cx = self.lat[b], self.ctx[b]
        (pkv, kpkv), (pq, kpq), (pqc, kpqc) = pkv_b[b % 2], pq_b[b % 2], pqc_b[b % 2]
        kb.dma(pkv[:, 0:CL], cx.PKV, reads=["PKV_" + cx.name], writes=[kpkv])
        kb.dma(pkv[:, CL:NK], lat.PKV, reads=["PKV_" + lat.name], writes=[kpkv])
        kb.dma(pq, lat.PQ.rearrange("c p t -> p c t"), reads=["PQ_" + lat.name], writes=[kpq])
        if not last:
            kb.dma(pqc, cx.PQ.rearrange("c p t -> p c t"), reads=["PQ_" + cx.name], writes=[kpqc])

    def kr_load(b, hb):
        lat, cx = self.lat[b], self.ctx[b]
        kb.dma(KT[hb][0][64:96, 0:CL], cx.KR, reads=["KR_" + cx.name], writes=[KT[hb][1] + ".r"])
        kb.dma(KT[hb][0][64:96, CL:NK], lat.KR, reads=["KR_" + lat.name], writes=[KT[hb][1] + ".r"])

    def build_steps(b, h):
        lat, cx = self.lat[b], self.ctx[b]
        (pkv, kpkv), (pq, kpq), (pqc, kpqc) = pkv_b[b % 2], pq_b[b % 2], pqc_b[b % 2]
        hb = h % 2
        kt_ap, kkt = KT[hb]
        va_ap, kva = VA[hb]
        steps = []

        def k_chunk(kc):
            k0 = kc * 512
            kw_ = min(512, NK - k0)
            pb, kpb = mbank()
            kb.pe(lambda e: e.matmul(pb[0:64, 0:kw_], wkv_ap[:, 0, h * 128:h * 128 + 64], pkv[:, k0:k0 + kw_], start=True, stop=True),
                  reads=[kwkv, kpkv], writes=[kpb])
            kb.dve(lambda e: e.tensor_copy(out=kt_ap[0:64, k0:k0 + kw_], in_=pb[0:64, 0:kw_]), reads=[kpb], writes=[kkt + ".n"])

        def v_group(g8):
            k0 = g8 * 8
            ng = min(8, NKT - k0)
            pb, kpb = mbank()
            for i in range(ng):
                kb.pe(lambda e, i=i: e.matmul(pb[:, i * 64:(i + 1) * 64], pkv[:, (k0 + i) * 128:(k0 + i + 1) * 128],
                                              wkv_ap[:, 0, h * 128 + 64:h * 128 + 128], start=True, stop=True),
                      reads=[kwkv, kpkv], writes=[kpb])
            kb.dve(lambda e: e.tensor_copy(out=va_ap[:, k0:k0 + ng, 0:64], in_=pb[:, 0:ng * 64].rearrange("p (a c) -> p a c", c=64)),
                   reads=[kpb], writes=[kva + ".v"])

        def q_chunk(qc, QW, rope, pq_ap, kpq_, qt_ap, kqt):
            q0 = qc * QW
            pa, kpa = mbank()
            for c_ in range(2):
                kb.pe(lambda e, c_=c_: e.matmul(pa[0:96, 0:QW], wq_ap[:, c_, h * 96:(h + 1) * 96], pq_ap[:, c_, q0:q0 + QW], start=(c_ == 0), stop=(c_ == 1)),
                      reads=[kwq[c_], kpq_], writes=[kpa])
            kb.dve(lambda e: e.tensor_scalar(out=qt_ap[0:64, q0:q0 + QW], in0=pa[0:64, 0:QW], scalar1=SCALE, scalar2=None, op0=ALU.mult),
                   reads=[kpa], writes=[kqt + ".n"])
            if rope:
                pb, kpb = mbank()
                for c_ in range(2):
                    kb.pe(lambda e, c_=c_: e.matmul(pb[0:96, 0:QW], wqr_ap[:, c_, h * 96:(h + 1) * 96], pq_ap[:, c_, q0:q0 + QW], start=(c_ == 0), stop=(c_ == 1)),
                          reads=[kwqr[c_], kpq_], writes=[kpb])
                kb.dve(lambda e: e.tensor_tensor(out=rt1[64:96, 0:QW], in0=pa[64:96, 0:QW], in1=cosq[64:96, q0:q0 + QW], op=ALU.mult),
                       reads=[kpa, kcosq], writes=[krt1])
                kb.dve(lambda e: e.tensor_tensor(out=rt2[64:96, 0:QW], in0=pb[64:96, 0:QW], in1=sinq[64:96, q0:q0 + QW], op=ALU.mult),
                       reads=[kpb, ksinq], writes=[krt2])
                kb.dve(lambda e: e.tensor_tensor(out=qt_ap[64:96, q0:q0 + QW], in0=rt1[64:96, 0:QW], in1=rt2[64:96, 0:QW], op=ALU.add),
                       reads=[krt1, krt2], writes=[kqt + ".r"])
            else:
                kb.dve(lambda e: e.tensor_scalar(out=qt_ap[64:96, q0:q0 + QW], in0=pa[64:96, 0:QW], scalar1=SCALE, scalar2=None, op0=ALU.mult),
                       reads=[kpa], writes=[kqt + ".r"])
        for kc in range((NK + 511) // 512):
            steps.append(lambda kc=kc: k_chunk(kc))
        for g8 in range((NKT + 7) // 8):
            steps.append(lambda g8=g8: v_group(g8))
        for qc in range(S // 512):
            steps.append(lambda qc=qc: q_chunk(qc, 512, True, pq, kpq, QT[hb][0], QT[hb][1]))
        if not last:
            steps.append(lambda: q_chunk(0, CL, False, pqc, kpqc, QTc[hb][0], QTc[hb][1]))
        return steps


    batch_loads(0)
    kr_load(0, 0)
    kr_load(0, 1)
    for b in range(NBC):
        lat, cx = self.lat[b], self.ctx[b]
        (pkv, kpkv), (pq, kpq), (pqc, kpqc) = pkv_b[b % 2], pq_b[b % 2], pqc_b[b % 2]

        if b == 0:
            for st_ in build_steps(0, 0):
                st_()
        for h in range(NH):
            hb = h % 2
            kt_ap, kkt = KT[hb]
            va_ap, kva = VA[hb]
            if h == 1 and b + 1 < NBC:
                batch_loads(b + 1)
            if h + 1 < NH:
                pending = build_steps(b, h + 1)
            elif b + 1 < NBC:
                kr_load(b + 1, 0)
                pending = build_steps(b + 1, 0)
            else:
                pending = []
            qsets = [(lat, S, NKT, QT[hb])]
            if not last:
                qsets.append((cx, CL, CL // 128, QTc[hb]))
            n_items_total = sum((nq_ // min(512, nq_)) * ((nkt_ + 1) // 2) for (_, nq_, nkt_, _) in qsets)
            every = max(1, (n_items_total - 8) // max(1, len(pending)))
            tick = [0]
            for (qs, nq, nkt, (qt_ap, kqt)) in qsets:
                QW = min(512, nq)
                npair = (nkt + 1) // 2
                items = [(qb, kp) for qb in range(nq // QW) for kp in range(npair)]
                po_of = {}
                for qb in range(nq // QW):
                    cnt_o[0] += 1
                    po_of[qb] = (self.bank(4 + cnt_o[0] % 2), cnt_o[0] % 2)

                def emit_S(ii, nkt=nkt, QW=QW, kt_ap=kt_ap, qt_ap=qt_ap, kkt=kkt, kqt=kqt, items=items):
                    qb, kp = items[ii]
                    q0 = qb * QW
                    ps_s = self.ps2[ii % 2]
                    kps = ["pb%d" % (2 * (ii % 2)), "pb%d" % (2 * (ii % 2) + 1)]
                    pt_ap, kpt = PT[ii % 2]
                    nh = min(2, nkt - 2 * kp)
                    for half in range(nh):
                        kt = 2 * kp + half
                        kb.pe(lambda e, ps_s=ps_s, half=half, kt=kt, q0=q0:
                              e.matmul(ps_s[:, half * 512:half * 512 + QW], kt_ap[0:96, kt * 128:(kt + 1) * 128], qt_ap[0:96, q0:q0 + QW], start=True, stop=True),
                              reads=[kkt + ".n", kkt + ".r", kqt + ".n", kqt + ".r"], writes=[kps[half]])
                    if QW == 512:
                        kb.act(lambda e, ps_s=ps_s, pt_ap=pt_ap, nh=nh: e.activation(out=pt_ap[:, 0:nh * 512], in_=ps_s[:, 0:nh * 512], func=AF.Exp),
                               reads=kps[0:nh], writes=[kpt])
                    else:
                        for half in range(nh):
                            kb.act(lambda e, ps_s=ps_s, pt_ap=pt_ap, half=half: e.activation(out=pt_ap[:, half * 512:half * 512 + QW], in_=ps_s[:, half * 512:half * 512 + QW], func=AF.Exp),
                                   reads=[kps[half]], writes=[kpt])

                def emit_PV(ii, nkt=nkt, QW=QW, va_ap=va_ap, kva=kva, qs=qs, h=h, npair=npair, items=items, po_of=po_of):
                    qb, kp = items[ii]
                    q0 = qb * QW
                    (po, kpo), par = po_of[qb]
                    pt_ap, kpt = PT[ii % 2]
                    nh = min(2, nkt - 2 * kp)
                    for half in range(nh):
                        kt = 2 * kp + half
                        kb.pe(lambda e, po=po, pt_ap=pt_ap, half=half, kt=kt:
                              e.matmul(po[:, 0:QW], va_ap[:, kt, :], pt_ap[:, half * 512:half * 512 + QW], start=(kt == 0), stop=(kt == nkt - 1)),
                              reads=[kva + ".v", kva + ".ones", kpt], writes=[kpo])
                    if kp != npair - 1:
                        return
                    o_ap, ko = oT[par]
                    r_ap, kr_ = rc[par]
                    s_ap, ks_ = stg[par]
                    nsub = QW // 128
                    kb.dve(lambda e, o_ap=o_ap, po=po: e.tensor_copy(out=o_ap[:, 0:QW], in_=po[:, 0:QW]), reads=[kpo], writes=[ko])
                    ptr, kptr = mbank()
                    for sub in range(nsub):
                        kb.pe(lambda e, ptr=ptr, o_ap=o_ap, sub=sub: e.transpose(ptr[:, sub * 128:(sub + 1) * 128], o_ap[:, sub * 128:(sub + 1) * 128], self.ident_f),
                              reads=[ko, "ident_f"], writes=[kptr])
                    kb.dve(lambda e, ptr=ptr, r_ap=r_ap, nsub=nsub: e.reciprocal(out=r_ap[:, 0:nsub, :], in_=ptr[:, 0:nsub * 128].rearrange("p (a c) -> p a c", c=128)[:, :, 64:65]),
                           reads=[kptr], writes=[kr_])
                    for sub in range(nsub):
                        kb.dve(lambda e, ptr=ptr, s_ap=s_ap, r_ap=r_ap, sub=sub: e.tensor_scalar(out=s_ap[:, sub, :], in0=ptr[:, sub * 128:sub * 128 + 64], scalar1=r_ap[:, sub, :], scalar2=None, op0=ALU.mult),
                               reads=[kptr, kr_], writes=[ks_ + ".%d" % sub])
                    kb.dma(qs.ATT[q0:q0 + QW, h * 64:(h + 1) * 64].rearrange("(a p) c -> p a c", p=128), s_ap[:, 0:nsub, :],
                           reads=[ks_ + ".%d" % sub for sub in range(nsub)], writes=["ATT_" + qs.name])

                emit_S(0)
                for ii in range(len(items)):
                    if ii + 1 < len(items):
                        emit_S(ii + 1)
                    emit_PV(ii)
                    tick[0] += 1
                    if pending and tick[0] % every == 0:
                        pending.pop(0)()
            while pending:
                pending.pop(0)()
            if h == NH - 1 and b + 1 < NBC:
                kr_load(b + 1, 1)
    self.phase_end()


Prog.phase_attn = _attn_phase


def _pool_phase(self, l):
    kb, ar, w = self.kb, self.arena, self.w
    last = (l == L - 1)
    pw = ar.alloc([4, 64], BF16, "pw")
    kb.dma(pw[0][0:64, :, :], w["pool_w"][l].rearrange("g i o -> i g o"), writes=[pw[1]], q="pool")
    psc, kpsc = ar.alloc([256], F32, "psc")
    kb.dma(psc, w["pool_scale"][l].partition_broadcast(128), writes=[kpsc])
    groups = [("lat", self.lat)] + ([] if last else [("ctx", self.ctx)])
    for nm, seqs in groups:
        bandc = self.k["band_" + nm]
        tab = self.consts["_band_tab_" + nm]
        nblk = bandc.shape[0]
        band, kband = ar.alloc([nblk, 512], BF16, "band")
        kb.dma(band, bandc.rearrange("b p t -> p b t"), writes=[kband])
        n = seqs[0].n
        NT = n // 128
        u, ku = ar.alloc([NT, 256], BF16, "u")
        dg = [ar.alloc([4, 512], BF16, "dg") for _ in range(2)]
        stg = [ar.alloc([4, 256], F32, "pstg") for _ in range(2)]
        it = 0
        for s in seqs:
            W, nsub = s.W, s.W // 128
            kb.dma(u, s.POOLU.rearrange("(a p) c -> p a c", p=128), reads=["POOLU_" + s.name], writes=[ku])
            for i in range(s.nt):
                d_ap, kd = dg[it % 2]
                s_ap, ks = stg[it % 2]
                for g in range(4):
                    ms = sorted(m for (g_, i_, m) in tab if g_ == g and i_ == i)
                    pb, kpb = self.nb()
                    for mi_, m in enumerate(ms):
                        bi = tab[(g, i, m)]
                        kb.pe(lambda e, pb=pb, g=g, m=m, bi=bi, W=W, mi_=mi_, nm_=len(ms), u=u, band=band: e.matmul(pb[0:64, 0:W], u[:, m, g * 64:(g + 1) * 64], band[:, bi, 0:W],
                                                                                            start=(mi_ == 0), stop=(mi_ == nm_ - 1)),
                              reads=[ku, kband], writes=[kpb])
                    self.evac(g, lambda e, pb=pb, d_ap=d_ap, g=g, W=W: e.activation(out=d_ap[0:64, g, 0:W], in_=pb[0:64, 0:W], func=AF.Copy),
                              lambda e, pb=pb, d_ap=d_ap, g=g, W=W: e.tensor_copy(out=d_ap[0:64, g, 0:W], in_=pb[0:64, 0:W]),
                              reads=[kpb], writes=[kd + ".%d" % g])
                for sub in range(nsub):
                    pb, kpb = self.nb()
                    for g in range(4):
                        kb.pe(lambda e, pb=pb, g=g, sub=sub, d_ap=d_ap: e.matmul(pb[:, g * 64:(g + 1) * 64], d_ap[0:64, g, sub * 128:(sub + 1) * 128], pw[0][0:64, g, :],
                                                                                start=True, stop=True),
                              reads=[kd + ".%d" % g, pw[1]], writes=[kpb])
                    kb.dve(lambda e, pb=pb, s_ap=s_ap, sub=sub: e.tensor_tensor(out=s_ap[:, sub, :], in0=pb[:, 0:256], in1=psc, op=ALU.mult),
                           reads=[kpb, kpsc], writes=[ks + ".%d" % sub])
                kb.dma(s.POOLO[i * W:(i + 1) * W, :].rearrange("(a p) c -> p a c", p=128), s_ap[:, 0:nsub, :],
                       reads=[ks + ".%d" % sub for sub in range(nsub)], writes=["POOLO_" + s.name])
                it += 1
    self.phase_end()


def _c1_phase(self, l):
    kb, ar, w = self.kb, self.arena, self.w
    last = (l == L - 1)
    wout = ar.alloc([8, D], BF16, "wout")
    self.load_cast_rows(wout, w["w_out"][l], 8)
    wo_ap, kwo = wout
    gout, kgout = ar.alloc([D], F32, "gout")
    kb.dma(gout, w["g_out"][l].partition_broadcast(128), writes=[kgout])
    xts = [ar.alloc([8, 512], F32, "xt") for _ in range(2)]
    att = [ar.alloc([4, 512], F32, "att") for _ in range(2)]
    pl = [ar.alloc([4, 256], F32, "pl") for _ in range(2)]
    hy = [ar.alloc([4, 256], F32, "hy") for _ in range(2)]
    junk, kjunk = ar.alloc([512], BF16, "junk")
    ss = [ar.alloc([3, 4], F32, "ss") for _ in range(2)]
    mrg = [ar.alloc([4, D], BF16, "mrg") for _ in range(2)]
    mT = [ar.alloc([8, 512], BF16, "mT") for _ in range(2)]
    seqs = self.lat + ([] if last else self.ctx)
    GR = ((0, 512, 0), (512, 256, 1), (768, 256, 2))
    tiles = [(s, i) for s in seqs for i in range(s.nt)]

    def front(it):
        s, i = tiles[it]
        W, nsub = s.W, s.W // 128
        HYO = self.HYO_lat if s.kind == "lat" else self.HYO_ctx
        t0 = i * W
        xt, kxt = xts[it % 2]
        a_ap, ka = att[it % 2]
        p_ap, kp = pl[it % 2]
        h_ap, kh = hy[it % 2]
        ss_ap, kss = ss[it % 2]
        m_ap, km = mrg[it % 2]
        t_ap, kt = mT[it % 2]
        kb.dma(xt[:, :, 0:W], s.XT[:, :, t0:t0 + W].rearrange("j p t -> p j t"), reads=["XT_%s.%d" % (s.name, i)], writes=[kxt + ".%d" % j for j in range(8)])
        kb.dma(a_ap[:, 0:nsub, :], s.ATT[t0:t0 + W, :].rearrange("(a p) c -> p a c", p=128), reads=["ATT_" + s.name], writes=[ka])
        kb.dma(p_ap[:, 0:nsub, :], s.POOLO[t0:t0 + W, :].rearrange("(a p) c -> p a c", p=128), reads=["POOLO_" + s.name], writes=[kp])
        kb.dma(h_ap[:, 0:nsub, :], HYO[t0:t0 + W, s.b * 256:(s.b + 1) * 256].rearrange("(a p) c -> p a c", p=128), reads=["HYO"], writes=[kh])
        kb.dve(lambda e: e.memset(ss_ap, 0.0), writes=[kss])
        srcs = ((a_ap, ka), (p_ap, kp), (h_ap, kh))
        for sub in range(nsub):
            for (c0, ng, gi) in GR:
                src, ksrc = srcs[gi]
                kb.act(lambda e, src=src, sub=sub, ng=ng, gi=gi: e.activation(out=junk[:, 0:ng], in_=src[:, sub, :], func=AF.Square,
                                                                            accum_out=ss_ap[:, gi, sub:sub + 1]),
                       reads=[ksrc, kss], writes=[kss, kjunk])
        for (c0, ng, gi) in GR:
            kb.act(lambda e, gi=gi, ng=ng: e.activation(out=ss_ap[:, gi, 0:nsub], in_=ss_ap[:, gi, 0:nsub], func=AF.Sqrt,
                                                        bias=self.eps[:, 0:1], scale=1.0 / ng),
                   reads=[kss, "eps"], writes=[kss])
        kb.dve(lambda e: e.reciprocal(out=ss_ap, in_=ss_ap), reads=[kss], writes=[kss])
        for sub in range(nsub):
            for (c0, ng, gi) in GR:
                src, ksrc = srcs[gi]
                kb.dve(lambda e, src=src, sub=sub, c0=c0, ng=ng, gi=gi:
                       e.scalar_tensor_tensor(out=m_ap[:, sub, c0:c0 + ng], in0=src[:, sub, :], scalar=ss_ap[:, gi, sub:sub + 1], in1=gout[:, c0:c0 + ng],
                                              op0=ALU.mult, op1=ALU.mult),
                       reads=[ksrc, kss, kgout], writes=[km + ".%d" % sub])
            pb, kpb = self.nb()
            pbb = pb.bitcast(BF16)
            for j in range(8):
                kb.pe(lambda e, pbb=pbb, sub=sub, j=j: e.transpose(pbb[:, j * 128:(j + 1) * 128], m_ap[:, sub, j * 128:(j + 1) * 128], self.ident_b),
                      reads=[km + ".%d" % sub, "ident_b"], writes=[kpb])
            self.evac(sub, lambda e, pbb=pbb, sub=sub: e.activation(out=t_ap[:, :, sub * 128:(sub + 1) * 128], in_=pbb.rearrange("p (j t) -> p j t", t=128), func=AF.Copy),
                      lambda e, pbb=pbb, sub=sub: e.tensor_copy(out=t_ap[:, :, sub * 128:(sub + 1) * 128], in_=pbb.rearrange("p (j t) -> p j t", t=128)),
                      reads=[kpb], writes=[kt + ".%d" % sub])

    def back(it):
        s, i = tiles[it]
        W, nsub = s.W, s.W // 128
        g1 = self.mod(l, 2, s.v)
        t0 = i * W
        xt, kxt = xts[it % 2]
        t_ap, kt = mT[it % 2]
        for oc in range(8):
            pb, kpb = self.nb()
            for k_ in range(8):
                kb.pe(lambda e, pb=pb, k_=k_, oc=oc: e.matmul(pb[:, 0:W], wo_ap[:, k_, oc * 128:(oc + 1) * 128], t_ap[:, k_, 0:W], start=(k_ == 0), stop=(k_ == 7)),
                      reads=[kwo + ".%d" % k_] + [kt + ".%d" % sub for sub in range(nsub)], writes=[kpb])
            kb.dve(lambda e, pb=pb, oc=oc: e.scalar_tensor_tensor(out=xt[:, oc, 0:W], in0=pb[:, 0:W], scalar=g1[:, oc:oc + 1], in1=xt[:, oc, 0:W],
                                                                op0=ALU.mult, op1=ALU.add),
                   reads=[kpb, kxt + ".%d" % oc, "modT"], writes=[kxt + ".%d" % oc])
        kb.dma(s.XT[:, :, t0:t0 + W].rearrange("j p t -> p j t"), xt[:, :, 0:W], reads=[kxt + ".%d" % j for j in range(8)], writes=["XT_%s.%d" % (s.name, i)])

    front(0)
    for it in range(len(tiles)):
        if it + 1 < len(tiles):
            front(it + 1)
        back(it)
    self.phase_end()


def _c2_phase(self, l):
    kb, ar, w = self.kb, self.arena, self.w
    last = (l == L - 1)
    w1 = ar.alloc([8, DFF], BF16, "w1")
    w2 = ar.alloc([32, D], BF16, "w2")
    self.load_cast_rows(w1, w["w_mlp1"][l], 8, split=2)
    self.load_cast_rows(w2, w["w_mlp2"][l], 32)
    w1_ap, kw1 = w1
    w2_ap, kw2 = w2
    xts = [ar.alloc([8, 512], F32, "xt") for _ in range(2)]
    rs, krs = ar.alloc([512], F32, "rs")
    hT, khT = ar.alloc([8, 512], BF16, "hT")
    hid, khid = ar.alloc([32, 512], BF16, "hid")
    tmp = hid[:, 0:16, :].rearrange("p a b -> p (a b)").bitcast(F32).rearrange("p (a b) -> p a b", b=512)
    ktmp = khid + ".tmp"
    sq = hid[:, 16:24, :]
    ksq = khid + ".sq"
    rl = [ar.alloc([512], F32, "rl")] * 2
    seqs = self.lat + ([] if last else self.ctx)
    tiles = [(s, i) for s in seqs for i in range(s.nt)]

    def load(it):
        s, i = tiles[it]
        W = s.W
        xt, kxt = xts[it % 2]
        kb.dma(xt[:, :, 0:W], s.XT[:, :, i * W:(i + 1) * W].rearrange("j p t -> p j t"), reads=["XT_%s.%d" % (s.name, i)], writes=[kxt + ".%d" % j for j in range(8)])

    load(0)
    for it in range(len(tiles)):
        s, i = tiles[it]
        W = s.W
        gm, sh, g2 = self.mod(l, 4, s.v), self.mod(l, 3, s.v), self.mod(l, 5, s.v)
        t0 = i * W
        xt, kxt = xts[it % 2]
        if it + 1 < len(tiles):
            load(it + 1)
        self.mod_norm(xt, kxt, W, gm, sh, sq, ksq, rs, krs, tmp, ktmp, hT, khT,
                      extra=lambda j: [khid + ".%d" % (2 * j), khid + ".%d" % (2 * j + 1)],
                      extra_sq=lambda ci: [khid + ".%d" % (16 + ci)])
        khT_all = [khT + ".%d" % j for j in range(8)]
        for hc in range(32):
            pb, kpb = self.nb()
            for j in range(8):
                kb.pe(lambda e, pb=pb, j=j, hc=hc, W=W: e.matmul(pb[:, 0:W], w1_ap[:, j, hc * 128:(hc + 1) * 128], hT[:, j, 0:W], start=(j == 0), stop=(j == 7)),
                      reads=[kw1 + ".%d" % j, khT_all[j]], writes=[kpb])
            r_ap, kr_ = rl[hc % 2]
            kb.act(lambda e, pb=pb, r_ap=r_ap, W=W: e.activation(out=r_ap[:, 0:W], in_=pb[:, 0:W], func=AF.Relu), reads=[kpb], writes=[kr_])
            kb.dve(lambda e, r_ap=r_ap, hc=hc, W=W: e.tensor_tensor(out=hid[:, hc, 0:W], in0=r_ap[:, 0:W], in1=r_ap[:, 0:W], op=ALU.mult),
                   reads=[kr_], writes=[khid + ".%d" % hc])
        for oc in range(8):
            pb, kpb = self.nb()
            for hc in range(32):
                kb.pe(lambda e, pb=pb, hc=hc, oc=oc, W=W: e.matmul(pb[:, 0:W], w2_ap[:, hc, oc * 128:(oc + 1) * 128], hid[:, hc, 0:W], start=(hc == 0), stop=(hc == 31)),
                      reads=[kw2 + ".%d" % hc, khid + ".%d" % hc], writes=[kpb])
            kb.dve(lambda e, pb=pb, oc=oc, W=W, g2=g2, xt=xt: e.scalar_tensor_tensor(out=xt[:, oc, 0:W], in0=pb[:, 0:W], scalar=g2[:, oc:oc + 1], in1=xt[:, oc, 0:W],
                                                                                op0=ALU.mult, op1=ALU.add),
                   reads=[kpb, kxt + ".%d" % oc, "modT"], writes=[kxt + ".%d" % oc])
        kb.dma(s.XT[:, :, t0:t0 + W].rearrange("j p t -> p j t"), xt[:, :, 0:W], reads=[kxt + ".%d" % j for j in range(8)], writes=["XT_%s.%d" % (s.name, i)])
    self.phase_end()


def _final_phase(self):
    kb, ar, w = self.kb, self.arena, self.w
    gf, kgf = ar.alloc([8], F32, "gf")
    for j in range(8):
        kb.dma(gf[:, j:j + 1], w["g_final"][j * 128:(j + 1) * 128].rearrange("(p o) -> p o", o=1), writes=[kgf])
    xts = [ar.alloc([8, 512], F32, "xt") for _ in range(2)]
    sq, ksq = ar.alloc([8, 512], BF16, "sq")
    rs, krs = ar.alloc([512], F32, "rs")
    xn, kxn = ar.alloc([8, 512], F32, "xn")
    yts = [ar.alloc([4, D], F32, "yt") for _ in range(2)]
    it = 0
    for s in self.lat:
        W = 512
        for i in range(s.nt):
            t0 = i * W
            xt, kxt = xts[it % 2]
            yt, kyt = yts[it % 2]
            kb.dma(xt, s.XT[:, :, t0:t0 + W].rearrange("j p t -> p j t"), reads=["XT_" + s.name], writes=[kxt + ".%d" % j for j in range(8)])
            self.fm_rstd([(xt[:, j, :], kxt + ".%d" % j, 128) for j in range(8)], D, W, sq, ksq, rs, krs)
            for j in range(8):
                kb.dve(lambda e, xt=xt, j=j: e.scalar_tensor_tensor(out=xn[:, j, :], in0=xt[:, j, :], scalar=gf[:, j:j + 1], in1=rs, op0=ALU.mult, op1=ALU.mult),
                       reads=[kxt + ".%d" % j, krs, kgf], writes=[kxn + ".%d" % j])
            for sub in range(4):
                pp = self.ps2[sub % 2]
                kpp = ["pb%d" % (2 * (sub % 2)), "pb%d" % (2 * (sub % 2) + 1)]
                for j in range(8):
                    kb.pe(lambda e, pp=pp, j=j, sub=sub: e.transpose(pp[:, j * 128:(j + 1) * 128], xn[:, j, sub * 128:(sub + 1) * 128], self.ident_f),
                          reads=[kxn + ".%d" % j, "ident_f"], writes=[kpp[j // 4]])
                self.evac(sub, lambda e, pp=pp, yt=yt, sub=sub: e.activation(out=yt[:, sub, :], in_=pp, func=AF.Copy),
                          lambda e, pp=pp, yt=yt, sub=sub: e.tensor_copy(out=yt[:, sub, :], in_=pp),
                          reads=kpp, writes=[kyt + ".%d" % sub])
            kb.dma(self.y_out[s.b][t0:t0 + W, :].rearrange("(a p) d -> p a d", p=128), yt, reads=[kyt + ".%d" % sub for sub in range(4)], writes=["y"])
            it += 1
    self.phase_end()


Prog.phase_pool = _pool_phase
Prog.phase_c1 = _c1_phase
Prog.phase_c2 = _c2_phase
Prog.phase_final = _final_phase


def _hy_h0(self, l, nm, seqs, n):
    kb, ar, w = self.kb, self.arena, self.w
    NT = n // 128
    cw, kcw = ar.alloc([6, 3], F32, "cw")
    cb, kcb = ar.alloc([6], F32, "cb")
    for c6 in range(6):
        for k_ in range(3):
            kb.dma(cw[:, c6, k_:k_ + 1], w["hy_conv_w"][l][k_, c6 * 128:(c6 + 1) * 128].rearrange("(p o) -> p o", o=1), writes=[kcw])
        kb.dma(cb[:, c6:c6 + 1], w["hy_conv_b"][l][c6 * 128:(c6 + 1) * 128].rearrange("(p o) -> p o", o=1), writes=[kcb])
    hp = [ar.alloc([n + 2], F32, "hp") for _ in range(2)]
    acc = [ar.alloc([n], F32, "acc") for _ in range(2)]
    ucb = [ar.alloc([n], BF16, "ucb") for _ in range(2)]
    tmk = [ar.alloc([NT, 512], BF16, "tmk") for _ in range(2)]
    dests = (getattr(self, "HV_" + nm), getattr(self, "HX1_" + nm), getattr(self, "HX2_" + nm))
    it = 0
    ngr = (NT + 7) // 8
    for kind in range(3):
        t_ap, kt = tmk[kind % 2]
        keys = []
        for si, s in enumerate(seqs):
            for half in range(2):
                c6 = kind * 2 + half
                col0 = si * 256 + half * 128
                h_ap, kh = hp[it % 2]
                a_ap, ka = acc[it % 2]
                u_ap, ku = ucb[it % 2]
                kb.dma(h_ap, s.HYP[c6], reads=["HYP_" + s.name], writes=[kh])
                kb.act(lambda e, h_ap=h_ap, a_ap=a_ap, c6=c6: e.activation(out=a_ap, in_=h_ap[:, 1:n + 1], func=AF.Identity, bias=cb[:, c6:c6 + 1], scale=cw[:, c6, 1:2]),
                       reads=[kh, kcw, kcb], writes=[ka])
                kb.dve(lambda e, h_ap=h_ap, a_ap=a_ap, c6=c6: e.scalar_tensor_tensor(out=a_ap, in0=h_ap[:, 0:n], scalar=cw[:, c6, 0:1], in1=a_ap, op0=ALU.mult, op1=ALU.add),
                       reads=[kh, kcw, ka], writes=[ka])
                kb.dve(lambda e, h_ap=h_ap, a_ap=a_ap, u_ap=u_ap, c6=c6: e.scalar_tensor_tensor(out=u_ap, in0=h_ap[:, 2:n + 2], scalar=cw[:, c6, 2:3], in1=a_ap, op0=ALU.mult, op1=ALU.add),
                       reads=[kh, kcw, ka], writes=[ku])
                for g8 in range(ngr):
                    ng = min(8, NT - g8 * 8)
                    pb, kpb = self.nb()
                    pbb = pb.bitcast(BF16)
                    for i in range(ng):
                        tt = g8 * 8 + i
                        kb.pe(lambda e, pbb=pbb, u_ap=u_ap, i=i, tt=tt: e.transpose(pbb[:, i * 128:(i + 1) * 128], u_ap[:, tt * 128:(tt + 1) * 128], self.ident_b),
                              reads=[ku, "ident_b"], writes=[kpb])
                    key = kt + ".%d.%d" % (col0, g8)
                    keys.append(key)
                    self.evac(g8, lambda e, pbb=pbb, t_ap=t_ap, g8=g8, ng=ng, col0=col0: e.activation(out=t_ap[:, g8 * 8:g8 * 8 + ng, col0:col0 + 128], in_=pbb[:, 0:ng * 128].rearrange("p (a c) -> p a c", c=128), func=AF.Copy),
                              lambda e, pbb=pbb, t_ap=t_ap, g8=g8, ng=ng, col0=col0: e.tensor_copy(out=t_ap[:, g8 * 8:g8 * 8 + ng, col0:col0 + 128], in_=pbb[:, 0:ng * 128].rearrange("p (a c) -> p a c", c=128)),
                              reads=[kpb], writes=[key])
                it += 1
        kb.dma(dests[kind].rearrange("(a p) c -> p a c", p=128), t_ap, reads=keys, writes=["HU_" + nm])
    self.phase_end()


def _hy_h1(self, l, nm, n):
    kb, ar, w = self.kb, self.arena, self.w
    NT = n // 128
    N2 = 2 * n
    CW = min(512, n)
    zT, kzT = ar.alloc([n], F32, "zT")
    kb.dma(zT[0:HY_EMB, :], self.k["hyz_" + nm], writes=[kzT])
    w1s, kw1s = ar.alloc([HY_FFN], F32, "w1s")
    w2s, kw2s = ar.alloc([HY_FFN], F32, "w2s")
    w3s, kw3s = ar.alloc([1024], F32, "w3s")
    kb.dma(w1s[0:HY_EMB, :], w["hy_f_w1"][l], writes=[kw1s])
    kb.dma(w2s[0:HY_FFN, :], w["hy_f_w2"][l], writes=[kw2s])
    kb.dma(w3s[0:HY_FFN, :], w["hy_f_w3"][l], writes=[kw3s])
    par, kpar = ar.alloc([6], F32, "par")
    for ci, nm_ in enumerate(("hy_f_b1", "hy_f_freq1", "hy_f_b2", "hy_f_freq2")):
        kb.dma(par[0:64, ci:ci + 1], w[nm_][l].rearrange("(p o) -> p o", o=1), writes=[kpar])
    kb.dve(lambda e: e.tensor_tensor(out=par[0:64, 4:5], in0=par[0:64, 0:1], in1=par[0:64, 1:2], op=ALU.mult), reads=[kpar], writes=[kpar])
    kb.dve(lambda e: e.tensor_tensor(out=par[0:64, 5:6], in0=par[0:64, 2:3], in1=par[0:64, 3:4], op=ALU.mult), reads=[kpar], writes=[kpar])
    h1T, kh1 = ar.alloc([n], F32, "h1T")
    h2T, kh2 = ar.alloc([n], F32, "h2T")
    arg, karg = ar.alloc([512], F32, "arg")
    kk, kkk = ar.alloc([512], F32, "kk")

    def layer(srcT, ksrc, wS, kwS, K, fcol, fbcol, dstT, kdst):
        for ch in range(n // CW):
            c0 = ch * CW
            pb, kpb = self.nb()
            kb.pe(lambda e, pb=pb, c0=c0: e.matmul(pb[0:64, 0:CW], wS[0:K, 0:64], srcT[0:K, c0:c0 + CW], start=True, stop=True),
                  reads=[ksrc, kwS], writes=[kpb])
            kb.act(lambda e, pb=pb: e.activation(out=arg[0:64, 0:CW], in_=pb[0:64, 0:CW], func=AF.Identity, bias=par[0:64, fbcol:fbcol + 1], scale=par[0:64, fcol:fcol + 1]),
                   reads=[kpb, kpar], writes=[karg])
            kb.dve(lambda e: e.tensor_scalar(out=kk[0:64, 0:CW], in0=arg[0:64, 0:CW], scalar1=1.0 / (2 * math.pi), scalar2=MAGIC, op0=ALU.mult, op1=ALU.add),
                   reads=[karg], writes=[kkk])
            kb.dve(lambda e: e.tensor_scalar(out=kk[0:64, 0:CW], in0=kk[0:64, 0:CW], scalar1=-MAGIC, scalar2=-2 * math.pi, op0=ALU.add, op1=ALU.mult),
                   reads=[kkk], writes=[kkk])
            kb.dve(lambda e: e.tensor_tensor(out=arg[0:64, 0:CW], in0=arg[0:64, 0:CW], in1=kk[0:64, 0:CW], op=ALU.add), reads=[karg, kkk], writes=[karg])
            kb.act(lambda e, c0=c0: e.activation(out=dstT[0:64, c0:c0 + CW], in_=arg[0:64, 0:CW], func=AF.Sin), reads=[karg], writes=[kdst])
    layer(zT, kzT, w1s, kw1s, HY_EMB, 1, 4, h1T, kh1)
    layer(h1T, kh1, w2s, kw2s, HY_FFN, 3, 5, h2T, kh2)
    HP, kHP = ar.alloc([NT, 512], BF16, "HP")
    HM, kHM = ar.alloc([NT, 512], BF16, "HM")
    dec = [ar.alloc([256], F32, "dec") for _ in range(2)]
    tp = [ar.alloc([4, 256], F32, "tp") for _ in range(2)]
    ab = [ar.alloc([4, 256], F32, "ab") for _ in range(2)]
    psZ = self.ps2[3]
    kZ = ["pb6", "pb7"]
    for tt in range(NT):
        pt = self.ps2[tt % 2]
        kpt = ["pb%d" % (2 * (tt % 2)), "pb%d" % (2 * (tt % 2) + 1)]
        d_ap, kd = dec[tt % 2]
        t_ap, ktp = tp[tt % 2]
        a_ap, kab = ab[tt % 2]
        for hf in range(2):
            kb.pe(lambda e, pt=pt, hf=hf, tt=tt: e.matmul(pt[:, hf * 512:(hf + 1) * 512], h2T[0:64, tt * 128:(tt + 1) * 128], w3s[0:64, hf * 512:(hf + 1) * 512], start=True, stop=True),
                  reads=[kh2, kw3s], writes=[kpt[hf]])
        kb.dma(d_ap, self.k["hydec_" + nm][tt * 128:(tt + 1) * 128, :], writes=[kd])
        for q in range(4):
            kb.dve(lambda e, pt=pt, t_ap=t_ap, d_ap=d_ap, q=q: e.tensor_tensor(out=t_ap[:, q, :], in0=pt[:, q * 256:(q + 1) * 256], in1=d_ap, op=ALU.mult),
                   reads=[kpt[q // 2], kd], writes=[ktp])
        if tt == 0:
            for q in (1, 3):
                kb.dve(lambda e, t_ap=t_ap, q=q: e.memset(t_ap[0:1, q, :], 0.0), reads=[ktp], writes=[ktp])
        kb.act(lambda e, t_ap=t_ap, a_ap=a_ap: e.activation(out=a_ap, in_=t_ap, func=AF.Abs), reads=[ktp], writes=[kab])
        for hf in range(2):
            kb.pe(lambda e, a_ap=a_ap, hf=hf, tt=tt: e.matmul(psZ[:, hf * 512:(hf + 1) * 512], self.ones_f, a_ap[:, 2 * hf:2 * hf + 2, :].rearrange("p a c -> p (a c)"),
                                                               start=(tt == 0), stop=(tt == NT - 1)),
                  reads=[kab, "ones_f"], writes=[kZ[hf]])
        for o in range(2):
            kb.dve(lambda e, t_ap=t_ap, o=o, tt=tt: e.tensor_tensor(out=HP[:, tt, o * 256:(o + 1) * 256], in0=t_ap[:, 2 * o, :], in1=t_ap[:, 2 * o + 1, :], op=ALU.add),
                   reads=[ktp], writes=[kHP + ".%d" % tt])
            kb.pool(lambda e, t_ap=t_ap, o=o, tt=tt: e.tensor_tensor(out=HM[:, tt, o * 256:(o + 1) * 256], in0=t_ap[:, 2 * o, :], in1=t_ap[:, 2 * o + 1, :], op=ALU.subtract),
                    reads=[ktp], writes=[kHM + ".%d" % tt])
    zc, kzc = ar.alloc([1024], F32, "zc")
    rz, krz = ar.alloc([512], F32, "rz")
    kb.act(lambda e: e.activation(out=zc, in_=psZ, func=AF.Copy), reads=kZ, writes=[kzc])
    for o in range(2):
        kb.dve(lambda e, o=o: e.tensor_tensor(out=rz[:, o * 256:(o + 1) * 256], in0=zc[:, o * 512:o * 512 + 256], in1=zc[:, o * 512 + 256:(o + 1) * 512], op=ALU.add),
               reads=[kzc], writes=[krz])
    kb.dve(lambda e: e.reciprocal(out=rz, in_=rz), reads=[krz], writes=[krz])
    cf, kcf = ar.alloc([2], F32, "cf")
    kb.dve(lambda e: e.memset(cf, 2.0 / N2), writes=[kcf])
    kb.dve(lambda e: e.memset(cf[0:1, 0:1], 1.0 / N2), reads=[kcf], writes=[kcf])
    Cb = [ar.alloc([NT, 128], BF16, "Cb") for _ in range(2)]
    Sb = [ar.alloc([NT, 128], BF16, "Sb") for _ in range(2)]
    hs = [ar.alloc([2, 512], BF16, "hs") for _ in range(2)]
    HS = getattr(self, "HS_" + nm)
    kHPall = [kHP + ".%d" % tt for tt in range(NT)]
    kHMall = [kHM + ".%d" % tt for tt in range(NT)]
    for ft in range(NT):
        c_ap, kc = Cb[ft % 2]
        s_ap, ks = Sb[ft % 2]
        h_ap, kh = hs[ft % 2]
        kb.dma(c_ap, self.k["dftc_" + nm][ft], writes=[kc])
        kb.dma(s_ap, self.k["dfts_" + nm][ft], writes=[ks])
        pre, kpre = self.nb()
        pim, kpim = self.nb()
        for tt in range(NT):
            kb.pe(lambda e, pre=pre, c_ap=c_ap, tt=tt: e.matmul(pre, c_ap[:, tt, :], HP[:, tt, :], start=(tt == 0), stop=(tt == NT - 1)), reads=[kc, kHPall[tt]], writes=[kpre])
        for tt in range(NT):
            kb.pe(lambda e, pim=pim, s_ap=s_ap, tt=tt: e.matmul(pim, s_ap[:, tt, :], HM[:, tt, :], start=(tt == 0), stop=(tt == NT - 1)), reads=[ks, kHMall[tt]], writes=[kpim])
        ccol = 0 if ft == 0 else 1
        kb.dve(lambda e, pre=pre, h_ap=h_ap, ccol=ccol: e.scalar_tensor_tensor(out=h_ap[:, 0, :], in0=pre, scalar=cf[:, ccol:ccol + 1], in1=rz, op0=ALU.mult, op1=ALU.mult),
               reads=[kpre, kcf, krz], writes=[kh + ".0"])
        kb.dve(lambda e, pim=pim, h_ap=h_ap, ccol=ccol: e.scalar_tensor_tensor(out=h_ap[:, 1, :], in0=pim, scalar=cf[:, ccol:ccol + 1], in1=rz, op0=ALU.mult, op1=ALU.mult),
               reads=[kpim, kcf, krz], writes=[kh + ".1"])
        kb.dma(HS[:, ft].rearrange("r p c -> p r c"), h_ap, reads=[kh + ".0", kh + ".1"], writes=["HS_" + nm])
    pmf, kpmf = ar.alloc([1], F32, "pmf")
    pmc, kpmc = ar.alloc([1], BF16, "pmc")
    kb.dma(pmf, self.k["pm1_" + nm][0:128].rearrange("(p o) -> p o", o=1), writes=[kpmf])
    kb.act(lambda e: e.activation(out=pmc, in_=pmf, func=AF.Copy), reads=[kpmf], writes=[kpmc])
    psn, kpsn = self.nb()
    for tt in range(NT):
        kb.pe(lambda e, tt=tt: e.matmul(psn[0:1, :], pmc[:, 0:1], HP[:, tt, :], start=(tt == 0), stop=(tt == NT - 1)), reads=[kpmc, kHPall[tt]], writes=[kpsn])
    hn, khn = ar.alloc([512], F32, "hn")
    kb.dve(lambda e: e.scalar_tensor_tensor(out=hn[0:1, :], in0=psn[0:1, :], scalar=1.0 / N2, in1=rz[0:1, :], op0=ALU.mult, op1=ALU.mult),
           reads=[kpsn, krz], writes=[khn])
    kb.dma(getattr(self, "HNQ_" + nm), hn[0:1, :], reads=[khn], writes=["HNQ_" + nm])
    self.phase_end()


def _hy_h2(self, l, nm, seqs, n):
    kb, ar, w = self.kb, self.arena, self.w
    NT = n // 128
    HV, HX1, HX2 = getattr(self, "HV_" + nm), getattr(self, "HX1_" + nm), getattr(self, "HX2_" + nm)
    HYO, HS, HNQ = getattr(self, "HYO_" + nm), getattr(self, "HS_" + nm), getattr(self, "HNQ_" + nm)
    U, kU = ar.alloc([NT, 512], BF16, "U")
    Zb, kZb = ar.alloc([NT, 512], BF16, "Zb")
    Y, kY = ar.alloc([2, NT, 512], BF16, "Y")
    kb.dma(U, HV.rearrange("(a p) c -> p a c", p=128), writes=[kU + ".%d" % tt for tt in range(NT)])
    Cb = [ar.alloc([NT, 128], BF16, "Cb") for _ in range(2)]
    Sb = [ar.alloc([NT, 128], BF16, "Sb") for _ in range(2)]
    hsb = [ar.alloc([2, 256], BF16, "hsb") for _ in range(2)]
    bias, kbias = ar.alloc([2, 256], F32, "bias")
    kb.dma(bias, w["hy_bias"][l].rearrange("o c -> (o c)").partition_broadcast(128), writes=[kbias])
    hnq, khnq = ar.alloc([512], F32, "hnq")
    kb.dma(hnq[0:1, :], HNQ, writes=[khnq])
    pmf, kpmf = ar.alloc([1], F32, "pmf")
    pmc, kpmc = ar.alloc([1], BF16, "pmc")
    pmrf, kpmrf = ar.alloc([128], F32, "pmrf")
    pmr, kpmr = ar.alloc([128], BF16, "pmr")
    kb.dma(pmf, self.k["pm1_" + nm][0:128].rearrange("(p o) -> p o", o=1), writes=[kpmf])
    kb.act(lambda e: e.activation(out=pmc, in_=pmf, func=AF.Copy), reads=[kpmf], writes=[kpmc])
    kb.dma(pmrf[0:1, :], self.k["pm1_" + nm][0:128].rearrange("(o t) -> o t", o=1), writes=[kpmrf])
    kb.act(lambda e: e.activation(out=pmr[0:1, :], in_=pmrf[0:1, :], func=AF.Copy), reads=[kpmrf], writes=[kpmr])
    tq = [[ar.alloc([256], F32, "tq") for _ in range(4)] for _ in range(2)]
    ynq, kynq = ar.alloc([512], BF16, "ynq")
    gt = [ar.alloc([512], BF16, "gt") for _ in range(2)]
    tb = [ar.alloc([512], F32, "tb") for _ in range(2)]
    t2b = [ar.alloc([512], F32, "t2b") for _ in range(2)]
    ostg = [ar.alloc([512], F32, "ostg") for _ in range(2)]
    for o in range(2):
        src, ksrc = (U, kU) if o == 0 else (Zb, kZb)
        ksrc_all = [ksrc + ".%d" % tt for tt in range(NT)]
        for ft in range(NT):
            c_ap, kc = Cb[ft % 2]
            s_ap, ks = Sb[ft % 2]
            h_ap, kh = hsb[ft % 2]
            kb.dma(c_ap, self.k["dftc_" + nm][ft], writes=[kc])
            kb.dma(s_ap, self.k["dfts_" + nm][ft], writes=[ks])
            kb.dma(h_ap, HS[:, ft, :, o * 256:(o + 1) * 256].rearrange("r p c -> p r c"), writes=[kh])
            pre, kpre = self.nb()
            pim, kpim = self.nb()
            for tt in range(NT):
                kb.pe(lambda e, pre=pre, c_ap=c_ap, tt=tt, src=src: e.matmul(pre, c_ap[:, tt, :], src[:, tt, :], start=(tt == 0), stop=(tt == NT - 1)),
                      reads=[kc, ksrc_all[tt]], writes=[kpre])
            for tt in range(NT):
                kb.pe(lambda e, pim=pim, s_ap=s_ap, tt=tt, src=src: e.matmul(pim, s_ap[:, tt, :], src[:, tt, :], start=(tt == 0), stop=(tt == NT - 1)),
                      reads=[ks, ksrc_all[tt]], writes=[kpim])
            for b in range(2):
                (t1, k1), (t2, k2), (t3, k3), (t4, k4) = tq[b]
                bs = slice(b * 256, (b + 1) * 256)
                kb.dve(lambda e, pre=pre, h_ap=h_ap, t1=t1, bs=bs: e.tensor_tensor(out=t1, in0=pre[:, bs], in1=h_ap[:, 0, :], op=ALU.mult), reads=[kpre, kh], writes=[k1])
                kb.dve(lambda e, pim=pim, h_ap=h_ap, t2=t2, bs=bs: e.tensor_tensor(out=t2, in0=pim[:, bs], in1=h_ap[:, 1, :], op=ALU.mult), reads=[kpim, kh], writes=[k2])
                kb.dve(lambda e, pre=pre, h_ap=h_ap, t3=t3, bs=bs: e.tensor_tensor(out=t3, in0=pre[:, bs], in1=h_ap[:, 1, :], op=ALU.mult), reads=[kpre, kh], writes=[k3])
                kb.dve(lambda e, pim=pim, h_ap=h_ap, t4=t4, bs=bs: e.tensor_tensor(out=t4, in0=pim[:, bs], in1=h_ap[:, 0, :], op=ALU.mult), reads=[kpim, kh], writes=[k4])
                kb.pool(lambda e, t1=t1, t2=t2, ft=ft, bs=bs: e.tensor_tensor(out=Y[:, 0, ft, bs], in0=t1, in1=t2, op=ALU.subtract), reads=[k1, k2], writes=[kY + ".0.%d" % ft])
                kb.pool(lambda e, t3=t3, t4=t4, ft=ft, bs=bs: e.tensor_tensor(out=Y[:, 1, ft, bs], in0=t3, in1=t4, op=ALU.add), reads=[k3, k4], writes=[kY + ".1.%d" % ft])
        psn, kpsn = self.nb()
        for tt in range(NT):
            kb.pe(lambda e, psn=psn, tt=tt, src=src: e.matmul(psn[0:1, :], pmc[:, 0:1], src[:, tt, :], start=(tt == 0), stop=(tt == NT - 1)), reads=[kpmc, ksrc_all[tt]], writes=[kpsn])
        for b in range(2):
            kb.dve(lambda e, psn=psn, b=b, o=o: e.tensor_tensor(out=ynq[0:1, b * 256:(b + 1) * 256], in0=psn[0:1, b * 256:(b + 1) * 256], in1=hnq[0:1, o * 256:(o + 1) * 256], op=ALU.mult),
                   reads=[kpsn, khnq], writes=[kynq])
        gateD = HX1 if o == 0 else HX2
        kY0 = [kY + ".0.%d" % ft for ft in range(NT)]
        kY1 = [kY + ".1.%d" % ft for ft in range(NT)]
        for j in range(NT):
            c_ap, kc = Cb[j % 2]
            s_ap, ks = Sb[j % 2]
            g_ap, kg = gt[j % 2]
            tb_ap, ktb = tb[j % 2]
            t2_ap, kt2 = t2b[j % 2]
            o_ap, ko = ostg[j % 2]
            kb.dma(c_ap, self.k["dftc_" + nm][j], writes=[kc])
            kb.dma(s_ap, self.k["dfts_" + nm][j], writes=[ks])
            kb.dma(g_ap, gateD[j * 128:(j + 1) * 128, :], writes=[kg])
            py, kpy = self.nb()
            for ft in range(NT):
                kb.pe(lambda e, py=py, c_ap=c_ap, ft=ft: e.matmul(py, c_ap[:, ft, :], Y[:, 0, ft, :], start=(ft == 0), stop=False), reads=[kc, kY0[ft]], writes=[kpy])
                kb.pe(lambda e, py=py, s_ap=s_ap, ft=ft: e.matmul(py, s_ap[:, ft, :], Y[:, 1, ft, :], start=False, stop=False), reads=[ks, kY1[ft]], writes=[kpy])
            kb.pe(lambda e, py=py: e.matmul(py, pmr[0:1, :], ynq[0:1, :], start=False, stop=True), reads=[kpmr, kynq], writes=[kpy])
            for b in range(2):
                kb.pool(lambda e, tb_ap=tb_ap, src=src, j=j, b=b, o=o: e.tensor_tensor(out=tb_ap[:, b * 256:(b + 1) * 256], in0=src[:, j, b * 256:(b + 1) * 256], in1=bias[:, o, :], op=ALU.mult),
                        reads=[ksrc_all[j], kbias], writes=[ktb])
            kb.dve(lambda e, py=py, tb_ap=tb_ap, t2_ap=t2_ap: e.tensor_tensor(out=t2_ap, in0=py, in1=tb_ap, op=ALU.add), reads=[kpy, ktb], writes=[kt2])
            if o == 0:
                kb.pool(lambda e, t2_ap=t2_ap, g_ap=g_ap, j=j: e.tensor_tensor(out=Zb[:, j, :], in0=t2_ap, in1=g_ap, op=ALU.mult), reads=[kt2, kg], writes=[kZb + ".%d" % j])
            else:
                kb.pool(lambda e, t2_ap=t2_ap, g_ap=g_ap, o_ap=o_ap: e.tensor_tensor(out=o_ap, in0=t2_ap, in1=g_ap, op=ALU.mult), reads=[kt2, kg], writes=[ko])
                kb.dma(HYO[j * 128:(j + 1) * 128, :], o_ap, reads=[ko], writes=["HYO"])
    self.phase_end()


def _hyena_phase(self, l):
    last = (l == L - 1)
    groups = [("lat", self.lat, S)] + ([] if last else [("ctx", self.ctx, CL)])
    for nm, seqs, n in groups:
        self.hy_h0(l, nm, seqs, n)
        self.hy_h1(l, nm, n)
        self.hy_h2(l, nm, seqs, n)


Prog.hy_h0 = _hy_h0
Prog.hy_h1 = _hy_h1
Prog.hy_h2 = _hy_h2
Prog.phase_hyena = _hyena_phase


def build_program(nc, dbg=(), scopes=False):
    P = Prog(nc, dbg=dbg)
    kb = P.kb
    kb.scopes = scopes
    kb.phase = "mod"
    P.declare()
    P.phase_mod()
    kb.phase = "t0"
    P.phase_t0()
    for l in range(L):
        kb.phase = "a%d" % l
        P.phase_a(l)
        kb.phase = "pool%d" % l
        P.phase_pool(l)
        kb.phase = "hy%d" % l
        P.phase_hyena(l)
        kb.phase = "attn%d" % l
        P.phase_attn(l)
        kb.phase = "c1_%d" % l
        P.phase_c1(l)
        kb.phase = "c2_%d" % l
        P.phase_c2(l)
    kb.phase = "final"
    P.phase_final()
    kb.emit()
    return P


_PROG_CACHE = {}


def kernel(**inputs):
    shared = _shared_inputs(inputs)
    nc = bass.Bass("TRN2", target_bir_lowering=False)
    P = build_program(nc)
    in_maps = []
    for core in range(NCORES):
        m = _core_inputs(inputs, core, shared)
        in_maps.append({k_: v for k_, v in m.items() if k_ in P.dram})
    res = run_bass_kernel_spmd(nc, in_maps, core_ids=list(range(NCORES)))
    out = np.concatenate([np.asarray(r["y"], dtype=np.float32) for r in res.results], axis=0)
    return out
```
